# Optimizing a Trainium2 kernel written in Bass

```python
import math
import jax, jax.numpy as jnp
from jax import lax
import numpy as np

D_MODEL = 1024
BATCH = 2
SEQ = 16384
DEPTH = 4

A_HEADS = 8
A_KV_HEADS = 2
A_HEAD_DIM = 64
A_WINDOW = 128
ROPE_THETA = 500000.0
A_ROPE_DIM = A_HEAD_DIM // 4
B_CHANNELS = D_MODEL // 2
B_CONV = 31
C_HEADS = 16
C_HEAD_DIM = 64
C_INNER = C_HEADS * C_HEAD_DIM
C_GROUPS = 2
C_STATE = 128
C_CONV = 4
C_CHUNK = 128
D_HEADS = 4
D_QK_DIM = 128
D_V_DIM = 256
D_CHUNK = 128
RET_THETA = 10000.0
FFN_HIDDEN = -(-(8 * D_MODEL) // (3 * 256)) * 256
ALPHA = (2 * DEPTH) ** 0.25
BETA = (8 * DEPTH) ** -0.25
NORM_EPS = 1e-5

N_EVEN = (DEPTH + 1) // 2
N_ODD = DEPTH // 2

EVEN_SPLITS = (A_HEADS * A_HEAD_DIM, A_KV_HEADS * A_HEAD_DIM, A_KV_HEADS * A_HEAD_DIM, 2 * B_CHANNELS)
EVEN_IN = sum(EVEN_SPLITS)
EVEN_MIX = A_HEADS * A_HEAD_DIM + B_CHANNELS
C_XBC = C_INNER + 2 * C_GROUPS * C_STATE
ODD_SPLITS = (C_INNER, C_XBC, C_HEADS, D_HEADS * D_QK_DIM, D_HEADS * D_QK_DIM, D_HEADS * D_V_DIM, D_HEADS * D_V_DIM)
ODD_IN = sum(ODD_SPLITS)
ODD_MIX = C_INNER + D_HEADS * D_V_DIM

kernel_name = "hybrid_swa_conformer_mamba2_retention_deepnorm"


def _split(t, sizes):
    idx = [int(v) for v in np.cumsum(sizes)[:-1]]
    return jnp.split(t, idx, axis=-1)


def layer_norm(x, g, b):
    xf = x.astype(jnp.float32)
    mu = jnp.mean(xf, axis=-1, keepdims=True)
    var = jnp.mean(jnp.square(xf - mu), axis=-1, keepdims=True)
    y = (xf - mu) * lax.rsqrt(var + NORM_EPS) * g.astype(jnp.float32) + b.astype(jnp.float32)
    return y.astype(x.dtype)


def causal_depthwise_conv(x, w, b):
    K, C = w.shape
    y = lax.conv_general_dilated(
        x, w[:, None, :].astype(x.dtype), window_strides=(1,), padding=[(K - 1, 0)],
        dimension_numbers=("NWC", "WIO", "NWC"), feature_group_count=C)
    return y + b.astype(x.dtype)


def apply_rotary(t, inv_freq):
    half = inv_freq.shape[0]
    S = t.shape[1]
    ang = jnp.arange(S, dtype=jnp.float32)[:, None] * inv_freq[None, :]
    cos = jnp.cos(ang)[None, :, None, :]
    sin = jnp.sin(ang)[None, :, None, :]
    tf = t.astype(jnp.float32)
    t1, t2, rest = tf[..., :half], tf[..., half:2 * half], tf[..., 2 * half:]
    out = jnp.concatenate([t1 * cos - t2 * sin, t2 * cos + t1 * sin, rest], axis=-1)
    return out.astype(t.dtype)


def swa_sink_attention(q, k, v, sinks):
    Bsz, S, Hq, Dh = q.shape
    Hkv = k.shape[2]
    G = Hq // Hkv
    W = A_WINDOW
    nb = S // W
    qb = q.reshape(Bsz, nb, W, Hkv, G, Dh)

    def with_prev(t):
        tb = t.reshape(Bsz, nb, W, Hkv, Dh)
        prev = jnp.pad(tb, ((0, 0), (1, 0), (0, 0), (0, 0), (0, 0)))[:, :-1]
        return jnp.concatenate([prev, tb], axis=2)

    kk, vv = with_prev(k), with_prev(v)
    s = jnp.einsum("bnqhgd,bnkhd->bnhgqk", qb, kk).astype(jnp.float32) * (Dh ** -0.5)
    i = jnp.arange(W)[:, None]
    j = jnp.arange(2 * W)[None, :]
    n = jnp.arange(nb)[:, None, None]
    valid = (j > i) & (j <= i + W) & (n * W + j >= W)
    s = jnp.where(valid[None, :, None, None], s, -jnp.inf)
    sink = sinks.astype(jnp.float32).reshape(Hkv, G)[None, None, :, :, None, None]
    m = jnp.maximum(jnp.max(s, axis=-1, keepdims=True), sink)
    p = jnp.exp(s - m)
    denom = jnp.sum(p, axis=-1, keepdims=True) + jnp.exp(sink - m)
    o = jnp.einsum("bnhgqk,bnkhd->bnqhgd", (p / denom).astype(v.dtype), vv)
    return o.reshape(Bsz, S, Hq * Dh)


def conformer_conv_branch(u, dw_w, dw_b, cn_g, cn_b):
    a, gate = jnp.split(u, 2, axis=-1)
    h = a * jax.nn.sigmoid(gate)
    h = causal_depthwise_conv(h, dw_w, dw_b)
    h = layer_norm(h, cn_g, cn_b)
    return jax.nn.silu(h)


def even_mixer(x, w_in, sinks, dw_w, dw_b, cn_g, cn_b, w_out):
    Bsz, S, _ = x.shape
    q, k, v, u = _split(x @ w_in, EVEN_SPLITS)
    half = A_ROPE_DIM // 2
    inv = jnp.power(jnp.float32(ROPE_THETA), -jnp.arange(half, dtype=jnp.float32) / half)
    q = apply_rotary(q.reshape(Bsz, S, A_HEADS, A_HEAD_DIM), inv)
    k = apply_rotary(k.reshape(Bsz, S, A_KV_HEADS, A_HEAD_DIM), inv)
    v = v.reshape(Bsz, S, A_KV_HEADS, A_HEAD_DIM)
    y_attn = swa_sink_attention(q, k, v, sinks).astype(x.dtype)
    y_conv = conformer_conv_branch(u, dw_w, dw_b, cn_g, cn_b).astype(x.dtype)
    return jnp.concatenate([y_attn, y_conv], axis=-1) @ w_out


def ssd_chunked(x, dt, a, bm, cm):
    Bsz, S, _ = x.shape
    L = C_CHUNK
    nc = S // L
    Hg = C_HEADS // C_GROUPS
    f32 = jnp.float32
    xh = x.astype(f32).reshape(Bsz, nc, L, C_GROUPS, Hg, C_HEAD_DIM)
    dtc = dt.reshape(Bsz, nc, L, C_GROUPS, Hg)
    bc = bm.astype(f32).reshape(Bsz, nc, L, C_GROUPS, C_STATE)
    cc = cm.astype(f32).reshape(Bsz, nc, L, C_GROUPS, C_STATE)
    acs = jnp.cumsum(dtc * a.reshape(C_GROUPS, Hg), axis=2)
    xdt = xh * dtc[..., None]
    seg = acs[:, :, :, None] - acs[:, :, None, :]
    causal = jnp.tril(jnp.ones((L, L), dtype=bool))
    decay = jnp.exp(jnp.where(causal[None, None, :, :, None, None], seg, -jnp.inf))
    cb = jnp.einsum("bclgn,bcsgn->bclsg", cc, bc)
    y_diag = jnp.einsum("bclsgh,bcsghp->bclghp", cb[..., None] * decay, xdt)
    to_end = jnp.exp(acs[:, :, -1:] - acs)
    states = jnp.einsum("bclgn,bclghp->bcghpn", bc, xdt * to_end[..., None])
    chunk_decay = jnp.exp(acs[:, :, -1])

    def step(carry, inp):
        st, dec = inp
        return carry * dec[..., None, None] + st, carry

    init = jnp.zeros((Bsz, C_GROUPS, Hg, C_HEAD_DIM, C_STATE), f32)
    _, prev = lax.scan(step, init, (jnp.moveaxis(states, 1, 0), jnp.moveaxis(chunk_decay, 1, 0)))
    prev = jnp.moveaxis(prev, 0, 1)
    y_off = jnp.einsum("bclgn,bcghpn->bclghp", cc, prev) * jnp.exp(acs)[..., None]
    return (y_diag + y_off).reshape(Bsz, S, C_INNER)


def mamba2_branch(z, xbc, dt, conv_w, conv_b, dt_bias, a_log, d_skip, norm_g):
    Bsz, S, _ = z.shape
    f32 = jnp.float32
    xbc = jax.nn.silu(causal_depthwise_conv(xbc, conv_w, conv_b))
    xs, bm, cm = _split(xbc, (C_INNER, C_GROUPS * C_STATE, C_GROUPS * C_STATE))
    dt = jax.nn.softplus(dt.astype(f32) + dt_bias.astype(f32))
    a = -jnp.exp(a_log.astype(f32))
    y = ssd_chunked(xs, dt, a, bm, cm)
    y = y + xs.astype(f32) * jnp.repeat(d_skip.astype(f32), C_HEAD_DIM)
    y = y * jax.nn.silu(z.astype(f32))
    yg = y.reshape(Bsz, S, C_GROUPS, C_INNER // C_GROUPS)
    yg = yg * lax.rsqrt(jnp.mean(yg * yg, axis=-1, keepdims=True) + NORM_EPS)
    return (yg.reshape(Bsz, S, C_INNER) * norm_g.astype(f32)).astype(z.dtype)


def retention_branch(q, k, v, g, gn_g, gn_b):
    Bsz, S, _ = q.shape
    f32 = jnp.float32
    L = D_CHUNK
    nc = S // L
    inv = 1.0 / jnp.power(jnp.float32(RET_THETA), jnp.linspace(0.0, 1.0, D_QK_DIM // 2, dtype=f32))
    q = apply_rotary(q.reshape(Bsz, S, D_HEADS, D_QK_DIM), inv)
    k = apply_rotary(k.reshape(Bsz, S, D_HEADS, D_QK_DIM), inv)
    qc = q.astype(f32).reshape(Bsz, nc, L, D_HEADS, D_QK_DIM)
    kc = k.astype(f32).reshape(Bsz, nc, L, D_HEADS, D_QK_DIM) * (D_QK_DIM ** -0.5)
    vc = v.astype(f32).reshape(Bsz, nc, L, D_HEADS, D_V_DIM)
    log_gamma = jnp.log(1.0 - jnp.power(2.0, -5.0 - jnp.arange(D_HEADS, dtype=f32)))
    idx = jnp.arange(L, dtype=f32)
    rel = idx[:, None] - idx[None, :]
    dmat = jnp.where(rel[None] >= 0, jnp.exp(log_gamma[:, None, None] * jnp.maximum(rel, 0.0)[None]), 0.0)
    scores = jnp.einsum("bclhd,bcshd->bchls", qc, kc) * dmat
    y = jnp.einsum("bchls,bcshv->bclhv", scores, vc)
    to_end = jnp.exp(log_gamma[None, :] * (L - 1 - idx)[:, None])
    kv = jnp.einsum("bclhd,bclhv->bchdv", kc * to_end[..., None], vc)
    chunk_decay = jnp.exp(log_gamma * L)

    def step(r, kv_c):
        return r * chunk_decay[:, None, None] + kv_c, r

    init = jnp.zeros((Bsz, D_HEADS, D_QK_DIM, D_V_DIM), f32)
    _, prev = lax.scan(step, init, jnp.moveaxis(kv, 1, 0))
    prev = jnp.moveaxis(prev, 0, 1)
    from_start = jnp.exp(log_gamma[None, :] * (idx + 1.0)[:, None])
    y = y + jnp.einsum("bclhd,bchdv->bclhv", qc, prev) * from_start[..., None]
    y = y.reshape(Bsz, S, D_HEADS, D_V_DIM)
    mu = jnp.mean(y, axis=-1, keepdims=True)
    var = jnp.mean(jnp.square(y - mu), axis=-1, keepdims=True)
    y = ((y - mu) * lax.rsqrt(var + NORM_EPS)).reshape(Bsz, S, D_HEADS * D_V_DIM)
    y = y * gn_g.astype(f32) + gn_b.astype(f32)
    return (jax.nn.silu(g.astype(f32)) * y).astype(g.dtype)


def odd_mixer(x, w_in, conv_w, conv_b, dt_bias, a_log, d_skip, ssm_norm_g, ret_gn_g, ret_gn_b, w_out):
    z, xbc, dt, rq, rk, rv, rg = _split(x @ w_in, ODD_SPLITS)
    y_ssm = mamba2_branch(z, xbc, dt, conv_w, conv_b, dt_bias, a_log, d_skip, ssm_norm_g)
    y_ret = retention_branch(rq, rk, rv, rg, ret_gn_g, ret_gn_b)
    return jnp.concatenate([y_ssm, y_ret], axis=-1) @ w_out


def swiglu(x, w_gate, w_up, w_down):
    return (jax.nn.silu(x @ w_gate) * (x @ w_up)) @ w_down


def setup_inputs(seed: int = 0) -> dict:
    key = jax.random.key(seed)
    ks = jax.random.split(key, 25)
    f32 = jnp.float32

    def nrm(k, shape, scale):
        return scale * jax.random.normal(k, shape, f32)

    dt0 = jnp.exp(jax.random.uniform(ks[18], (N_ODD, C_HEADS), f32, math.log(1e-3), math.log(1e-1)))
    return {
        "x": jax.random.normal(ks[0], (BATCH, SEQ, D_MODEL), f32),
        "ln_mix_g": 1.0 + nrm(ks[1], (DEPTH, D_MODEL), 0.02),
        "ln_mix_b": nrm(ks[2], (DEPTH, D_MODEL), 0.02),
        "ln_ffn_g": 1.0 + nrm(ks[3], (DEPTH, D_MODEL), 0.02),
        "ln_ffn_b": nrm(ks[4], (DEPTH, D_MODEL), 0.02),
        "ffn_w_gate": nrm(ks[5], (DEPTH, D_MODEL, FFN_HIDDEN), D_MODEL ** -0.5),
        "ffn_w_up": nrm(ks[6], (DEPTH, D_MODEL, FFN_HIDDEN), D_MODEL ** -0.5),
        "ffn_w_down": nrm(ks[7], (DEPTH, FFN_HIDDEN, D_MODEL), BETA * FFN_HIDDEN ** -0.5),
        "ev_w_in": nrm(ks[8], (N_EVEN, D_MODEL, EVEN_IN), D_MODEL ** -0.5),
        "ev_sinks": nrm(ks[9], (N_EVEN, A_HEADS), 0.5),
        "ev_dw_w": nrm(ks[10], (N_EVEN, B_CONV, B_CHANNELS), B_CONV ** -0.5),
        "ev_dw_b": nrm(ks[11], (N_EVEN, B_CHANNELS), 0.02),
        "ev_cn_g": 1.0 + nrm(ks[12], (N_EVEN, B_CHANNELS), 0.02),
        "ev_cn_b": nrm(ks[13], (N_EVEN, B_CHANNELS), 0.02),
        "ev_w_out": nrm(ks[14], (N_EVEN, EVEN_MIX, D_MODEL), BETA * EVEN_MIX ** -0.5),
        "od_w_in": nrm(ks[15], (N_ODD, D_MODEL, ODD_IN), D_MODEL ** -0.5),
        "od_conv_w": nrm(ks[16], (N_ODD, C_CONV, C_XBC), C_CONV ** -0.5),
        "od_conv_b": nrm(ks[17], (N_ODD, C_XBC), 0.02),
        "od_dt_bias": dt0 + jnp.log(-jnp.expm1(-dt0)),
        "od_a_log": jnp.log(jax.random.uniform(ks[19], (N_ODD, C_HEADS), f32, 1.0, 16.0)),
        "od_d_skip": 1.0 + nrm(ks[20], (N_ODD, C_HEADS), 0.1),
        "od_ssm_norm_g": 1.0 + nrm(ks[21], (N_ODD, C_INNER), 0.02),
        "od_ret_gn_g": 1.0 + nrm(ks[22], (N_ODD, D_HEADS * D_V_DIM), 0.02),
        "od_ret_gn_b": nrm(ks[23], (N_ODD, D_HEADS * D_V_DIM), 0.02),
        "od_w_out": nrm(ks[24], (N_ODD, ODD_MIX, D_MODEL), BETA * ODD_MIX ** -0.5),
    }


def reference(x, ln_mix_g, ln_mix_b, ln_ffn_g, ln_ffn_b, ffn_w_gate, ffn_w_up, ffn_w_down,
              ev_w_in, ev_sinks, ev_dw_w, ev_dw_b, ev_cn_g, ev_cn_b, ev_w_out,
              od_w_in, od_conv_w, od_conv_b, od_dt_bias, od_a_log, od_d_skip, od_ssm_norm_g,
              od_ret_gn_g, od_ret_gn_b, od_w_out):
    for layer in range(DEPTH):
        if layer % 2 == 0:
            e = layer // 2
            h = even_mixer(x, ev_w_in[e], ev_sinks[e], ev_dw_w[e], ev_dw_b[e],
                           ev_cn_g[e], ev_cn_b[e], ev_w_out[e])
        else:
            o = layer // 2
            h = odd_mixer(x, od_w_in[o], od_conv_w[o], od_conv_b[o], od_dt_bias[o], od_a_log[o],
                          od_d_skip[o], od_ssm_norm_g[o], od_ret_gn_g[o], od_ret_gn_b[o], od_w_out[o])
        x = layer_norm(ALPHA * x + h.astype(x.dtype), ln_mix_g[layer], ln_mix_b[layer])
        f = swiglu(x, ffn_w_gate[layer], ffn_w_up[layer], ffn_w_down[layer])
        x = layer_norm(ALPHA * x + f.astype(x.dtype), ln_ffn_g[layer], ln_ffn_b[layer])
    return x
```

```python
import numpy as np
from contextlib import ExitStack
import concourse.bass as bass
import concourse.mybir as mybir
from concourse.bass_utils import run_bass_kernel_spmd

F32 = mybir.dt.float32
BF16 = mybir.dt.bfloat16
ALU = mybir.AluOpType
AF = mybir.ActivationFunctionType

D = 1024
FH = 2816
DEPTH = 4
ALPHA = (2 * DEPTH) ** 0.25
EPS = 1e-5
NCORES = 8


class Buf:
    __slots__ = ("name", "w", "rs")

    def __init__(self, name):
        self.name = name
        self.w = None
        self.rs = []


class Op:
    __slots__ = ("stream", "fn", "deps", "needed", "sem", "val", "is_dma", "idx")


class _Rec:
    def __init__(self):
        self.call = None

    def __getattr__(self, name):
        def f(*a, **kw):
            assert self.call is None, "one engine instruction per op"
            self.call = (name, a, kw)
            return None
        return f


class Sched:
    STRICT = ("act", "dve", "pool")
    NDMA = 6

    def __init__(self, nc, es):
        self.nc = nc
        self.streams = {k: [] for k in ("pe", "act", "dve", "pool", "sp")}
        self.sem = {k: es.enter_context(nc.semaphore("pg_" + k)) for k in self.streams}
        self.dsem = {q: [es.enter_context(nc.semaphore(f"dq_{q}{i}")) for i in range(self.NDMA)]
                     for q in ("sp", "pool", "act")}
        self.dcnt = {q: 0 for q in self.dsem}
        self.dring = {q: [None] * self.NDMA for q in self.dsem}
        self.bar = {k: () for k in self.streams}
        self.nops = 0

    def _add(self, stream, fn, reads, writes, is_dma=False, extra=()):
        op = Op()
        op.stream = stream
        rec = _Rec()
        fn(rec)
        name_, a_, kw_ = rec.call
        op.fn = lambda e, name_=name_, a_=a_, kw_=kw_: getattr(e, name_)(*a_, **kw_)
        op.is_dma = is_dma
        op.needed = False
        op.sem = None
        op.val = 0
        deps = []
        for b in reads:
            if b.w is not None:
                deps.append(b.w)
        for b in writes:
            if b.w is not None:
                deps.append(b.w)
            deps.extend(b.rs)
        deps.extend(extra)
        if self.bar[stream]:
            deps.extend(self.bar[stream])
            self.bar[stream] = ()
        seen = set()
        dd = []
        for d in deps:
            if id(d) in seen:
                continue
            seen.add(id(d))
            if (not d.is_dma) and d.stream == stream and stream not in self.STRICT:
                continue
            dd.append(d)
            d.needed = True
        op.deps = dd
        for b in reads:
            b.rs.append(op)
        for b in writes:
            b.w = op
            b.rs = []
        op.idx = len(self.streams[stream])
        self.streams[stream].append(op)
        self.nops += 1
        return op

    def op(self, stream, fn, reads=(), writes=()):
        return self._add(stream, fn, reads, writes)

    def dma(self, q, out, in_, reads=(), writes=(), **kw):
        i = self.dcnt[q]
        self.dcnt[q] += 1
        slot = i % self.NDMA
        prev = self.dring[q][slot]
        extra = (prev,) if prev is not None else ()
        op = self._add(q, lambda e: e.dma_start(out=out, in_=in_, **kw), reads, writes, is_dma=True, extra=extra)
        op.sem = self.dsem[q][slot]
        op.val = 16 * (i // self.NDMA + 1)
        op.needed = True
        self.dring[q][slot] = op
        return op

    def barrier(self):
        last = []
        for k, lst in self.streams.items():
            for op in reversed(lst):
                if not op.is_dma:
                    last.append(op)
                    break
        for q in self.dring:
            last.extend(o for o in self.dring[q] if o is not None)
        for k in self.streams:
            self.bar[k] = tuple(last)

    def emit(self, final_ops=()):
        for k, lst in self.streams.items():
            c = 0
            for op in lst:
                if op.is_dma:
                    continue
                if op.needed:
                    c += 1
                    op.sem = self.sem[k]
                    op.val = c
        streams = self.streams
        print('SCHED ops', {k: len(v) for k, v in streams.items()}, 'semmax', {k: max([o.val for o in v if not o.is_dma] + [0]) for k, v in streams.items()}, flush=True)
        with self.nc.Block() as block:
            def run(k):
                def body(e):
                    waited = {}

                    def wait(d):
                        key = id(d.sem)
                        if waited.get(key, 0) < d.val:
                            e.wait_ge(d.sem, d.val)
                            waited[key] = d.val
                    for op in streams[k]:
                        for d in op.deps:
                            wait(d)
                        ins = op.fn(e)
                        if op.is_dma:
                            ins.then_inc(op.sem, 16)
                        elif op.needed:
                            ins.then_inc(op.sem, 1)
                    if k == "sp":
                        for d in final_ops:
                            wait(d)
                return body
            block.tensor(run("pe"))
            block.scalar(run("act"))
            block.vector(run("dve"))
            block.gpsimd(run("pool"))
            block.sync(run("sp"))


class Ctx:
    pass


_UID = [0]


def _sb(c, es, name, shape, dt):
    _UID[0] += 1
    return es.enter_context(c.nc.sbuf_tensor(f"{name}_{_UID[0]}", list(shape), dt))


class PsumPool:
    def __init__(self, c, n=8):
        self.t = [c.es.enter_context(c.nc.psum_tensor(f"ps{i}", [128, 512], F32)) for i in range(n)]
        self.b = [Buf(f"ps{i}") for i in range(n)]
        self.i = 0
        self.n = n

    def get(self):
        i = self.i
        self.i = (self.i + 1) % self.n
        return self.t[i], self.b[i]


def load_bcast_row(c, q, dst_tile, dst_buf, dram_ap_row):
    n = dst_tile.shape[-1]
    return c.S.dma(q, dst_tile[:, :], dram_ap_row.to_broadcast([128, n]), writes=[dst_buf])


def emit_xT(c, xT, xT_buf, tslot, x_tile, x_buf, xbf, xbf_buf):
    S = c.S
    S.op("act", lambda e: e.copy(out=xbf[:, :], in_=x_tile[:, :]), reads=[x_buf], writes=[xbf_buf])
    pt, pb = c.ps.get()
    ptb = pt[:, :].bitcast(BF16)
    for k in range(8):
        S.op("pe", lambda e, k=k: e.transpose(out=ptb[:, k * 128:(k + 1) * 128], in_=xbf[:, k * 128:(k + 1) * 128],
                                               identity=c.ident_bf[:, :]),
             reads=[xbf_buf, c.ident_buf], writes=[pb])
    S.op("dve", lambda e: e.tensor_copy(out=xT[:, :, tslot * 128:(tslot + 1) * 128],
                                        in_=ptb.rearrange("p (k t) -> p k t", k=8)),
         reads=[pb], writes=[xT_buf])


def emit_ln_epilogue(c, es_bufs, ps_halves, x_old, x_old_buf, g_t, b_t, gb_buf, out_tile, out_buf):
    S = c.S
    v, v_buf, st, mv, sm_buf = es_bufs
    for hf in range(2):
        pt, pb = ps_halves[hf]
        S.op("dve", lambda e, hf=hf, pt=pt: e.scalar_tensor_tensor(
            out=v[:, hf * 512:(hf + 1) * 512], in0=x_old[:, hf * 512:(hf + 1) * 512], scalar=float(ALPHA),
            in1=pt[:, :], op0=ALU.mult, op1=ALU.add), reads=[x_old_buf, pb], writes=[v_buf])
    ln_core(c, v, v_buf, st, mv, sm_buf, g_t, b_t, gb_buf, out_tile, out_buf)


def ln_core(c, v, v_buf, st, mv, sm_buf, g_t, b_t, gb_buf, out_tile, out_buf):
    S = c.S
    for hf in range(2):
        S.op("dve", lambda e, hf=hf: e.bn_stats(out=st[:, hf * 6:(hf + 1) * 6], in_=v[:, hf * 512:(hf + 1) * 512]),
             reads=[v_buf], writes=[sm_buf])
    S.op("dve", lambda e: e.bn_aggr(out=mv[:, 0:2], in_=st[:, 0:12]), reads=[sm_buf], writes=[sm_buf])
    S.op("dve", lambda e: e.tensor_scalar_add(out=mv[:, 2:3], in0=mv[:, 1:2], scalar1=float(EPS)),
         reads=[sm_buf], writes=[sm_buf])
    S.op("act", lambda e: e.activation(out=mv[:, 2:3], in_=mv[:, 2:3], func=AF.Sqrt), reads=[sm_buf], writes=[sm_buf])
    S.op("dve", lambda e: e.reciprocal(out=mv[:, 2:3], in_=mv[:, 2:3]), reads=[sm_buf], writes=[sm_buf])
    S.op("dve", lambda e: e.scalar_tensor_tensor(out=mv[:, 3:4], in0=mv[:, 0:1], scalar=-1.0, in1=mv[:, 2:3],
                                                 op0=ALU.mult, op1=ALU.mult), reads=[sm_buf], writes=[sm_buf])
    S.op("act", lambda e: e.activation(out=v[:, :], in_=v[:, :], func=AF.Identity, bias=mv[:, 3:4], scale=mv[:, 2:3]),
         reads=[v_buf, sm_buf], writes=[v_buf])
    S.op("dve", lambda e: e.tensor_tensor(out=v[:, :], in0=v[:, :], in1=g_t[:, :], op=ALU.mult),
         reads=[v_buf, gb_buf], writes=[v_buf])
    S.op("dve", lambda e: e.tensor_tensor(out=out_tile[:, :], in0=v[:, :], in1=b_t[:, :], op=ALU.add),
         reads=[v_buf, gb_buf], writes=[out_buf])


def emit_ffn(c, layer, src, dst, NT):
    S, nc = c.S, c.nc
    T = min(1024, NT)
    ntile = T // 128
    nblk = T // 512
    outs = []
    with ExitStack() as es:
        xT = _sb(c, es, "f_xT", [128, 8, T], BF16)
        xT_buf = Buf("f_xT")
        hT = _sb(c, es, "f_hT", [128, 22, T], BF16)
        hT_bufs = [Buf(f"f_hT{j}") for j in range(22)]
        NWB = 2
        wg = [_sb(c, es, f"f_wg{i}", [128, 8, 512], BF16) for i in range(NWB)]
        wu = [_sb(c, es, f"f_wu{i}", [128, 8, 512], BF16) for i in range(NWB)]
        wg_b = [Buf(f"f_wg{i}") for i in range(NWB)]
        wu_b = [Buf(f"f_wu{i}") for i in range(NWB)]
        wd = _sb(c, es, "f_wd", [128, 22, 1024], BF16)
        jgroups = [(j0, min(4, 22 - j0)) for j0 in range(0, 22, 4)]
        wd_b = [Buf(f"f_wd{g}") for g in range(len(jgroups))]
        NXB = 2
        xin = [_sb(c, es, f"f_xin{i}", [128, 1024], F32) for i in range(NXB)]
        xin_b = [Buf(f"f_xin{i}") for i in range(NXB)]
        xbf = [_sb(c, es, f"f_xbf{i}", [128, 1024], BF16) for i in range(NXB)]
        xbf_b = [Buf(f"f_xbf{i}") for i in range(NXB)]
        sg = [_sb(c, es, f"f_sg{i}", [128, 512], F32) for i in range(2)]
        sg_b = [Buf(f"f_sg{i}") for i in range(2)]
        v = [_sb(c, es, f"f_v{i}", [128, 1024], F32) for i in range(2)]
        v_b = [Buf(f"f_v{i}") for i in range(2)]
        st = [_sb(c, es, f"f_st{i}", [128, 12], F32) for i in range(2)]
        mv = [_sb(c, es, f"f_mv{i}", [128, 4], F32) for i in range(2)]
        sm_b = [Buf(f"f_sm{i}") for i in range(2)]
        xo = [_sb(c, es, f"f_xo{i}", [128, 1024], F32) for i in range(2)]
        xo_b = [Buf(f"f_xo{i}") for i in range(2)]
        g_t = _sb(c, es, "f_g", [128, 1024], F32)
        b_t = _sb(c, es, "f_b", [128, 1024], F32)
        gb_buf = Buf("f_gb")
        load_bcast_row(c, "sp", g_t, gb_buf, c.dram[f"ln_ffn_g{layer}"])
        load_bcast_row(c, "sp", b_t, gb_buf, c.dram[f"ln_ffn_b{layer}"])
        Wg = c.dram[f"ffn_w_gate{layer}"]
        Wu = c.dram[f"ffn_w_up{layer}"]
        Wd = c.dram[f"ffn_w_down{layer}"]
        xcnt = 0
        wcnt = 0
        ecnt = 0
        first = True
        for g0 in range(0, NT, T):
            for t in range(ntile):
                i = xcnt % NXB
                xcnt += 1
                r0 = g0 + t * 128
                S.dma("sp", xin[i][:, :], src[r0:r0 + 128, :], writes=[xin_b[i]])
                emit_xT(c, xT, xT_buf, t, xin[i], xin_b[i], xbf[i], xbf_b[i])
            for gi, (j0, nj) in enumerate(jgroups):
                wi = wcnt % NWB
                wcnt += 1
                c0 = j0 * 128
                ncol = nj * 128
                S.dma("pool", wg[wi][:, :, 0:ncol], Wg[:, c0:c0 + ncol].rearrange("(k p) n -> p k n", p=128),
                      writes=[wg_b[wi]])
                S.dma("pool", wu[wi][:, :, 0:ncol], Wu[:, c0:c0 + ncol].rearrange("(k p) n -> p k n", p=128),
                      writes=[wu_b[wi]])
                if first:
                    S.dma("pool", wd[:, j0:j0 + nj, :], Wd[c0:c0 + ncol, :].rearrange("(j p) n -> p j n", p=128),
                          writes=[wd_b[gi]])
                for jj in range(nj):
                    j = j0 + jj
                    for nb in range(nblk):
                        pg, pgb = c.ps.get()
                        pu, pub = c.ps.get()
                        for k in range(8):
                            S.op("pe", lambda e, k=k, pg=pg, jj=jj, nb=nb, wi=wi: e.matmul(
                                pg[:, :], lhsT=wg[wi][:, k, jj * 128:(jj + 1) * 128], rhs=xT[:, k, nb * 512:(nb + 1) * 512],
                                start=(k == 0), stop=(k == 7)), reads=[wg_b[wi], xT_buf], writes=[pgb])
                        for k in range(8):
                            S.op("pe", lambda e, k=k, pu=pu, jj=jj, nb=nb, wi=wi: e.matmul(
                                pu[:, :], lhsT=wu[wi][:, k, jj * 128:(jj + 1) * 128], rhs=xT[:, k, nb * 512:(nb + 1) * 512],
                                start=(k == 0), stop=(k == 7)), reads=[wu_b[wi], xT_buf], writes=[pub])
                        si = (j * nblk + nb) % 2
                        S.op("act", lambda e, si=si, pg=pg: e.activation(out=sg[si][:, :], in_=pg[:, :], func=AF.Silu),
                             reads=[pgb], writes=[sg_b[si]])
                        S.op("dve", lambda e, si=si, pu=pu, j=j, nb=nb: e.tensor_tensor(
                            out=hT[:, j, nb * 512:(nb + 1) * 512], in0=sg[si][:, :], in1=pu[:, :], op=ALU.mult),
                            reads=[sg_b[si], pub], writes=[hT_bufs[j]])
            first = False
            for t in range(ntile):
                halves = [c.ps.get(), c.ps.get()]
                for j in range(22):
                    for hf in range(2):
                        ph, phb = halves[hf]
                        S.op("pe", lambda e, ph=ph, j=j, hf=hf, t=t: e.matmul(
                            ph[:, :], lhsT=hT[:, j, t * 128:(t + 1) * 128], rhs=wd[:, j, hf * 512:(hf + 1) * 512],
                            start=(j == 0), stop=(j == 21)), reads=[hT_bufs[j], wd_b[j // 4]], writes=[phb])
                i = ecnt % 2
                ecnt += 1
                r0 = g0 + t * 128
                S.dma("sp", xin[i][:, :], src[r0:r0 + 128, :], writes=[xin_b[i]])
                emit_ln_epilogue(c, (v[i], v_b[i], st[i], mv[i], sm_b[i]), halves, xin[i], xin_b[i],
                                 g_t, b_t, gb_buf, xo[i], xo_b[i])
                outs.append(S.dma("sp", dst[r0:r0 + 128, :], xo[i][:, :], reads=[xo_b[i]]))
    S.barrier()
    return outs


A_MASK_NEG = -30000.0
import os as _os
DBG_STOP = int(_os.environ.get("DBG_STOP", "0"))


class _Stop(Exception):
    pass


def _chk(n):
    if DBG_STOP == n:
        raise _Stop()


def ps_bf(pt):
    return pt[:, :].bitcast(BF16)


def emit_even(c, e, layer, src, dst, halo, NT):
    S, nc = c.S, c.nc
    GT = 512
    ngroups = NT // GT
    outs = []
    dr = c.dram
    with ExitStack() as es:
      try:
          sb = lambda name, shape, dt: _sb(c, es, "e_" + name, shape, dt)
          Win = sb("win", [128, 8, 1792], BF16); Win_b = Buf("win")
          Wout = sb("wout", [128, 8, 1024], BF16); Wout_b = Buf("wout")
          dg = sb("dg", [128, 4, 31, 128], BF16); dg_b = Buf("dg")
          wk = sb("wk", [31, 512], F32); wk_b = Buf("wk")
          wcol = sb("wcol", [128, 4, 32], F32); wcol_b = Buf("wcol")
          cvec = sb("cvec", [128, 16], F32); cvec_b = Buf("cvec")
          sinks = sb("sinks", [128, 8], F32); sinks_b = Buf("sinks")
          amask = sb("amask", [128, 256], F32); amask0 = sb("amask0", [128, 256], F32); am_b = Buf("amask")
          rope = sb("rope", [128, NT // 128 + 1, 16], F32); rope_b = Buf("rope")
          g_t = sb("g", [128, 1024], F32); b_t = sb("b", [128, 1024], F32); gb_buf = Buf("gb")
          xT = sb("xT", [128, 8, GT], BF16); xT_b = Buf("xT")
          xTh = sb("xTh", [128, 8, 128], BF16); xTh_b = Buf("xTh")
          hbuf = [sb(f"hbuf{i}", [128, 4, 32 + GT], BF16) for i in range(2)]
          hb_body = [Buf(f"hb{i}") for i in range(2)]
          hb_pre = [Buf(f"hp{i}") for i in range(2)]
          cv = sb("cv", [128, 4, GT], F32); cv_b = [Buf(f"cv{i}") for i in range(4)]
          sq = sb("sq", [128, GT], F32); sq_b = Buf("sq")
          tg = [sb(f"tg{i}", [128, GT], F32) for i in range(2)]; tg_b = [Buf(f"tg{i}") for i in range(2)]
          mean = sb("mean", [128, GT], F32); msq = sb("msq", [128, GT], F32); rstd = sb("rstd", [128, GT], F32)
          stat_b = Buf("stat")
          ta = [sb(f"ta{i}", [128, GT], F32) for i in range(2)]; ta_b = [Buf(f"ta{i}") for i in range(2)]
          yT = sb("yT", [128, 8, GT], BF16); yT_b = [Buf(f"yT{i}") for i in range(8)]
          xin = [sb(f"xin{i}", [128, 1024], F32) for i in range(2)]; xin_b = [Buf(f"xin{i}") for i in range(2)]
          xbf = [sb(f"xbf{i}", [128, 1024], BF16) for i in range(2)]; xbf_b = [Buf(f"xbf{i}") for i in range(2)]
          qb = sb("qb", [128, 8, 64], BF16); qb_b = Buf("qb")
          kb = sb("kb", [128, 2, 64], BF16); kb_b = Buf("kb")
          rt = [sb(f"rt{i}", [128, 8, 8], F32) for i in range(2)]; rt_b = [Buf(f"rt{i}") for i in range(2)]
          NR = 3
          vb = [sb(f"vb{i}", [128, 128], BF16) for i in range(NR)]; vb_b = [Buf(f"vb{i}") for i in range(NR)]
          kT = [sb(f"kT{i}", [128, 128], BF16) for i in range(NR)]; kT_b = [Buf(f"kT{i}") for i in range(NR)]
          qT = sb("qT", [128, 4, 128], BF16); qT_b = Buf("qT")
          sm = sb("sm", [128, 8, 256], F32); sm_b = [Buf(f"sm{i}") for i in range(4)]
          pb = sb("pb", [128, 8, 256], BF16); pb_b = Buf("pb")
          pT = sb("pT", [128, 16, 128], BF16); pT_b = [Buf(f"pT{i}") for i in range(2)]
          att = sb("att", [128, 40], F32); att_b = Buf("att")
          ob = sb("ob", [128, 8, 64], BF16); ob_b = Buf("ob")
          v = [sb(f"v{i}", [128, 1024], F32) for i in range(2)]; v_b = [Buf(f"v{i}") for i in range(2)]
          st = [sb(f"st{i}", [128, 12], F32) for i in range(2)]
          mv = [sb(f"mv{i}", [128, 4], F32) for i in range(2)]; smm_b = [Buf(f"smm{i}") for i in range(2)]
          xo = [sb(f"xo{i}", [128, 1024], F32) for i in range(2)]; xo_b = [Buf(f"xo{i}") for i in range(2)]

          S.dma("pool", Win[:, :, :], dr[f"ev_w_in{e}"].rearrange("(k p) n -> p k n", p=128), writes=[Win_b])
          S.dma("pool", Wout[:, :, :], dr[f"ev_w_out{e}"].rearrange("(k p) n -> p k n", p=128), writes=[Wout_b])
          S.dma("sp", wk[:, :], dr[f"ev_dw_w{e}"], writes=[wk_b])
          S.dma("sp", cvec[:, 0:4], dr[f"ev_dw_b{e}"].rearrange("o (c p) -> p (o c)", p=128), writes=[cvec_b], allow_slow_non_contiguous=True)
          S.dma("sp", cvec[:, 4:8], dr[f"ev_cn_g{e}"].rearrange("o (c p) -> p (o c)", p=128), writes=[cvec_b], allow_slow_non_contiguous=True)
          S.dma("sp", cvec[:, 8:12], dr[f"ev_cn_b{e}"].rearrange("o (c p) -> p (o c)", p=128), writes=[cvec_b], allow_slow_non_contiguous=True)
          load_bcast_row(c, "sp", sinks, sinks_b, dr[f"ev_sinks{e}"])
          S.dma("sp", amask[:, :], dr["amask"], writes=[am_b])
          S.dma("sp", amask0[:, :], dr["amask0"], writes=[am_b])
          S.dma("sp", rope[:, :, :], dr["rope_a"], writes=[rope_b])
          load_bcast_row(c, "sp", g_t, gb_buf, dr[f"ln_mix_g{layer}"])
          load_bcast_row(c, "sp", b_t, gb_buf, dr[f"ln_mix_b{layer}"])
          S.op("dve", lambda en: en.tensor_scalar_mul(out=cvec[:, 4:12], in0=cvec[:, 4:12], scalar1=0.5),
               reads=[cvec_b], writes=[cvec_b])
          for cc in range(4):
              pt, ptb_ = c.ps.get()
              S.op("pe", lambda en, cc=cc, pt=pt: en.transpose(out=pt[:, 0:31], in_=wk[0:31, cc * 128:(cc + 1) * 128],
                                                              identity=c.ident_f[0:31, 0:31]),
                   reads=[wk_b, c.identf_buf], writes=[ptb_])
              S.op("dve", lambda en, cc=cc, pt=pt: en.tensor_copy(out=wcol[:, cc, 0:31], in_=pt[:, 0:31]),
                   reads=[ptb_], writes=[wcol_b])
          for cc in range(4):
              for k in range(31):
                  S.op("dve", lambda en, cc=cc, k=k: en.tensor_scalar(
                      out=dg[:, cc, k, :], in0=c.ident_f[:, :], scalar1=wcol[:, cc, k:k + 1], scalar2=0.5,
                      op0=ALU.mult, op1=ALU.mult), reads=[wcol_b, c.identf_buf], writes=[dg_b])

          _chk(1)
          xcnt = [0]

          def load_xT(row0, dstT, dstT_b, slot, src_ap):
              i = xcnt[0] % 2
              xcnt[0] += 1
              S.dma("sp", xin[i][:, :], src_ap[row0:row0 + 128, :], writes=[xin_b[i]])
              emit_xT(c, dstT, dstT_b, slot, xin[i], xin_b[i], xbf[i], xbf_b[i])

          def glu_chunk(xTsrc, xTsrc_b, ncols, hb, hb_buf, col0):
              for cc in range(4):
                  pa, pab = c.ps.get()
                  pg, pgb = c.ps.get()
                  for k in range(8):
                      S.op("pe", lambda en, k=k, cc=cc, pa=pa: en.matmul(
                          pa[:, 0:ncols], lhsT=Win[:, k, 768 + cc * 128:768 + (cc + 1) * 128], rhs=xTsrc[:, k, 0:ncols],
                          start=(k == 0), stop=(k == 7)), reads=[Win_b, xTsrc_b], writes=[pab])
                  for k in range(8):
                      S.op("pe", lambda en, k=k, cc=cc, pg=pg: en.matmul(
                          pg[:, 0:ncols], lhsT=Win[:, k, 1280 + cc * 128:1280 + (cc + 1) * 128], rhs=xTsrc[:, k, 0:ncols],
                          start=(k == 0), stop=(k == 7)), reads=[Win_b, xTsrc_b], writes=[pgb])
                  ti = cc % 2
                  S.op("act", lambda en, ti=ti, pg=pg: en.activation(out=tg[ti][:, 0:ncols], in_=pg[:, 0:ncols],
                                                                      func=AF.Tanh, scale=0.5),
                       reads=[pgb], writes=[tg_b[ti]])
                  S.op("dve", lambda en, ti=ti, pa=pa, cc=cc: en.scalar_tensor_tensor(
                      out=hb[:, cc, col0:col0 + ncols], in0=tg[ti][:, 0:ncols], scalar=1.0, in1=pa[:, 0:ncols],
                      op0=ALU.add, op1=ALU.mult), reads=[tg_b[ti], pab], writes=[hb_buf])

          def kv_tile(xTsrc, xTsrc_b, col0, ring_i, tile_idx, with_q):
              pkv, pkvb = c.ps.get()
              for k in range(8):
                  S.op("pe", lambda en, k=k: en.matmul(pkv[:, 0:256], lhsT=xTsrc[:, k, col0:col0 + 128],
                                                       rhs=Win[:, k, 512:768], start=(k == 0), stop=(k == 7)),
                       reads=[Win_b, xTsrc_b], writes=[pkvb])
              if with_q:
                  pq, pqb = c.ps.get()
                  for k in range(8):
                      S.op("pe", lambda en, k=k: en.matmul(pq[:, :], lhsT=xTsrc[:, k, col0:col0 + 128],
                                                           rhs=Win[:, k, 0:512], start=(k == 0), stop=(k == 7)),
                           reads=[Win_b, xTsrc_b], writes=[pqb])

              def rope_apply(s3, psrc_b, dstt, dst_b, hs):
                  nh = len(hs)
                  H = int(np.prod(hs))
                  full = [128] + list(hs)
                  cs = rope[:, tile_idx, 0:8]
                  sn = rope[:, tile_idx, 8:16]
                  for _ in range(nh):
                      cs = cs.unsqueeze(1)
                      sn = sn.unsqueeze(1)
                  cs = cs.to_broadcast(full + [8])
                  sn = sn.to_broadcast(full + [8])
                  if nh == 1:
                      r0, r1 = rt[0][:, 0:H, :], rt[1][:, 0:H, :]
                  else:
                      r0 = rt[0][:, 0:H, :].rearrange("p (a b) d -> p a b d", a=hs[0])
                      r1 = rt[1][:, 0:H, :].rearrange("p (a b) d -> p a b d", a=hs[0])
                  sl = (slice(None),) * (1 + nh)
                  t1, t2 = s3[sl + (slice(0, 8),)], s3[sl + (slice(8, 16),)]
                  S.op("dve", lambda en: en.tensor_tensor(out=r0, in0=t1, in1=cs, op=ALU.mult),
                       reads=[psrc_b, rope_b], writes=[rt_b[0]])
                  S.op("dve", lambda en: en.tensor_tensor(out=r1, in0=t2, in1=sn, op=ALU.mult),
                       reads=[psrc_b, rope_b], writes=[rt_b[1]])
                  S.op("dve", lambda en: en.tensor_tensor(out=dstt[sl + (slice(0, 8),)], in0=r0, in1=r1, op=ALU.subtract),
                       reads=[rt_b[0], rt_b[1]], writes=[dst_b])
                  S.op("dve", lambda en: en.tensor_tensor(out=r0, in0=t2, in1=cs, op=ALU.mult),
                       reads=[psrc_b, rope_b], writes=[rt_b[0]])
                  S.op("dve", lambda en: en.tensor_tensor(out=r1, in0=t1, in1=sn, op=ALU.mult),
                       reads=[psrc_b, rope_b], writes=[rt_b[1]])
                  S.op("dve", lambda en: en.tensor_tensor(out=dstt[sl + (slice(8, 16),)], in0=r0, in1=r1, op=ALU.add),
                       reads=[rt_b[0], rt_b[1]], writes=[dst_b])
                  S.op("act", lambda en: en.copy(out=dstt[sl + (slice(16, 64),)], in_=s3[sl + (slice(16, 64),)]),
                       reads=[psrc_b], writes=[dst_b])

              rope_apply(pkv[:, 0:128].rearrange("p (h d) -> p h d", h=2), pkvb, kb[:, :, :], kb_b, [2])
              S.op("act", lambda en: en.copy(out=vb[ring_i][:, :], in_=pkv[:, 128:256]), reads=[pkvb], writes=[vb_b[ring_i]])
              pt, ptb_ = c.ps.get()
              ptb = ps_bf(pt)
              S.op("pe", lambda en: en.transpose(out=ptb[:, 0:128], in_=kb[:, :, :].rearrange("p h d -> p (h d)"),
                                                 identity=c.ident_bf[:, :]), reads=[kb_b, c.ident_buf], writes=[ptb_])
              S.op("act", lambda en: en.copy(out=kT[ring_i][:, :], in_=ptb[:, 0:128]), reads=[ptb_], writes=[kT_b[ring_i]])
              if with_q:
                  rope_apply(pq[:, :].rearrange("p (g j d) -> p g j d", g=2, j=4), pqb,
                             qb[:, :, :].rearrange("p (j g) d -> p g j d", g=2), qb_b, [2, 4])
                  pt2, pt2b_ = c.ps.get()
                  pt2b = ps_bf(pt2)
                  qflat = qb[:, :, :].rearrange("p h d -> p (h d)")
                  for j in range(4):
                      S.op("pe", lambda en, j=j: en.transpose(out=pt2b[:, j * 128:(j + 1) * 128],
                                                              in_=qflat[:, j * 128:(j + 1) * 128], identity=c.ident_bf[:, :]),
                           reads=[qb_b, c.ident_buf], writes=[pt2b_])
                  S.op("dve", lambda en: en.tensor_copy(out=qT[:, :, :], in_=pt2b[:, 0:512].rearrange("p (j t) -> p j t", j=4)),
                       reads=[pt2b_], writes=[qT_b])

          _chk(2)
          load_xT(0, xTh, xTh_b, 0, halo)
          glu_chunk(xTh, xTh_b, 128, hbuf[1], hb_body[1], 32 + GT - 128)
          kv_tile(xTh, xTh_b, 0, (NR - 1), 0, False)
          _chk(3)
          blk_global = 0
          for g in range(ngroups):
              hb = hbuf[g % 2]
              hprev = hbuf[(g + 1) % 2]
              for t in range(4):
                  load_xT(g * GT + t * 128, xT, xT_b, t, src)
              S.op("act", lambda en, hb=hb, hprev=hprev: en.copy(out=hb[:, :, 2:32], in_=hprev[:, :, GT + 2:GT + 32]),
                   reads=[hb_body[(g + 1) % 2]], writes=[hb_pre[g % 2]])
              glu_chunk(xT, xT_b, GT, hb, hb_body[g % 2], 32)
              _chk(4)
              for cc in range(4):
                  pc, pcb = c.ps.get()
                  for k in range(31):
                      S.op("pe", lambda en, cc=cc, k=k, pc=pc, hb=hb: en.matmul(
                          pc[:, :], lhsT=dg[:, cc, k, :], rhs=hb[:, cc, 2 + k:2 + k + GT], start=(k == 0), stop=(k == 30)),
                          reads=[dg_b, hb_body[g % 2], hb_pre[g % 2]], writes=[pcb])
                  S.op("act", lambda en, cc=cc, pc=pc: en.activation(out=cv[:, cc, :], in_=pc[:, :], func=AF.Identity,
                                                                      bias=cvec[:, cc:cc + 1], scale=1.0),
                       reads=[pcb, cvec_b], writes=[cv_b[cc]])
              _chk(5)
              p1, p1b = c.ps.get()
              p2, p2b = c.ps.get()
              for cc in range(4):
                  S.op("pe", lambda en, cc=cc: en.matmul(p1[:, :], lhsT=c.ones_f[:, :], rhs=cv[:, cc, :],
                                                         start=(cc == 0), stop=(cc == 3)),
                       reads=[cv_b[cc], c.ones_buf], writes=[p1b])
              for cc in range(4):
                  S.op("act", lambda en, cc=cc: en.activation(out=sq[:, :], in_=cv[:, cc, :], func=AF.Square),
                       reads=[cv_b[cc]], writes=[sq_b])
                  S.op("pe", lambda en, cc=cc: en.matmul(p2[:, :], lhsT=c.ones_f[:, :], rhs=sq[:, :],
                                                         start=(cc == 0), stop=(cc == 3)),
                       reads=[sq_b, c.ones_buf], writes=[p2b])
              S.op("dve", lambda en: en.tensor_scalar_mul(out=mean[:, :], in0=p1[:, :], scalar1=1.0 / 512.0),
                   reads=[p1b], writes=[stat_b])
              S.op("dve", lambda en: en.tensor_tensor(out=msq[:, :], in0=mean[:, :], in1=mean[:, :], op=ALU.mult),
                   reads=[stat_b], writes=[stat_b])
              S.op("dve", lambda en: en.scalar_tensor_tensor(out=rstd[:, :], in0=p2[:, :], scalar=1.0 / 512.0, in1=msq[:, :],
                                                             op0=ALU.mult, op1=ALU.subtract), reads=[p2b, stat_b], writes=[stat_b])
              S.op("dve", lambda en: en.tensor_scalar_add(out=rstd[:, :], in0=rstd[:, :], scalar1=float(EPS)),
                   reads=[stat_b], writes=[stat_b])
              S.op("act", lambda en: en.activation(out=rstd[:, :], in_=rstd[:, :], func=AF.Sqrt), reads=[stat_b], writes=[stat_b])
              S.op("dve", lambda en: en.reciprocal(out=rstd[:, :], in_=rstd[:, :]), reads=[stat_b], writes=[stat_b])
              for cc in range(4):
                  i = cc % 2
                  S.op("dve", lambda en, cc=cc, i=i: en.tensor_tensor(out=ta[i][:, :], in0=cv[:, cc, :], in1=mean[:, :],
                                                                     op=ALU.subtract), reads=[cv_b[cc], stat_b], writes=[ta_b[i]])
                  S.op("dve", lambda en, i=i: en.tensor_tensor(out=ta[i][:, :], in0=ta[i][:, :], in1=rstd[:, :], op=ALU.mult),
                       reads=[ta_b[i], stat_b], writes=[ta_b[i]])
                  S.op("act", lambda en, cc=cc, i=i: en.activation(out=ta[i][:, :], in_=ta[i][:, :], func=AF.Identity,
                                                                    bias=cvec[:, 8 + cc:9 + cc], scale=cvec[:, 4 + cc:5 + cc]),
                       reads=[ta_b[i], cvec_b], writes=[ta_b[i]])
                  S.op("act", lambda en, i=i: en.activation(out=tg[i][:, :], in_=ta[i][:, :], func=AF.Tanh),
                       reads=[ta_b[i]], writes=[tg_b[i]])
                  S.op("dve", lambda en, cc=cc, i=i: en.scalar_tensor_tensor(
                      out=yT[:, 4 + cc, :], in0=tg[i][:, :], scalar=1.0, in1=ta[i][:, :], op0=ALU.add, op1=ALU.mult),
                      reads=[tg_b[i], ta_b[i]], writes=[yT_b[4 + cc]])
              _chk(6)
              for t in range(4):
                  bi = blk_global
                  blk_global += 1
                  cur = bi % NR
                  prv = (bi - 1) % NR
                  kv_tile(xT, xT_b, t * 128, cur, bi + 1, True)
                  msk = amask0 if bi == 0 else amask
                  for bank in range(4):
                      pscr, pscb = c.ps.get()
                      for hh in range(2):
                          h = bank * 2 + hh
                          gk = h // 4
                          lq = qT[gk * 64:gk * 64 + 64, h % 4, :]
                          S.op("pe", lambda en, pscr=pscr, hh=hh, lq=lq, gk=gk, prv=prv: en.matmul(
                              pscr[:, hh * 256:hh * 256 + 128], lhsT=lq, rhs=kT[prv][gk * 64:gk * 64 + 64, :],
                              start=True, stop=True), reads=[qT_b, kT_b[prv]], writes=[pscb])
                          S.op("pe", lambda en, pscr=pscr, hh=hh, lq=lq, gk=gk, cur=cur: en.matmul(
                              pscr[:, hh * 256 + 128:hh * 256 + 256], lhsT=lq, rhs=kT[cur][gk * 64:gk * 64 + 64, :],
                              start=True, stop=True), reads=[qT_b, kT_b[cur]], writes=[pscb])
                      S.op("dve", lambda en, bank=bank, pscr=pscr, msk=msk: en.tensor_tensor(
                          out=sm[:, bank * 2:bank * 2 + 2, :], in0=pscr[:, :].rearrange("p (h k) -> p h k", h=2),
                          in1=msk[:, :].unsqueeze(1).to_broadcast([128, 2, 256]), op=ALU.add),
                          reads=[pscb, am_b], writes=[sm_b[bank]])
                  S.op("dve", lambda en: en.tensor_reduce(out=att[:, 0:8], in_=sm[:, :, :], axis=mybir.AxisListType.X, op=ALU.max),
                       reads=sm_b, writes=[att_b])
                  S.op("dve", lambda en: en.scalar_tensor_tensor(out=att[:, 0:8], in0=att[:, 0:8], scalar=0.125, in1=sinks[:, :],
                                                                 op0=ALU.mult, op1=ALU.max), reads=[att_b, sinks_b], writes=[att_b])
                  S.op("dve", lambda en: en.tensor_scalar_mul(out=att[:, 8:16], in0=att[:, 0:8], scalar1=-1.0),
                       reads=[att_b], writes=[att_b])
                  for h in range(8):
                      S.op("act", lambda en, h=h: en.activation(out=pb[:, h, :], in_=sm[:, h, :], func=AF.Exp,
                                                                bias=att[:, 8 + h:9 + h], scale=0.125, accum_out=att[:, 16 + h:17 + h]),
                           reads=[sm_b[h // 2], att_b], writes=[pb_b, att_b])
                  S.op("dve", lambda en: en.tensor_tensor(out=att[:, 24:32], in0=sinks[:, :], in1=att[:, 0:8], op=ALU.subtract),
                       reads=[att_b, sinks_b], writes=[att_b])
                  S.op("act", lambda en: en.activation(out=att[:, 24:32], in_=att[:, 24:32], func=AF.Exp), reads=[att_b], writes=[att_b])
                  S.op("dve", lambda en: en.tensor_tensor(out=att[:, 32:40], in0=att[:, 16:24], in1=att[:, 24:32], op=ALU.add),
                       reads=[att_b], writes=[att_b])
                  S.op("dve", lambda en: en.reciprocal(out=att[:, 32:40], in_=att[:, 32:40]), reads=[att_b], writes=[att_b])
                  for half2 in range(2):
                      ptt, pttb_ = c.ps.get()
                      pttb = ps_bf(ptt)
                      for j in range(8):
                          idx = half2 * 8 + j
                          h, hf = idx // 2, idx % 2
                          S.op("pe", lambda en, j=j, h=h, hf=hf, pttb=pttb: en.transpose(
                              out=pttb[:, j * 128:(j + 1) * 128], in_=pb[:, h, hf * 128:(hf + 1) * 128], identity=c.ident_bf[:, :]),
                              reads=[pb_b, c.ident_buf], writes=[pttb_])
                      eng = "act" if half2 == 0 else "dve"
                      if eng == "act":
                          S.op("act", lambda en, half2=half2, pttb=pttb: en.copy(
                              out=pT[:, half2 * 8:half2 * 8 + 8, :], in_=pttb[:, :].rearrange("p (j t) -> p j t", j=8)),
                              reads=[pttb_], writes=[pT_b[half2]])
                      else:
                          S.op("dve", lambda en, half2=half2, pttb=pttb: en.tensor_copy(
                              out=pT[:, half2 * 8:half2 * 8 + 8, :], in_=pttb[:, :].rearrange("p (j t) -> p j t", j=8)),
                              reads=[pttb_], writes=[pT_b[half2]])
                  po, pob = c.ps.get()
                  for h in range(8):
                      gk = h // 4
                      S.op("pe", lambda en, h=h, gk=gk, prv=prv: en.matmul(po[:, h * 64:(h + 1) * 64], lhsT=pT[:, h * 2, :],
                                                                             rhs=vb[prv][:, gk * 64:(gk + 1) * 64], start=True, stop=False),
                           reads=[pT_b[h // 4], vb_b[prv]], writes=[pob])
                      S.op("pe", lambda en, h=h, gk=gk, cur=cur: en.matmul(po[:, h * 64:(h + 1) * 64], lhsT=pT[:, h * 2 + 1, :],
                                                                             rhs=vb[cur][:, gk * 64:(gk + 1) * 64], start=False, stop=True),
                           reads=[pT_b[h // 4], vb_b[cur]], writes=[pob])
                  S.op("dve", lambda en: en.tensor_tensor(out=ob[:, :, :], in0=po[:, :].rearrange("p (h d) -> p h d", h=8),
                                                          in1=att[:, 32:40].unsqueeze(2).to_broadcast([128, 8, 64]), op=ALU.mult),
                       reads=[pob, att_b], writes=[ob_b])
                  pt3, pt3b_ = c.ps.get()
                  pt3b = ps_bf(pt3)
                  oflat = ob[:, :, :].rearrange("p h d -> p (h d)")
                  for j in range(4):
                      S.op("pe", lambda en, j=j: en.transpose(out=pt3b[:, j * 128:(j + 1) * 128], in_=oflat[:, j * 128:(j + 1) * 128],
                                                              identity=c.ident_bf[:, :]), reads=[ob_b, c.ident_buf], writes=[pt3b_])
                  S.op("act", lambda en, t=t: en.copy(out=yT[:, 0:4, t * 128:(t + 1) * 128],
                                                      in_=pt3b[:, 0:512].rearrange("p (j t) -> p j t", j=4)),
                       reads=[pt3b_], writes=yT_b[0:4])
              _chk(7)
              for t in range(4):
                  halves = [c.ps.get(), c.ps.get()]
                  for kc in range(8):
                      for hf in range(2):
                          ph, phb = halves[hf]
                          S.op("pe", lambda en, ph=ph, kc=kc, hf=hf, t=t: en.matmul(
                              ph[:, :], lhsT=yT[:, kc, t * 128:(t + 1) * 128], rhs=Wout[:, kc, hf * 512:(hf + 1) * 512],
                              start=(kc == 0), stop=(kc == 7)), reads=[yT_b[kc], Wout_b], writes=[phb])
                  i = xcnt[0] % 2
                  xcnt[0] += 1
                  r0 = g * GT + t * 128
                  S.dma("sp", xin[i][:, :], src[r0:r0 + 128, :], writes=[xin_b[i]])
                  emit_ln_epilogue(c, (v[i], v_b[i], st[i], mv[i], smm_b[i]), halves, xin[i], xin_b[i],
                                   g_t, b_t, gb_buf, xo[i], xo_b[i])
                  outs.append(S.dma("sp", dst[r0:r0 + 128, :], xo[i][:, :], reads=[xo_b[i]]))
      except _Stop:
        pass
    S.barrier()
    return outs


OZ, OXBC, ODT, ORQ, ORK, ORV, ORG = 0, 1024, 2560, 2576, 3088, 3600, 4624
ST_W = 2064


def emit_odd(c, o, layer, src, dst, halo, NT):
    S, nc = c.S, c.nc
    GT = 512
    ngroups = NT // GT
    nchunks = NT // 128
    dr = c.dram
    outs = []
    Win_d = dr[f"od_w_in{o}"]
    with ExitStack() as es0:
        sb0 = lambda name, shape, dt: _sb(c, es0, "o_" + name, shape, dt)
        Sst = sb0("Sst", [128, 1024], F32); Sst_b = Buf("Sst")
        Rst = sb0("Rst", [128, 1024], F32); Rst_b = Buf("Rst")
        Atot = sb0("Atot", [128, 16], F32); Atot_b = Buf("Atot")
        otab = sb0("otab", [128, 32], F32); otab_b = Buf("otab")
        ptab = sb0("ptab", [128, 64], F32); ptab_b = Buf("ptab")
        wk5 = sb0("wk5", [5, 1536], F32); wk5_b = Buf("wk5")
        cw = sb0("cw", [128, 12, 5], F32); cw_b = Buf("cw")
        triu = sb0("triu", [128, 128], F32)
        trisl = sb0("trisl", [128, 128], F32)
        m01 = sb0("m01", [128, 128], F32)
        tri_b = Buf("tri")
        xin = [sb0(f"xin{i}", [128, 1024], F32) for i in range(2)]; xin_b = [Buf(f"oxin{i}") for i in range(2)]
        xbf = [sb0(f"xbf{i}", [128, 1024], BF16) for i in range(2)]; xbf_b = [Buf(f"oxbf{i}") for i in range(2)]
        xT = sb0("xT", [128, 8, GT], BF16); xT_b = Buf("oxT")
        xTh = sb0("xTh", [128, 8, 128], BF16); xTh_b = Buf("oxTh")
        sm16 = sb0("sm16", [128, 12, 16], F32); sm16_b = Buf("sm16")
        S.dma("sp", otab[:, :], dr["odd_tab"], writes=[otab_b])
        load_bcast_row(c, "sp", ptab[:, 0:16], ptab_b, dr[f"od_a_log{o}"])
        load_bcast_row(c, "sp", ptab[:, 16:32], ptab_b, dr[f"od_dt_bias{o}"])
        load_bcast_row(c, "sp", ptab[:, 32:48], ptab_b, dr[f"od_d_skip{o}"])
        S.dma("sp", wk5[0:4, :], dr[f"od_conv_w{o}"], writes=[wk5_b])
        S.dma("sp", wk5[4:5, :], dr[f"od_conv_b{o}"], writes=[wk5_b])
        S.dma("sp", triu[:, :], dr["triu"], writes=[tri_b])
        S.dma("sp", trisl[:, :], dr["trisl"], writes=[tri_b])
        S.dma("sp", m01[:, :], dr["triu"], writes=[tri_b])
        S.op("act", lambda en: en.activation(out=ptab[:, 0:16], in_=ptab[:, 0:16], func=AF.Exp), reads=[ptab_b], writes=[ptab_b])
        S.op("dve", lambda en: en.tensor_scalar_mul(out=ptab[:, 0:16], in0=ptab[:, 0:16], scalar1=-1.0), reads=[ptab_b], writes=[ptab_b])
        for cc in range(12):
            pt, ptb_ = c.ps.get()
            S.op("pe", lambda en, cc=cc, pt=pt: en.transpose(out=pt[:, 0:5], in_=wk5[0:5, cc * 128:(cc + 1) * 128],
                                                            identity=c.ident_f[0:5, 0:5]), reads=[wk5_b, c.identf_buf], writes=[ptb_])
            S.op("dve", lambda en, cc=cc, pt=pt: en.tensor_scalar_mul(out=cw[:, cc, :], in0=pt[:, 0:5], scalar1=0.5),
                 reads=[ptb_], writes=[cw_b])

        xcnt = [0]

        def load_xT(row0, dstT, dstT_b, slot, src_ap):
            i = xcnt[0] % 2
            xcnt[0] += 1
            S.dma("sp", xin[i][:, :], src_ap[row0:row0 + 128, :], writes=[xin_b[i]])
            emit_xT(c, dstT, dstT_b, slot, xin[i], xin_b[i], xbf[i], xbf_b[i])

        def wload(es, name, col0, ncol):
            t = _sb(c, es, "o_w" + name, [128, 8, ncol], BF16)
            b = Buf("w" + name)
            step = 1024
            for s0 in range(0, ncol, step):
                n = min(step, ncol - s0)
                S.dma("pool", t[:, :, s0:s0 + n], Win_d[:, col0 + s0:col0 + s0 + n].rearrange("(k p) n -> p k n", p=128),
                      writes=[b])
            return t, b

        def ssd_setup(es):
            d = {}
            sb = lambda name, shape, dt: _sb(c, es, "s_" + name, shape, dt)
            d["Wx"], d["Wx_b"] = wload(es, "xbc", OXBC, 1536 + 16)
            d["cin"] = [sb(f"cin{i}", [128, 3 + GT], F32) for i in range(2)]; d["cin_b"] = [Buf(f"cin{i}") for i in range(2)]
            d["acc"] = [sb(f"acc{i}", [128, GT], F32) for i in range(2)]; d["acc_b"] = [Buf(f"acc{i}") for i in range(2)]
            d["tg"] = [sb(f"tg{i}", [128, GT], F32) for i in range(2)]; d["tg_b"] = [Buf(f"stg{i}") for i in range(2)]
            d["hist"] = sb("hist", [128, 12, 3], F32); d["hist_b"] = [Buf(f"hist{i}") for i in range(12)]
            d["xbcT"] = sb("xbcT", [128, 12, GT], BF16); d["xbcT_b"] = [Buf(f"xbcT{i}") for i in range(12)]
            d["Btok"] = sb("Btok", [128, 256], BF16); d["Btok_b"] = Buf("Btok")
            d["xdte"] = sb("xdte", [128, 1024], BF16); d["xdte_b"] = Buf("xdte")
            d["tmpS"] = sb("tmpS", [128, 1024], F32); d["tmpS_b"] = Buf("tmpS")
            return d

        def ssd_features(d, xTsrc, xTsrc_b, ncols, halo_mode):
            Wx, Wx_b = d["Wx"], d["Wx_b"]
            for cc in range(12):
                pp, ppb = c.ps.get()
                for k in range(8):
                    S.op("pe", lambda en, k=k, cc=cc, pp=pp: en.matmul(pp[:, 0:ncols], lhsT=Wx[:, k, cc * 128:(cc + 1) * 128],
                                                                       rhs=xTsrc[:, k, 0:ncols], start=(k == 0), stop=(k == 7)),
                         reads=[Wx_b, xTsrc_b], writes=[ppb])
                if halo_mode:
                    S.op("act", lambda en, cc=cc, pp=pp: en.copy(out=d["hist"][:, cc, :], in_=pp[:, ncols - 3:ncols]),
                         reads=[ppb], writes=[d["hist_b"][cc]])
                    continue
                i = cc % 2
                cin, cin_b = d["cin"][i], d["cin_b"][i]
                acc, acc_b = d["acc"][i], d["acc_b"][i]
                tgx, tgx_b = d["tg"][i], d["tg_b"][i]
                S.op("act", lambda en, cin=cin, pp=pp: en.copy(out=cin[:, 3:3 + ncols], in_=pp[:, 0:ncols]), reads=[ppb], writes=[cin_b])
                S.op("act", lambda en, cin=cin, cc=cc: en.copy(out=cin[:, 0:3], in_=d["hist"][:, cc, :]),
                     reads=[d["hist_b"][cc]], writes=[cin_b])
                S.op("act", lambda en, cin=cin, cc=cc: en.copy(out=d["hist"][:, cc, :], in_=cin[:, ncols:ncols + 3]),
                     reads=[cin_b], writes=[d["hist_b"][cc]])
                S.op("dve", lambda en, cin=cin, acc=acc, cc=cc: en.tensor_scalar(
                    out=acc[:, 0:ncols], in0=cin[:, 0:ncols], scalar1=cw[:, cc, 0:1], scalar2=cw[:, cc, 4:5],
                    op0=ALU.mult, op1=ALU.add), reads=[cin_b, cw_b], writes=[acc_b])
                for k in range(1, 4):
                    S.op("dve", lambda en, cin=cin, acc=acc, cc=cc, k=k: en.scalar_tensor_tensor(
                        out=acc[:, 0:ncols], in0=cin[:, k:k + ncols], scalar=cw[:, cc, k:k + 1], in1=acc[:, 0:ncols],
                        op0=ALU.mult, op1=ALU.add), reads=[cin_b, cw_b, acc_b], writes=[acc_b])
                S.op("act", lambda en, acc=acc, tgx=tgx: en.activation(out=tgx[:, 0:ncols], in_=acc[:, 0:ncols], func=AF.Tanh),
                     reads=[acc_b], writes=[tgx_b])
                S.op("dve", lambda en, acc=acc, tgx=tgx, cc=cc: en.scalar_tensor_tensor(
                    out=d["xbcT"][:, cc, 0:ncols], in0=tgx[:, 0:ncols], scalar=1.0, in1=acc[:, 0:ncols],
                    op0=ALU.add, op1=ALU.mult), reads=[tgx_b, acc_b], writes=[d["xbcT_b"][cc]])

        def ssd_dt_group(d):
            Wx, Wx_b = d["Wx"], d["Wx_b"]
            pd, pdb = c.ps.get()
            for t in range(4):
                for k in range(8):
                    S.op("pe", lambda en, k=k, t=t: en.matmul(pd[:, t * 16:(t + 1) * 16], lhsT=xT[:, k, t * 128:(t + 1) * 128],
                                                              rhs=Wx[:, k, 1536:1552], start=(k == 0), stop=(k == 7)),
                         reads=[Wx_b, xT_b], writes=[pdb])
            xr = sm16[:, 0:4, :]
            S.op("dve", lambda en: en.tensor_tensor(out=xr, in0=pd[:, 0:64].rearrange("p (t h) -> p t h", t=4),
                                                    in1=ptab[:, 16:32].unsqueeze(1).to_broadcast([128, 4, 16]), op=ALU.add),
                 reads=[pdb, ptab_b], writes=[sm16_b])
            ab = sm16[:, 4:8, :]
            S.op("act", lambda en: en.activation(out=ab, in_=xr, func=AF.Abs), reads=[sm16_b], writes=[sm16_b])
            S.op("act", lambda en: en.activation(out=ab, in_=ab, func=AF.Exp, scale=-1.0), reads=[sm16_b], writes=[sm16_b])
            S.op("act", lambda en: en.activation(out=ab, in_=ab, func=AF.Ln, bias=1.0, scale=1.0), reads=[sm16_b], writes=[sm16_b])
            S.op("dve", lambda en: en.tensor_scalar_max(out=xr, in0=xr, scalar1=0.0), reads=[sm16_b], writes=[sm16_b])
            S.op("dve", lambda en: en.tensor_tensor(out=xr, in0=xr, in1=ab, op=ALU.add), reads=[sm16_b], writes=[sm16_b])
            S.op("dve", lambda en: en.tensor_tensor(out=ab, in0=xr, in1=ptab[:, 0:16].unsqueeze(1).to_broadcast([128, 4, 16]),
                                                    op=ALU.mult), reads=[sm16_b, ptab_b], writes=[sm16_b])

        def ssd_chunk_scalars(t):
            da = sm16[:, 4 + t, :]
            pa, pab = c.ps.get()
            S.op("pe", lambda en: en.matmul(pa[:, 0:16], lhsT=triu[:, :], rhs=da, start=True, stop=True),
                 reads=[tri_b, sm16_b], writes=[pab])
            S.op("pe", lambda en: en.matmul(pa[:, 16:32], lhsT=c.ones_f[:, :], rhs=da, start=True, stop=True),
                 reads=[c.ones_buf, sm16_b], writes=[pab])
            S.op("act", lambda en: en.copy(out=sm16[:, 8, :], in_=pa[:, 0:16]), reads=[pab], writes=[sm16_b])
            S.op("act", lambda en: en.activation(out=sm16[:, 9, :], in_=pa[:, 0:16], func=AF.Exp), reads=[pab], writes=[sm16_b])
            S.op("dve", lambda en: en.tensor_tensor(out=sm16[:, 10, :], in0=pa[:, 16:32], in1=sm16[:, 8, :], op=ALU.subtract),
                 reads=[pab, sm16_b], writes=[sm16_b])
            S.op("act", lambda en: en.activation(out=sm16[:, 10, :], in_=sm16[:, 10, :], func=AF.Exp), reads=[sm16_b], writes=[sm16_b])
            S.op("dve", lambda en: en.tensor_tensor(out=sm16[:, 10, :], in0=sm16[:, 10, :], in1=sm16[:, t, :], op=ALU.mult),
                 reads=[sm16_b], writes=[sm16_b])
            S.op("act", lambda en: en.activation(out=sm16[:, 11, :], in_=pa[:, 16:32], func=AF.Exp), reads=[pab], writes=[sm16_b])
            S.op("dve", lambda en: en.tensor_tensor(out=Atot[:, :], in0=Atot[:, :], in1=pa[:, 16:32], op=ALU.add),
                 reads=[pab, Atot_b], writes=[Atot_b])

        def ssd_tok_and_state(d, t, xs_extra=None):
            xbcT, xbcT_b = d["xbcT"], d["xbcT_b"]
            px, pxb = c.ps.get()
            pxv = ps_bf(px)
            for j in range(8):
                S.op("pe", lambda en, j=j: en.transpose(out=pxv[:, j * 128:(j + 1) * 128], in_=xbcT[:, j, t * 128:(t + 1) * 128],
                                                        identity=c.ident_bf[:, :]), reads=[xbcT_b[j], c.ident_buf], writes=[pxb])
            pB, pBb = c.ps.get()
            pBv = ps_bf(pB)
            for j in range(2):
                S.op("pe", lambda en, j=j: en.transpose(out=pBv[:, j * 128:(j + 1) * 128], in_=xbcT[:, 8 + j, t * 128:(t + 1) * 128],
                                                        identity=c.ident_bf[:, :]), reads=[xbcT_b[8 + j], c.ident_buf], writes=[pBb])
            S.op("act", lambda en: en.copy(out=d["Btok"][:, :], in_=pBv[:, 0:256]), reads=[pBb], writes=[d["Btok_b"]])
            xs3 = pxv[:, 0:1024].rearrange("p (h q) -> p h q", h=16)
            S.op("dve", lambda en: en.tensor_tensor(out=d["xdte"][:, :].rearrange("p (h q) -> p h q", h=16), in0=xs3,
                                                    in1=sm16[:, 10, :].unsqueeze(2).to_broadcast([128, 16, 64]), op=ALU.mult),
                 reads=[pxb, sm16_b], writes=[d["xdte_b"]])
            if xs_extra is not None:
                xs_extra(xs3, pxb)
            pst = [c.ps.get(), c.ps.get()]
            for g in range(2):
                S.op("pe", lambda en, g=g: en.matmul(pst[g][0][:, :], lhsT=d["Btok"][:, g * 128:(g + 1) * 128],
                                                     rhs=d["xdte"][:, g * 512:(g + 1) * 512], start=True, stop=True),
                     reads=[d["Btok_b"], d["xdte_b"]], writes=[pst[g][1]])
            return pst

        def ssd_state_update(d, pst):
            S.op("dve", lambda en: en.tensor_tensor(out=Sst[:, :].rearrange("p (h q) -> p h q", h=16),
                                                    in0=Sst[:, :].rearrange("p (h q) -> p h q", h=16),
                                                    in1=sm16[:, 11, :].unsqueeze(2).to_broadcast([128, 16, 64]), op=ALU.mult),
                 reads=[Sst_b, sm16_b], writes=[Sst_b])
            for g in range(2):
                S.op("dve", lambda en, g=g: en.tensor_tensor(out=Sst[:, g * 512:(g + 1) * 512], in0=Sst[:, g * 512:(g + 1) * 512],
                                                             in1=pst[g][0][:, :], op=ALU.add), reads=[Sst_b, pst[g][1]], writes=[Sst_b])

        def ret_setup(es, with_q):
            d = {}
            sb = lambda name, shape, dt: _sb(c, es, "r_" + name, shape, dt)
            if with_q:
                d["Wr"], d["Wr_b"] = wload(es, "ret", ORQ, 3072)
                d["off"] = {"q": 0, "k": 512, "v": 1024, "g": 2048}
            else:
                d["Wr"], d["Wr_b"] = wload(es, "ret", ORK, 1536)
                d["off"] = {"k": 0, "v": 512}
            d["rope"] = [sb(f"rope{i}", [128, 128], F32) for i in range(2)]; d["rope_b"] = [Buf(f"rrope{i}") for i in range(2)]
            d["rr"] = [sb(f"rr{i}", [128, 4, 64], F32) for i in range(2)]; d["rr_b"] = [Buf(f"rr{i}") for i in range(2)]
            d["kr"] = sb("kr", [128, 4, 128], F32); d["kr_b"] = Buf("kr")
            d["kp"] = sb("kp", [128, 512], BF16); d["kp_b"] = Buf("kp")
            d["vb"] = sb("vb", [128, 1024], BF16); d["vb_b"] = Buf("rvb")
            return d

        def ret_rope(d, psrc, psrc_b, ri, scale_cols, dstt, dst_b):
            s3 = psrc.rearrange("p (h e) -> p h e", h=4)
            rope_t, rope_tb = d["rope"][ri], d["rope_b"][ri]
            cs = rope_t[:, 0:64].unsqueeze(1).to_broadcast([128, 4, 64])
            sn = rope_t[:, 64:128].unsqueeze(1).to_broadcast([128, 4, 64])
            t1, t2 = s3[:, :, 0:64], s3[:, :, 64:128]
            r0, r1 = d["rr"][0], d["rr"][1]
            kr = d["kr"]
            S.op("dve", lambda en: en.tensor_tensor(out=r0[:, :, :], in0=t1, in1=cs, op=ALU.mult), reads=[psrc_b, rope_tb], writes=[d["rr_b"][0]])
            S.op("dve", lambda en: en.tensor_tensor(out=r1[:, :, :], in0=t2, in1=sn, op=ALU.mult), reads=[psrc_b, rope_tb], writes=[d["rr_b"][1]])
            S.op("dve", lambda en: en.tensor_tensor(out=kr[:, :, 0:64], in0=r0[:, :, :], in1=r1[:, :, :], op=ALU.subtract),
                 reads=d["rr_b"], writes=[d["kr_b"]])
            S.op("dve", lambda en: en.tensor_tensor(out=r0[:, :, :], in0=t2, in1=cs, op=ALU.mult), reads=[psrc_b, rope_tb], writes=[d["rr_b"][0]])
            S.op("dve", lambda en: en.tensor_tensor(out=r1[:, :, :], in0=t1, in1=sn, op=ALU.mult), reads=[psrc_b, rope_tb], writes=[d["rr_b"][1]])
            S.op("dve", lambda en: en.tensor_tensor(out=kr[:, :, 64:128], in0=r0[:, :, :], in1=r1[:, :, :], op=ALU.add),
                 reads=d["rr_b"], writes=[d["kr_b"]])
            S.op("dve", lambda en: en.tensor_tensor(out=dstt[:, :].rearrange("p (h e) -> p h e", h=4), in0=kr[:, :, :],
                                                    in1=otab[:, scale_cols[0]:scale_cols[1]].unsqueeze(2).to_broadcast([128, 4, 128]),
                                                    op=ALU.mult), reads=[d["kr_b"], otab_b], writes=[dst_b])

        def ret_kv(d, t, chunk_idx):
            Wr, Wr_b, off = d["Wr"], d["Wr_b"], d["off"]
            ri = chunk_idx % 2
            S.dma("sp", d["rope"][ri][:, :], dr["rope_d"][:, chunk_idx, :], writes=[d["rope_b"][ri]])
            pk, pkb = c.ps.get()
            for k in range(8):
                S.op("pe", lambda en, k=k: en.matmul(pk[:, :], lhsT=xT[:, k, t * 128:(t + 1) * 128],
                                                     rhs=Wr[:, k, off["k"]:off["k"] + 512], start=(k == 0), stop=(k == 7)),
                     reads=[Wr_b, xT_b], writes=[pkb])
            ret_rope(d, pk[:, :], pkb, ri, (4, 8), d["kp"], d["kp_b"])
            for hf in range(2):
                pv, pvb = c.ps.get()
                for k in range(8):
                    S.op("pe", lambda en, k=k, hf=hf, pv=pv: en.matmul(
                        pv[:, :], lhsT=xT[:, k, t * 128:(t + 1) * 128],
                        rhs=Wr[:, k, off["v"] + hf * 512:off["v"] + (hf + 1) * 512], start=(k == 0), stop=(k == 7)),
                        reads=[Wr_b, xT_b], writes=[pvb])
                S.op("act", lambda en, hf=hf, pv=pv: en.copy(out=d["vb"][:, hf * 512:(hf + 1) * 512], in_=pv[:, :]),
                     reads=[pvb], writes=[d["vb_b"]])

        def ret_state_mm(d):
            pkv = [c.ps.get(), c.ps.get()]
            for h in range(4):
                pt, ptb_ = pkv[h // 2]
                S.op("pe", lambda en, h=h, pt=pt: en.matmul(pt[:, (h % 2) * 256:(h % 2) * 256 + 256], lhsT=d["kp"][:, h * 128:(h + 1) * 128],
                                                            rhs=d["vb"][:, h * 256:(h + 1) * 256], start=True, stop=True),
                     reads=[d["kp_b"], d["vb_b"]], writes=[ptb_])
            return pkv

        def ret_state_update(pkv):
            for hf in range(2):
                S.op("dve", lambda en, hf=hf: en.tensor_tensor(out=Rst[:, hf * 512:(hf + 1) * 512], in0=Rst[:, hf * 512:(hf + 1) * 512],
                                                               in1=pkv[hf][0][:, :], op=ALU.add), reads=[Rst_b, pkv[hf][1]], writes=[Rst_b])
            S.op("dve", lambda en: en.tensor_tensor(out=Rst[:, :].rearrange("p (h v) -> p h v", h=4),
                                                    in0=Rst[:, :].rearrange("p (h v) -> p h v", h=4),
                                                    in1=otab[:, 8:12].unsqueeze(2).to_broadcast([128, 4, 256]), op=ALU.mult),
                 reads=[Rst_b, otab_b], writes=[Rst_b])

        def zero_states():
            S.op("dve", lambda en: en.memset(Sst[:, :], 0.0), writes=[Sst_b])
            S.op("dve", lambda en: en.memset(Rst[:, :], 0.0), writes=[Rst_b])
            S.op("dve", lambda en: en.memset(Atot[:, :], 0.0), writes=[Atot_b])

        zero_states()
        with ExitStack() as es:
            ds = ssd_setup(es)
            dq = ret_setup(es, False)
            load_xT(0, xTh, xTh_b, 0, halo)
            ssd_features(ds, xTh, xTh_b, 128, True)
            for g in range(ngroups):
                for t in range(4):
                    load_xT(g * GT + t * 128, xT, xT_b, t, src)
                ssd_features(ds, xT, xT_b, GT, False)
                ssd_dt_group(ds)
                for t in range(4):
                    ssd_chunk_scalars(t)
                    pst = ssd_tok_and_state(ds, t)
                    ssd_state_update(ds, pst)
                    ret_kv(dq, t, g * 4 + t)
                    pkv = ret_state_mm(dq)
                    ret_state_update(pkv)
        S.barrier()
        (loc_s, loc_r), (all_s, all_r) = c.st_loc[o], c.st_all[o]
        locb = [Buf("loc_s"), Buf("loc_r")]
        allb = [Buf("all_s"), Buf("all_r")]
        S.dma("sp", loc_s[:, 0:1024], Sst[:, :], reads=[Sst_b], writes=[locb[0]])
        S.dma("sp", loc_s[:, 1024:1040], Atot[:, :], reads=[Atot_b], writes=[locb[0]])
        S.dma("sp", loc_r[:, :], Rst[:, :], reads=[Rst_b], writes=[locb[1]])
        for (lo, al, lb, ab_) in ((loc_s, all_s, locb[0], allb[0]), (loc_r, all_r, locb[1], allb[1])):
            if c.use_cc:
                S.op("pool", lambda en, lo=lo, al=al: en.collective_compute(
                    "AllGather", ALU.bypass, replica_groups=[[0, 1, 2, 3], [4, 5, 6, 7]], ins=[lo], outs=[al]),
                    reads=[lb], writes=[ab_])
            else:
                for i in range(4):
                    S.dma("sp", al[i * 128:(i + 1) * 128, :], lo, reads=[lb], writes=[ab_])
        with ExitStack() as es:
            sb = lambda name, shape, dt: _sb(c, es, "c_" + name, shape, dt)
            rec = [sb(f"rec{i}", [128, 1040], F32) for i in range(2)]; rec_b = [Buf(f"rec{i}") for i in range(2)]
            rer = [sb(f"rer{i}", [128, 1024], F32) for i in range(2)]; rer_b = [Buf(f"rer{i}") for i in range(2)]
            cf = sb("cf", [128, 16], F32); cf_b = Buf("cf")
            zero_states()
            for i in range(4):
                rb, rbb = rec[i % 2], rec_b[i % 2]
                rr_, rrb = rer[i % 2], rer_b[i % 2]
                S.dma("sp", rb[:, :], all_s[i * 128:(i + 1) * 128, :], reads=[allb[0]], writes=[rbb])
                S.dma("sp", rr_[:, :], all_r[i * 128:(i + 1) * 128, :], reads=[allb[1]], writes=[rrb])
                S.op("act", lambda en, rb=rb: en.activation(out=cf[:, :], in_=rb[:, 1024:1040], func=AF.Exp), reads=[rbb], writes=[cf_b])
                S.op("dve", lambda en, i=i: en.tensor_scalar(out=cf[:, :], in0=cf[:, :], scalar1=-1.0, scalar2=otab[:, 28 + i:29 + i],
                                                             op0=ALU.add, op1=ALU.mult), reads=[cf_b, otab_b], writes=[cf_b])
                S.op("dve", lambda en: en.tensor_scalar_add(out=cf[:, :], in0=cf[:, :], scalar1=1.0), reads=[cf_b], writes=[cf_b])
                S.op("dve", lambda en: en.tensor_tensor(out=Sst[:, :].rearrange("p (h q) -> p h q", h=16),
                                                        in0=Sst[:, :].rearrange("p (h q) -> p h q", h=16),
                                                        in1=cf[:, :].unsqueeze(2).to_broadcast([128, 16, 64]), op=ALU.mult),
                     reads=[Sst_b, cf_b], writes=[Sst_b])
                S.op("dve", lambda en, i=i, rb=rb: en.scalar_tensor_tensor(out=Sst[:, :], in0=rb[:, 0:1024], scalar=otab[:, 28 + i:29 + i],
                                                                           in1=Sst[:, :], op0=ALU.mult, op1=ALU.add),
                     reads=[rbb, otab_b, Sst_b], writes=[Sst_b])
                S.op("dve", lambda en, i=i, rr_=rr_: en.tensor_tensor(
                    out=rr_[:, :].rearrange("p (h v) -> p h v", h=4), in0=rr_[:, :].rearrange("p (h v) -> p h v", h=4),
                    in1=otab[:, 12 + 4 * i:16 + 4 * i].unsqueeze(2).to_broadcast([128, 4, 256]), op=ALU.mult),
                    reads=[rrb, otab_b], writes=[rrb])
                S.op("dve", lambda en, rr_=rr_: en.tensor_tensor(out=Rst[:, :], in0=Rst[:, :], in1=rr_[:, :], op=ALU.add),
                     reads=[rrb, Rst_b], writes=[Rst_b])
        S.barrier()

        ysT_d = c.ysT_d
        ysd_b = [Buf(f"ysd{i}") for i in range(nchunks)]
        with ExitStack() as es:
            ds = ssd_setup(es)
            sb = lambda name, shape, dt: _sb(c, es, "b_" + name, shape, dt)
            Wz, Wz_b = wload(es, "z", OZ, 1024)
            sz = sb("sz", [128, 1024], F32); sz_b = Buf("sz")
            thz = sb("thz", [128, 1024], F32); thz_b = Buf("thz")
            Sbf = sb("Sbf", [128, 1024], BF16); Sbf_b = Buf("Sbf")
            cbm = sb("cbm", [128, 2, 128], F32); cbm_b = Buf("cbm")
            Xs = sb("Xs", [128, 8, 128], F32); Xs_b = Buf("Xs")
            ET = sb("ET", [128, 8, 128], F32); ET_b = Buf("ET")
            PT = sb("PT", [128, 16, 128], BF16); PT_b = [Buf(f"PT{g}") for g in range(2)]
            xdt = sb("xdt", [128, 1024], BF16); xdt_b = Buf("xdt")
            xsD = sb("xsD", [128, 1024], BF16); xsD_b = Buf("xsD")
            yv = sb("yv", [128, 1024], F32); yv_b = Buf("yv")
            ysb = sb("ysb", [128, 1024], BF16); ysb_b = Buf("ysb")
            ysT = [sb(f"ysT{i}", [128, 8, 128], BF16) for i in range(2)]; ysT_b = [Buf(f"ysT{i}") for i in range(2)]
            rms = sb("rms", [128, 8], F32); rms_b = Buf("rms")
            S.op("dve", lambda en: en.memset(Atot[:, :], 0.0), writes=[Atot_b])
            load_xT(0, xTh, xTh_b, 0, halo)
            ssd_features(ds, xTh, xTh_b, 128, True)
            for g in range(ngroups):
                for t in range(4):
                    load_xT(g * GT + t * 128, xT, xT_b, t, src)
                ssd_features(ds, xT, xT_b, GT, False)
                ssd_dt_group(ds)
                xbcT, xbcT_b = ds["xbcT"], ds["xbcT_b"]
                for t in range(4):
                    ci = g * 4 + t
                    tc0 = t * 128
                    ssd_chunk_scalars(t)
                    S.op("act", lambda en: en.copy(out=Sbf[:, :], in_=Sst[:, :]), reads=[Sst_b], writes=[Sbf_b])
                    for hf in range(2):
                        pz, pzb = c.ps.get()
                        for k in range(8):
                            S.op("pe", lambda en, k=k, hf=hf, pz=pz: en.matmul(
                                pz[:, :], lhsT=xT[:, k, tc0:tc0 + 128], rhs=Wz[:, k, hf * 512:(hf + 1) * 512],
                                start=(k == 0), stop=(k == 7)), reads=[Wz_b, xT_b], writes=[pzb])
                        S.op("act", lambda en, hf=hf, pz=pz: en.activation(out=thz[:, hf * 512:(hf + 1) * 512], in_=pz[:, :],
                                                                            func=AF.Tanh, scale=0.5), reads=[pzb], writes=[thz_b])
                        S.op("dve", lambda en, hf=hf, pz=pz: en.scalar_tensor_tensor(
                            out=sz[:, hf * 512:(hf + 1) * 512], in0=thz[:, hf * 512:(hf + 1) * 512], scalar=1.0, in1=pz[:, :],
                            op0=ALU.add, op1=ALU.mult), reads=[thz_b, pzb], writes=[sz_b])
                    pcb, pcbb = c.ps.get()
                    for gg in range(2):
                        S.op("pe", lambda en, gg=gg: en.matmul(pcb[:, gg * 128:(gg + 1) * 128], lhsT=xbcT[:, 8 + gg, tc0:tc0 + 128],
                                                               rhs=xbcT[:, 10 + gg, tc0:tc0 + 128], start=True, stop=True),
                             reads=[xbcT_b[8 + gg], xbcT_b[10 + gg]], writes=[pcbb])
                    S.op("dve", lambda en: en.tensor_tensor(out=cbm[:, :, :], in0=pcb[:, 0:256].rearrange("p (g l) -> p g l", g=2),
                                                            in1=m01[:, :].unsqueeze(1).to_broadcast([128, 2, 128]), op=ALU.mult),
                         reads=[pcbb, tri_b], writes=[cbm_b])

                    def xs_extra(xs3, pxb):
                        S.op("dve", lambda en: en.tensor_tensor(out=xdt[:, :].rearrange("p (h q) -> p h q", h=16), in0=xs3,
                                                                in1=sm16[:, t, :].unsqueeze(2).to_broadcast([128, 16, 64]), op=ALU.mult),
                             reads=[pxb, sm16_b], writes=[xdt_b])
                        S.op("dve", lambda en: en.tensor_tensor(out=xsD[:, :].rearrange("p (h q) -> p h q", h=16), in0=xs3,
                                                                in1=ptab[:, 32:48].unsqueeze(2).to_broadcast([128, 16, 64]), op=ALU.mult),
                             reads=[pxb, ptab_b], writes=[xsD_b])
                    pst = ssd_tok_and_state(ds, t, xs_extra)
                    for gg in range(2):
                        S.op("dve", lambda en, gg=gg: en.tensor_tensor(
                            out=Xs[:, :, :], in0=sm16[:, 4 + t, gg * 8:(gg + 1) * 8].unsqueeze(2).to_broadcast([128, 8, 128]),
                            in1=triu[:, :].unsqueeze(1).to_broadcast([128, 8, 128]), op=ALU.mult),
                            reads=[sm16_b, tri_b], writes=[Xs_b])
                        pseg = [c.ps.get(), c.ps.get()]
                        for q in range(2):
                            S.op("pe", lambda en, q=q, pseg=pseg: en.matmul(
                                pseg[q][0][:, :], lhsT=trisl[:, :], rhs=Xs[:, q * 4:(q + 1) * 4, :].rearrange("p h l -> p (h l)"),
                                start=True, stop=True), reads=[tri_b, Xs_b], writes=[pseg[q][1]])
                            S.op("act", lambda en, q=q, pseg=pseg: en.activation(
                                out=ET[:, q * 4:(q + 1) * 4, :].rearrange("p h l -> p (h l)"), in_=pseg[q][0][:, :], func=AF.Exp),
                                reads=[pseg[q][1]], writes=[ET_b])
                        S.op("dve", lambda en, gg=gg: en.tensor_tensor(
                            out=PT[:, gg * 8:(gg + 1) * 8, :], in0=ET[:, :, :],
                            in1=cbm[:, gg, :].unsqueeze(1).to_broadcast([128, 8, 128]), op=ALU.mult),
                            reads=[ET_b, cbm_b], writes=[PT_b[gg]])
                    for gg in range(2):
                        po, pob = c.ps.get()
                        S.op("pe", lambda en, gg=gg, po=po: en.matmul(po[:, :], lhsT=xbcT[:, 10 + gg, tc0:tc0 + 128],
                                                                      rhs=Sbf[:, gg * 512:(gg + 1) * 512], start=True, stop=True),
                             reads=[xbcT_b[10 + gg], Sbf_b], writes=[pob])
                        S.op("dve", lambda en, gg=gg, po=po: en.tensor_tensor(
                            out=yv[:, gg * 512:(gg + 1) * 512].rearrange("p (h q) -> p h q", h=8),
                            in0=po[:, :].rearrange("p (h q) -> p h q", h=8),
                            in1=sm16[:, 9, gg * 8:(gg + 1) * 8].unsqueeze(2).to_broadcast([128, 8, 64]), op=ALU.mult),
                            reads=[pob, sm16_b], writes=[yv_b])
                    ssd_state_update(ds, pst)
                    for gg in range(2):
                        pd_, pdb_ = c.ps.get()
                        S.op("pe", lambda en, gg=gg, pd_=pd_: en.matmul(pd_[:, :], lhsT=c.ident_bf[:, :], rhs=xsD[:, gg * 512:(gg + 1) * 512],
                                                                        start=True, stop=False), reads=[c.ident_buf, xsD_b], writes=[pdb_])
                        for hh in range(8):
                            h = gg * 8 + hh
                            S.op("pe", lambda en, h=h, hh=hh, pd_=pd_: en.matmul(
                                pd_[:, hh * 64:(hh + 1) * 64], lhsT=PT[:, h, :], rhs=xdt[:, h * 64:(h + 1) * 64],
                                start=False, stop=(hh == 7)), reads=[PT_b[gg], xdt_b], writes=[pdb_])
                        S.op("dve", lambda en, gg=gg, pd_=pd_: en.tensor_tensor(out=yv[:, gg * 512:(gg + 1) * 512],
                                                                                in0=yv[:, gg * 512:(gg + 1) * 512], in1=pd_[:, :], op=ALU.add),
                             reads=[yv_b, pdb_], writes=[yv_b])
                    S.op("dve", lambda en: en.scalar_tensor_tensor(out=yv[:, :], in0=yv[:, :], scalar=0.5, in1=sz[:, :],
                                                                   op0=ALU.mult, op1=ALU.mult), reads=[yv_b, sz_b], writes=[yv_b])
                    for gg in range(2):
                        S.op("act", lambda en, gg=gg: en.activation(out=thz[:, gg * 512:(gg + 1) * 512], in_=yv[:, gg * 512:(gg + 1) * 512],
                                                                    func=AF.Square, accum_out=rms[:, gg:gg + 1]),
                             reads=[yv_b], writes=[thz_b, rms_b])
                    S.op("dve", lambda en: en.tensor_scalar(out=rms[:, 2:4], in0=rms[:, 0:2], scalar1=1.0 / 512.0, scalar2=float(EPS),
                                                            op0=ALU.mult, op1=ALU.add), reads=[rms_b], writes=[rms_b])
                    S.op("act", lambda en: en.activation(out=rms[:, 2:4], in_=rms[:, 2:4], func=AF.Sqrt), reads=[rms_b], writes=[rms_b])
                    S.op("dve", lambda en: en.reciprocal(out=rms[:, 2:4], in_=rms[:, 2:4]), reads=[rms_b], writes=[rms_b])
                    S.op("dve", lambda en: en.tensor_tensor(out=ysb[:, :].rearrange("p (g q) -> p g q", g=2),
                                                            in0=yv[:, :].rearrange("p (g q) -> p g q", g=2),
                                                            in1=rms[:, 2:4].unsqueeze(2).to_broadcast([128, 2, 512]), op=ALU.mult),
                         reads=[yv_b, rms_b], writes=[ysb_b])
                    pt, ptb_ = c.ps.get()
                    ptv = ps_bf(pt)
                    for j in range(8):
                        S.op("pe", lambda en, j=j, ptv=ptv: en.transpose(out=ptv[:, j * 128:(j + 1) * 128], in_=ysb[:, j * 128:(j + 1) * 128],
                                                                         identity=c.ident_bf[:, :]), reads=[ysb_b, c.ident_buf], writes=[ptb_])
                    yi = ci % 2
                    S.op("act", lambda en, yi=yi, ptv=ptv: en.copy(out=ysT[yi][:, :, :], in_=ptv[:, :].rearrange("p (j t) -> p j t", j=8)),
                         reads=[ptb_], writes=[ysT_b[yi]])
                    S.dma("sp", ysT_d[:, :, ci * 128:(ci + 1) * 128].rearrange("k p t -> p k t"), ysT[yi][:, :, :],
                          reads=[ysT_b[yi]], writes=[ysd_b[ci]])
        S.barrier()

        with ExitStack() as es:
            dq = ret_setup(es, True)
            sb = lambda name, shape, dt: _sb(c, es, "d_" + name, shape, dt)
            Wr, Wr_b, off = dq["Wr"], dq["Wr_b"], dq["off"]
            Wout = sb("wout", [128, 16, 1024], BF16); Wout_b = Buf("owout")
            ng = sb("ng", [128, 8], F32); ng_b = Buf("ng")
            gng = sb("gng", [128, 1024], F32); gnb = sb("gnb", [128, 1024], F32); gn_b = Buf("gn")
            g_t = sb("g", [128, 1024], F32); b_t = sb("b", [128, 1024], F32); gb_buf = Buf("ogb")
            qp = sb("qp", [128, 512], BF16); qp_b = Buf("qp")
            qT = sb("qT", [128, 4, 128], BF16); qT_b = Buf("oqT")
            kT = sb("kT", [128, 4, 128], BF16); kT_b = Buf("okT")
            sg = sb("sg", [128, 1024], F32); sg_b = Buf("sg")
            thg = sb("thg", [128, 1024], F32); thg_b = Buf("thg")
            scT = sb("scT", [128, 4, 128], BF16); scT_b = Buf("scT")
            Rbf = sb("Rbf", [128, 1024], BF16); Rbf_b = Buf("Rbf")
            yr = sb("yr", [128, 1024], F32); yr_b = Buf("yr")
            yrb = sb("yrb", [128, 1024], BF16); yrb_b = Buf("yrb")
            yrT = sb("yrT", [128, 8, 128], BF16); yrT_b = Buf("yrT")
            ysl = [sb(f"ysl{i}", [128, 8, 128], BF16) for i in range(2)]; ysl_b = [Buf(f"ysl{i}") for i in range(2)]
            gst = sb("gst", [128, 4, 6], F32); gmv = sb("gmv", [128, 4, 4], F32); gs_b = Buf("gs")
            v = sb("v", [128, 1024], F32); v_b = Buf("ov")
            st = sb("st", [128, 12], F32); mv = sb("mv", [128, 4], F32); smm_b = Buf("osmm")
            xo = [sb(f"xo{i}", [128, 1024], F32) for i in range(2)]; xo_b = [Buf(f"oxo{i}") for i in range(2)]
            xres = sb("xres", [128, 1024], F32); xres_b = Buf("xres")
            S.dma("pool", Wout[:, 0:8, :], dr[f"od_w_out{o}"][0:1024, :].rearrange("(k p) n -> p k n", p=128), writes=[Wout_b])
            S.dma("pool", Wout[:, 8:16, :], dr[f"od_w_out{o}"][1024:2048, :].rearrange("(k p) n -> p k n", p=128), writes=[Wout_b])
            S.dma("sp", ng[:, :], dr[f"od_ssm_norm_g{o}"].rearrange("o (c p) -> p (o c)", p=128), writes=[ng_b], allow_slow_non_contiguous=True)
            load_bcast_row(c, "sp", gng, gn_b, dr[f"od_ret_gn_g{o}"])
            load_bcast_row(c, "sp", gnb, gn_b, dr[f"od_ret_gn_b{o}"])
            load_bcast_row(c, "sp", g_t, gb_buf, dr[f"ln_mix_g{layer}"])
            load_bcast_row(c, "sp", b_t, gb_buf, dr[f"ln_mix_b{layer}"])
            for kc in range(8):
                S.op("dve", lambda en, kc=kc: en.tensor_scalar_mul(out=Wout[:, kc, :], in0=Wout[:, kc, :], scalar1=ng[:, kc:kc + 1]),
                     reads=[Wout_b, ng_b], writes=[Wout_b])
            for g in range(ngroups):
                for t in range(4):
                    load_xT(g * GT + t * 128, xT, xT_b, t, src)
                for t in range(4):
                    ci = g * 4 + t
                    tc0 = t * 128
                    ri = ci % 2
                    yi = ci % 2
                    S.dma("sp", ysl[yi][:, :, :], ysT_d[:, :, ci * 128:(ci + 1) * 128].rearrange("k p t -> p k t"),
                          reads=[ysd_b[ci]], writes=[ysl_b[yi]])
                    ret_kv(dq, t, ci)
                    pq, pqb = c.ps.get()
                    for k in range(8):
                        S.op("pe", lambda en, k=k: en.matmul(pq[:, :], lhsT=xT[:, k, tc0:tc0 + 128], rhs=Wr[:, k, 0:512],
                                                             start=(k == 0), stop=(k == 7)), reads=[Wr_b, xT_b], writes=[pqb])
                    ret_rope(dq, pq[:, :], pqb, ri, (0, 4), qp, qp_b)
                    for (srcp, srcp_b, dT, dT_b) in ((qp, qp_b, qT, qT_b), (dq["kp"], dq["kp_b"], kT, kT_b)):
                        pt, ptb_ = c.ps.get()
                        ptv = ps_bf(pt)
                        for j in range(4):
                            S.op("pe", lambda en, j=j, ptv=ptv, srcp=srcp: en.transpose(
                                out=ptv[:, j * 128:(j + 1) * 128], in_=srcp[:, j * 128:(j + 1) * 128], identity=c.ident_bf[:, :]),
                                reads=[srcp_b, c.ident_buf], writes=[ptb_])
                        S.op("act", lambda en, ptv=ptv, dT=dT: en.copy(out=dT[:, :, :], in_=ptv[:, 0:512].rearrange("p (j t) -> p j t", j=4)),
                             reads=[ptb_], writes=[dT_b])
                    for hf in range(2):
                        pg, pgb = c.ps.get()
                        for k in range(8):
                            S.op("pe", lambda en, k=k, hf=hf, pg=pg: en.matmul(
                                pg[:, :], lhsT=xT[:, k, tc0:tc0 + 128], rhs=Wr[:, k, 2048 + hf * 512:2048 + (hf + 1) * 512],
                                start=(k == 0), stop=(k == 7)), reads=[Wr_b, xT_b], writes=[pgb])
                        S.op("act", lambda en, hf=hf, pg=pg: en.activation(out=thg[:, hf * 512:(hf + 1) * 512], in_=pg[:, :],
                                                                            func=AF.Tanh, scale=0.5), reads=[pgb], writes=[thg_b])
                        S.op("dve", lambda en, hf=hf, pg=pg: en.scalar_tensor_tensor(
                            out=sg[:, hf * 512:(hf + 1) * 512], in0=thg[:, hf * 512:(hf + 1) * 512], scalar=1.0, in1=pg[:, :],
                            op0=ALU.add, op1=ALU.mult), reads=[thg_b, pgb], writes=[sg_b])
                    psc, pscb = c.ps.get()
                    for h in range(4):
                        S.op("pe", lambda en, h=h: en.matmul(psc[:, h * 128:(h + 1) * 128], lhsT=kT[:, h, :], rhs=qT[:, h, :],
                                                             start=True, stop=True), reads=[kT_b, qT_b], writes=[pscb])
                    S.op("dve", lambda en: en.tensor_tensor(out=scT[:, :, :], in0=psc[:, :].rearrange("p (h l) -> p h l", h=4),
                                                            in1=m01[:, :].unsqueeze(1).to_broadcast([128, 4, 128]), op=ALU.mult),
                         reads=[pscb, tri_b], writes=[scT_b])
                    S.op("act", lambda en: en.copy(out=Rbf[:, :], in_=Rst[:, :]), reads=[Rst_b], writes=[Rbf_b])
                    py = [c.ps.get(), c.ps.get()]
                    for h in range(4):
                        pt, ptb_ = py[h // 2]
                        cs_ = slice((h % 2) * 256, (h % 2) * 256 + 256)
                        S.op("pe", lambda en, h=h, pt=pt, cs_=cs_: en.matmul(pt[:, cs_], lhsT=scT[:, h, :], rhs=dq["vb"][:, h * 256:(h + 1) * 256],
                                                                             start=True, stop=False), reads=[scT_b, dq["vb_b"]], writes=[ptb_])
                        S.op("pe", lambda en, h=h, pt=pt, cs_=cs_: en.matmul(pt[:, cs_], lhsT=qT[:, h, :], rhs=Rbf[:, h * 256:(h + 1) * 256],
                                                                             start=False, stop=True), reads=[qT_b, Rbf_b], writes=[ptb_])
                    pkv = ret_state_mm(dq)
                    ret_state_update(pkv)
                    for h in range(4):
                        pt, ptb_ = py[h // 2]
                        cs_ = slice((h % 2) * 256, (h % 2) * 256 + 256)
                        S.op("dve", lambda en, h=h, pt=pt, cs_=cs_: en.bn_stats(out=gst[:, h, :], in_=pt[:, cs_]), reads=[ptb_], writes=[gs_b])
                        S.op("dve", lambda en, h=h: en.bn_aggr(out=gmv[:, h, 0:2], in_=gst[:, h, :]), reads=[gs_b], writes=[gs_b])
                    S.op("dve", lambda en: en.tensor_scalar_add(out=gmv[:, :, 2:3], in0=gmv[:, :, 1:2], scalar1=float(EPS)), reads=[gs_b], writes=[gs_b])
                    S.op("act", lambda en: en.activation(out=gmv[:, :, 2:3], in_=gmv[:, :, 2:3], func=AF.Sqrt), reads=[gs_b], writes=[gs_b])
                    S.op("dve", lambda en: en.reciprocal(out=gmv[:, :, 2:3], in_=gmv[:, :, 2:3]), reads=[gs_b], writes=[gs_b])
                    S.op("dve", lambda en: en.scalar_tensor_tensor(out=gmv[:, :, 3:4], in0=gmv[:, :, 0:1], scalar=-1.0, in1=gmv[:, :, 2:3],
                                                                   op0=ALU.mult, op1=ALU.mult), reads=[gs_b], writes=[gs_b])
                    for h in range(4):
                        pt, ptb_ = py[h // 2]
                        cs_ = slice((h % 2) * 256, (h % 2) * 256 + 256)
                        S.op("act", lambda en, h=h, pt=pt, cs_=cs_: en.activation(out=yr[:, h * 256:(h + 1) * 256], in_=pt[:, cs_], func=AF.Identity,
                                                                                  bias=gmv[:, h, 3:4], scale=gmv[:, h, 2:3]),
                             reads=[ptb_, gs_b], writes=[yr_b])
                    S.op("dve", lambda en: en.tensor_tensor(out=yr[:, :], in0=yr[:, :], in1=gng[:, :], op=ALU.mult), reads=[yr_b, gn_b], writes=[yr_b])
                    S.op("dve", lambda en: en.tensor_tensor(out=yr[:, :], in0=yr[:, :], in1=gnb[:, :], op=ALU.add), reads=[yr_b, gn_b], writes=[yr_b])
                    S.op("dve", lambda en: en.scalar_tensor_tensor(out=yrb[:, :], in0=yr[:, :], scalar=0.5, in1=sg[:, :],
                                                                   op0=ALU.mult, op1=ALU.mult), reads=[yr_b, sg_b], writes=[yrb_b])
                    pt, ptb_ = c.ps.get()
                    ptv = ps_bf(pt)
                    for j in range(8):
                        S.op("pe", lambda en, j=j, ptv=ptv: en.transpose(out=ptv[:, j * 128:(j + 1) * 128], in_=yrb[:, j * 128:(j + 1) * 128],
                                                                         identity=c.ident_bf[:, :]), reads=[yrb_b, c.ident_buf], writes=[ptb_])
                    S.op("act", lambda en, ptv=ptv: en.copy(out=yrT[:, :, :], in_=ptv[:, :].rearrange("p (j t) -> p j t", j=8)),
                         reads=[ptb_], writes=[yrT_b])
                    halves = [c.ps.get(), c.ps.get()]
                    for kc in range(16):
                        lt = ysl[yi][:, kc, :] if kc < 8 else yrT[:, kc - 8, :]
                        lb = ysl_b[yi] if kc < 8 else yrT_b
                        for hf in range(2):
                            ph, phb = halves[hf]
                            S.op("pe", lambda en, ph=ph, kc=kc, hf=hf, lt=lt: en.matmul(
                                ph[:, :], lhsT=lt, rhs=Wout[:, kc, hf * 512:(hf + 1) * 512], start=(kc == 0), stop=(kc == 15)),
                                reads=[lb, Wout_b], writes=[phb])
                    r0 = g * GT + t * 128
                    S.dma("sp", xres[:, :], src[r0:r0 + 128, :], writes=[xres_b])
                    xi = ci % 2
                    emit_ln_epilogue(c, (v, v_b, st, mv, smm_b), halves, xres, xres_b, g_t, b_t, gb_buf, xo[xi], xo_b[xi])
                    outs.append(S.dma("sp", dst[r0:r0 + 128, :], xo[xi][:, :], reads=[xo_b[xi]]))
    S.barrier()
    return outs


def emit_halo_exchange(c, xsrc, NT, hidx):
    S, nc = c.S, c.nc
    hl_loc = nc.dram_tensor(f"hl_loc{hidx}", [128, D], F32, kind="Internal").ap()
    hl_all = nc.dram_tensor(f"hl_all{hidx}", [4 * 128, D], F32, kind="Internal").ap()
    halo_d = nc.dram_tensor(f"halo_d{hidx}", [128, D], F32, kind="Internal").ap()
    lb, ab_, hb_ = Buf("hl_loc"), Buf("hl_all"), Buf("halo_d")
    with ExitStack() as es:
        sb = lambda name, shape, dt: _sb(c, es, "h_" + name, shape, dt)
        t0 = sb("t0", [128, D], F32); t0_b = Buf("ht0")
        rec = [sb(f"rec{i}", [128, D], F32) for i in range(2)]; rec_b = [Buf(f"hrec{i}") for i in range(2)]
        acc = sb("acc", [128, D], F32); acc_b = Buf("hacc")
        hsel = sb("hsel", [128, 4], F32); hsel_b = Buf("hsel")
        S.dma("sp", hsel[:, :], c.dram["hsel"], writes=[hsel_b])
        S.dma("sp", t0[:, :], xsrc[NT - 128:NT, :], writes=[t0_b])
        S.dma("sp", hl_loc, t0[:, :], reads=[t0_b], writes=[lb])
        if c.use_cc:
            S.op("pool", lambda en: en.collective_compute("AllGather", ALU.bypass, replica_groups=[[0, 1, 2, 3], [4, 5, 6, 7]],
                                                          ins=[hl_loc], outs=[hl_all]), reads=[lb], writes=[ab_])
        else:
            for i in range(4):
                S.dma("sp", hl_all[i * 128:(i + 1) * 128, :], hl_loc, reads=[lb], writes=[ab_])
        for i in range(4):
            rb, rbb = rec[i % 2], rec_b[i % 2]
            S.dma("sp", rb[:, :], hl_all[i * 128:(i + 1) * 128, :], reads=[ab_], writes=[rbb])
            if i == 0:
                S.op("dve", lambda en, rb=rb: en.tensor_scalar_mul(out=acc[:, :], in0=rb[:, :], scalar1=hsel[:, 0:1]),
                     reads=[rbb, hsel_b], writes=[acc_b])
            else:
                S.op("dve", lambda en, rb=rb, i=i: en.scalar_tensor_tensor(out=acc[:, :], in0=rb[:, :], scalar=hsel[:, i:i + 1],
                                                                           in1=acc[:, :], op0=ALU.mult, op1=ALU.add),
                     reads=[rbb, hsel_b, acc_b], writes=[acc_b])
        S.dma("sp", halo_d, acc[:, :], reads=[acc_b], writes=[hb_])
    S.barrier()
    return halo_d


EVEN_IN = 1792
ODD_IN = 5648

PER_LAYER_SHAPES = {
    "ln_mix_g": [1, D], "ln_mix_b": [1, D], "ln_ffn_g": [1, D], "ln_ffn_b": [1, D],
    "ffn_w_gate": [D, FH], "ffn_w_up": [D, FH], "ffn_w_down": [FH, D],
}
EVEN_SHAPES = {
    "ev_w_in": [D, EVEN_IN], "ev_sinks": [1, 8], "ev_dw_w": [31, 512], "ev_dw_b": [1, 512],
    "ev_cn_g": [1, 512], "ev_cn_b": [1, 512], "ev_w_out": [D, D],
}
ODD_SHAPES = {
    "od_w_in": [D, ODD_IN], "od_conv_w": [4, 1536], "od_conv_b": [1, 1536], "od_dt_bias": [1, 16],
    "od_a_log": [1, 16], "od_d_skip": [1, 16], "od_ssm_norm_g": [1, D], "od_ret_gn_g": [1, D],
    "od_ret_gn_b": [1, D], "od_w_out": [2 * D, D],
}


def stage_inputs(stages):
    need = {"x": None, "ident": [128, 128], "ones": [128, 128]}
    for kind, idx in stages:
        if kind == "ffn":
            for k in ("ln_ffn_g", "ln_ffn_b", "ffn_w_gate", "ffn_w_up", "ffn_w_down"):
                need[f"{k}{idx}"] = PER_LAYER_SHAPES[k]
        elif kind == "even":
            layer = 2 * idx
            for k in ("ln_mix_g", "ln_mix_b"):
                need[f"{k}{layer}"] = PER_LAYER_SHAPES[k]
            for k, s in EVEN_SHAPES.items():
                need[f"{k}{idx}"] = s
            need["amask"] = [128, 256]
            need["amask0"] = [128, 256]
            need["rope_a"] = None
            need["halo"] = [128, D]
        elif kind == "odd":
            layer = 2 * idx + 1
            for k in ("ln_mix_g", "ln_mix_b"):
                need[f"{k}{layer}"] = PER_LAYER_SHAPES[k]
            for k, s in ODD_SHAPES.items():
                need[f"{k}{idx}"] = s
            need["halo"] = [128, D]
            need["odd_tab"] = [128, 32]
            need["triu"] = [128, 128]
            need["trisl"] = [128, 128]
            need["rope_d"] = None
    if sum(1 for k, _ in stages if k in ("even", "odd")) > 1:
        need["hsel"] = [128, 4]
    return need


def build_program(NT, stages, use_cc=True):
    nc = bass.Bass("TRN2", target_bir_lowering=False)
    c = Ctx()
    c.nc = nc
    c.NT = NT
    c.use_cc = use_cc
    c.st_loc = {}
    c.st_all = {}
    for kind, idx in stages:
        if kind == "odd":
            c.st_loc[idx] = (nc.dram_tensor(f"loc_s{idx}", [128, 1040], F32, kind="Internal").ap(),
                             nc.dram_tensor(f"loc_r{idx}", [128, 1024], F32, kind="Internal").ap())
            c.st_all[idx] = (nc.dram_tensor(f"all_s{idx}", [4 * 128, 1040], F32, kind="Internal").ap(),
                             nc.dram_tensor(f"all_r{idx}", [4 * 128, 1024], F32, kind="Internal").ap())
            c.ysT_d = nc.dram_tensor("ysT_d", [8, 128, NT], BF16, kind="Internal").ap() if not hasattr(c, "ysT_d") else c.ysT_d
    es = ExitStack()
    c.es = es
    c.dram = {}
    need = stage_inputs(stages)
    need["x"] = [NT, D]
    if "rope_a" in need:
        need["rope_a"] = [128, NT // 128 + 1, 16]
    if "rope_d" in need:
        need["rope_d"] = [128, NT // 128, 128]
    for name, shape in need.items():
        c.dram[name] = nc.dram_tensor(name, list(shape), F32, kind="ExternalInput").ap()
    y = nc.dram_tensor("y", [NT, D], F32, kind="ExternalOutput").ap()
    xa = nc.dram_tensor("xa", [NT, D], F32, kind="Internal").ap()
    xb = nc.dram_tensor("xb", [NT, D], F32, kind="Internal").ap()
    with es:
        c.S = Sched(nc, es)
        c.ps = PsumPool(c)
        c.ident_f = _sb(c, es, "ident_f", [128, 128], F32)
        c.ident_bf = _sb(c, es, "ident_bf", [128, 128], BF16)
        c.ones_f = _sb(c, es, "ones_f", [128, 128], F32)
        c.ident_buf = Buf("ident")
        c.identf_buf = Buf("identf")
        c.ones_buf = Buf("ones")
        c.S.dma("sp", c.ident_f[:, :], c.dram["ident"], writes=[c.identf_buf])
        c.S.dma("sp", c.ones_f[:, :], c.dram["ones"], writes=[c.ones_buf])
        c.S.op("dve", lambda e: e.tensor_copy(out=c.ident_bf[:, :], in_=c.ident_f[:, :]), reads=[c.identf_buf],
               writes=[c.ident_buf])
        cur = c.dram["x"]
        bufs = [xa, xb]
        outs = []
        nmix = 0
        for si, (kind, idx) in enumerate(stages):
            last = si == len(stages) - 1
            dst = y if last else bufs[si % 2]
            if kind in ("even", "odd"):
                halo = c.dram["halo"] if nmix == 0 else emit_halo_exchange(c, cur, NT, nmix)
                nmix += 1
            if kind == "ffn":
                outs = emit_ffn(c, idx, cur, dst, NT)
            elif kind == "even":
                outs = emit_even(c, idx, 2 * idx, cur, dst, halo, NT)
            elif kind == "odd":
                outs = emit_odd(c, idx, 2 * idx + 1, cur, dst, halo, NT)
            else:
                raise NotImplementedError(kind)
            cur = dst
        c.S.emit(final_ops=outs)
    return nc


ROPE_THETA = 500000.0


def rope_table_a(pos0, NT):
    nt = NT // 128 + 1
    pos = (pos0 - 128 + np.arange(nt * 128)).astype(np.float32)
    inv = np.power(np.float32(ROPE_THETA), -np.arange(8, dtype=np.float32) / np.float32(8)).astype(np.float32)
    ang = (pos[:, None] * inv[None, :]).astype(np.float32)
    tab = np.concatenate([np.cos(ang), np.sin(ang)], axis=1).astype(np.float32)
    return np.ascontiguousarray(tab.reshape(nt, 128, 16).transpose(1, 0, 2))


RET_THETA = 10000.0


def rope_table_d(pos0, NT):
    nt = NT // 128
    pos = (pos0 + np.arange(nt * 128)).astype(np.float32)
    inv = (1.0 / np.power(np.float32(RET_THETA), np.linspace(0.0, 1.0, 64, dtype=np.float32))).astype(np.float32)
    ang = (pos[:, None] * inv[None, :]).astype(np.float32)
    tab = np.concatenate([np.cos(ang), np.sin(ang)], axis=1).astype(np.float32)
    return np.ascontiguousarray(tab.reshape(nt, 128, 128).transpose(1, 0, 2))


def odd_table(r, NT):
    h = np.arange(4, dtype=np.float64)
    lg = np.log(1.0 - np.power(2.0, -5.0 - h))
    l = np.arange(128, dtype=np.float64)[:, None]
    t = np.zeros((128, 32), np.float64)
    t[:, 0:4] = np.exp(lg[None, :] * (l + 1.0))
    t[:, 4:8] = np.exp(-lg[None, :] * (l + 1.0)) * (128.0 ** -0.5)
    t[:, 8:12] = np.exp(lg * 128.0)[None, :]
    for i in range(4):
        if i < r:
            t[:, 12 + 4 * i:16 + 4 * i] = np.exp(lg * float(NT * (r - 1 - i)))[None, :]
            t[:, 28 + i] = 1.0
    return t.astype(np.float32)


def attn_masks(first):
    i = np.arange(128)[:, None]
    j = np.arange(256)[None, :]
    valid = (j > i) & (j <= i + 128)
    m = np.where(valid, 0.0, A_MASK_NEG).astype(np.float32)
    m0 = m.copy()
    if first:
        m0[:, :128] = A_MASK_NEG
    return m, m0


def make_in_map(inp, x_shard, halo, cidx, NT, stages):
    need = stage_inputs(stages)
    r = cidx % 4
    m = {}
    for name in need:
        if name == "x":
            m[name] = np.ascontiguousarray(x_shard, dtype=np.float32)
        elif name == "ident":
            m[name] = np.eye(128, dtype=np.float32)
        elif name == "ones":
            m[name] = np.ones((128, 128), dtype=np.float32)
        elif name == "halo":
            m[name] = np.ascontiguousarray(halo, dtype=np.float32)
        elif name == "amask":
            m[name] = attn_masks(False)[0]
        elif name == "amask0":
            m[name] = attn_masks(r == 0)[1]
        elif name == "rope_a":
            m[name] = rope_table_a(r * NT, NT)
        elif name == "rope_d":
            m[name] = rope_table_d(r * NT, NT)
        elif name == "odd_tab":
            m[name] = odd_table(r, NT)
        elif name == "hsel":
            hs = np.zeros((128, 4), np.float32)
            if r > 0:
                hs[:, r - 1] = 1.0
            m[name] = hs
        elif name == "triu":
            m[name] = np.triu(np.ones((128, 128), np.float32))
        elif name == "trisl":
            m[name] = np.tril(np.ones((128, 128), np.float32), -1).T.copy().T if False else (np.arange(128)[:, None] > np.arange(128)[None, :]).astype(np.float32)
        else:
            base = name.rstrip("0123456789")
            idx = int(name[len(base):])
            m[name] = np.ascontiguousarray(np.asarray(inp[base][idx], dtype=np.float32).reshape(need[name]))
    return m


SEQ = 16384
BATCH = 2
FUSED = True
ALL_STAGES = [("even", 0), ("ffn", 0), ("odd", 0), ("ffn", 1), ("even", 1), ("ffn", 2), ("odd", 1), ("ffn", 3)]


def run_stages(inp, x, stages):
    S_ = x.shape[1]
    NT = S_ // 4
    nc = build_program(NT, stages)
    in_maps = []
    for cidx in range(NCORES):
        b, r = cidx // 4, cidx % 4
        halo = x[b, r * NT - 128:r * NT] if r > 0 else np.zeros((128, D), np.float32)
        in_maps.append(make_in_map(inp, x[b, r * NT:(r + 1) * NT], halo, cidx, NT, stages))
    res = run_bass_kernel_spmd(nc, in_maps, core_ids=list(range(NCORES)))
    out = np.empty_like(x)
    for cidx in range(NCORES):
        b, r = cidx // 4, cidx % 4
        out[b, r * NT:(r + 1) * NT] = res.results[cidx]["y"]
    return out


def kernel(**inputs):
    inp = {k: np.asarray(v) for k, v in inputs.items()}
    x = np.ascontiguousarray(inp["x"], dtype=np.float32)
    if FUSED:
        return run_stages(inp, x, ALL_STAGES)
    for li in range(DEPTH):
        x = run_stages(inp, x, ALL_STAGES[2 * li:2 * li + 2])
    return x
```

```python
import numpy as np
import os as _os_env
from contextlib import ExitStack
import concourse.bass as bass
import concourse.mybir as mybir
from concourse.bass_utils import run_bass_kernel_spmd

F32 = mybir.dt.float32
BF16 = mybir.dt.bfloat16
ALU = mybir.AluOpType
AF = mybir.ActivationFunctionType

D = 1024
FH = 2816
DEPTH = 4
ALPHA = (2 * DEPTH) ** 0.25
EPS = 1e-5
NCORES = 8


class Buf:
    __slots__ = ("name", "w", "rs")

    def __init__(self, name):
        self.name = name
        self.w = None
        self.rs = []


class Op:
    __slots__ = ("stream", "fn", "deps", "adeps", "needed", "sem", "val", "is_dma", "idx", "seg", "cost", "lat", "dq_index")


class _Rec:
    def __init__(self):
        self.call = None

    def __getattr__(self, name):
        def f(*a, **kw):
            assert self.call is None, "one engine instruction per op"
            self.call = (name, a, kw)
            return None
        return f


_NOWAR = bool(_os_env.environ.get('NOWAR'))


class Sched:
    STRICT = tuple(x for x in _os_env.environ.get("STRICT_ENG", "act,dve,pool").split(",") if x)
    NDMA = 6
    CHAIN = tuple(x for x in _os_env.environ.get("CHAIN_ENG", "act").split(",") if x)
    REORDER = _os_env.environ.get("REORDER", "1") == "1"

    def __init__(self, nc, es):
        self.nc = nc
        self.streams = {k: [] for k in ("pe", "act", "dve", "pool", "sp")}
        self.sem = {k: es.enter_context(nc.semaphore("pg_" + k)) for k in self.streams}
        self.dsem = {q: [es.enter_context(nc.semaphore(f"dq_{q}{i}")) for i in range(self.NDMA)]
                     for q in ("sp", "pool", "act")}
        self.dcnt = {q: 0 for q in self.dsem}
        self.dring = {q: [None] * self.NDMA for q in self.dsem}
        self.seg = 0
        self.all_ops = []
        self.last_on = {}

    @staticmethod
    def _cost(stream, name, a, kw, is_dma):
        def fsz(ap):
            sh = ap.shape
            n = 1
            for s_ in sh[1:]:
                n *= int(s_)
            return n
        try:
            if is_dma:
                o = kw.get("out")
                nbytes = fsz(o) * int(o.shape[0]) * 4
                return (1000.0 if stream == "pool" else 100.0), 2000.0 + nbytes / 150.0
            if name == "collective_compute":
                return 1000.0, 60000.0
            if stream == "pe":
                if name == "transpose":
                    return 110.0, 1700.0
                rhs = kw.get("rhs")
                n = fsz(rhs)
                mult = 4.0 if rhs.dtype == F32 else 1.0
                return mult * (50.0 + 0.48 * n), 1700.0
            o = kw.get("out", a[0] if a else None)
            f = fsz(o) if o is not None else 64
            if stream == "act":
                return 230.0 + 1.0 * f, 1700.0
            if stream == "dve":
                return 130.0 + 1.0 * f, 1700.0
            return 200.0 + 2.0 * f, 200.0
        except Exception:
            return 300.0, 200.0

    def _add(self, stream, fn, reads, writes, is_dma=False, extra=()):
        op = Op()
        op.stream = stream
        rec = _Rec()
        fn(rec)
        name_, a_, kw_ = rec.call
        op.fn = lambda e, name_=name_, a_=a_, kw_=kw_: getattr(e, name_)(*a_, **kw_)
        op.is_dma = is_dma
        op.needed = False
        op.sem = None
        op.val = 0
        op.seg = self.seg
        op.cost, op.lat = self._cost(stream, name_, a_, kw_, is_dma)
        deps = []
        for b in reads:
            if b.w is not None:
                deps.append(b.w)
        for b in writes:
            if b.w is not None:
                deps.append(b.w)
            if not _NOWAR:
                deps.extend(b.rs)
        deps.extend(extra)
        seen = set()
        dd = []
        for d in deps:
            if id(d) in seen:
                continue
            seen.add(id(d))
            dd.append(d)
        if stream in self.CHAIN and self.last_on.get(stream) is not None and self.last_on[stream].seg == self.seg:
            lo = self.last_on[stream]
            if id(lo) not in seen:
                dd.append(lo)
        self.last_on[stream] = op
        op.adeps = dd
        for b in reads:
            b.rs.append(op)
        for b in writes:
            b.w = op
            b.rs = []
        op.idx = len(self.all_ops)
        self.all_ops.append(op)
        return op

    def op(self, stream, fn, reads=(), writes=()):
        return self._add(stream, fn, reads, writes)

    def dma(self, q, out, in_, reads=(), writes=(), **kw):
        i = self.dcnt[q]
        self.dcnt[q] += 1
        slot = i % self.NDMA
        prev = self.dring[q][slot]
        extra = (prev,) if prev is not None else ()
        op = self._add(q, lambda e: e.dma_start(out=out, in_=in_, **kw), reads, writes, is_dma=True, extra=extra)
        op.sem = self.dsem[q][slot]
        op.val = 16 * (i // self.NDMA + 1)
        op.needed = True
        op.dq_index = i
        self.dring[q][slot] = op
        return op

    def barrier(self):
        self.seg += 1

    def _schedule(self):
        import heapq
        streams = {k: [] for k in self.streams}
        ops = self.all_ops
        if not self.REORDER:
            self.est_ns = 0.0
            cur_seg = 0
            fence = []
            first = {k: False for k in self.streams}
            last_dma = {q: {} for q in self.dsem}
            for op in ops:
                if op.seg != cur_seg:
                    cur_seg = op.seg
                    fence = []
                    for k, lst in streams.items():
                        for o2 in reversed(lst):
                            if not o2.is_dma:
                                fence.append(o2)
                                break
                    for q in last_dma:
                        fence.extend(last_dma[q].values())
                    first = {k: True for k in self.streams}
                if first[op.stream]:
                    first[op.stream] = False
                    op.adeps = list(op.adeps) + [f for f in fence if f is not op]
                streams[op.stream].append(op)
                if op.is_dma:
                    last_dma[op.stream][op.dq_index % self.NDMA] = op
            return streams
        nseg = self.seg + 1
        by_seg = [[] for _ in range(nseg)]
        for op in ops:
            by_seg[op.seg].append(op)
        fin = {}
        free = {k: 0.0 for k in self.streams}
        last_dma = {q: {} for q in self.dsem}
        fence = []
        tnow = 0.0
        for s in range(nseg):
            seg_ops = by_seg[s]
            if not seg_ops:
                continue
            first_in_stream = {k: True for k in self.streams}
            nun = {}
            users = {}
            for op in seg_ops:
                c = 0
                for d in op.adeps:
                    if d.seg == s:
                        c += 1
                        users.setdefault(id(d), []).append(op)
                nun[id(op)] = c
            ready = {k: [] for k in self.streams}

            def est_ready(op):
                t = tnow
                for d in op.adeps:
                    if d.seg == s:
                        f = fin[id(d)]
                        if d.stream != op.stream or d.is_dma:
                            f += d.lat
                        t = max(t, f)
                return t
            for op in seg_ops:
                if nun[id(op)] == 0:
                    heapq.heappush(ready[op.stream], (est_ready(op), op.idx, op))
            nleft = len(seg_ops)
            while nleft:
                best = None
                for k in self.streams:
                    if not ready[k]:
                        continue
                    if self.REORDER:
                        cand = None
                        tmp = []
                        while ready[k] and ready[k][0][0] <= free[k]:
                            tmp.append(heapq.heappop(ready[k]))
                        if tmp:
                            cand = min(tmp, key=lambda x: x[1])
                            for x in tmp:
                                if x is not cand:
                                    heapq.heappush(ready[k], x)
                            st = free[k]
                        else:
                            cand = heapq.heappop(ready[k])
                            st = cand[0]
                    else:
                        cand = min(ready[k], key=lambda x: x[1])
                        ready[k].remove(cand)
                        heapq.heapify(ready[k])
                        st = max(cand[0], free[k])
                    if best is None or (st, cand[1]) < (best[0], best[1][1]):
                        if best is not None:
                            heapq.heappush(ready[best[2]], best[1])
                        best = (st, cand, k)
                    else:
                        heapq.heappush(ready[k], cand)
                st, cand, k = best
                op = cand[2]
                if first_in_stream[k]:
                    first_in_stream[k] = False
                    if fence:
                        op.adeps = list(op.adeps) + [f for f in fence if f is not op]
                free[k] = st + op.cost
                fin[id(op)] = st + op.cost
                streams[k].append(op)
                if op.is_dma:
                    last_dma[k][op.dq_index % self.NDMA] = op
                nleft -= 1
                for u in users.get(id(op), ()):
                    nun[id(u)] -= 1
                    if nun[id(u)] == 0:
                        heapq.heappush(ready[u.stream], (est_ready(u), u.idx, u))
            fence = []
            for k, lst in streams.items():
                for op in reversed(lst):
                    if not op.is_dma:
                        fence.append(op)
                        break
            for q in last_dma:
                fence.extend(last_dma[q].values())
            tnow = max(list(free.values()) + [fin[id(f)] + f.lat for f in fence])
            for k in free:
                free[k] = tnow
        self.est_ns = max(free.values())
        return streams

    def emit(self, final_ops=()):
        streams = self._schedule()
        self.streams = streams
        for k, lst in streams.items():
            for op in lst:
                dd = []
                for d in op.adeps:
                    if (not d.is_dma) and d.stream == k and k not in self.STRICT:
                        continue
                    dd.append(d)
                    d.needed = True
                op.deps = dd
        for k, lst in streams.items():
            cnt = 0
            for op in lst:
                if op.is_dma:
                    continue
                if op.needed:
                    cnt += 1
                    op.sem = self.sem[k]
                    op.val = cnt
        print('SCHED ops', {k: len(v) for k, v in streams.items()}, 'semmax',
              {k: max([o.val for o in v if not o.is_dma] + [0]) for k, v in streams.items()},
              'est_ms', round(self.est_ns / 1e6, 3), 'busy_ms', {k: round(sum(o.cost for o in v) / 1e6, 3) for k, v in streams.items()}, flush=True)
        with self.nc.Block() as block:
            def run(k):
                def body(e):
                    waited = {}

                    def wait(d):
                        key = id(d.sem)
                        if waited.get(key, 0) < d.val:
                            e.wait_ge(d.sem, d.val)
                            waited[key] = d.val
                    for op in streams[k]:
                        for d in op.deps:
                            wait(d)
                        ins = op.fn(e)
                        if op.is_dma:
                            ins.then_inc(op.sem, 16)
                        elif op.needed:
                            ins.then_inc(op.sem, 1)
                    if k == "sp":
                        for d in final_ops:
                            wait(d)
                return body
            block.tensor(run("pe"))
            block.scalar(run("act"))
            block.vector(run("dve"))
            block.gpsimd(run("pool"))
            block.sync(run("sp"))


class Ctx:
    pass


_UID = [0]


def _sb(c, es, name, shape, dt):
    _UID[0] += 1
    return es.enter_context(c.nc.sbuf_tensor(f"{name}_{_UID[0]}", list(shape), dt))


class PsumPool:
    def __init__(self, c, n=8):
        self.t = [c.es.enter_context(c.nc.psum_tensor(f"ps{i}", [128, 512], F32)) for i in range(n)]
        self.b = [Buf(f"ps{i}") for i in range(n)]
        self.i = 0
        self.n = n

    def get(self):
        i = self.i
        self.i = (self.i + 1) % self.n
        return self.t[i], self.b[i]


def load_bcast_row(c, q, dst_tile, dst_buf, dram_ap_row):
    n = dst_tile.shape[-1]
    return c.S.dma(q, dst_tile[:, :], dram_ap_row.to_broadcast([128, n]), writes=[dst_buf])


def emit_xT(c, xT, xT_buf, tslot, x_tile, x_buf, xbf, xbf_buf):
    S = c.S
    S.op("act", lambda e: e.copy(out=xbf[:, :], in_=x_tile[:, :]), reads=[x_buf], writes=[xbf_buf])
    pt, pb = c.ps.get()
    ptb = pt[:, :].bitcast(BF16)
    for k in range(8):
        S.op("pe", lambda e, k=k: e.transpose(out=ptb[:, k * 128:(k + 1) * 128], in_=xbf[:, k * 128:(k + 1) * 128],
                                               identity=c.ident_bf[:, :]),
             reads=[xbf_buf, c.ident_buf], writes=[pb])
    S.op("dve", lambda e: e.tensor_copy(out=xT[:, :, tslot * 128:(tslot + 1) * 128],
                                        in_=ptb.rearrange("p (k t) -> p k t", k=8)),
         reads=[pb], writes=[xT_buf])


def emit_ln_epilogue(c, es_bufs, ps_halves, x_old, x_old_buf, g_t, b_t, gb_buf, out_tile, out_buf):
    S = c.S
    v, v_buf, st, mv, sm_buf = es_bufs
    for hf in range(2):
        pt, pb = ps_halves[hf]
        S.op("dve", lambda e, hf=hf, pt=pt: e.scalar_tensor_tensor(
            out=v[:, hf * 512:(hf + 1) * 512], in0=x_old[:, hf * 512:(hf + 1) * 512], scalar=float(ALPHA),
            in1=pt[:, :], op0=ALU.mult, op1=ALU.add), reads=[x_old_buf, pb], writes=[v_buf])
    ln_core(c, v, v_buf, st, mv, sm_buf, g_t, b_t, gb_buf, out_tile, out_buf)


def ln_core(c, v, v_buf, st, mv, sm_buf, g_t, b_t, gb_buf, out_tile, out_buf):
    S = c.S
    for hf in range(2):
        S.op("dve", lambda e, hf=hf: e.bn_stats(out=st[:, hf * 6:(hf + 1) * 6], in_=v[:, hf * 512:(hf + 1) * 512]),
             reads=[v_buf], writes=[sm_buf])
    S.op("dve", lambda e: e.bn_aggr(out=mv[:, 0:2], in_=st[:, 0:12]), reads=[sm_buf], writes=[sm_buf])
    S.op("dve", lambda e: e.tensor_scalar_add(out=mv[:, 2:3], in0=mv[:, 1:2], scalar1=float(EPS)),
         reads=[sm_buf], writes=[sm_buf])
    S.op("act", lambda e: e.activation(out=mv[:, 2:3], in_=mv[:, 2:3], func=AF.Sqrt), reads=[sm_buf], writes=[sm_buf])
    S.op("dve", lambda e: e.reciprocal(out=mv[:, 2:3], in_=mv[:, 2:3]), reads=[sm_buf], writes=[sm_buf])
    S.op("dve", lambda e: e.scalar_tensor_tensor(out=mv[:, 3:4], in0=mv[:, 0:1], scalar=-1.0, in1=mv[:, 2:3],
                                                 op0=ALU.mult, op1=ALU.mult), reads=[sm_buf], writes=[sm_buf])
    S.op("act", lambda e: e.activation(out=v[:, :], in_=v[:, :], func=AF.Identity, bias=mv[:, 3:4], scale=mv[:, 2:3]),
         reads=[v_buf, sm_buf], writes=[v_buf])
    S.op("dve", lambda e: e.tensor_tensor(out=v[:, :], in0=v[:, :], in1=g_t[:, :], op=ALU.mult),
         reads=[v_buf, gb_buf], writes=[v_buf])
    S.op("dve", lambda e: e.tensor_tensor(out=out_tile[:, :], in0=v[:, :], in1=b_t[:, :], op=ALU.add),
         reads=[v_buf, gb_buf], writes=[out_buf])


def emit_ffn(c, layer, src, dst, NT):
    S, nc = c.S, c.nc
    T = min(1024, NT)
    ntile = T // 128
    nblk = T // 512
    outs = []
    with ExitStack() as es:
        xT = _sb(c, es, "f_xT", [128, 8, T], BF16)
        xT_buf = Buf("f_xT")
        hT = _sb(c, es, "f_hT", [128, 22, T], BF16)
        hT_bufs = [Buf(f"f_hT{j}") for j in range(22)]
        NWB = 2
        wg = [_sb(c, es, f"f_wg{i}", [128, 8, 512], BF16) for i in range(NWB)]
        wu = [_sb(c, es, f"f_wu{i}", [128, 8, 512], BF16) for i in range(NWB)]
        wg_b = [Buf(f"f_wg{i}") for i in range(NWB)]
        wu_b = [Buf(f"f_wu{i}") for i in range(NWB)]
        wd = _sb(c, es, "f_wd", [128, 22, 1024], BF16)
        jgroups = [(j0, min(4, 22 - j0)) for j0 in range(0, 22, 4)]
        wd_b = [Buf(f"f_wd{g}") for g in range(len(jgroups))]
        NXB = 2
        xin = [_sb(c, es, f"f_xin{i}", [128, 1024], F32) for i in range(NXB)]
        xin_b = [Buf(f"f_xin{i}") for i in range(NXB)]
        xbf = [_sb(c, es, f"f_xbf{i}", [128, 1024], BF16) for i in range(NXB)]
        xbf_b = [Buf(f"f_xbf{i}") for i in range(NXB)]
        sg = [_sb(c, es, f"f_sg{i}", [128, 512], F32) for i in range(2)]
        sg_b = [Buf(f"f_sg{i}") for i in range(2)]
        v = [_sb(c, es, f"f_v{i}", [128, 1024], F32) for i in range(2)]
        v_b = [Buf(f"f_v{i}") for i in range(2)]
        st = [_sb(c, es, f"f_st{i}", [128, 12], F32) for i in range(2)]
        mv = [_sb(c, es, f"f_mv{i}", [128, 4], F32) for i in range(2)]
        sm_b = [Buf(f"f_sm{i}") for i in range(2)]
        xo = [_sb(c, es, f"f_xo{i}", [128, 1024], F32) for i in range(2)]
        xo_b = [Buf(f"f_xo{i}") for i in range(2)]
        g_t = _sb(c, es, "f_g", [128, 1024], F32)
        b_t = _sb(c, es, "f_b", [128, 1024], F32)
        gb_buf = Buf("f_gb")
        load_bcast_row(c, "sp", g_t, gb_buf, c.dram[f"ln_ffn_g{layer}"])
        load_bcast_row(c, "sp", b_t, gb_buf, c.dram[f"ln_ffn_b{layer}"])
        Wg = c.dram[f"ffn_w_gate{layer}"]
        Wu = c.dram[f"ffn_w_up{layer}"]
        Wd = c.dram[f"ffn_w_down{layer}"]
        xcnt = 0
        wcnt = 0
        ecnt = 0
        first = True
        for g0 in range(0, NT, T):
            for t in range(ntile):
                i = xcnt % NXB
                xcnt += 1
                r0 = g0 + t * 128
                S.dma("sp", xin[i][:, :], src[r0:r0 + 128, :], writes=[xin_b[i]])
                emit_xT(c, xT, xT_buf, t, xin[i], xin_b[i], xbf[i], xbf_b[i])
            for gi, (j0, nj) in enumerate(jgroups):
                wi = wcnt % NWB
                wcnt += 1
                c0 = j0 * 128
                ncol = nj * 128
                S.dma("pool", wg[wi][:, :, 0:ncol], Wg[:, c0:c0 + ncol].rearrange("(k p) n -> p k n", p=128),
                      writes=[wg_b[wi]])
                S.dma("pool", wu[wi][:, :, 0:ncol], Wu[:, c0:c0 + ncol].rearrange("(k p) n -> p k n", p=128),
                      writes=[wu_b[wi]])
                if first:
                    S.dma("pool", wd[:, j0:j0 + nj, :], Wd[c0:c0 + ncol, :].rearrange("(j p) n -> p j n", p=128),
                          writes=[wd_b[gi]])
                for jj in range(nj):
                    j = j0 + jj
                    for nb in range(nblk):
                        pg, pgb = c.ps.get()
                        pu, pub = c.ps.get()
                        for k in range(8):
                            S.op("pe", lambda e, k=k, pg=pg, jj=jj, nb=nb, wi=wi: e.matmul(
                                pg[:, :], lhsT=wg[wi][:, k, jj * 128:(jj + 1) * 128], rhs=xT[:, k, nb * 512:(nb + 1) * 512],
                                start=(k == 0), stop=(k == 7)), reads=[wg_b[wi], xT_buf], writes=[pgb])
                        for k in range(8):
                            S.op("pe", lambda e, k=k, pu=pu, jj=jj, nb=nb, wi=wi: e.matmul(
                                pu[:, :], lhsT=wu[wi][:, k, jj * 128:(jj + 1) * 128], rhs=xT[:, k, nb * 512:(nb + 1) * 512],
                                start=(k == 0), stop=(k == 7)), reads=[wu_b[wi], xT_buf], writes=[pub])
                        si = (j * nblk + nb) % 2
                        S.op("act", lambda e, si=si, pg=pg: e.activation(out=sg[si][:, :], in_=pg[:, :], func=AF.Silu),
                             reads=[pgb], writes=[sg_b[si]])
                        S.op("dve", lambda e, si=si, pu=pu, j=j, nb=nb: e.tensor_tensor(
                            out=hT[:, j, nb * 512:(nb + 1) * 512], in0=sg[si][:, :], in1=pu[:, :], op=ALU.mult),
                            reads=[sg_b[si], pub], writes=[hT_bufs[j]])
            first = False
            for t in range(ntile):
                halves = [c.ps.get(), c.ps.get()]
                for j in range(22):
                    for hf in range(2):
                        ph, phb = halves[hf]
                        S.op("pe", lambda e, ph=ph, j=j, hf=hf, t=t: e.matmul(
                            ph[:, :], lhsT=hT[:, j, t * 128:(t + 1) * 128], rhs=wd[:, j, hf * 512:(hf + 1) * 512],
                            start=(j == 0), stop=(j == 21)), reads=[hT_bufs[j], wd_b[j // 4]], writes=[phb])
                i = ecnt % 2
                ecnt += 1
                r0 = g0 + t * 128
                S.dma("sp", xin[i][:, :], src[r0:r0 + 128, :], writes=[xin_b[i]])
                emit_ln_epilogue(c, (v[i], v_b[i], st[i], mv[i], sm_b[i]), halves, xin[i], xin_b[i],
                                 g_t, b_t, gb_buf, xo[i], xo_b[i])
                outs.append(S.dma("sp", dst[r0:r0 + 128, :], xo[i][:, :], reads=[xo_b[i]]))
    S.barrier()
    return outs


A_MASK_NEG = -30000.0
import os as _os
DBG_STOP = int(_os.environ.get("DBG_STOP", "0"))


class _Stop(Exception):
    pass


def _chk(n):
    if DBG_STOP == n:
        raise _Stop()


def ps_bf(pt):
    return pt[:, :].bitcast(BF16)


def emit_even(c, e, layer, src, dst, halo, NT):
    S, nc = c.S, c.nc
    GT = 512
    ngroups = NT // GT
    outs = []
    dr = c.dram
    with ExitStack() as es:
      try:
          sb = lambda name, shape, dt: _sb(c, es, "e_" + name, shape, dt)
          Win = sb("win", [128, 8, 1792], BF16); Win_b = Buf("win")
          Wout = sb("wout", [128, 8, 1024], BF16); Wout_b = Buf("wout")
          dg = sb("dg", [128, 4, 31, 128], BF16); dg_b = Buf("dg")
          wk = sb("wk", [31, 512], F32); wk_b = Buf("wk")
          wcol = sb("wcol", [128, 4, 32], F32); wcol_b = Buf("wcol")
          cvec = sb("cvec", [128, 16], F32); cvec_b = Buf("cvec")
          sinks = sb("sinks", [128, 8], F32); sinks_b = Buf("sinks")
          amask = sb("amask", [128, 256], F32); amask0 = sb("amask0", [128, 256], F32); am_b = Buf("amask")
          rope = sb("rope", [128, NT // 128 + 1, 16], F32); rope_b = Buf("rope")
          g_t = sb("g", [128, 1024], F32); b_t = sb("b", [128, 1024], F32); gb_buf = Buf("gb")
          xT = sb("xT", [128, 8, GT], BF16); xT_b = Buf("xT")
          xTh = sb("xTh", [128, 8, 128], BF16); xTh_b = Buf("xTh")
          hbuf = [sb(f"hbuf{i}", [128, 4, 32 + GT], BF16) for i in range(2)]
          hb_body = [Buf(f"hb{i}") for i in range(2)]
          hb_pre = [Buf(f"hp{i}") for i in range(2)]
          cv = sb("cv", [128, 4, GT], F32); cv_b = [Buf(f"cv{i}") for i in range(4)]
          sq = sb("sq", [128, GT], F32); sq_b = Buf("sq")
          tg = [sb(f"tg{i}", [128, GT], F32) for i in range(2)]; tg_b = [Buf(f"tg{i}") for i in range(2)]
          mean = sb("mean", [128, GT], F32); msq = sb("msq", [128, GT], F32); rstd = sb("rstd", [128, GT], F32)
          stat_b = Buf("stat")
          ta = [sb(f"ta{i}", [128, GT], F32) for i in range(2)]; ta_b = [Buf(f"ta{i}") for i in range(2)]
          yT = sb("yT", [128, 8, GT], BF16); yT_b = [Buf(f"yT{i}") for i in range(8)]
          xin = [sb(f"xin{i}", [128, 1024], F32) for i in range(2)]; xin_b = [Buf(f"xin{i}") for i in range(2)]
          xbf = [sb(f"xbf{i}", [128, 1024], BF16) for i in range(2)]; xbf_b = [Buf(f"xbf{i}") for i in range(2)]
          qb = sb("qb", [128, 8, 64], BF16); qb_b = Buf("qb")
          kb = sb("kb", [128, 2, 64], BF16); kb_b = Buf("kb")
          rt = [sb(f"rt{i}", [128, 8, 8], F32) for i in range(2)]; rt_b = [Buf(f"rt{i}") for i in range(2)]
          NR = 3
          vb = [sb(f"vb{i}", [128, 128], BF16) for i in range(NR)]; vb_b = [Buf(f"vb{i}") for i in range(NR)]
          kT = [sb(f"kT{i}", [128, 128], BF16) for i in range(NR)]; kT_b = [Buf(f"kT{i}") for i in range(NR)]
          qT = sb("qT", [128, 4, 128], BF16); qT_b = Buf("qT")
          sm = sb("sm", [128, 8, 256], F32); sm_b = [Buf(f"sm{i}") for i in range(4)]
          pb = sb("pb", [128, 8, 256], BF16); pb_b = Buf("pb")
          pT = sb("pT", [128, 16, 128], BF16); pT_b = [Buf(f"pT{i}") for i in range(2)]
          att = sb("att", [128, 40], F32); att_b = Buf("att")
          ob = sb("ob", [128, 8, 64], BF16); ob_b = Buf("ob")
          v = [sb(f"v{i}", [128, 1024], F32) for i in range(2)]; v_b = [Buf(f"v{i}") for i in range(2)]
          st = [sb(f"st{i}", [128, 12], F32) for i in range(2)]
          mv = [sb(f"mv{i}", [128, 4], F32) for i in range(2)]; smm_b = [Buf(f"smm{i}") for i in range(2)]
          xo = [sb(f"xo{i}", [128, 1024], F32) for i in range(2)]; xo_b = [Buf(f"xo{i}") for i in range(2)]

          S.dma("pool", Win[:, :, :], dr[f"ev_w_in{e}"].rearrange("(k p) n -> p k n", p=128), writes=[Win_b])
          S.dma("pool", Wout[:, :, :], dr[f"ev_w_out{e}"].rearrange("(k p) n -> p k n", p=128), writes=[Wout_b])
          S.dma("sp", wk[:, :], dr[f"ev_dw_w{e}"], writes=[wk_b])
          S.dma("sp", cvec[:, 0:4], dr[f"ev_dw_b{e}"].rearrange("o (c p) -> p (o c)", p=128), writes=[cvec_b], allow_slow_non_contiguous=True)
          S.dma("sp", cvec[:, 4:8], dr[f"ev_cn_g{e}"].rearrange("o (c p) -> p (o c)", p=128), writes=[cvec_b], allow_slow_non_contiguous=True)
          S.dma("sp", cvec[:, 8:12], dr[f"ev_cn_b{e}"].rearrange("o (c p) -> p (o c)", p=128), writes=[cvec_b], allow_slow_non_contiguous=True)
          load_bcast_row(c, "sp", sinks, sinks_b, dr[f"ev_sinks{e}"])
          S.dma("sp", amask[:, :], dr["amask"], writes=[am_b])
          S.dma("sp", amask0[:, :], dr["amask0"], writes=[am_b])
          S.dma("sp", rope[:, :, :], dr["rope_a"], writes=[rope_b])
          load_bcast_row(c, "sp", g_t, gb_buf, dr[f"ln_mix_g{layer}"])
          load_bcast_row(c, "sp", b_t, gb_buf, dr[f"ln_mix_b{layer}"])
          S.op("dve", lambda en: en.tensor_scalar_mul(out=cvec[:, 4:12], in0=cvec[:, 4:12], scalar1=0.5),
               reads=[cvec_b], writes=[cvec_b])
          for cc in range(4):
              pt, ptb_ = c.ps.get()
              S.op("pe", lambda en, cc=cc, pt=pt: en.transpose(out=pt[:, 0:31], in_=wk[0:31, cc * 128:(cc + 1) * 128],
                                                              identity=c.ident_f[0:31, 0:31]),
                   reads=[wk_b, c.identf_buf], writes=[ptb_])
              S.op("dve", lambda en, cc=cc, pt=pt: en.tensor_copy(out=wcol[:, cc, 0:31], in_=pt[:, 0:31]),
                   reads=[ptb_], writes=[wcol_b])
          for cc in range(4):
              for k in range(31):
                  S.op("dve", lambda en, cc=cc, k=k: en.tensor_scalar(
                      out=dg[:, cc, k, :], in0=c.ident_f[:, :], scalar1=wcol[:, cc, k:k + 1], scalar2=0.5,
                      op0=ALU.mult, op1=ALU.mult), reads=[wcol_b, c.identf_buf], writes=[dg_b])

          _chk(1)
          xcnt = [0]

          def load_xT(row0, dstT, dstT_b, slot, src_ap):
              i = xcnt[0] % 2
              xcnt[0] += 1
              S.dma("sp", xin[i][:, :], src_ap[row0:row0 + 128, :], writes=[xin_b[i]])
              emit_xT(c, dstT, dstT_b, slot, xin[i], xin_b[i], xbf[i], xbf_b[i])

          def glu_chunk(xTsrc, xTsrc_b, ncols, hb, hb_buf, col0):
              for cc in range(4):
                  pa, pab = c.ps.get()
                  pg, pgb = c.ps.get()
                  for k in range(8):
                      S.op("pe", lambda en, k=k, cc=cc, pa=pa: en.matmul(
                          pa[:, 0:ncols], lhsT=Win[:, k, 768 + cc * 128:768 + (cc + 1) * 128], rhs=xTsrc[:, k, 0:ncols],
                          start=(k == 0), stop=(k == 7)), reads=[Win_b, xTsrc_b], writes=[pab])
                  for k in range(8):
                      S.op("pe", lambda en, k=k, cc=cc, pg=pg: en.matmul(
                          pg[:, 0:ncols], lhsT=Win[:, k, 1280 + cc * 128:1280 + (cc + 1) * 128], rhs=xTsrc[:, k, 0:ncols],
                          start=(k == 0), stop=(k == 7)), reads=[Win_b, xTsrc_b], writes=[pgb])
                  ti = cc % 2
                  S.op("act", lambda en, ti=ti, pg=pg: en.activation(out=tg[ti][:, 0:ncols], in_=pg[:, 0:ncols],
                                                                      func=AF.Tanh, scale=0.5),
                       reads=[pgb], writes=[tg_b[ti]])
                  S.op("dve", lambda en, ti=ti, pa=pa, cc=cc: en.scalar_tensor_tensor(
                      out=hb[:, cc, col0:col0 + ncols], in0=tg[ti][:, 0:ncols], scalar=1.0, in1=pa[:, 0:ncols],
                      op0=ALU.add, op1=ALU.mult), reads=[tg_b[ti], pab], writes=[hb_buf])

          def kv_tile(xTsrc, xTsrc_b, col0, ring_i, tile_idx, with_q):
              pkv, pkvb = c.ps.get()
              for k in range(8):
                  S.op("pe", lambda en, k=k: en.matmul(pkv[:, 0:256], lhsT=xTsrc[:, k, col0:col0 + 128],
                                                       rhs=Win[:, k, 512:768], start=(k == 0), stop=(k == 7)),
                       reads=[Win_b, xTsrc_b], writes=[pkvb])
              if with_q:
                  pq, pqb = c.ps.get()
                  for k in range(8):
                      S.op("pe", lambda en, k=k: en.matmul(pq[:, :], lhsT=xTsrc[:, k, col0:col0 + 128],
                                                           rhs=Win[:, k, 0:512], start=(k == 0), stop=(k == 7)),
                           reads=[Win_b, xTsrc_b], writes=[pqb])

              def rope_apply(s3, psrc_b, dstt, dst_b, hs):
                  nh = len(hs)
                  H = int(np.prod(hs))
                  full = [128] + list(hs)
                  cs = rope[:, tile_idx, 0:8]
                  sn = rope[:, tile_idx, 8:16]
                  for _ in range(nh):
                      cs = cs.unsqueeze(1)
                      sn = sn.unsqueeze(1)
                  cs = cs.to_broadcast(full + [8])
                  sn = sn.to_broadcast(full + [8])
                  if nh == 1:
                      r0, r1 = rt[0][:, 0:H, :], rt[1][:, 0:H, :]
                  else:
                      r0 = rt[0][:, 0:H, :].rearrange("p (a b) d -> p a b d", a=hs[0])
                      r1 = rt[1][:, 0:H, :].rearrange("p (a b) d -> p a b d", a=hs[0])
                  sl = (slice(None),) * (1 + nh)
                  t1, t2 = s3[sl + (slice(0, 8),)], s3[sl + (slice(8, 16),)]
                  S.op("dve", lambda en: en.tensor_tensor(out=r0, in0=t1, in1=cs, op=ALU.mult),
                       reads=[psrc_b, rope_b], writes=[rt_b[0]])
                  S.op("dve", lambda en: en.tensor_tensor(out=r1, in0=t2, in1=sn, op=ALU.mult),
                       reads=[psrc_b, rope_b], writes=[rt_b[1]])
                  S.op("dve", lambda en: en.tensor_tensor(out=dstt[sl + (slice(0, 8),)], in0=r0, in1=r1, op=ALU.subtract),
                       reads=[rt_b[0], rt_b[1]], writes=[dst_b])
                  S.op("dve", lambda en: en.tensor_tensor(out=r0, in0=t2, in1=cs, op=ALU.mult),
                       reads=[psrc_b, rope_b], writes=[rt_b[0]])
                  S.op("dve", lambda en: en.tensor_tensor(out=r1, in0=t1, in1=sn, op=ALU.mult),
                       reads=[psrc_b, rope_b], writes=[rt_b[1]])
                  S.op("dve", lambda en: en.tensor_tensor(out=dstt[sl + (slice(8, 16),)], in0=r0, in1=r1, op=ALU.add),
                       reads=[rt_b[0], rt_b[1]], writes=[dst_b])
                  S.op("act", lambda en: en.copy(out=dstt[sl + (slice(16, 64),)], in_=s3[sl + (slice(16, 64),)]),
                       reads=[psrc_b], writes=[dst_b])

              rope_apply(pkv[:, 0:128].rearrange("p (h d) -> p h d", h=2), pkvb, kb[:, :, :], kb_b, [2])
              S.op("act", lambda en: en.copy(out=vb[ring_i][:, :], in_=pkv[:, 128:256]), reads=[pkvb], writes=[vb_b[ring_i]])
              pt, ptb_ = c.ps.get()
              ptb = ps_bf(pt)
              S.op("pe", lambda en: en.transpose(out=ptb[:, 0:128], in_=kb[:, :, :].rearrange("p h d -> p (h d)"),
                                                 identity=c.ident_bf[:, :]), reads=[kb_b, c.ident_buf], writes=[ptb_])
              S.op("act", lambda en: en.copy(out=kT[ring_i][:, :], in_=ptb[:, 0:128]), reads=[ptb_], writes=[kT_b[ring_i]])
              if with_q:
                  rope_apply(pq[:, :].rearrange("p (g j d) -> p g j d", g=2, j=4), pqb,
                             qb[:, :, :].rearrange("p (j g) d -> p g j d", g=2), qb_b, [2, 4])
                  pt2, pt2b_ = c.ps.get()
                  pt2b = ps_bf(pt2)
                  qflat = qb[:, :, :].rearrange("p h d -> p (h d)")
                  for j in range(4):
                      S.op("pe", lambda en, j=j: en.transpose(out=pt2b[:, j * 128:(j + 1) * 128],
                                                              in_=qflat[:, j * 128:(j + 1) * 128], identity=c.ident_bf[:, :]),
                           reads=[qb_b, c.ident_buf], writes=[pt2b_])
                  S.op("dve", lambda en: en.tensor_copy(out=qT[:, :, :], in_=pt2b[:, 0:512].rearrange("p (j t) -> p j t", j=4)),
                       reads=[pt2b_], writes=[qT_b])

          _chk(2)
          load_xT(0, xTh, xTh_b, 0, halo)
          glu_chunk(xTh, xTh_b, 128, hbuf[1], hb_body[1], 32 + GT - 128)
          kv_tile(xTh, xTh_b, 0, (NR - 1), 0, False)
          _chk(3)
          blk_global = 0
          for g in range(ngroups):
              hb = hbuf[g % 2]
              hprev = hbuf[(g + 1) % 2]
              for t in range(4):
                  load_xT(g * GT + t * 128, xT, xT_b, t, src)
              S.op("act", lambda en, hb=hb, hprev=hprev: en.copy(out=hb[:, :, 2:32], in_=hprev[:, :, GT + 2:GT + 32]),
                   reads=[hb_body[(g + 1) % 2]], writes=[hb_pre[g % 2]])
              glu_chunk(xT, xT_b, GT, hb, hb_body[g % 2], 32)
              _chk(4)
              for cc in range(4):
                  pc, pcb = c.ps.get()
                  for k in range(31):
                      S.op("pe", lambda en, cc=cc, k=k, pc=pc, hb=hb: en.matmul(
                          pc[:, :], lhsT=dg[:, cc, k, :], rhs=hb[:, cc, 2 + k:2 + k + GT], start=(k == 0), stop=(k == 30)),
                          reads=[dg_b, hb_body[g % 2], hb_pre[g % 2]], writes=[pcb])
                  S.op("act", lambda en, cc=cc, pc=pc: en.activation(out=cv[:, cc, :], in_=pc[:, :], func=AF.Identity,
                                                                      bias=cvec[:, cc:cc + 1], scale=1.0),
                       reads=[pcb, cvec_b], writes=[cv_b[cc]])
              _chk(5)
              p1, p1b = c.ps.get()
              p2, p2b = c.ps.get()
              for cc in range(4):
                  S.op("pe", lambda en, cc=cc: en.matmul(p1[:, :], lhsT=c.ones_f[:, :], rhs=cv[:, cc, :],
                                                         start=(cc == 0), stop=(cc == 3)),
                       reads=[cv_b[cc], c.ones_buf], writes=[p1b])
              for cc in range(4):
                  S.op("act", lambda en, cc=cc: en.activation(out=sq[:, :], in_=cv[:, cc, :], func=AF.Square),
                       reads=[cv_b[cc]], writes=[sq_b])
                  S.op("pe", lambda en, cc=cc: en.matmul(p2[:, :], lhsT=c.ones_f[:, :], rhs=sq[:, :],
                                                         start=(cc == 0), stop=(cc == 3)),
                       reads=[sq_b, c.ones_buf], writes=[p2b])
              S.op("dve", lambda en: en.tensor_scalar_mul(out=mean[:, :], in0=p1[:, :], scalar1=1.0 / 512.0),
                   reads=[p1b], writes=[stat_b])
              S.op("dve", lambda en: en.tensor_tensor(out=msq[:, :], in0=mean[:, :], in1=mean[:, :], op=ALU.mult),
                   reads=[stat_b], writes=[stat_b])
              S.op("dve", lambda en: en.scalar_tensor_tensor(out=rstd[:, :], in0=p2[:, :], scalar=1.0 / 512.0, in1=msq[:, :],
                                                             op0=ALU.mult, op1=ALU.subtract), reads=[p2b, stat_b], writes=[stat_b])
              S.op("dve", lambda en: en.tensor_scalar_add(out=rstd[:, :], in0=rstd[:, :], scalar1=float(EPS)),
                   reads=[stat_b], writes=[stat_b])
              S.op("act", lambda en: en.activation(out=rstd[:, :], in_=rstd[:, :], func=AF.Sqrt), reads=[stat_b], writes=[stat_b])
              S.op("dve", lambda en: en.reciprocal(out=rstd[:, :], in_=rstd[:, :]), reads=[stat_b], writes=[stat_b])
              for cc in range(4):
                  i = cc % 2
                  S.op("dve", lambda en, cc=cc, i=i: en.tensor_tensor(out=ta[i][:, :], in0=cv[:, cc, :], in1=mean[:, :],
                                                                     op=ALU.subtract), reads=[cv_b[cc], stat_b], writes=[ta_b[i]])
                  S.op("dve", lambda en, i=i: en.tensor_tensor(out=ta[i][:, :], in0=ta[i][:, :], in1=rstd[:, :], op=ALU.mult),
                       reads=[ta_b[i], stat_b], writes=[ta_b[i]])
                  S.op("act", lambda en, cc=cc, i=i: en.activation(out=ta[i][:, :], in_=ta[i][:, :], func=AF.Identity,
                                                                    bias=cvec[:, 8 + cc:9 + cc], scale=cvec[:, 4 + cc:5 + cc]),
                       reads=[ta_b[i], cvec_b], writes=[ta_b[i]])
                  S.op("act", lambda en, i=i: en.activation(out=tg[i][:, :], in_=ta[i][:, :], func=AF.Tanh),
                       reads=[ta_b[i]], writes=[tg_b[i]])
                  S.op("dve", lambda en, cc=cc, i=i: en.scalar_tensor_tensor(
                      out=yT[:, 4 + cc, :], in0=tg[i][:, :], scalar=1.0, in1=ta[i][:, :], op0=ALU.add, op1=ALU.mult),
                      reads=[tg_b[i], ta_b[i]], writes=[yT_b[4 + cc]])
              _chk(6)
              for t in range(4):
                  bi = blk_global
                  blk_global += 1
                  cur = bi % NR
                  prv = (bi - 1) % NR
                  kv_tile(xT, xT_b, t * 128, cur, bi + 1, True)
                  msk = amask0 if bi == 0 else amask
                  for bank in range(4):
                      pscr, pscb = c.ps.get()
                      for hh in range(2):
                          h = bank * 2 + hh
                          gk = h // 4
                          lq = qT[gk * 64:gk * 64 + 64, h % 4, :]
                          S.op("pe", lambda en, pscr=pscr, hh=hh, lq=lq, gk=gk, prv=prv: en.matmul(
                              pscr[:, hh * 256:hh * 256 + 128], lhsT=lq, rhs=kT[prv][gk * 64:gk * 64 + 64, :],
                              start=True, stop=True), reads=[qT_b, kT_b[prv]], writes=[pscb])
                          S.op("pe", lambda en, pscr=pscr, hh=hh, lq=lq, gk=gk, cur=cur: en.matmul(
                              pscr[:, hh * 256 + 128:hh * 256 + 256], lhsT=lq, rhs=kT[cur][gk * 64:gk * 64 + 64, :],
                              start=True, stop=True), reads=[qT_b, kT_b[cur]], writes=[pscb])
                      S.op("dve", lambda en, bank=bank, pscr=pscr, msk=msk: en.tensor_tensor(
                          out=sm[:, bank * 2:bank * 2 + 2, :], in0=pscr[:, :].rearrange("p (h k) -> p h k", h=2),
                          in1=msk[:, :].unsqueeze(1).to_broadcast([128, 2, 256]), op=ALU.add),
                          reads=[pscb, am_b], writes=[sm_b[bank]])
                  S.op("dve", lambda en: en.tensor_reduce(out=att[:, 0:8], in_=sm[:, :, :], axis=mybir.AxisListType.X, op=ALU.max),
                       reads=sm_b, writes=[att_b])
                  S.op("dve", lambda en: en.scalar_tensor_tensor(out=att[:, 0:8], in0=att[:, 0:8], scalar=0.125, in1=sinks[:, :],
                                                                 op0=ALU.mult, op1=ALU.max), reads=[att_b, sinks_b], writes=[att_b])
                  S.op("dve", lambda en: en.tensor_scalar_mul(out=att[:, 8:16], in0=att[:, 0:8], scalar1=-1.0),
                       reads=[att_b], writes=[att_b])
                  for h in range(8):
                      S.op("act", lambda en, h=h: en.activation(out=pb[:, h, :], in_=sm[:, h, :], func=AF.Exp,
                                                                bias=att[:, 8 + h:9 + h], scale=0.125, accum_out=att[:, 16 + h:17 + h]),
                           reads=[sm_b[h // 2], att_b], writes=[pb_b, att_b])
                  S.op("dve", lambda en: en.tensor_tensor(out=att[:, 24:32], in0=sinks[:, :], in1=att[:, 0:8], op=ALU.subtract),
                       reads=[att_b, sinks_b], writes=[att_b])
                  S.op("act", lambda en: en.activation(out=att[:, 24:32], in_=att[:, 24:32], func=AF.Exp), reads=[att_b], writes=[att_b])
                  S.op("dve", lambda en: en.tensor_tensor(out=att[:, 32:40], in0=att[:, 16:24], in1=att[:, 24:32], op=ALU.add),
                       reads=[att_b], writes=[att_b])
                  S.op("dve", lambda en: en.reciprocal(out=att[:, 32:40], in_=att[:, 32:40]), reads=[att_b], writes=[att_b])
                  for half2 in range(2):
                      ptt, pttb_ = c.ps.get()
                      pttb = ps_bf(ptt)
                      for j in range(8):
                          idx = half2 * 8 + j
                          h, hf = idx // 2, idx % 2
                          S.op("pe", lambda en, j=j, h=h, hf=hf, pttb=pttb: en.transpose(
                              out=pttb[:, j * 128:(j + 1) * 128], in_=pb[:, h, hf * 128:(hf + 1) * 128], identity=c.ident_bf[:, :]),
                              reads=[pb_b, c.ident_buf], writes=[pttb_])
                      eng = "act" if half2 == 0 else "dve"
                      if eng == "act":
                          S.op("act", lambda en, half2=half2, pttb=pttb: en.copy(
                              out=pT[:, half2 * 8:half2 * 8 + 8, :], in_=pttb[:, :].rearrange("p (j t) -> p j t", j=8)),
                              reads=[pttb_], writes=[pT_b[half2]])
                      else:
                          S.op("dve", lambda en, half2=half2, pttb=pttb: en.tensor_copy(
                              out=pT[:, half2 * 8:half2 * 8 + 8, :], in_=pttb[:, :].rearrange("p (j t) -> p j t", j=8)),
                              reads=[pttb_], writes=[pT_b[half2]])
                  po, pob = c.ps.get()
                  for h in range(8):
                      gk = h // 4
                      S.op("pe", lambda en, h=h, gk=gk, prv=prv: en.matmul(po[:, h * 64:(h + 1) * 64], lhsT=pT[:, h * 2, :],
                                                                             rhs=vb[prv][:, gk * 64:(gk + 1) * 64], start=True, stop=False),
                           reads=[pT_b[h // 4], vb_b[prv]], writes=[pob])
                      S.op("pe", lambda en, h=h, gk=gk, cur=cur: en.matmul(po[:, h * 64:(h + 1) * 64], lhsT=pT[:, h * 2 + 1, :],
                                                                             rhs=vb[cur][:, gk * 64:(gk + 1) * 64], start=False, stop=True),
                           reads=[pT_b[h // 4], vb_b[cur]], writes=[pob])
                  S.op("dve", lambda en: en.tensor_tensor(out=ob[:, :, :], in0=po[:, :].rearrange("p (h d) -> p h d", h=8),
                                                          in1=att[:, 32:40].unsqueeze(2).to_broadcast([128, 8, 64]), op=ALU.mult),
                       reads=[pob, att_b], writes=[ob_b])
                  pt3, pt3b_ = c.ps.get()
                  pt3b = ps_bf(pt3)
                  oflat = ob[:, :, :].rearrange("p h d -> p (h d)")
                  for j in range(4):
                      S.op("pe", lambda en, j=j: en.transpose(out=pt3b[:, j * 128:(j + 1) * 128], in_=oflat[:, j * 128:(j + 1) * 128],
                                                              identity=c.ident_bf[:, :]), reads=[ob_b, c.ident_buf], writes=[pt3b_])
                  S.op("act", lambda en, t=t: en.copy(out=yT[:, 0:4, t * 128:(t + 1) * 128],
                                                      in_=pt3b[:, 0:512].rearrange("p (j t) -> p j t", j=4)),
                       reads=[pt3b_], writes=yT_b[0:4])
              _chk(7)
              for t in range(4):
                  halves = [c.ps.get(), c.ps.get()]
                  for kc in range(8):
                      for hf in range(2):
                          ph, phb = halves[hf]
                          S.op("pe", lambda en, ph=ph, kc=kc, hf=hf, t=t: en.matmul(
                              ph[:, :], lhsT=yT[:, kc, t * 128:(t + 1) * 128], rhs=Wout[:, kc, hf * 512:(hf + 1) * 512],
                              start=(kc == 0), stop=(kc == 7)), reads=[yT_b[kc], Wout_b], writes=[phb])
                  i = xcnt[0] % 2
                  xcnt[0] += 1
                  r0 = g * GT + t * 128
                  S.dma("sp", xin[i][:, :], src[r0:r0 + 128, :], writes=[xin_b[i]])
                  emit_ln_epilogue(c, (v[i], v_b[i], st[i], mv[i], smm_b[i]), halves, xin[i], xin_b[i],
                                   g_t, b_t, gb_buf, xo[i], xo_b[i])
                  outs.append(S.dma("sp", dst[r0:r0 + 128, :], xo[i][:, :], reads=[xo_b[i]]))
      except _Stop:
        pass
    S.barrier()
    return outs


OZ, OXBC, ODT, ORQ, ORK, ORV, ORG = 0, 1024, 2560, 2576, 3088, 3600, 4624
ST_W = 2064


def emit_odd(c, o, layer, src, dst, halo, NT):
    S, nc = c.S, c.nc
    GT = 512
    ngroups = NT // GT
    nchunks = NT // 128
    dr = c.dram
    outs = []
    Win_d = dr[f"od_w_in{o}"]
    with ExitStack() as es0:
        sb0 = lambda name, shape, dt: _sb(c, es0, "o_" + name, shape, dt)
        Sst = sb0("Sst", [128, 1024], F32); Sst_b = Buf("Sst")
        Rst = sb0("Rst", [128, 1024], F32); Rst_b = Buf("Rst")
        Atot = sb0("Atot", [128, 16], F32); Atot_b = Buf("Atot")
        otab = sb0("otab", [128, 32], F32); otab_b = Buf("otab")
        ptab = sb0("ptab", [128, 64], F32); ptab_b = Buf("ptab")
        wk5 = sb0("wk5", [5, 1536], F32); wk5_b = Buf("wk5")
        cw = sb0("cw", [128, 12, 5], F32); cw_b = Buf("cw")
        triu = sb0("triu", [128, 128], F32)
        trisl = sb0("trisl", [128, 128], F32)
        m01 = sb0("m01", [128, 128], F32)
        tri_b = Buf("tri")
        xin = [sb0(f"xin{i}", [128, 1024], F32) for i in range(2)]; xin_b = [Buf(f"oxin{i}") for i in range(2)]
        xbf = [sb0(f"xbf{i}", [128, 1024], BF16) for i in range(2)]; xbf_b = [Buf(f"oxbf{i}") for i in range(2)]
        xT = sb0("xT", [128, 8, GT], BF16); xT_b = Buf("oxT")
        xTh = sb0("xTh", [128, 8, 128], BF16); xTh_b = Buf("oxTh")
        sm16 = sb0("sm16", [128, 12, 16], F32); sm16_b = Buf("sm16")
        S.dma("sp", otab[:, :], dr["odd_tab"], writes=[otab_b])
        load_bcast_row(c, "sp", ptab[:, 0:16], ptab_b, dr[f"od_a_log{o}"])
        load_bcast_row(c, "sp", ptab[:, 16:32], ptab_b, dr[f"od_dt_bias{o}"])
        load_bcast_row(c, "sp", ptab[:, 32:48], ptab_b, dr[f"od_d_skip{o}"])
        S.dma("sp", wk5[0:4, :], dr[f"od_conv_w{o}"], writes=[wk5_b])
        S.dma("sp", wk5[4:5, :], dr[f"od_conv_b{o}"], writes=[wk5_b])
        S.dma("sp", triu[:, :], dr["triu"], writes=[tri_b])
        S.dma("sp", trisl[:, :], dr["trisl"], writes=[tri_b])
        S.dma("sp", m01[:, :], dr["triu"], writes=[tri_b])
        S.op("act", lambda en: en.activation(out=ptab[:, 0:16], in_=ptab[:, 0:16], func=AF.Exp), reads=[ptab_b], writes=[ptab_b])
        S.op("dve", lambda en: en.tensor_scalar_mul(out=ptab[:, 0:16], in0=ptab[:, 0:16], scalar1=-1.0), reads=[ptab_b], writes=[ptab_b])
        for cc in range(12):
            pt, ptb_ = c.ps.get()
            S.op("pe", lambda en, cc=cc, pt=pt: en.transpose(out=pt[:, 0:5], in_=wk5[0:5, cc * 128:(cc + 1) * 128],
                                                            identity=c.ident_f[0:5, 0:5]), reads=[wk5_b, c.identf_buf], writes=[ptb_])
            S.op("dve", lambda en, cc=cc, pt=pt: en.tensor_scalar_mul(out=cw[:, cc, :], in0=pt[:, 0:5], scalar1=0.5),
                 reads=[ptb_], writes=[cw_b])

        xcnt = [0]

        def load_xT(row0, dstT, dstT_b, slot, src_ap):
            i = xcnt[0] % 2
            xcnt[0] += 1
            S.dma("sp", xin[i][:, :], src_ap[row0:row0 + 128, :], writes=[xin_b[i]])
            emit_xT(c, dstT, dstT_b, slot, xin[i], xin_b[i], xbf[i], xbf_b[i])

        def wload(es, name, col0, ncol):
            t = _sb(c, es, "o_w" + name, [128, 8, ncol], BF16)
            b = Buf("w" + name)
            step = 1024
            for s0 in range(0, ncol, step):
                n = min(step, ncol - s0)
                S.dma("pool", t[:, :, s0:s0 + n], Win_d[:, col0 + s0:col0 + s0 + n].rearrange("(k p) n -> p k n", p=128),
                      writes=[b])
            return t, b

        def ssd_setup(es):
            d = {}
            sb = lambda name, shape, dt: _sb(c, es, "s_" + name, shape, dt)
            d["Wx"], d["Wx_b"] = wload(es, "xbc", OXBC, 1536 + 16)
            d["cin"] = [sb(f"cin{i}", [128, 3 + GT], F32) for i in range(2)]; d["cin_b"] = [Buf(f"cin{i}") for i in range(2)]
            d["acc"] = [sb(f"acc{i}", [128, GT], F32) for i in range(2)]; d["acc_b"] = [Buf(f"acc{i}") for i in range(2)]
            d["tg"] = [sb(f"tg{i}", [128, GT], F32) for i in range(2)]; d["tg_b"] = [Buf(f"stg{i}") for i in range(2)]
            d["hist"] = sb("hist", [128, 12, 3], F32); d["hist_b"] = [Buf(f"hist{i}") for i in range(12)]
            d["xbcT"] = sb("xbcT", [128, 12, GT], BF16); d["xbcT_b"] = [Buf(f"xbcT{i}") for i in range(12)]
            d["Btok"] = sb("Btok", [128, 256], BF16); d["Btok_b"] = Buf("Btok")
            d["xdte"] = sb("xdte", [128, 1024], BF16); d["xdte_b"] = Buf("xdte")
            d["tmpS"] = sb("tmpS", [128, 1024], F32); d["tmpS_b"] = Buf("tmpS")
            return d

        def ssd_features(d, xTsrc, xTsrc_b, ncols, halo_mode):
            Wx, Wx_b = d["Wx"], d["Wx_b"]
            for cc in range(12):
                pp, ppb = c.ps.get()
                for k in range(8):
                    S.op("pe", lambda en, k=k, cc=cc, pp=pp: en.matmul(pp[:, 0:ncols], lhsT=Wx[:, k, cc * 128:(cc + 1) * 128],
                                                                       rhs=xTsrc[:, k, 0:ncols], start=(k == 0), stop=(k == 7)),
                         reads=[Wx_b, xTsrc_b], writes=[ppb])
                if halo_mode:
                    S.op("act", lambda en, cc=cc, pp=pp: en.copy(out=d["hist"][:, cc, :], in_=pp[:, ncols - 3:ncols]),
                         reads=[ppb], writes=[d["hist_b"][cc]])
                    continue
                i = cc % 2
                cin, cin_b = d["cin"][i], d["cin_b"][i]
                acc, acc_b = d["acc"][i], d["acc_b"][i]
                tgx, tgx_b = d["tg"][i], d["tg_b"][i]
                S.op("act", lambda en, cin=cin, pp=pp: en.copy(out=cin[:, 3:3 + ncols], in_=pp[:, 0:ncols]), reads=[ppb], writes=[cin_b])
                S.op("act", lambda en, cin=cin, cc=cc: en.copy(out=cin[:, 0:3], in_=d["hist"][:, cc, :]),
                     reads=[d["hist_b"][cc]], writes=[cin_b])
                S.op("act", lambda en, cin=cin, cc=cc: en.copy(out=d["hist"][:, cc, :], in_=cin[:, ncols:ncols + 3]),
                     reads=[cin_b], writes=[d["hist_b"][cc]])
                S.op("dve", lambda en, cin=cin, acc=acc, cc=cc: en.tensor_scalar(
                    out=acc[:, 0:ncols], in0=cin[:, 0:ncols], scalar1=cw[:, cc, 0:1], scalar2=cw[:, cc, 4:5],
                    op0=ALU.mult, op1=ALU.add), reads=[cin_b, cw_b], writes=[acc_b])
                for k in range(1, 4):
                    S.op("dve", lambda en, cin=cin, acc=acc, cc=cc, k=k: en.scalar_tensor_tensor(
                        out=acc[:, 0:ncols], in0=cin[:, k:k + ncols], scalar=cw[:, cc, k:k + 1], in1=acc[:, 0:ncols],
                        op0=ALU.mult, op1=ALU.add), reads=[cin_b, cw_b, acc_b], writes=[acc_b])
                S.op("act", lambda en, acc=acc, tgx=tgx: en.activation(out=tgx[:, 0:ncols], in_=acc[:, 0:ncols], func=AF.Tanh),
                     reads=[acc_b], writes=[tgx_b])
                S.op("dve", lambda en, acc=acc, tgx=tgx, cc=cc: en.scalar_tensor_tensor(
                    out=d["xbcT"][:, cc, 0:ncols], in0=tgx[:, 0:ncols], scalar=1.0, in1=acc[:, 0:ncols],
                    op0=ALU.add, op1=ALU.mult), reads=[tgx_b, acc_b], writes=[d["xbcT_b"][cc]])

        def ssd_dt_group(d):
            Wx, Wx_b = d["Wx"], d["Wx_b"]
            pd, pdb = c.ps.get()
            for t in range(4):
                for k in range(8):
                    S.op("pe", lambda en, k=k, t=t: en.matmul(pd[:, t * 16:(t + 1) * 16], lhsT=xT[:, k, t * 128:(t + 1) * 128],
                                                              rhs=Wx[:, k, 1536:1552], start=(k == 0), stop=(k == 7)),
                         reads=[Wx_b, xT_b], writes=[pdb])
            xr = sm16[:, 0:4, :]
            S.op("dve", lambda en: en.tensor_tensor(out=xr, in0=pd[:, 0:64].rearrange("p (t h) -> p t h", t=4),
                                                    in1=ptab[:, 16:32].unsqueeze(1).to_broadcast([128, 4, 16]), op=ALU.add),
                 reads=[pdb, ptab_b], writes=[sm16_b])
            ab = sm16[:, 4:8, :]
            S.op("act", lambda en: en.activation(out=ab, in_=xr, func=AF.Abs), reads=[sm16_b], writes=[sm16_b])
            S.op("act", lambda en: en.activation(out=ab, in_=ab, func=AF.Exp, scale=-1.0), reads=[sm16_b], writes=[sm16_b])
            S.op("act", lambda en: en.activation(out=ab, in_=ab, func=AF.Ln, bias=1.0, scale=1.0), reads=[sm16_b], writes=[sm16_b])
            S.op("dve", lambda en: en.tensor_scalar_max(out=xr, in0=xr, scalar1=0.0), reads=[sm16_b], writes=[sm16_b])
            S.op("dve", lambda en: en.tensor_tensor(out=xr, in0=xr, in1=ab, op=ALU.add), reads=[sm16_b], writes=[sm16_b])
            S.op("dve", lambda en: en.tensor_tensor(out=ab, in0=xr, in1=ptab[:, 0:16].unsqueeze(1).to_broadcast([128, 4, 16]),
                                                    op=ALU.mult), reads=[sm16_b, ptab_b], writes=[sm16_b])

        def ssd_chunk_scalars(t):
            da = sm16[:, 4 + t, :]
            pa, pab = c.ps.get()
            S.op("pe", lambda en: en.matmul(pa[:, 0:16], lhsT=triu[:, :], rhs=da, start=True, stop=True),
                 reads=[tri_b, sm16_b], writes=[pab])
            S.op("pe", lambda en: en.matmul(pa[:, 16:32], lhsT=c.ones_f[:, :], rhs=da, start=True, stop=True),
                 reads=[c.ones_buf, sm16_b], writes=[pab])
            S.op("act", lambda en: en.copy(out=sm16[:, 8, :], in_=pa[:, 0:16]), reads=[pab], writes=[sm16_b])
            S.op("act", lambda en: en.activation(out=sm16[:, 9, :], in_=pa[:, 0:16], func=AF.Exp), reads=[pab], writes=[sm16_b])
            S.op("dve", lambda en: en.tensor_tensor(out=sm16[:, 10, :], in0=pa[:, 16:32], in1=sm16[:, 8, :], op=ALU.subtract),
                 reads=[pab, sm16_b], writes=[sm16_b])
            S.op("act", lambda en: en.activation(out=sm16[:, 10, :], in_=sm16[:, 10, :], func=AF.Exp), reads=[sm16_b], writes=[sm16_b])
            S.op("dve", lambda en: en.tensor_tensor(out=sm16[:, 10, :], in0=sm16[:, 10, :], in1=sm16[:, t, :], op=ALU.mult),
                 reads=[sm16_b], writes=[sm16_b])
            S.op("act", lambda en: en.activation(out=sm16[:, 11, :], in_=pa[:, 16:32], func=AF.Exp), reads=[pab], writes=[sm16_b])
            S.op("dve", lambda en: en.tensor_tensor(out=Atot[:, :], in0=Atot[:, :], in1=pa[:, 16:32], op=ALU.add),
                 reads=[pab, Atot_b], writes=[Atot_b])

        def ssd_tok_and_state(d, t, xs_extra=None):
            xbcT, xbcT_b = d["xbcT"], d["xbcT_b"]
            px, pxb = c.ps.get()
            pxv = ps_bf(px)
            for j in range(8):
                S.op("pe", lambda en, j=j: en.transpose(out=pxv[:, j * 128:(j + 1) * 128], in_=xbcT[:, j, t * 128:(t + 1) * 128],
                                                        identity=c.ident_bf[:, :]), reads=[xbcT_b[j], c.ident_buf], writes=[pxb])
            pB, pBb = c.ps.get()
            pBv = ps_bf(pB)
            for j in range(2):
                S.op("pe", lambda en, j=j: en.transpose(out=pBv[:, j * 128:(j + 1) * 128], in_=xbcT[:, 8 + j, t * 128:(t + 1) * 128],
                                                        identity=c.ident_bf[:, :]), reads=[xbcT_b[8 + j], c.ident_buf], writes=[pBb])
            S.op("act", lambda en: en.copy(out=d["Btok"][:, :], in_=pBv[:, 0:256]), reads=[pBb], writes=[d["Btok_b"]])
            xs3 = pxv[:, 0:1024].rearrange("p (h q) -> p h q", h=16)
            S.op("dve", lambda en: en.tensor_tensor(out=d["xdte"][:, :].rearrange("p (h q) -> p h q", h=16), in0=xs3,
                                                    in1=sm16[:, 10, :].unsqueeze(2).to_broadcast([128, 16, 64]), op=ALU.mult),
                 reads=[pxb, sm16_b], writes=[d["xdte_b"]])
            if xs_extra is not None:
                xs_extra(xs3, pxb)
            pst = [c.ps.get(), c.ps.get()]
            for g in range(2):
                S.op("pe", lambda en, g=g: en.matmul(pst[g][0][:, :], lhsT=d["Btok"][:, g * 128:(g + 1) * 128],
                                                     rhs=d["xdte"][:, g * 512:(g + 1) * 512], start=True, stop=True),
                     reads=[d["Btok_b"], d["xdte_b"]], writes=[pst[g][1]])
            return pst

        def ssd_state_update(d, pst):
            S.op("dve", lambda en: en.tensor_tensor(out=Sst[:, :].rearrange("p (h q) -> p h q", h=16),
                                                    in0=Sst[:, :].rearrange("p (h q) -> p h q", h=16),
                                                    in1=sm16[:, 11, :].unsqueeze(2).to_broadcast([128, 16, 64]), op=ALU.mult),
                 reads=[Sst_b, sm16_b], writes=[Sst_b])
            for g in range(2):
                S.op("dve", lambda en, g=g: en.tensor_tensor(out=Sst[:, g * 512:(g + 1) * 512], in0=Sst[:, g * 512:(g + 1) * 512],
                                                             in1=pst[g][0][:, :], op=ALU.add), reads=[Sst_b, pst[g][1]], writes=[Sst_b])

        def ret_setup(es, with_q):
            d = {}
            sb = lambda name, shape, dt: _sb(c, es, "r_" + name, shape, dt)
            if with_q:
                d["Wr"], d["Wr_b"] = wload(es, "ret", ORQ, 3072)
                d["off"] = {"q": 0, "k": 512, "v": 1024, "g": 2048}
            else:
                d["Wr"], d["Wr_b"] = wload(es, "ret", ORK, 1536)
                d["off"] = {"k": 0, "v": 512}
            d["rope"] = [sb(f"rope{i}", [128, 128], F32) for i in range(2)]; d["rope_b"] = [Buf(f"rrope{i}") for i in range(2)]
            d["rr"] = [sb(f"rr{i}", [128, 4, 64], F32) for i in range(2)]; d["rr_b"] = [Buf(f"rr{i}") for i in range(2)]
            d["kr"] = sb("kr", [128, 4, 128], F32); d["kr_b"] = Buf("kr")
            d["kp"] = sb("kp", [128, 512], BF16); d["kp_b"] = Buf("kp")
            d["vb"] = sb("vb", [128, 1024], BF16); d["vb_b"] = Buf("rvb")
            return d

        def ret_rope(d, psrc, psrc_b, ri, scale_cols, dstt, dst_b):
            s3 = psrc.rearrange("p (h e) -> p h e", h=4)
            rope_t, rope_tb = d["rope"][ri], d["rope_b"][ri]
            cs = rope_t[:, 0:64].unsqueeze(1).to_broadcast([128, 4, 64])
            sn = rope_t[:, 64:128].unsqueeze(1).to_broadcast([128, 4, 64])
            t1, t2 = s3[:, :, 0:64], s3[:, :, 64:128]
            r0, r1 = d["rr"][0], d["rr"][1]
            kr = d["kr"]
            S.op("dve", lambda en: en.tensor_tensor(out=r0[:, :, :], in0=t1, in1=cs, op=ALU.mult), reads=[psrc_b, rope_tb], writes=[d["rr_b"][0]])
            S.op("dve", lambda en: en.tensor_tensor(out=r1[:, :, :], in0=t2, in1=sn, op=ALU.mult), reads=[psrc_b, rope_tb], writes=[d["rr_b"][1]])
            S.op("dve", lambda en: en.tensor_tensor(out=kr[:, :, 0:64], in0=r0[:, :, :], in1=r1[:, :, :], op=ALU.subtract),
                 reads=d["rr_b"], writes=[d["kr_b"]])
            S.op("dve", lambda en: en.tensor_tensor(out=r0[:, :, :], in0=t2, in1=cs, op=ALU.mult), reads=[psrc_b, rope_tb], writes=[d["rr_b"][0]])
            S.op("dve", lambda en: en.tensor_tensor(out=r1[:, :, :], in0=t1, in1=sn, op=ALU.mult), reads=[psrc_b, rope_tb], writes=[d["rr_b"][1]])
            S.op("dve", lambda en: en.tensor_tensor(out=kr[:, :, 64:128], in0=r0[:, :, :], in1=r1[:, :, :], op=ALU.add),
                 reads=d["rr_b"], writes=[d["kr_b"]])
            S.op("dve", lambda en: en.tensor_tensor(out=dstt[:, :].rearrange("p (h e) -> p h e", h=4), in0=kr[:, :, :],
                                                    in1=otab[:, scale_cols[0]:scale_cols[1]].unsqueeze(2).to_broadcast([128, 4, 128]),
                                                    op=ALU.mult), reads=[d["kr_b"], otab_b], writes=[dst_b])

        def ret_kv(d, t, chunk_idx):
            Wr, Wr_b, off = d["Wr"], d["Wr_b"], d["off"]
            ri = chunk_idx % 2
            S.dma("sp", d["rope"][ri][:, :], dr["rope_d"][:, chunk_idx, :], writes=[d["rope_b"][ri]])
            pk, pkb = c.ps.get()
            for k in range(8):
                S.op("pe", lambda en, k=k: en.matmul(pk[:, :], lhsT=xT[:, k, t * 128:(t + 1) * 128],
                                                     rhs=Wr[:, k, off["k"]:off["k"] + 512], start=(k == 0), stop=(k == 7)),
                     reads=[Wr_b, xT_b], writes=[pkb])
            ret_rope(d, pk[:, :], pkb, ri, (4, 8), d["kp"], d["kp_b"])
            for hf in range(2):
                pv, pvb = c.ps.get()
                for k in range(8):
                    S.op("pe", lambda en, k=k, hf=hf, pv=pv: en.matmul(
                        pv[:, :], lhsT=xT[:, k, t * 128:(t + 1) * 128],
                        rhs=Wr[:, k, off["v"] + hf * 512:off["v"] + (hf + 1) * 512], start=(k == 0), stop=(k == 7)),
                        reads=[Wr_b, xT_b], writes=[pvb])
                S.op("act", lambda en, hf=hf, pv=pv: en.copy(out=d["vb"][:, hf * 512:(hf + 1) * 512], in_=pv[:, :]),
                     reads=[pvb], writes=[d["vb_b"]])

        def ret_state_mm(d):
            pkv = [c.ps.get(), c.ps.get()]
            for h in range(4):
                pt, ptb_ = pkv[h // 2]
                S.op("pe", lambda en, h=h, pt=pt: en.matmul(pt[:, (h % 2) * 256:(h % 2) * 256 + 256], lhsT=d["kp"][:, h * 128:(h + 1) * 128],
                                                            rhs=d["vb"][:, h * 256:(h + 1) * 256], start=True, stop=True),
                     reads=[d["kp_b"], d["vb_b"]], writes=[ptb_])
            return pkv

        def ret_state_update(pkv):
            for hf in range(2):
                S.op("dve", lambda en, hf=hf: en.tensor_tensor(out=Rst[:, hf * 512:(hf + 1) * 512], in0=Rst[:, hf * 512:(hf + 1) * 512],
                                                               in1=pkv[hf][0][:, :], op=ALU.add), reads=[Rst_b, pkv[hf][1]], writes=[Rst_b])
            S.op("dve", lambda en: en.tensor_tensor(out=Rst[:, :].rearrange("p (h v) -> p h v", h=4),
                                                    in0=Rst[:, :].rearrange("p (h v) -> p h v", h=4),
                                                    in1=otab[:, 8:12].unsqueeze(2).to_broadcast([128, 4, 256]), op=ALU.mult),
                 reads=[Rst_b, otab_b], writes=[Rst_b])

        def zero_states():
            S.op("dve", lambda en: en.memset(Sst[:, :], 0.0), writes=[Sst_b])
            S.op("dve", lambda en: en.memset(Rst[:, :], 0.0), writes=[Rst_b])
            S.op("dve", lambda en: en.memset(Atot[:, :], 0.0), writes=[Atot_b])

        zero_states()
        with ExitStack() as es:
            ds = ssd_setup(es)
            dq = ret_setup(es, False)
            load_xT(0, xTh, xTh_b, 0, halo)
            ssd_features(ds, xTh, xTh_b, 128, True)
            for g in range(ngroups):
                for t in range(4):
                    load_xT(g * GT + t * 128, xT, xT_b, t, src)
                ssd_features(ds, xT, xT_b, GT, False)
                ssd_dt_group(ds)
                for t in range(4):
                    ssd_chunk_scalars(t)
                    pst = ssd_tok_and_state(ds, t)
                    ssd_state_update(ds, pst)
                    ret_kv(dq, t, g * 4 + t)
                    pkv = ret_state_mm(dq)
                    ret_state_update(pkv)
        S.barrier()
        (loc_s, loc_r), (all_s, all_r) = c.st_loc[o], c.st_all[o]
        locb = [Buf("loc_s"), Buf("loc_r")]
        allb = [Buf("all_s"), Buf("all_r")]
        S.dma("sp", loc_s[:, 0:1024], Sst[:, :], reads=[Sst_b], writes=[locb[0]])
        S.dma("sp", loc_s[:, 1024:1040], Atot[:, :], reads=[Atot_b], writes=[locb[0]])
        S.dma("sp", loc_r[:, :], Rst[:, :], reads=[Rst_b], writes=[locb[1]])
        for (lo, al, lb, ab_) in ((loc_s, all_s, locb[0], allb[0]), (loc_r, all_r, locb[1], allb[1])):
            if c.use_cc:
                S.op("pool", lambda en, lo=lo, al=al: en.collective_compute(
                    "AllGather", ALU.bypass, replica_groups=[[0, 1, 2, 3], [4, 5, 6, 7]], ins=[lo], outs=[al]),
                    reads=[lb], writes=[ab_])
            else:
                for i in range(4):
                    S.dma("sp", al[i * 128:(i + 1) * 128, :], lo, reads=[lb], writes=[ab_])
        with ExitStack() as es:
            sb = lambda name, shape, dt: _sb(c, es, "c_" + name, shape, dt)
            rec = [sb(f"rec{i}", [128, 1040], F32) for i in range(2)]; rec_b = [Buf(f"rec{i}") for i in range(2)]
            rer = [sb(f"rer{i}", [128, 1024], F32) for i in range(2)]; rer_b = [Buf(f"rer{i}") for i in range(2)]
            cf = sb("cf", [128, 16], F32); cf_b = Buf("cf")
            zero_states()
            for i in range(4):
                rb, rbb = rec[i % 2], rec_b[i % 2]
                rr_, rrb = rer[i % 2], rer_b[i % 2]
                S.dma("sp", rb[:, :], all_s[i * 128:(i + 1) * 128, :], reads=[allb[0]], writes=[rbb])
                S.dma("sp", rr_[:, :], all_r[i * 128:(i + 1) * 128, :], reads=[allb[1]], writes=[rrb])
                S.op("act", lambda en, rb=rb: en.activation(out=cf[:, :], in_=rb[:, 1024:1040], func=AF.Exp), reads=[rbb], writes=[cf_b])
                S.op("dve", lambda en, i=i: en.tensor_scalar(out=cf[:, :], in0=cf[:, :], scalar1=-1.0, scalar2=otab[:, 28 + i:29 + i],
                                                             op0=ALU.add, op1=ALU.mult), reads=[cf_b, otab_b], writes=[cf_b])
                S.op("dve", lambda en: en.tensor_scalar_add(out=cf[:, :], in0=cf[:, :], scalar1=1.0), reads=[cf_b], writes=[cf_b])
                S.op("dve", lambda en: en.tensor_tensor(out=Sst[:, :].rearrange("p (h q) -> p h q", h=16),
                                                        in0=Sst[:, :].rearrange("p (h q) -> p h q", h=16),
                                                        in1=cf[:, :].unsqueeze(2).to_broadcast([128, 16, 64]), op=ALU.mult),
                     reads=[Sst_b, cf_b], writes=[Sst_b])
                S.op("dve", lambda en, i=i, rb=rb: en.scalar_tensor_tensor(out=Sst[:, :], in0=rb[:, 0:1024], scalar=otab[:, 28 + i:29 + i],
                                                                           in1=Sst[:, :], op0=ALU.mult, op1=ALU.add),
                     reads=[rbb, otab_b, Sst_b], writes=[Sst_b])
                S.op("dve", lambda en, i=i, rr_=rr_: en.tensor_tensor(
                    out=rr_[:, :].rearrange("p (h v) -> p h v", h=4), in0=rr_[:, :].rearrange("p (h v) -> p h v", h=4),
                    in1=otab[:, 12 + 4 * i:16 + 4 * i].unsqueeze(2).to_broadcast([128, 4, 256]), op=ALU.mult),
                    reads=[rrb, otab_b], writes=[rrb])
                S.op("dve", lambda en, rr_=rr_: en.tensor_tensor(out=Rst[:, :], in0=Rst[:, :], in1=rr_[:, :], op=ALU.add),
                     reads=[rrb, Rst_b], writes=[Rst_b])
        S.barrier()

        ysT_d = c.ysT_d
        ysd_b = [Buf(f"ysd{i}") for i in range(nchunks)]
        with ExitStack() as es:
            ds = ssd_setup(es)
            sb = lambda name, shape, dt: _sb(c, es, "b_" + name, shape, dt)
            Wz, Wz_b = wload(es, "z", OZ, 1024)
            sz = sb("sz", [128, 1024], F32); sz_b = Buf("sz")
            thz = sb("thz", [128, 1024], F32); thz_b = Buf("thz")
            Sbf = sb("Sbf", [128, 1024], BF16); Sbf_b = Buf("Sbf")
            cbm = sb("cbm", [128, 2, 128], F32); cbm_b = Buf("cbm")
            Xs = sb("Xs", [128, 8, 128], F32); Xs_b = Buf("Xs")
            ET = sb("ET", [128, 8, 128], F32); ET_b = Buf("ET")
            PT = sb("PT", [128, 16, 128], BF16); PT_b = [Buf(f"PT{g}") for g in range(2)]
            xdt = sb("xdt", [128, 1024], BF16); xdt_b = Buf("xdt")
            xsD = sb("xsD", [128, 1024], BF16); xsD_b = Buf("xsD")
            yv = sb("yv", [128, 1024], F32); yv_b = Buf("yv")
            ysb = sb("ysb", [128, 1024], BF16); ysb_b = Buf("ysb")
            ysT = [sb(f"ysT{i}", [128, 8, 128], BF16) for i in range(2)]; ysT_b = [Buf(f"ysT{i}") for i in range(2)]
            rms = sb("rms", [128, 8], F32); rms_b = Buf("rms")
            S.op("dve", lambda en: en.memset(Atot[:, :], 0.0), writes=[Atot_b])
            load_xT(0, xTh, xTh_b, 0, halo)
            ssd_features(ds, xTh, xTh_b, 128, True)
            for g in range(ngroups):
                for t in range(4):
                    load_xT(g * GT + t * 128, xT, xT_b, t, src)
                ssd_features(ds, xT, xT_b, GT, False)
                ssd_dt_group(ds)
                xbcT, xbcT_b = ds["xbcT"], ds["xbcT_b"]
                for t in range(4):
                    ci = g * 4 + t
                    tc0 = t * 128
                    ssd_chunk_scalars(t)
                    S.op("act", lambda en: en.copy(out=Sbf[:, :], in_=Sst[:, :]), reads=[Sst_b], writes=[Sbf_b])
                    for hf in range(2):
                        pz, pzb = c.ps.get()
                        for k in range(8):
                            S.op("pe", lambda en, k=k, hf=hf, pz=pz: en.matmul(
                                pz[:, :], lhsT=xT[:, k, tc0:tc0 + 128], rhs=Wz[:, k, hf * 512:(hf + 1) * 512],
                                start=(k == 0), stop=(k == 7)), reads=[Wz_b, xT_b], writes=[pzb])
                        S.op("act", lambda en, hf=hf, pz=pz: en.activation(out=thz[:, hf * 512:(hf + 1) * 512], in_=pz[:, :],
                                                                            func=AF.Tanh, scale=0.5), reads=[pzb], writes=[thz_b])
                        S.op("dve", lambda en, hf=hf, pz=pz: en.scalar_tensor_tensor(
                            out=sz[:, hf * 512:(hf + 1) * 512], in0=thz[:, hf * 512:(hf + 1) * 512], scalar=1.0, in1=pz[:, :],
                            op0=ALU.add, op1=ALU.mult), reads=[thz_b, pzb], writes=[sz_b])
                    pcb, pcbb = c.ps.get()
                    for gg in range(2):
                        S.op("pe", lambda en, gg=gg: en.matmul(pcb[:, gg * 128:(gg + 1) * 128], lhsT=xbcT[:, 8 + gg, tc0:tc0 + 128],
                                                               rhs=xbcT[:, 10 + gg, tc0:tc0 + 128], start=True, stop=True),
                             reads=[xbcT_b[8 + gg], xbcT_b[10 + gg]], writes=[pcbb])
                    S.op("dve", lambda en: en.tensor_tensor(out=cbm[:, :, :], in0=pcb[:, 0:256].rearrange("p (g l) -> p g l", g=2),
                                                            in1=m01[:, :].unsqueeze(1).to_broadcast([128, 2, 128]), op=ALU.mult),
                         reads=[pcbb, tri_b], writes=[cbm_b])

                    def xs_extra(xs3, pxb):
                        S.op("dve", lambda en: en.tensor_tensor(out=xdt[:, :].rearrange("p (h q) -> p h q", h=16), in0=xs3,
                                                                in1=sm16[:, t, :].unsqueeze(2).to_broadcast([128, 16, 64]), op=ALU.mult),
                             reads=[pxb, sm16_b], writes=[xdt_b])
                        S.op("dve", lambda en: en.tensor_tensor(out=xsD[:, :].rearrange("p (h q) -> p h q", h=16), in0=xs3,
                                                                in1=ptab[:, 32:48].unsqueeze(2).to_broadcast([128, 16, 64]), op=ALU.mult),
                             reads=[pxb, ptab_b], writes=[xsD_b])
                    pst = ssd_tok_and_state(ds, t, xs_extra)
                    for gg in range(2):
                        S.op("dve", lambda en, gg=gg: en.tensor_tensor(
                            out=Xs[:, :, :], in0=sm16[:, 4 + t, gg * 8:(gg + 1) * 8].unsqueeze(2).to_broadcast([128, 8, 128]),
                            in1=triu[:, :].unsqueeze(1).to_broadcast([128, 8, 128]), op=ALU.mult),
                            reads=[sm16_b, tri_b], writes=[Xs_b])
                        pseg = [c.ps.get(), c.ps.get()]
                        for q in range(2):
                            S.op("pe", lambda en, q=q, pseg=pseg: en.matmul(
                                pseg[q][0][:, :], lhsT=trisl[:, :], rhs=Xs[:, q * 4:(q + 1) * 4, :].rearrange("p h l -> p (h l)"),
                                start=True, stop=True), reads=[tri_b, Xs_b], writes=[pseg[q][1]])
                            S.op("act", lambda en, q=q, pseg=pseg: en.activation(
                                out=ET[:, q * 4:(q + 1) * 4, :].rearrange("p h l -> p (h l)"), in_=pseg[q][0][:, :], func=AF.Exp),
                                reads=[pseg[q][1]], writes=[ET_b])
                        S.op("dve", lambda en, gg=gg: en.tensor_tensor(
                            out=PT[:, gg * 8:(gg + 1) * 8, :], in0=ET[:, :, :],
                            in1=cbm[:, gg, :].unsqueeze(1).to_broadcast([128, 8, 128]), op=ALU.mult),
                            reads=[ET_b, cbm_b], writes=[PT_b[gg]])
                    for gg in range(2):
                        po, pob = c.ps.get()
                        S.op("pe", lambda en, gg=gg, po=po: en.matmul(po[:, :], lhsT=xbcT[:, 10 + gg, tc0:tc0 + 128],
                                                                      rhs=Sbf[:, gg * 512:(gg + 1) * 512], start=True, stop=True),
                             reads=[xbcT_b[10 + gg], Sbf_b], writes=[pob])
                        S.op("dve", lambda en, gg=gg, po=po: en.tensor_tensor(
                            out=yv[:, gg * 512:(gg + 1) * 512].rearrange("p (h q) -> p h q", h=8),
                            in0=po[:, :].rearrange("p (h q) -> p h q", h=8),
                            in1=sm16[:, 9, gg * 8:(gg + 1) * 8].unsqueeze(2).to_broadcast([128, 8, 64]), op=ALU.mult),
                            reads=[pob, sm16_b], writes=[yv_b])
                    ssd_state_update(ds, pst)
                    for gg in range(2):
                        pd_, pdb_ = c.ps.get()
                        S.op("pe", lambda en, gg=gg, pd_=pd_: en.matmul(pd_[:, :], lhsT=c.ident_bf[:, :], rhs=xsD[:, gg * 512:(gg + 1) * 512],
                                                                        start=True, stop=False), reads=[c.ident_buf, xsD_b], writes=[pdb_])
                        for hh in range(8):
                            h = gg * 8 + hh
                            S.op("pe", lambda en, h=h, hh=hh, pd_=pd_: en.matmul(
                                pd_[:, hh * 64:(hh + 1) * 64], lhsT=PT[:, h, :], rhs=xdt[:, h * 64:(h + 1) * 64],
                                start=False, stop=(hh == 7)), reads=[PT_b[gg], xdt_b], writes=[pdb_])
                        S.op("dve", lambda en, gg=gg, pd_=pd_: en.tensor_tensor(out=yv[:, gg * 512:(gg + 1) * 512],
                                                                                in0=yv[:, gg * 512:(gg + 1) * 512], in1=pd_[:, :], op=ALU.add),
                             reads=[yv_b, pdb_], writes=[yv_b])
                    S.op("dve", lambda en: en.scalar_tensor_tensor(out=yv[:, :], in0=yv[:, :], scalar=0.5, in1=sz[:, :],
                                                                   op0=ALU.mult, op1=ALU.mult), reads=[yv_b, sz_b], writes=[yv_b])
                    for gg in range(2):
                        S.op("act", lambda en, gg=gg: en.activation(out=thz[:, gg * 512:(gg + 1) * 512], in_=yv[:, gg * 512:(gg + 1) * 512],
                                                                    func=AF.Square, accum_out=rms[:, gg:gg + 1]),
                             reads=[yv_b], writes=[thz_b, rms_b])
                    S.op("dve", lambda en: en.tensor_scalar(out=rms[:, 2:4], in0=rms[:, 0:2], scalar1=1.0 / 512.0, scalar2=float(EPS),
                                                            op0=ALU.mult, op1=ALU.add), reads=[rms_b], writes=[rms_b])
                    S.op("act", lambda en: en.activation(out=rms[:, 2:4], in_=rms[:, 2:4], func=AF.Sqrt), reads=[rms_b], writes=[rms_b])
                    S.op("dve", lambda en: en.reciprocal(out=rms[:, 2:4], in_=rms[:, 2:4]), reads=[rms_b], writes=[rms_b])
                    S.op("dve", lambda en: en.tensor_tensor(out=ysb[:, :].rearrange("p (g q) -> p g q", g=2),
                                                            in0=yv[:, :].rearrange("p (g q) -> p g q", g=2),
                                                            in1=rms[:, 2:4].unsqueeze(2).to_broadcast([128, 2, 512]), op=ALU.mult),
                         reads=[yv_b, rms_b], writes=[ysb_b])
                    pt, ptb_ = c.ps.get()
                    ptv = ps_bf(pt)
                    for j in range(8):
                        S.op("pe", lambda en, j=j, ptv=ptv: en.transpose(out=ptv[:, j * 128:(j + 1) * 128], in_=ysb[:, j * 128:(j + 1) * 128],
                                                                         identity=c.ident_bf[:, :]), reads=[ysb_b, c.ident_buf], writes=[ptb_])
                    yi = ci % 2
                    S.op("act", lambda en, yi=yi, ptv=ptv: en.copy(out=ysT[yi][:, :, :], in_=ptv[:, :].rearrange("p (j t) -> p j t", j=8)),
                         reads=[ptb_], writes=[ysT_b[yi]])
                    S.dma("sp", ysT_d[:, :, ci * 128:(ci + 1) * 128].rearrange("k p t -> p k t"), ysT[yi][:, :, :],
                          reads=[ysT_b[yi]], writes=[ysd_b[ci]])
        S.barrier()

        with ExitStack() as es:
            dq = ret_setup(es, True)
            sb = lambda name, shape, dt: _sb(c, es, "d_" + name, shape, dt)
            Wr, Wr_b, off = dq["Wr"], dq["Wr_b"], dq["off"]
            Wout = sb("wout", [128, 16, 1024], BF16); Wout_b = Buf("owout")
            ng = sb("ng", [128, 8], F32); ng_b = Buf("ng")
            gng = sb("gng", [128, 1024], F32); gnb = sb("gnb", [128, 1024], F32); gn_b = Buf("gn")
            g_t = sb("g", [128, 1024], F32); b_t = sb("b", [128, 1024], F32); gb_buf = Buf("ogb")
            qp = sb("qp", [128, 512], BF16); qp_b = Buf("qp")
            qT = sb("qT", [128, 4, 128], BF16); qT_b = Buf("oqT")
            kT = sb("kT", [128, 4, 128], BF16); kT_b = Buf("okT")
            sg = sb("sg", [128, 1024], F32); sg_b = Buf("sg")
            thg = sb("thg", [128, 1024], F32); thg_b = Buf("thg")
            scT = sb("scT", [128, 4, 128], BF16); scT_b = Buf("scT")
            Rbf = sb("Rbf", [128, 1024], BF16); Rbf_b = Buf("Rbf")
            yr = sb("yr", [128, 1024], F32); yr_b = Buf("yr")
            yrb = sb("yrb", [128, 1024], BF16); yrb_b = Buf("yrb")
            yrT = sb("yrT", [128, 8, 128], BF16); yrT_b = Buf("yrT")
            ysl = [sb(f"ysl{i}", [128, 8, 128], BF16) for i in range(2)]; ysl_b = [Buf(f"ysl{i}") for i in range(2)]
            gst = sb("gst", [128, 4, 6], F32); gmv = sb("gmv", [128, 4, 4], F32); gs_b = Buf("gs")
            v = sb("v", [128, 1024], F32); v_b = Buf("ov")
            st = sb("st", [128, 12], F32); mv = sb("mv", [128, 4], F32); smm_b = Buf("osmm")
            xo = [sb(f"xo{i}", [128, 1024], F32) for i in range(2)]; xo_b = [Buf(f"oxo{i}") for i in range(2)]
            xres = sb("xres", [128, 1024], F32); xres_b = Buf("xres")
            S.dma("pool", Wout[:, 0:8, :], dr[f"od_w_out{o}"][0:1024, :].rearrange("(k p) n -> p k n", p=128), writes=[Wout_b])
            S.dma("pool", Wout[:, 8:16, :], dr[f"od_w_out{o}"][1024:2048, :].rearrange("(k p) n -> p k n", p=128), writes=[Wout_b])
            S.dma("sp", ng[:, :], dr[f"od_ssm_norm_g{o}"].rearrange("o (c p) -> p (o c)", p=128), writes=[ng_b], allow_slow_non_contiguous=True)
            load_bcast_row(c, "sp", gng, gn_b, dr[f"od_ret_gn_g{o}"])
            load_bcast_row(c, "sp", gnb, gn_b, dr[f"od_ret_gn_b{o}"])
            load_bcast_row(c, "sp", g_t, gb_buf, dr[f"ln_mix_g{layer}"])
            load_bcast_row(c, "sp", b_t, gb_buf, dr[f"ln_mix_b{layer}"])
            for kc in range(8):
                S.op("dve", lambda en, kc=kc: en.tensor_scalar_mul(out=Wout[:, kc, :], in0=Wout[:, kc, :], scalar1=ng[:, kc:kc + 1]),
                     reads=[Wout_b, ng_b], writes=[Wout_b])
            for g in range(ngroups):
                for t in range(4):
                    load_xT(g * GT + t * 128, xT, xT_b, t, src)
                for t in range(4):
                    ci = g * 4 + t
                    tc0 = t * 128
                    ri = ci % 2
                    yi = ci % 2
                    S.dma("sp", ysl[yi][:, :, :], ysT_d[:, :, ci * 128:(ci + 1) * 128].rearrange("k p t -> p k t"),
                          reads=[ysd_b[ci]], writes=[ysl_b[yi]])
                    ret_kv(dq, t, ci)
                    pq, pqb = c.ps.get()
                    for k in range(8):
                        S.op("pe", lambda en, k=k: en.matmul(pq[:, :], lhsT=xT[:, k, tc0:tc0 + 128], rhs=Wr[:, k, 0:512],
                                                             start=(k == 0), stop=(k == 7)), reads=[Wr_b, xT_b], writes=[pqb])
                    ret_rope(dq, pq[:, :], pqb, ri, (0, 4), qp, qp_b)
                    for (srcp, srcp_b, dT, dT_b) in ((qp, qp_b, qT, qT_b), (dq["kp"], dq["kp_b"], kT, kT_b)):
                        pt, ptb_ = c.ps.get()
                        ptv = ps_bf(pt)
                        for j in range(4):
                            S.op("pe", lambda en, j=j, ptv=ptv, srcp=srcp: en.transpose(
                                out=ptv[:, j * 128:(j + 1) * 128], in_=srcp[:, j * 128:(j + 1) * 128], identity=c.ident_bf[:, :]),
                                reads=[srcp_b, c.ident_buf], writes=[ptb_])
                        S.op("act", lambda en, ptv=ptv, dT=dT: en.copy(out=dT[:, :, :], in_=ptv[:, 0:512].rearrange("p (j t) -> p j t", j=4)),
                             reads=[ptb_], writes=[dT_b])
                    for hf in range(2):
                        pg, pgb = c.ps.get()
                        for k in range(8):
                            S.op("pe", lambda en, k=k, hf=hf, pg=pg: en.matmul(
                                pg[:, :], lhsT=xT[:, k, tc0:tc0 + 128], rhs=Wr[:, k, 2048 + hf * 512:2048 + (hf + 1) * 512],
                                start=(k == 0), stop=(k == 7)), reads=[Wr_b, xT_b], writes=[pgb])
                        S.op("act", lambda en, hf=hf, pg=pg: en.activation(out=thg[:, hf * 512:(hf + 1) * 512], in_=pg[:, :],
                                                                            func=AF.Tanh, scale=0.5), reads=[pgb], writes=[thg_b])
                        S.op("dve", lambda en, hf=hf, pg=pg: en.scalar_tensor_tensor(
                            out=sg[:, hf * 512:(hf + 1) * 512], in0=thg[:, hf * 512:(hf + 1) * 512], scalar=1.0, in1=pg[:, :],
                            op0=ALU.add, op1=ALU.mult), reads=[thg_b, pgb], writes=[sg_b])
                    psc, pscb = c.ps.get()
                    for h in range(4):
                        S.op("pe", lambda en, h=h: en.matmul(psc[:, h * 128:(h + 1) * 128], lhsT=kT[:, h, :], rhs=qT[:, h, :],
                                                             start=True, stop=True), reads=[kT_b, qT_b], writes=[pscb])
                    S.op("dve", lambda en: en.tensor_tensor(out=scT[:, :, :], in0=psc[:, :].rearrange("p (h l) -> p h l", h=4),
                                                            in1=m01[:, :].unsqueeze(1).to_broadcast([128, 4, 128]), op=ALU.mult),
                         reads=[pscb, tri_b], writes=[scT_b])
                    S.op("act", lambda en: en.copy(out=Rbf[:, :], in_=Rst[:, :]), reads=[Rst_b], writes=[Rbf_b])
                    py = [c.ps.get(), c.ps.get()]
                    for h in range(4):
                        pt, ptb_ = py[h // 2]
                        cs_ = slice((h % 2) * 256, (h % 2) * 256 + 256)
                        S.op("pe", lambda en, h=h, pt=pt, cs_=cs_: en.matmul(pt[:, cs_], lhsT=scT[:, h, :], rhs=dq["vb"][:, h * 256:(h + 1) * 256],
                                                                             start=True, stop=False), reads=[scT_b, dq["vb_b"]], writes=[ptb_])
                        S.op("pe", lambda en, h=h, pt=pt, cs_=cs_: en.matmul(pt[:, cs_], lhsT=qT[:, h, :], rhs=Rbf[:, h * 256:(h + 1) * 256],
                                                                             start=False, stop=True), reads=[qT_b, Rbf_b], writes=[ptb_])
                    pkv = ret_state_mm(dq)
                    ret_state_update(pkv)
                    for h in range(4):
                        pt, ptb_ = py[h // 2]
                        cs_ = slice((h % 2) * 256, (h % 2) * 256 + 256)
                        S.op("dve", lambda en, h=h, pt=pt, cs_=cs_: en.bn_stats(out=gst[:, h, :], in_=pt[:, cs_]), reads=[ptb_], writes=[gs_b])
                        S.op("dve", lambda en, h=h: en.bn_aggr(out=gmv[:, h, 0:2], in_=gst[:, h, :]), reads=[gs_b], writes=[gs_b])
                    S.op("dve", lambda en: en.tensor_scalar_add(out=gmv[:, :, 2:3], in0=gmv[:, :, 1:2], scalar1=float(EPS)), reads=[gs_b], writes=[gs_b])
                    S.op("act", lambda en: en.activation(out=gmv[:, :, 2:3], in_=gmv[:, :, 2:3], func=AF.Sqrt), reads=[gs_b], writes=[gs_b])
                    S.op("dve", lambda en: en.reciprocal(out=gmv[:, :, 2:3], in_=gmv[:, :, 2:3]), reads=[gs_b], writes=[gs_b])
                    S.op("dve", lambda en: en.scalar_tensor_tensor(out=gmv[:, :, 3:4], in0=gmv[:, :, 0:1], scalar=-1.0, in1=gmv[:, :, 2:3],
                                                                   op0=ALU.mult, op1=ALU.mult), reads=[gs_b], writes=[gs_b])
                    for h in range(4):
                        pt, ptb_ = py[h // 2]
                        cs_ = slice((h % 2) * 256, (h % 2) * 256 + 256)
                        S.op("act", lambda en, h=h, pt=pt, cs_=cs_: en.activation(out=yr[:, h * 256:(h + 1) * 256], in_=pt[:, cs_], func=AF.Identity,
                                                                                  bias=gmv[:, h, 3:4], scale=gmv[:, h, 2:3]),
                             reads=[ptb_, gs_b], writes=[yr_b])
                    S.op("dve", lambda en: en.tensor_tensor(out=yr[:, :], in0=yr[:, :], in1=gng[:, :], op=ALU.mult), reads=[yr_b, gn_b], writes=[yr_b])
                    S.op("dve", lambda en: en.tensor_tensor(out=yr[:, :], in0=yr[:, :], in1=gnb[:, :], op=ALU.add), reads=[yr_b, gn_b], writes=[yr_b])
                    S.op("dve", lambda en: en.scalar_tensor_tensor(out=yrb[:, :], in0=yr[:, :], scalar=0.5, in1=sg[:, :],
                                                                   op0=ALU.mult, op1=ALU.mult), reads=[yr_b, sg_b], writes=[yrb_b])
                    pt, ptb_ = c.ps.get()
                    ptv = ps_bf(pt)
                    for j in range(8):
                        S.op("pe", lambda en, j=j, ptv=ptv: en.transpose(out=ptv[:, j * 128:(j + 1) * 128], in_=yrb[:, j * 128:(j + 1) * 128],
                                                                         identity=c.ident_bf[:, :]), reads=[yrb_b, c.ident_buf], writes=[ptb_])
                    S.op("act", lambda en, ptv=ptv: en.copy(out=yrT[:, :, :], in_=ptv[:, :].rearrange("p (j t) -> p j t", j=8)),
                         reads=[ptb_], writes=[yrT_b])
                    halves = [c.ps.get(), c.ps.get()]
                    for kc in range(16):
                        lt = ysl[yi][:, kc, :] if kc < 8 else yrT[:, kc - 8, :]
                        lb = ysl_b[yi] if kc < 8 else yrT_b
                        for hf in range(2):
                            ph, phb = halves[hf]
                            S.op("pe", lambda en, ph=ph, kc=kc, hf=hf, lt=lt: en.matmul(
                                ph[:, :], lhsT=lt, rhs=Wout[:, kc, hf * 512:(hf + 1) * 512], start=(kc == 0), stop=(kc == 15)),
                                reads=[lb, Wout_b], writes=[phb])
                    r0 = g * GT + t * 128
                    S.dma("sp", xres[:, :], src[r0:r0 + 128, :], writes=[xres_b])
                    xi = ci % 2
                    emit_ln_epilogue(c, (v, v_b, st, mv, smm_b), halves, xres, xres_b, g_t, b_t, gb_buf, xo[xi], xo_b[xi])
                    outs.append(S.dma("sp", dst[r0:r0 + 128, :], xo[xi][:, :], reads=[xo_b[xi]]))
    S.barrier()
    return outs


def emit_halo_exchange(c, xsrc, NT, hidx):
    S, nc = c.S, c.nc
    hl_loc = nc.dram_tensor(f"hl_loc{hidx}", [128, D], F32, kind="Internal").ap()
    hl_all = nc.dram_tensor(f"hl_all{hidx}", [4 * 128, D], F32, kind="Internal").ap()
    halo_d = nc.dram_tensor(f"halo_d{hidx}", [128, D], F32, kind="Internal").ap()
    lb, ab_, hb_ = Buf("hl_loc"), Buf("hl_all"), Buf("halo_d")
    with ExitStack() as es:
        sb = lambda name, shape, dt: _sb(c, es, "h_" + name, shape, dt)
        t0 = sb("t0", [128, D], F32); t0_b = Buf("ht0")
        rec = [sb(f"rec{i}", [128, D], F32) for i in range(2)]; rec_b = [Buf(f"hrec{i}") for i in range(2)]
        acc = sb("acc", [128, D], F32); acc_b = Buf("hacc")
        hsel = sb("hsel", [128, 4], F32); hsel_b = Buf("hsel")
        S.dma("sp", hsel[:, :], c.dram["hsel"], writes=[hsel_b])
        S.dma("sp", t0[:, :], xsrc[NT - 128:NT, :], writes=[t0_b])
        S.dma("sp", hl_loc, t0[:, :], reads=[t0_b], writes=[lb])
        if c.use_cc:
            S.op("pool", lambda en: en.collective_compute("AllGather", ALU.bypass, replica_groups=[[0, 1, 2, 3], [4, 5, 6, 7]],
                                                          ins=[hl_loc], outs=[hl_all]), reads=[lb], writes=[ab_])
        else:
            for i in range(4):
                S.dma("sp", hl_all[i * 128:(i + 1) * 128, :], hl_loc, reads=[lb], writes=[ab_])
        for i in range(4):
            rb, rbb = rec[i % 2], rec_b[i % 2]
            S.dma("sp", rb[:, :], hl_all[i * 128:(i + 1) * 128, :], reads=[ab_], writes=[rbb])
            if i == 0:
                S.op("dve", lambda en, rb=rb: en.tensor_scalar_mul(out=acc[:, :], in0=rb[:, :], scalar1=hsel[:, 0:1]),
                     reads=[rbb, hsel_b], writes=[acc_b])
            else:
                S.op("dve", lambda en, rb=rb, i=i: en.scalar_tensor_tensor(out=acc[:, :], in0=rb[:, :], scalar=hsel[:, i:i + 1],
                                                                           in1=acc[:, :], op0=ALU.mult, op1=ALU.add),
                     reads=[rbb, hsel_b, acc_b], writes=[acc_b])
        S.dma("sp", halo_d, acc[:, :], reads=[acc_b], writes=[hb_])
    S.barrier()
    return halo_d


EVEN_IN = 1792
ODD_IN = 5648

PER_LAYER_SHAPES = {
    "ln_mix_g": [1, D], "ln_mix_b": [1, D], "ln_ffn_g": [1, D], "ln_ffn_b": [1, D],
    "ffn_w_gate": [D, FH], "ffn_w_up": [D, FH], "ffn_w_down": [FH, D],
}
EVEN_SHAPES = {
    "ev_w_in": [D, EVEN_IN], "ev_sinks": [1, 8], "ev_dw_w": [31, 512], "ev_dw_b": [1, 512],
    "ev_cn_g": [1, 512], "ev_cn_b": [1, 512], "ev_w_out": [D, D],
}
ODD_SHAPES = {
    "od_w_in": [D, ODD_IN], "od_conv_w": [4, 1536], "od_conv_b": [1, 1536], "od_dt_bias": [1, 16],
    "od_a_log": [1, 16], "od_d_skip": [1, 16], "od_ssm_norm_g": [1, D], "od_ret_gn_g": [1, D],
    "od_ret_gn_b": [1, D], "od_w_out": [2 * D, D],
}


def stage_inputs(stages):
    need = {"x": None, "ident": [128, 128], "ones": [128, 128]}
    for kind, idx in stages:
        if kind == "ffn":
            for k in ("ln_ffn_g", "ln_ffn_b", "ffn_w_gate", "ffn_w_up", "ffn_w_down"):
                need[f"{k}{idx}"] = PER_LAYER_SHAPES[k]
        elif kind == "even":
            layer = 2 * idx
            for k in ("ln_mix_g", "ln_mix_b"):
                need[f"{k}{layer}"] = PER_LAYER_SHAPES[k]
            for k, s in EVEN_SHAPES.items():
                need[f"{k}{idx}"] = s
            need["amask"] = [128, 256]
            need["amask0"] = [128, 256]
            need["rope_a"] = None
            need["halo"] = [128, D]
        elif kind == "odd":
            layer = 2 * idx + 1
            for k in ("ln_mix_g", "ln_mix_b"):
                need[f"{k}{layer}"] = PER_LAYER_SHAPES[k]
            for k, s in ODD_SHAPES.items():
                need[f"{k}{idx}"] = s
            need["halo"] = [128, D]
            need["odd_tab"] = [128, 32]
            need["triu"] = [128, 128]
            need["trisl"] = [128, 128]
            need["rope_d"] = None
    if sum(1 for k, _ in stages if k in ("even", "odd")) > 1:
        need["hsel"] = [128, 4]
    return need


def build_program(NT, stages, use_cc=True):
    nc = bass.Bass("TRN2", target_bir_lowering=False)
    c = Ctx()
    c.nc = nc
    c.NT = NT
    c.use_cc = use_cc
    c.st_loc = {}
    c.st_all = {}
    for kind, idx in stages:
        if kind == "odd":
            c.st_loc[idx] = (nc.dram_tensor(f"loc_s{idx}", [128, 1040], F32, kind="Internal").ap(),
                             nc.dram_tensor(f"loc_r{idx}", [128, 1024], F32, kind="Internal").ap())
            c.st_all[idx] = (nc.dram_tensor(f"all_s{idx}", [4 * 128, 1040], F32, kind="Internal").ap(),
                             nc.dram_tensor(f"all_r{idx}", [4 * 128, 1024], F32, kind="Internal").ap())
            c.ysT_d = nc.dram_tensor("ysT_d", [8, 128, NT], BF16, kind="Internal").ap() if not hasattr(c, "ysT_d") else c.ysT_d
    es = ExitStack()
    c.es = es
    c.dram = {}
    need = stage_inputs(stages)
    need["x"] = [NT, D]
    if "rope_a" in need:
        need["rope_a"] = [128, NT // 128 + 1, 16]
    if "rope_d" in need:
        need["rope_d"] = [128, NT // 128, 128]
    for name, shape in need.items():
        c.dram[name] = nc.dram_tensor(name, list(shape), F32, kind="ExternalInput").ap()
    y = nc.dram_tensor("y", [NT, D], F32, kind="ExternalOutput").ap()
    xa = nc.dram_tensor("xa", [NT, D], F32, kind="Internal").ap()
    xb = nc.dram_tensor("xb", [NT, D], F32, kind="Internal").ap()
    with es:
        c.S = Sched(nc, es)
        c.ps = PsumPool(c)
        c.ident_f = _sb(c, es, "ident_f", [128, 128], F32)
        c.ident_bf = _sb(c, es, "ident_bf", [128, 128], BF16)
        c.ones_f = _sb(c, es, "ones_f", [128, 128], F32)
        c.ident_buf = Buf("ident")
        c.identf_buf = Buf("identf")
        c.ones_buf = Buf("ones")
        c.S.dma("sp", c.ident_f[:, :], c.dram["ident"], writes=[c.identf_buf])
        c.S.dma("sp", c.ones_f[:, :], c.dram["ones"], writes=[c.ones_buf])
        c.S.op("dve", lambda e: e.tensor_copy(out=c.ident_bf[:, :], in_=c.ident_f[:, :]), reads=[c.identf_buf],
               writes=[c.ident_buf])
        cur = c.dram["x"]
        bufs = [xa, xb]
        outs = []
        nmix = 0
        for si, (kind, idx) in enumerate(stages):
            last = si == len(stages) - 1
            dst = y if last else bufs[si % 2]
            if kind in ("even", "odd"):
                halo = c.dram["halo"] if nmix == 0 else emit_halo_exchange(c, cur, NT, nmix)
                nmix += 1
            if kind == "ffn":
                outs = emit_ffn(c, idx, cur, dst, NT)
            elif kind == "even":
                outs = emit_even(c, idx, 2 * idx, cur, dst, halo, NT)
            elif kind == "odd":
                outs = emit_odd(c, idx, 2 * idx + 1, cur, dst, halo, NT)
            else:
                raise NotImplementedError(kind)
            cur = dst
        c.S.emit(final_ops=outs)
    return nc


ROPE_THETA = 500000.0


def rope_table_a(pos0, NT):
    nt = NT // 128 + 1
    pos = (pos0 - 128 + np.arange(nt * 128)).astype(np.float32)
    inv = np.power(np.float32(ROPE_THETA), -np.arange(8, dtype=np.float32) / np.float32(8)).astype(np.float32)
    ang = (pos[:, None] * inv[None, :]).astype(np.float32)
    tab = np.concatenate([np.cos(ang), np.sin(ang)], axis=1).astype(np.float32)
    return np.ascontiguousarray(tab.reshape(nt, 128, 16).transpose(1, 0, 2))


RET_THETA = 10000.0


def rope_table_d(pos0, NT):
    nt = NT // 128
    pos = (pos0 + np.arange(nt * 128)).astype(np.float32)
    inv = (1.0 / np.power(np.float32(RET_THETA), np.linspace(0.0, 1.0, 64, dtype=np.float32))).astype(np.float32)
    ang = (pos[:, None] * inv[None, :]).astype(np.float32)
    tab = np.concatenate([np.cos(ang), np.sin(ang)], axis=1).astype(np.float32)
    return np.ascontiguousarray(tab.reshape(nt, 128, 128).transpose(1, 0, 2))


def odd_table(r, NT):
    h = np.arange(4, dtype=np.float64)
    lg = np.log(1.0 - np.power(2.0, -5.0 - h))
    l = np.arange(128, dtype=np.float64)[:, None]
    t = np.zeros((128, 32), np.float64)
    t[:, 0:4] = np.exp(lg[None, :] * (l + 1.0))
    t[:, 4:8] = np.exp(-lg[None, :] * (l + 1.0)) * (128.0 ** -0.5)
    t[:, 8:12] = np.exp(lg * 128.0)[None, :]
    for i in range(4):
        if i < r:
            t[:, 12 + 4 * i:16 + 4 * i] = np.exp(lg * float(NT * (r - 1 - i)))[None, :]
            t[:, 28 + i] = 1.0
    return t.astype(np.float32)


def attn_masks(first):
    i = np.arange(128)[:, None]
    j = np.arange(256)[None, :]
    valid = (j > i) & (j <= i + 128)
    m = np.where(valid, 0.0, A_MASK_NEG).astype(np.float32)
    m0 = m.copy()
    if first:
        m0[:, :128] = A_MASK_NEG
    return m, m0


def make_in_map(inp, x_shard, halo, cidx, NT, stages):
    need = stage_inputs(stages)
    r = cidx % 4
    m = {}
    for name in need:
        if name == "x":
            m[name] = np.ascontiguousarray(x_shard, dtype=np.float32)
        elif name == "ident":
            m[name] = np.eye(128, dtype=np.float32)
        elif name == "ones":
            m[name] = np.ones((128, 128), dtype=np.float32)
        elif name == "halo":
            m[name] = np.ascontiguousarray(halo, dtype=np.float32)
        elif name == "amask":
            m[name] = attn_masks(False)[0]
        elif name == "amask0":
            m[name] = attn_masks(r == 0)[1]
        elif name == "rope_a":
            m[name] = rope_table_a(r * NT, NT)
        elif name == "rope_d":
            m[name] = rope_table_d(r * NT, NT)
        elif name == "odd_tab":
            m[name] = odd_table(r, NT)
        elif name == "hsel":
            hs = np.zeros((128, 4), np.float32)
            if r > 0:
                hs[:, r - 1] = 1.0
            m[name] = hs
        elif name == "triu":
            m[name] = np.triu(np.ones((128, 128), np.float32))
        elif name == "trisl":
            m[name] = np.tril(np.ones((128, 128), np.float32), -1).T.copy().T if False else (np.arange(128)[:, None] > np.arange(128)[None, :]).astype(np.float32)
        else:
            base = name.rstrip("0123456789")
            idx = int(name[len(base):])
            m[name] = np.ascontiguousarray(np.asarray(inp[base][idx], dtype=np.float32).reshape(need[name]))
    return m


SEQ = 16384
BATCH = 2
FUSED = True
ALL_STAGES = [("even", 0), ("ffn", 0), ("odd", 0), ("ffn", 1), ("even", 1), ("ffn", 2), ("odd", 1), ("ffn", 3)]


def run_stages(inp, x, stages):
    S_ = x.shape[1]
    NT = S_ // 4
    nc = build_program(NT, stages)
    in_maps = []
    for cidx in range(NCORES):
        b, r = cidx // 4, cidx % 4
        halo = x[b, r * NT - 128:r * NT] if r > 0 else np.zeros((128, D), np.float32)
        in_maps.append(make_in_map(inp, x[b, r * NT:(r + 1) * NT], halo, cidx, NT, stages))
    res = run_bass_kernel_spmd(nc, in_maps, core_ids=list(range(NCORES)))
    out = np.empty_like(x)
    for cidx in range(NCORES):
        b, r = cidx // 4, cidx % 4
        out[b, r * NT:(r + 1) * NT] = res.results[cidx]["y"]
    return out


def kernel(**inputs):
    inp = {k: np.asarray(v) for k, v in inputs.items()}
    x = np.ascontiguousarray(inp["x"], dtype=np.float32)
    if FUSED:
        return run_stages(inp, x, ALL_STAGES)
    for li in range(DEPTH):
        x = run_stages(inp, x, ALL_STAGES[2 * li:2 * li + 2])
    return x
```

```python
import numpy as np
import os as _os_env
from contextlib import ExitStack
import concourse.bass as bass
import concourse.mybir as mybir
from concourse.bass_utils import run_bass_kernel_spmd

F32 = mybir.dt.float32
BF16 = mybir.dt.bfloat16
ALU = mybir.AluOpType
AF = mybir.ActivationFunctionType

D = 1024
FH = 2816
DEPTH = 4
ALPHA = (2 * DEPTH) ** 0.25
EPS = 1e-5
NCORES = 8


class Buf:
    __slots__ = ("name", "w", "rs")

    def __init__(self, name):
        self.name = name
        self.w = None
        self.rs = []


class Op:
    __slots__ = ("stream", "fn", "deps", "adeps", "needed", "sem", "val", "is_dma", "idx", "seg", "cost", "lat", "dq_index")


class _Rec:
    def __init__(self):
        self.call = None

    def __getattr__(self, name):
        def f(*a, **kw):
            assert self.call is None, "one engine instruction per op"
            self.call = (name, a, kw)
            return None
        return f


_NOWAR = bool(_os_env.environ.get('NOWAR'))


class Sched:
    STRICT = tuple(x for x in _os_env.environ.get("STRICT_ENG", "act,dve,pool").split(",") if x)
    NDMA = 6
    CHAIN = tuple(x for x in _os_env.environ.get("CHAIN_ENG", "act").split(",") if x)
    REORDER = _os_env.environ.get("REORDER", "1") == "1"

    def __init__(self, nc, es):
        self.nc = nc
        self.streams = {k: [] for k in ("pe", "act", "dve", "pool", "sp")}
        self.sem = {k: es.enter_context(nc.semaphore("pg_" + k)) for k in self.streams}
        self.dsem = {q: [es.enter_context(nc.semaphore(f"dq_{q}{i}")) for i in range(self.NDMA)]
                     for q in ("sp", "pool", "act")}
        self.dcnt = {q: 0 for q in self.dsem}
        self.dring = {q: [None] * self.NDMA for q in self.dsem}
        self.seg = 0
        self.all_ops = []
        self.last_on = {}

    @staticmethod
    def _cost(stream, name, a, kw, is_dma):
        def fsz(ap):
            sh = ap.shape
            n = 1
            for s_ in sh[1:]:
                n *= int(s_)
            return n
        try:
            if is_dma:
                o = kw.get("out")
                nbytes = fsz(o) * int(o.shape[0]) * 4
                return (1000.0 if stream == "pool" else 100.0), 2000.0 + nbytes / 150.0
            if name == "collective_compute":
                return 1000.0, 60000.0
            if stream == "pe":
                if name == "transpose":
                    return 110.0, 300.0
                rhs = kw.get("rhs")
                n = fsz(rhs)
                mult = 4.0 if rhs.dtype == F32 else 1.0
                return mult * (40.0 + 0.47 * n), 300.0
            o = kw.get("out", a[0] if a else None)
            f = fsz(o) if o is not None else 64
            if stream == "act":
                return 170.0 + 0.9 * f, 300.0
            if stream == "dve":
                return 70.0 + 0.66 * f, 300.0
            return 200.0 + 2.0 * f, 200.0
        except Exception:
            return 300.0, 200.0

    def _add(self, stream, fn, reads, writes, is_dma=False, extra=()):
        op = Op()
        op.stream = stream
        rec = _Rec()
        fn(rec)
        name_, a_, kw_ = rec.call
        op.fn = lambda e, name_=name_, a_=a_, kw_=kw_: getattr(e, name_)(*a_, **kw_)
        op.is_dma = is_dma
        op.needed = False
        op.sem = None
        op.val = 0
        op.seg = self.seg
        op.cost, op.lat = self._cost(stream, name_, a_, kw_, is_dma)
        deps = []
        for b in reads:
            if b.w is not None:
                deps.append(b.w)
        for b in writes:
            if b.w is not None:
                deps.append(b.w)
            if not _NOWAR:
                deps.extend(b.rs)
        deps.extend(extra)
        seen = set()
        dd = []
        for d in deps:
            if id(d) in seen:
                continue
            seen.add(id(d))
            dd.append(d)
        if stream in self.CHAIN and self.last_on.get(stream) is not None and self.last_on[stream].seg == self.seg:
            lo = self.last_on[stream]
            if id(lo) not in seen:
                dd.append(lo)
        self.last_on[stream] = op
        op.adeps = dd
        for b in reads:
            b.rs.append(op)
        for b in writes:
            b.w = op
            b.rs = []
        op.idx = len(self.all_ops)
        self.all_ops.append(op)
        return op

    def op(self, stream, fn, reads=(), writes=()):
        return self._add(stream, fn, reads, writes)

    def dma(self, q, out, in_, reads=(), writes=(), **kw):
        i = self.dcnt[q]
        self.dcnt[q] += 1
        slot = i % self.NDMA
        prev = self.dring[q][slot]
        extra = (prev,) if prev is not None else ()
        op = self._add(q, lambda e: e.dma_start(out=out, in_=in_, **kw), reads, writes, is_dma=True, extra=extra)
        op.sem = self.dsem[q][slot]
        op.val = 16 * (i // self.NDMA + 1)
        op.needed = True
        op.dq_index = i
        self.dring[q][slot] = op
        return op

    def barrier(self):
        self.seg += 1

    def _schedule(self):
        import heapq
        streams = {k: [] for k in self.streams}
        ops = self.all_ops
        if not self.REORDER:
            self.est_ns = 0.0
            cur_seg = 0
            fence = []
            first = {k: False for k in self.streams}
            last_dma = {q: {} for q in self.dsem}
            for op in ops:
                if op.seg != cur_seg:
                    cur_seg = op.seg
                    fence = []
                    for k, lst in streams.items():
                        for o2 in reversed(lst):
                            if not o2.is_dma:
                                fence.append(o2)
                                break
                    for q in last_dma:
                        fence.extend(last_dma[q].values())
                    first = {k: True for k in self.streams}
                if first[op.stream]:
                    first[op.stream] = False
                    op.adeps = list(op.adeps) + [f for f in fence if f is not op]
                streams[op.stream].append(op)
                if op.is_dma:
                    last_dma[op.stream][op.dq_index % self.NDMA] = op
            return streams
        nseg = self.seg + 1
        by_seg = [[] for _ in range(nseg)]
        for op in ops:
            by_seg[op.seg].append(op)
        fin = {}
        free = {k: 0.0 for k in self.streams}
        last_dma = {q: {} for q in self.dsem}
        fence = []
        tnow = 0.0
        for s in range(nseg):
            seg_ops = by_seg[s]
            if not seg_ops:
                continue
            first_in_stream = {k: True for k in self.streams}
            nun = {}
            users = {}
            for op in seg_ops:
                c = 0
                for d in op.adeps:
                    if d.seg == s:
                        c += 1
                        users.setdefault(id(d), []).append(op)
                nun[id(op)] = c
            ready = {k: [] for k in self.streams}

            def est_ready(op):
                t = tnow
                for d in op.adeps:
                    if d.seg == s:
                        f = fin[id(d)]
                        if d.stream != op.stream or d.is_dma:
                            f += d.lat
                        elif op.stream in self.STRICT:
                            f += 200.0
                        t = max(t, f)
                return t
            for op in seg_ops:
                if nun[id(op)] == 0:
                    heapq.heappush(ready[op.stream], (est_ready(op), op.idx, op))
            nleft = len(seg_ops)
            while nleft:
                best = None
                for k in self.streams:
                    if not ready[k]:
                        continue
                    if self.REORDER:
                        cand = None
                        tmp = []
                        while ready[k] and ready[k][0][0] <= free[k]:
                            tmp.append(heapq.heappop(ready[k]))
                        if tmp:
                            cand = min(tmp, key=lambda x: x[1])
                            for x in tmp:
                                if x is not cand:
                                    heapq.heappush(ready[k], x)
                            st = free[k]
                        else:
                            cand = heapq.heappop(ready[k])
                            st = cand[0]
                    else:
                        cand = min(ready[k], key=lambda x: x[1])
                        ready[k].remove(cand)
                        heapq.heapify(ready[k])
                        st = max(cand[0], free[k])
                    if best is None or (st, cand[1]) < (best[0], best[1][1]):
                        if best is not None:
                            heapq.heappush(ready[best[2]], best[1])
                        best = (st, cand, k)
                    else:
                        heapq.heappush(ready[k], cand)
                st, cand, k = best
                op = cand[2]
                if first_in_stream[k]:
                    first_in_stream[k] = False
                    if fence:
                        op.adeps = list(op.adeps) + [f for f in fence if f is not op]
                free[k] = st + op.cost
                fin[id(op)] = st + op.cost
                streams[k].append(op)
                if op.is_dma:
                    last_dma[k][op.dq_index % self.NDMA] = op
                nleft -= 1
                for u in users.get(id(op), ()):
                    nun[id(u)] -= 1
                    if nun[id(u)] == 0:
                        heapq.heappush(ready[u.stream], (est_ready(u), u.idx, u))
            fence = []
            for k, lst in streams.items():
                for op in reversed(lst):
                    if not op.is_dma:
                        fence.append(op)
                        break
            for q in last_dma:
                fence.extend(last_dma[q].values())
            tnow = max(list(free.values()) + [fin[id(f)] + f.lat for f in fence])
            for k in free:
                free[k] = tnow
        self.est_ns = max(free.values())
        return streams

    def emit(self, final_ops=()):
        streams = self._schedule()
        self.streams = streams
        for k, lst in streams.items():
            for op in lst:
                dd = []
                for d in op.adeps:
                    if (not d.is_dma) and d.stream == k and k not in self.STRICT:
                        continue
                    dd.append(d)
                    d.needed = True
                op.deps = dd
        for k, lst in streams.items():
            cnt = 0
            for op in lst:
                if op.is_dma:
                    continue
                if op.needed:
                    cnt += 1
                    op.sem = self.sem[k]
                    op.val = cnt
        print('SCHED ops', {k: len(v) for k, v in streams.items()}, 'semmax',
              {k: max([o.val for o in v if not o.is_dma] + [0]) for k, v in streams.items()},
              'est_ms', round(self.est_ns / 1e6, 3), 'busy_ms', {k: round(sum(o.cost for o in v) / 1e6, 3) for k, v in streams.items()}, flush=True)
        with self.nc.Block() as block:
            def run(k):
                def body(e):
                    waited = {}

                    def wait(d):
                        key = id(d.sem)
                        if waited.get(key, 0) < d.val:
                            e.wait_ge(d.sem, d.val)
                            waited[key] = d.val
                    for op in streams[k]:
                        for d in op.deps:
                            wait(d)
                        ins = op.fn(e)
                        if op.is_dma:
                            ins.then_inc(op.sem, 16)
                        elif op.needed:
                            ins.then_inc(op.sem, 1)
                    if k == "sp":
                        for d in final_ops:
                            wait(d)
                return body
            block.tensor(run("pe"))
            block.scalar(run("act"))
            block.vector(run("dve"))
            block.gpsimd(run("pool"))
            block.sync(run("sp"))


class Ctx:
    pass


_UID = [0]


def _sb(c, es, name, shape, dt):
    _UID[0] += 1
    return es.enter_context(c.nc.sbuf_tensor(f"{name}_{_UID[0]}", list(shape), dt))


class PsumPool:
    def __init__(self, c, n=8):
        self.t = [c.es.enter_context(c.nc.psum_tensor(f"ps{i}", [128, 512], F32)) for i in range(n)]
        self.b = [Buf(f"ps{i}") for i in range(n)]
        self.i = 0
        self.n = n

    def get(self):
        i = self.i
        self.i = (self.i + 1) % self.n
        return self.t[i], self.b[i]


def load_bcast_row(c, q, dst_tile, dst_buf, dram_ap_row):
    n = dst_tile.shape[-1]
    return c.S.dma(q, dst_tile[:, :], dram_ap_row.to_broadcast([128, n]), writes=[dst_buf])


def emit_xT(c, xT, xT_buf, tslot, x_tile, x_buf, xbf, xbf_buf):
    S = c.S
    S.op("act", lambda e: e.copy(out=xbf[:, :], in_=x_tile[:, :]), reads=[x_buf], writes=[xbf_buf])
    pt, pb = c.ps.get()
    ptb = pt[:, :].bitcast(BF16)
    for k in range(8):
        S.op("pe", lambda e, k=k: e.transpose(out=ptb[:, k * 128:(k + 1) * 128], in_=xbf[:, k * 128:(k + 1) * 128],
                                               identity=c.ident_bf[:, :]),
             reads=[xbf_buf, c.ident_buf], writes=[pb])
    S.op("dve", lambda e: e.tensor_copy(out=xT[:, :, tslot * 128:(tslot + 1) * 128],
                                        in_=ptb.rearrange("p (k t) -> p k t", k=8)),
         reads=[pb], writes=[xT_buf])


def emit_ln_epilogue(c, es_bufs, ps_halves, x_old, x_old_buf, g_t, b_t, gb_buf, out_tile, out_buf):
    S = c.S
    v, v_buf, st, mv, sm_buf = es_bufs
    for hf in range(2):
        pt, pb = ps_halves[hf]
        S.op("dve", lambda e, hf=hf, pt=pt: e.scalar_tensor_tensor(
            out=v[:, hf * 512:(hf + 1) * 512], in0=x_old[:, hf * 512:(hf + 1) * 512], scalar=float(ALPHA),
            in1=pt[:, :], op0=ALU.mult, op1=ALU.add), reads=[x_old_buf, pb], writes=[v_buf])
    ln_core(c, v, v_buf, st, mv, sm_buf, g_t, b_t, gb_buf, out_tile, out_buf)


def ln_core(c, v, v_buf, st, mv, sm_buf, g_t, b_t, gb_buf, out_tile, out_buf):
    S = c.S
    for hf in range(2):
        S.op("dve", lambda e, hf=hf: e.bn_stats(out=st[:, hf * 6:(hf + 1) * 6], in_=v[:, hf * 512:(hf + 1) * 512]),
             reads=[v_buf], writes=[sm_buf])
    S.op("dve", lambda e: e.bn_aggr(out=mv[:, 0:2], in_=st[:, 0:12]), reads=[sm_buf], writes=[sm_buf])
    S.op("dve", lambda e: e.tensor_scalar_add(out=mv[:, 2:3], in0=mv[:, 1:2], scalar1=float(EPS)),
         reads=[sm_buf], writes=[sm_buf])
    S.op("act", lambda e: e.activation(out=mv[:, 2:3], in_=mv[:, 2:3], func=AF.Sqrt), reads=[sm_buf], writes=[sm_buf])
    S.op("dve", lambda e: e.reciprocal(out=mv[:, 2:3], in_=mv[:, 2:3]), reads=[sm_buf], writes=[sm_buf])
    S.op("dve", lambda e: e.scalar_tensor_tensor(out=mv[:, 3:4], in0=mv[:, 0:1], scalar=-1.0, in1=mv[:, 2:3],
                                                 op0=ALU.mult, op1=ALU.mult), reads=[sm_buf], writes=[sm_buf])
    S.op("act", lambda e: e.activation(out=v[:, :], in_=v[:, :], func=AF.Identity, bias=mv[:, 3:4], scale=mv[:, 2:3]),
         reads=[v_buf, sm_buf], writes=[v_buf])
    S.op("dve", lambda e: e.tensor_tensor(out=v[:, :], in0=v[:, :], in1=g_t[:, :], op=ALU.mult),
         reads=[v_buf, gb_buf], writes=[v_buf])
    S.op("dve", lambda e: e.tensor_tensor(out=out_tile[:, :], in0=v[:, :], in1=b_t[:, :], op=ALU.add),
         reads=[v_buf, gb_buf], writes=[out_buf])


def emit_ffn(c, layer, src, dst, NT):
    S, nc = c.S, c.nc
    T = min(1024, NT)
    ntile = T // 128
    nblk = T // 512
    outs = []
    with ExitStack() as es:
        xT = _sb(c, es, "f_xT", [128, 8, T], BF16)
        xT_buf = Buf("f_xT")
        hT = _sb(c, es, "f_hT", [128, 22, T], BF16)
        hT_bufs = [Buf(f"f_hT{j}") for j in range(22)]
        NWB = 2
        wg = [_sb(c, es, f"f_wg{i}", [128, 8, 512], BF16) for i in range(NWB)]
        wu = [_sb(c, es, f"f_wu{i}", [128, 8, 512], BF16) for i in range(NWB)]
        wg_b = [Buf(f"f_wg{i}") for i in range(NWB)]
        wu_b = [Buf(f"f_wu{i}") for i in range(NWB)]
        wd = _sb(c, es, "f_wd", [128, 22, 1024], BF16)
        jgroups = [(j0, min(4, 22 - j0)) for j0 in range(0, 22, 4)]
        wd_b = [Buf(f"f_wd{g}") for g in range(len(jgroups))]
        NXB = 2
        xin = [_sb(c, es, f"f_xin{i}", [128, 1024], F32) for i in range(NXB)]
        xin_b = [Buf(f"f_xin{i}") for i in range(NXB)]
        xbf = [_sb(c, es, f"f_xbf{i}", [128, 1024], BF16) for i in range(NXB)]
        xbf_b = [Buf(f"f_xbf{i}") for i in range(NXB)]
        sg = [_sb(c, es, f"f_sg{i}", [128, 512], F32) for i in range(2)]
        sg_b = [Buf(f"f_sg{i}") for i in range(2)]
        v = [_sb(c, es, f"f_v{i}", [128, 1024], F32) for i in range(2)]
        v_b = [Buf(f"f_v{i}") for i in range(2)]
        st = [_sb(c, es, f"f_st{i}", [128, 12], F32) for i in range(2)]
        mv = [_sb(c, es, f"f_mv{i}", [128, 4], F32) for i in range(2)]
        sm_b = [Buf(f"f_sm{i}") for i in range(2)]
        xo = [_sb(c, es, f"f_xo{i}", [128, 1024], F32) for i in range(2)]
        xo_b = [Buf(f"f_xo{i}") for i in range(2)]
        g_t = _sb(c, es, "f_g", [128, 1024], F32)
        b_t = _sb(c, es, "f_b", [128, 1024], F32)
        gb_buf = Buf("f_gb")
        load_bcast_row(c, "sp", g_t, gb_buf, c.dram[f"ln_ffn_g{layer}"])
        load_bcast_row(c, "sp", b_t, gb_buf, c.dram[f"ln_ffn_b{layer}"])
        Wg = c.dram[f"ffn_w_gate{layer}"]
        Wu = c.dram[f"ffn_w_up{layer}"]
        Wd = c.dram[f"ffn_w_down{layer}"]
        xcnt = 0
        wcnt = 0
        ecnt = 0
        first = True
        for g0 in range(0, NT, T):
            for t in range(ntile):
                i = xcnt % NXB
                xcnt += 1
                r0 = g0 + t * 128
                S.dma("sp", xin[i][:, :], src[r0:r0 + 128, :], writes=[xin_b[i]])
                emit_xT(c, xT, xT_buf, t, xin[i], xin_b[i], xbf[i], xbf_b[i])
            for gi, (j0, nj) in enumerate(jgroups):
                wi = wcnt % NWB
                wcnt += 1
                c0 = j0 * 128
                ncol = nj * 128
                S.dma("pool", wg[wi][:, :, 0:ncol], Wg[:, c0:c0 + ncol].rearrange("(k p) n -> p k n", p=128),
                      writes=[wg_b[wi]])
                S.dma("pool", wu[wi][:, :, 0:ncol], Wu[:, c0:c0 + ncol].rearrange("(k p) n -> p k n", p=128),
                      writes=[wu_b[wi]])
                if first:
                    S.dma("pool", wd[:, j0:j0 + nj, :], Wd[c0:c0 + ncol, :].rearrange("(j p) n -> p j n", p=128),
                          writes=[wd_b[gi]])
                for jj in range(nj):
                    j = j0 + jj
                    for nb in range(nblk):
                        pg, pgb = c.ps.get()
                        pu, pub = c.ps.get()
                        for k in range(8):
                            S.op("pe", lambda e, k=k, pg=pg, jj=jj, nb=nb, wi=wi: e.matmul(
                                pg[:, :], lhsT=wg[wi][:, k, jj * 128:(jj + 1) * 128], rhs=xT[:, k, nb * 512:(nb + 1) * 512],
                                start=(k == 0), stop=(k == 7)), reads=[wg_b[wi], xT_buf], writes=[pgb])
                        for k in range(8):
                            S.op("pe", lambda e, k=k, pu=pu, jj=jj, nb=nb, wi=wi: e.matmul(
                                pu[:, :], lhsT=wu[wi][:, k, jj * 128:(jj + 1) * 128], rhs=xT[:, k, nb * 512:(nb + 1) * 512],
                                start=(k == 0), stop=(k == 7)), reads=[wu_b[wi], xT_buf], writes=[pub])
                        si = (j * nblk + nb) % 2
                        S.op("act", lambda e, si=si, pg=pg: e.activation(out=sg[si][:, :], in_=pg[:, :], func=AF.Silu),
                             reads=[pgb], writes=[sg_b[si]])
                        S.op("dve", lambda e, si=si, pu=pu, j=j, nb=nb: e.tensor_tensor(
                            out=hT[:, j, nb * 512:(nb + 1) * 512], in0=sg[si][:, :], in1=pu[:, :], op=ALU.mult),
                            reads=[sg_b[si], pub], writes=[hT_bufs[j]])
            first = False
            for t in range(ntile):
                halves = [c.ps.get(), c.ps.get()]
                for j in range(22):
                    for hf in range(2):
                        ph, phb = halves[hf]
                        S.op("pe", lambda e, ph=ph, j=j, hf=hf, t=t: e.matmul(
                            ph[:, :], lhsT=hT[:, j, t * 128:(t + 1) * 128], rhs=wd[:, j, hf * 512:(hf + 1) * 512],
                            start=(j == 0), stop=(j == 21)), reads=[hT_bufs[j], wd_b[j // 4]], writes=[phb])
                i = ecnt % 2
                ecnt += 1
                r0 = g0 + t * 128
                S.dma("sp", xin[i][:, :], src[r0:r0 + 128, :], writes=[xin_b[i]])
                emit_ln_epilogue(c, (v[i], v_b[i], st[i], mv[i], sm_b[i]), halves, xin[i], xin_b[i],
                                 g_t, b_t, gb_buf, xo[i], xo_b[i])
                outs.append(S.dma("sp", dst[r0:r0 + 128, :], xo[i][:, :], reads=[xo_b[i]]))
    S.barrier()
    return outs


A_MASK_NEG = -30000.0
import os as _os
DBG_STOP = int(_os.environ.get("DBG_STOP", "0"))


class _Stop(Exception):
    pass


def _chk(n):
    if DBG_STOP == n:
        raise _Stop()


def ps_bf(pt):
    return pt[:, :].bitcast(BF16)


def emit_even(c, e, layer, src, dst, halo, NT):
    S, nc = c.S, c.nc
    GT = 512
    ngroups = NT // GT
    outs = []
    dr = c.dram
    with ExitStack() as es:
      try:
          sb = lambda name, shape, dt: _sb(c, es, "e_" + name, shape, dt)
          Win = sb("win", [128, 8, 1792], BF16); Win_b = Buf("win")
          Wout = sb("wout", [128, 8, 1024], BF16); Wout_b = Buf("wout")
          dg = sb("dg", [128, 4, 31, 128], BF16); dg_b = Buf("dg")
          wk = sb("wk", [31, 512], F32); wk_b = Buf("wk")
          wcol = sb("wcol", [128, 4, 32], F32); wcol_b = Buf("wcol")
          cvec = sb("cvec", [128, 16], F32); cvec_b = Buf("cvec")
          sinks = sb("sinks", [128, 8], F32); sinks_b = Buf("sinks")
          amask = sb("amask", [128, 256], F32); amask0 = sb("amask0", [128, 256], F32); am_b = Buf("amask")
          rope = sb("rope", [128, NT // 128 + 1, 16], F32); rope_b = Buf("rope")
          g_t = sb("g", [128, 1024], F32); b_t = sb("b", [128, 1024], F32); gb_buf = Buf("gb")
          xT = sb("xT", [128, 8, GT], BF16); xT_b = Buf("xT")
          xTh = sb("xTh", [128, 8, 128], BF16); xTh_b = Buf("xTh")
          hbuf = [sb(f"hbuf{i}", [128, 4, 32 + GT], BF16) for i in range(2)]
          hb_body = [Buf(f"hb{i}") for i in range(2)]
          hb_pre = [Buf(f"hp{i}") for i in range(2)]
          cv = sb("cv", [128, 4, GT], F32); cv_b = [Buf(f"cv{i}") for i in range(4)]
          sq = sb("sq", [128, GT], F32); sq_b = Buf("sq")
          tg = [sb(f"tg{i}", [128, GT], F32) for i in range(2)]; tg_b = [Buf(f"tg{i}") for i in range(2)]
          mean = sb("mean", [128, GT], F32); msq = sb("msq", [128, GT], F32); rstd = sb("rstd", [128, GT], F32)
          stat_b = Buf("stat")
          ta = [sb(f"ta{i}", [128, GT], F32) for i in range(2)]; ta_b = [Buf(f"ta{i}") for i in range(2)]
          yT = sb("yT", [128, 8, GT], BF16); yT_b = [Buf(f"yT{i}") for i in range(8)]
          xin = [sb(f"xin{i}", [128, 1024], F32) for i in range(2)]; xin_b = [Buf(f"xin{i}") for i in range(2)]
          xbf = [sb(f"xbf{i}", [128, 1024], BF16) for i in range(2)]; xbf_b = [Buf(f"xbf{i}") for i in range(2)]
          qb = sb("qb", [128, 8, 64], BF16); qb_b = Buf("qb")
          kb = sb("kb", [128, 2, 64], BF16); kb_b = Buf("kb")
          rt = [sb(f"rt{i}", [128, 8, 8], F32) for i in range(2)]; rt_b = [Buf(f"rt{i}") for i in range(2)]
          NR = 3
          vb = [sb(f"vb{i}", [128, 128], BF16) for i in range(NR)]; vb_b = [Buf(f"vb{i}") for i in range(NR)]
          kT = [sb(f"kT{i}", [128, 128], BF16) for i in range(NR)]; kT_b = [Buf(f"kT{i}") for i in range(NR)]
          qT = sb("qT", [128, 4, 128], BF16); qT_b = Buf("qT")
          sm = sb("sm", [128, 8, 256], F32); sm_b = [Buf(f"sm{i}") for i in range(4)]
          pb = sb("pb", [128, 8, 256], BF16); pb_b = Buf("pb")
          pT = sb("pT", [128, 16, 128], BF16); pT_b = [Buf(f"pT{i}") for i in range(2)]
          att = sb("att", [128, 40], F32); att_b = Buf("att")
          ob = sb("ob", [128, 8, 64], BF16); ob_b = Buf("ob")
          v = [sb(f"v{i}", [128, 1024], F32) for i in range(2)]; v_b = [Buf(f"v{i}") for i in range(2)]
          st = [sb(f"st{i}", [128, 12], F32) for i in range(2)]
          mv = [sb(f"mv{i}", [128, 4], F32) for i in range(2)]; smm_b = [Buf(f"smm{i}") for i in range(2)]
          xo = [sb(f"xo{i}", [128, 1024], F32) for i in range(2)]; xo_b = [Buf(f"xo{i}") for i in range(2)]

          S.dma("pool", Win[:, :, :], dr[f"ev_w_in{e}"].rearrange("(k p) n -> p k n", p=128), writes=[Win_b])
          S.dma("pool", Wout[:, :, :], dr[f"ev_w_out{e}"].rearrange("(k p) n -> p k n", p=128), writes=[Wout_b])
          S.dma("sp", wk[:, :], dr[f"ev_dw_w{e}"], writes=[wk_b])
          S.dma("sp", cvec[:, 0:4], dr[f"ev_dw_b{e}"].rearrange("o (c p) -> p (o c)", p=128), writes=[cvec_b], allow_slow_non_contiguous=True)
          S.dma("sp", cvec[:, 4:8], dr[f"ev_cn_g{e}"].rearrange("o (c p) -> p (o c)", p=128), writes=[cvec_b], allow_slow_non_contiguous=True)
          S.dma("sp", cvec[:, 8:12], dr[f"ev_cn_b{e}"].rearrange("o (c p) -> p (o c)", p=128), writes=[cvec_b], allow_slow_non_contiguous=True)
          load_bcast_row(c, "sp", sinks, sinks_b, dr[f"ev_sinks{e}"])
          S.dma("sp", amask[:, :], dr["amask"], writes=[am_b])
          S.dma("sp", amask0[:, :], dr["amask0"], writes=[am_b])
          S.dma("sp", rope[:, :, :], dr["rope_a"], writes=[rope_b])
          load_bcast_row(c, "sp", g_t, gb_buf, dr[f"ln_mix_g{layer}"])
          load_bcast_row(c, "sp", b_t, gb_buf, dr[f"ln_mix_b{layer}"])
          S.op("dve", lambda en: en.tensor_scalar_mul(out=cvec[:, 4:12], in0=cvec[:, 4:12], scalar1=0.5),
               reads=[cvec_b], writes=[cvec_b])
          for cc in range(4):
              pt, ptb_ = c.ps.get()
              S.op("pe", lambda en, cc=cc, pt=pt: en.transpose(out=pt[:, 0:31], in_=wk[0:31, cc * 128:(cc + 1) * 128],
                                                              identity=c.ident_f[0:31, 0:31]),
                   reads=[wk_b, c.identf_buf], writes=[ptb_])
              S.op("dve", lambda en, cc=cc, pt=pt: en.tensor_copy(out=wcol[:, cc, 0:31], in_=pt[:, 0:31]),
                   reads=[ptb_], writes=[wcol_b])
          for cc in range(4):
              for k in range(31):
                  S.op("dve", lambda en, cc=cc, k=k: en.tensor_scalar(
                      out=dg[:, cc, k, :], in0=c.ident_f[:, :], scalar1=wcol[:, cc, k:k + 1], scalar2=0.5,
                      op0=ALU.mult, op1=ALU.mult), reads=[wcol_b, c.identf_buf], writes=[dg_b])

          _chk(1)
          xcnt = [0]

          def load_xT(row0, dstT, dstT_b, slot, src_ap):
              i = xcnt[0] % 2
              xcnt[0] += 1
              S.dma("sp", xin[i][:, :], src_ap[row0:row0 + 128, :], writes=[xin_b[i]])
              emit_xT(c, dstT, dstT_b, slot, xin[i], xin_b[i], xbf[i], xbf_b[i])

          def glu_chunk(xTsrc, xTsrc_b, ncols, hb, hb_buf, col0):
              for cc in range(4):
                  pa, pab = c.ps.get()
                  pg, pgb = c.ps.get()
                  for k in range(8):
                      S.op("pe", lambda en, k=k, cc=cc, pa=pa: en.matmul(
                          pa[:, 0:ncols], lhsT=Win[:, k, 768 + cc * 128:768 + (cc + 1) * 128], rhs=xTsrc[:, k, 0:ncols],
                          start=(k == 0), stop=(k == 7)), reads=[Win_b, xTsrc_b], writes=[pab])
                  for k in range(8):
                      S.op("pe", lambda en, k=k, cc=cc, pg=pg: en.matmul(
                          pg[:, 0:ncols], lhsT=Win[:, k, 1280 + cc * 128:1280 + (cc + 1) * 128], rhs=xTsrc[:, k, 0:ncols],
                          start=(k == 0), stop=(k == 7)), reads=[Win_b, xTsrc_b], writes=[pgb])
                  ti = cc % 2
                  S.op("act", lambda en, ti=ti, pg=pg: en.activation(out=tg[ti][:, 0:ncols], in_=pg[:, 0:ncols],
                                                                      func=AF.Tanh, scale=0.5),
                       reads=[pgb], writes=[tg_b[ti]])
                  S.op("dve", lambda en, ti=ti, pa=pa, cc=cc: en.scalar_tensor_tensor(
                      out=hb[:, cc, col0:col0 + ncols], in0=tg[ti][:, 0:ncols], scalar=1.0, in1=pa[:, 0:ncols],
                      op0=ALU.add, op1=ALU.mult), reads=[tg_b[ti], pab], writes=[hb_buf])

          def kv_tile(xTsrc, xTsrc_b, col0, ring_i, tile_idx, with_q):
              pkv, pkvb = c.ps.get()
              for k in range(8):
                  S.op("pe", lambda en, k=k: en.matmul(pkv[:, 0:256], lhsT=xTsrc[:, k, col0:col0 + 128],
                                                       rhs=Win[:, k, 512:768], start=(k == 0), stop=(k == 7)),
                       reads=[Win_b, xTsrc_b], writes=[pkvb])
              if with_q:
                  pq, pqb = c.ps.get()
                  for k in range(8):
                      S.op("pe", lambda en, k=k: en.matmul(pq[:, :], lhsT=xTsrc[:, k, col0:col0 + 128],
                                                           rhs=Win[:, k, 0:512], start=(k == 0), stop=(k == 7)),
                           reads=[Win_b, xTsrc_b], writes=[pqb])

              def rope_apply(s3, psrc_b, dstt, dst_b, hs):
                  nh = len(hs)
                  H = int(np.prod(hs))
                  full = [128] + list(hs)
                  cs = rope[:, tile_idx, 0:8]
                  sn = rope[:, tile_idx, 8:16]
                  for _ in range(nh):
                      cs = cs.unsqueeze(1)
                      sn = sn.unsqueeze(1)
                  cs = cs.to_broadcast(full + [8])
                  sn = sn.to_broadcast(full + [8])
                  if nh == 1:
                      r0, r1 = rt[0][:, 0:H, :], rt[1][:, 0:H, :]
                  else:
                      r0 = rt[0][:, 0:H, :].rearrange("p (a b) d -> p a b d", a=hs[0])
                      r1 = rt[1][:, 0:H, :].rearrange("p (a b) d -> p a b d", a=hs[0])
                  sl = (slice(None),) * (1 + nh)
                  t1, t2 = s3[sl + (slice(0, 8),)], s3[sl + (slice(8, 16),)]
                  S.op("dve", lambda en: en.tensor_tensor(out=r0, in0=t1, in1=cs, op=ALU.mult),
                       reads=[psrc_b, rope_b], writes=[rt_b[0]])
                  S.op("dve", lambda en: en.tensor_tensor(out=r1, in0=t2, in1=sn, op=ALU.mult),
                       reads=[psrc_b, rope_b], writes=[rt_b[1]])
                  S.op("dve", lambda en: en.tensor_tensor(out=dstt[sl + (slice(0, 8),)], in0=r0, in1=r1, op=ALU.subtract),
                       reads=[rt_b[0], rt_b[1]], writes=[dst_b])
                  S.op("dve", lambda en: en.tensor_tensor(out=r0, in0=t2, in1=cs, op=ALU.mult),
                       reads=[psrc_b, rope_b], writes=[rt_b[0]])
                  S.op("dve", lambda en: en.tensor_tensor(out=r1, in0=t1, in1=sn, op=ALU.mult),
                       reads=[psrc_b, rope_b], writes=[rt_b[1]])
                  S.op("dve", lambda en: en.tensor_tensor(out=dstt[sl + (slice(8, 16),)], in0=r0, in1=r1, op=ALU.add),
                       reads=[rt_b[0], rt_b[1]], writes=[dst_b])
                  S.op("act", lambda en: en.copy(out=dstt[sl + (slice(16, 64),)], in_=s3[sl + (slice(16, 64),)]),
                       reads=[psrc_b], writes=[dst_b])

              rope_apply(pkv[:, 0:128].rearrange("p (h d) -> p h d", h=2), pkvb, kb[:, :, :], kb_b, [2])
              S.op("act", lambda en: en.copy(out=vb[ring_i][:, :], in_=pkv[:, 128:256]), reads=[pkvb], writes=[vb_b[ring_i]])
              pt, ptb_ = c.ps.get()
              ptb = ps_bf(pt)
              S.op("pe", lambda en: en.transpose(out=ptb[:, 0:128], in_=kb[:, :, :].rearrange("p h d -> p (h d)"),
                                                 identity=c.ident_bf[:, :]), reads=[kb_b, c.ident_buf], writes=[ptb_])
              S.op("act", lambda en: en.copy(out=kT[ring_i][:, :], in_=ptb[:, 0:128]), reads=[ptb_], writes=[kT_b[ring_i]])
              if with_q:
                  rope_apply(pq[:, :].rearrange("p (g j d) -> p g j d", g=2, j=4), pqb,
                             qb[:, :, :].rearrange("p (j g) d -> p g j d", g=2), qb_b, [2, 4])
                  pt2, pt2b_ = c.ps.get()
                  pt2b = ps_bf(pt2)
                  qflat = qb[:, :, :].rearrange("p h d -> p (h d)")
                  for j in range(4):
                      S.op("pe", lambda en, j=j: en.transpose(out=pt2b[:, j * 128:(j + 1) * 128],
                                                              in_=qflat[:, j * 128:(j + 1) * 128], identity=c.ident_bf[:, :]),
                           reads=[qb_b, c.ident_buf], writes=[pt2b_])
                  S.op("dve", lambda en: en.tensor_copy(out=qT[:, :, :], in_=pt2b[:, 0:512].rearrange("p (j t) -> p j t", j=4)),
                       reads=[pt2b_], writes=[qT_b])

          _chk(2)
          load_xT(0, xTh, xTh_b, 0, halo)
          glu_chunk(xTh, xTh_b, 128, hbuf[1], hb_body[1], 32 + GT - 128)
          kv_tile(xTh, xTh_b, 0, (NR - 1), 0, False)
          _chk(3)
          blk_global = 0
          for g in range(ngroups):
              hb = hbuf[g % 2]
              hprev = hbuf[(g + 1) % 2]
              for t in range(4):
                  load_xT(g * GT + t * 128, xT, xT_b, t, src)
              S.op("act", lambda en, hb=hb, hprev=hprev: en.copy(out=hb[:, :, 2:32], in_=hprev[:, :, GT + 2:GT + 32]),
                   reads=[hb_body[(g + 1) % 2]], writes=[hb_pre[g % 2]])
              glu_chunk(xT, xT_b, GT, hb, hb_body[g % 2], 32)
              _chk(4)
              for cc in range(4):
                  pc, pcb = c.ps.get()
                  for k in range(31):
                      S.op("pe", lambda en, cc=cc, k=k, pc=pc, hb=hb: en.matmul(
                          pc[:, :], lhsT=dg[:, cc, k, :], rhs=hb[:, cc, 2 + k:2 + k + GT], start=(k == 0), stop=(k == 30)),
                          reads=[dg_b, hb_body[g % 2], hb_pre[g % 2]], writes=[pcb])
                  S.op("act", lambda en, cc=cc, pc=pc: en.activation(out=cv[:, cc, :], in_=pc[:, :], func=AF.Identity,
                                                                      bias=cvec[:, cc:cc + 1], scale=1.0),
                       reads=[pcb, cvec_b], writes=[cv_b[cc]])
              _chk(5)
              p1, p1b = c.ps.get()
              p2, p2b = c.ps.get()
              for cc in range(4):
                  S.op("pe", lambda en, cc=cc: en.matmul(p1[:, :], lhsT=c.ones_f[:, :], rhs=cv[:, cc, :],
                                                         start=(cc == 0), stop=(cc == 3)),
                       reads=[cv_b[cc], c.ones_buf], writes=[p1b])
              for cc in range(4):
                  S.op("act", lambda en, cc=cc: en.activation(out=sq[:, :], in_=cv[:, cc, :], func=AF.Square),
                       reads=[cv_b[cc]], writes=[sq_b])
                  S.op("pe", lambda en, cc=cc: en.matmul(p2[:, :], lhsT=c.ones_f[:, :], rhs=sq[:, :],
                                                         start=(cc == 0), stop=(cc == 3)),
                       reads=[sq_b, c.ones_buf], writes=[p2b])
              S.op("dve", lambda en: en.tensor_scalar_mul(out=mean[:, :], in0=p1[:, :], scalar1=1.0 / 512.0),
                   reads=[p1b], writes=[stat_b])
              S.op("dve", lambda en: en.tensor_tensor(out=msq[:, :], in0=mean[:, :], in1=mean[:, :], op=ALU.mult),
                   reads=[stat_b], writes=[stat_b])
              S.op("dve", lambda en: en.scalar_tensor_tensor(out=rstd[:, :], in0=p2[:, :], scalar=1.0 / 512.0, in1=msq[:, :],
                                                             op0=ALU.mult, op1=ALU.subtract), reads=[p2b, stat_b], writes=[stat_b])
              S.op("dve", lambda en: en.tensor_scalar_add(out=rstd[:, :], in0=rstd[:, :], scalar1=float(EPS)),
                   reads=[stat_b], writes=[stat_b])
              S.op("act", lambda en: en.activation(out=rstd[:, :], in_=rstd[:, :], func=AF.Sqrt), reads=[stat_b], writes=[stat_b])
              S.op("dve", lambda en: en.reciprocal(out=rstd[:, :], in_=rstd[:, :]), reads=[stat_b], writes=[stat_b])
              for cc in range(4):
                  i = cc % 2
                  S.op("dve", lambda en, cc=cc, i=i: en.tensor_tensor(out=ta[i][:, :], in0=cv[:, cc, :], in1=mean[:, :],
                                                                     op=ALU.subtract), reads=[cv_b[cc], stat_b], writes=[ta_b[i]])
                  S.op("dve", lambda en, i=i: en.tensor_tensor(out=ta[i][:, :], in0=ta[i][:, :], in1=rstd[:, :], op=ALU.mult),
                       reads=[ta_b[i], stat_b], writes=[ta_b[i]])
                  S.op("act", lambda en, cc=cc, i=i: en.activation(out=ta[i][:, :], in_=ta[i][:, :], func=AF.Identity,
                                                                    bias=cvec[:, 8 + cc:9 + cc], scale=cvec[:, 4 + cc:5 + cc]),
                       reads=[ta_b[i], cvec_b], writes=[ta_b[i]])
                  S.op("act", lambda en, i=i: en.activation(out=tg[i][:, :], in_=ta[i][:, :], func=AF.Tanh),
                       reads=[ta_b[i]], writes=[tg_b[i]])
                  S.op("dve", lambda en, cc=cc, i=i: en.scalar_tensor_tensor(
                      out=yT[:, 4 + cc, :], in0=tg[i][:, :], scalar=1.0, in1=ta[i][:, :], op0=ALU.add, op1=ALU.mult),
                      reads=[tg_b[i], ta_b[i]], writes=[yT_b[4 + cc]])
              _chk(6)
              for t in range(4):
                  bi = blk_global
                  blk_global += 1
                  cur = bi % NR
                  prv = (bi - 1) % NR
                  kv_tile(xT, xT_b, t * 128, cur, bi + 1, True)
                  msk = amask0 if bi == 0 else amask
                  for bank in range(4):
                      pscr, pscb = c.ps.get()
                      for hh in range(2):
                          h = bank * 2 + hh
                          gk = h // 4
                          lq = qT[gk * 64:gk * 64 + 64, h % 4, :]
                          S.op("pe", lambda en, pscr=pscr, hh=hh, lq=lq, gk=gk, prv=prv: en.matmul(
                              pscr[:, hh * 256:hh * 256 + 128], lhsT=lq, rhs=kT[prv][gk * 64:gk * 64 + 64, :],
                              start=True, stop=True), reads=[qT_b, kT_b[prv]], writes=[pscb])
                          S.op("pe", lambda en, pscr=pscr, hh=hh, lq=lq, gk=gk, cur=cur: en.matmul(
                              pscr[:, hh * 256 + 128:hh * 256 + 256], lhsT=lq, rhs=kT[cur][gk * 64:gk * 64 + 64, :],
                              start=True, stop=True), reads=[qT_b, kT_b[cur]], writes=[pscb])
                      S.op("dve", lambda en, bank=bank, pscr=pscr, msk=msk: en.tensor_tensor(
                          out=sm[:, bank * 2:bank * 2 + 2, :], in0=pscr[:, :].rearrange("p (h k) -> p h k", h=2),
                          in1=msk[:, :].unsqueeze(1).to_broadcast([128, 2, 256]), op=ALU.add),
                          reads=[pscb, am_b], writes=[sm_b[bank]])
                  S.op("dve", lambda en: en.tensor_reduce(out=att[:, 0:8], in_=sm[:, :, :], axis=mybir.AxisListType.X, op=ALU.max),
                       reads=sm_b, writes=[att_b])
                  S.op("dve", lambda en: en.scalar_tensor_tensor(out=att[:, 0:8], in0=att[:, 0:8], scalar=0.125, in1=sinks[:, :],
                                                                 op0=ALU.mult, op1=ALU.max), reads=[att_b, sinks_b], writes=[att_b])
                  S.op("dve", lambda en: en.tensor_scalar_mul(out=att[:, 8:16], in0=att[:, 0:8], scalar1=-1.0),
                       reads=[att_b], writes=[att_b])
                  for h in range(8):
                      S.op("act", lambda en, h=h: en.activation(out=pb[:, h, :], in_=sm[:, h, :], func=AF.Exp,
                                                                bias=att[:, 8 + h:9 + h], scale=0.125, accum_out=att[:, 16 + h:17 + h]),
                           reads=[sm_b[h // 2], att_b], writes=[pb_b, att_b])
                  S.op("dve", lambda en: en.tensor_tensor(out=att[:, 24:32], in0=sinks[:, :], in1=att[:, 0:8], op=ALU.subtract),
                       reads=[att_b, sinks_b], writes=[att_b])
                  S.op("act", lambda en: en.activation(out=att[:, 24:32], in_=att[:, 24:32], func=AF.Exp), reads=[att_b], writes=[att_b])
                  S.op("dve", lambda en: en.tensor_tensor(out=att[:, 32:40], in0=att[:, 16:24], in1=att[:, 24:32], op=ALU.add),
                       reads=[att_b], writes=[att_b])
                  S.op("dve", lambda en: en.reciprocal(out=att[:, 32:40], in_=att[:, 32:40]), reads=[att_b], writes=[att_b])
                  for half2 in range(2):
                      ptt, pttb_ = c.ps.get()
                      pttb = ps_bf(ptt)
                      for j in range(8):
                          idx = half2 * 8 + j
                          h, hf = idx // 2, idx % 2
                          S.op("pe", lambda en, j=j, h=h, hf=hf, pttb=pttb: en.transpose(
                              out=pttb[:, j * 128:(j + 1) * 128], in_=pb[:, h, hf * 128:(hf + 1) * 128], identity=c.ident_bf[:, :]),
                              reads=[pb_b, c.ident_buf], writes=[pttb_])
                      eng = "act" if half2 == 0 else "dve"
                      if eng == "act":
                          S.op("act", lambda en, half2=half2, pttb=pttb: en.copy(
                              out=pT[:, half2 * 8:half2 * 8 + 8, :], in_=pttb[:, :].rearrange("p (j t) -> p j t", j=8)),
                              reads=[pttb_], writes=[pT_b[half2]])
                      else:
                          S.op("dve", lambda en, half2=half2, pttb=pttb: en.tensor_copy(
                              out=pT[:, half2 * 8:half2 * 8 + 8, :], in_=pttb[:, :].rearrange("p (j t) -> p j t", j=8)),
                              reads=[pttb_], writes=[pT_b[half2]])
                  po, pob = c.ps.get()
                  for h in range(8):
                      gk = h // 4
                      S.op("pe", lambda en, h=h, gk=gk, prv=prv: en.matmul(po[:, h * 64:(h + 1) * 64], lhsT=pT[:, h * 2, :],
                                                                             rhs=vb[prv][:, gk * 64:(gk + 1) * 64], start=True, stop=False),
                           reads=[pT_b[h // 4], vb_b[prv]], writes=[pob])
                      S.op("pe", lambda en, h=h, gk=gk, cur=cur: en.matmul(po[:, h * 64:(h + 1) * 64], lhsT=pT[:, h * 2 + 1, :],
                                                                             rhs=vb[cur][:, gk * 64:(gk + 1) * 64], start=False, stop=True),
                           reads=[pT_b[h // 4], vb_b[cur]], writes=[pob])
                  S.op("dve", lambda en: en.tensor_tensor(out=ob[:, :, :], in0=po[:, :].rearrange("p (h d) -> p h d", h=8),
                                                          in1=att[:, 32:40].unsqueeze(2).to_broadcast([128, 8, 64]), op=ALU.mult),
                       reads=[pob, att_b], writes=[ob_b])
                  pt3, pt3b_ = c.ps.get()
                  pt3b = ps_bf(pt3)
                  oflat = ob[:, :, :].rearrange("p h d -> p (h d)")
                  for j in range(4):
                      S.op("pe", lambda en, j=j: en.transpose(out=pt3b[:, j * 128:(j + 1) * 128], in_=oflat[:, j * 128:(j + 1) * 128],
                                                              identity=c.ident_bf[:, :]), reads=[ob_b, c.ident_buf], writes=[pt3b_])
                  S.op("act", lambda en, t=t: en.copy(out=yT[:, 0:4, t * 128:(t + 1) * 128],
                                                      in_=pt3b[:, 0:512].rearrange("p (j t) -> p j t", j=4)),
                       reads=[pt3b_], writes=yT_b[0:4])
              _chk(7)
              for t in range(4):
                  halves = [c.ps.get(), c.ps.get()]
                  for kc in range(8):
                      for hf in range(2):
                          ph, phb = halves[hf]
                          S.op("pe", lambda en, ph=ph, kc=kc, hf=hf, t=t: en.matmul(
                              ph[:, :], lhsT=yT[:, kc, t * 128:(t + 1) * 128], rhs=Wout[:, kc, hf * 512:(hf + 1) * 512],
                              start=(kc == 0), stop=(kc == 7)), reads=[yT_b[kc], Wout_b], writes=[phb])
                  i = xcnt[0] % 2
                  xcnt[0] += 1
                  r0 = g * GT + t * 128
                  S.dma("sp", xin[i][:, :], src[r0:r0 + 128, :], writes=[xin_b[i]])
                  emit_ln_epilogue(c, (v[i], v_b[i], st[i], mv[i], smm_b[i]), halves, xin[i], xin_b[i],
                                   g_t, b_t, gb_buf, xo[i], xo_b[i])
                  outs.append(S.dma("sp", dst[r0:r0 + 128, :], xo[i][:, :], reads=[xo_b[i]]))
      except _Stop:
        pass
    S.barrier()
    return outs


OZ, OXBC, ODT, ORQ, ORK, ORV, ORG = 0, 1024, 2560, 2576, 3088, 3600, 4624
ST_W = 2064


def emit_odd(c, o, layer, src, dst, halo, NT):
    S, nc = c.S, c.nc
    GT = 512
    ngroups = NT // GT
    nchunks = NT // 128
    dr = c.dram
    outs = []
    Win_d = dr[f"od_w_in{o}"]
    with ExitStack() as es0:
        sb0 = lambda name, shape, dt: _sb(c, es0, "o_" + name, shape, dt)
        Sst = sb0("Sst", [128, 1024], F32); Sst_b = Buf("Sst")
        Rst = sb0("Rst", [128, 1024], F32); Rst_b = Buf("Rst")
        Atot = sb0("Atot", [128, 16], F32); Atot_b = Buf("Atot")
        otab = sb0("otab", [128, 32], F32); otab_b = Buf("otab")
        ptab = sb0("ptab", [128, 64], F32); ptab_b = Buf("ptab")
        wk5 = sb0("wk5", [5, 1536], F32); wk5_b = Buf("wk5")
        cw = sb0("cw", [128, 12, 5], F32); cw_b = Buf("cw")
        triu = sb0("triu", [128, 128], F32)
        trisl = sb0("trisl", [128, 128], F32)
        m01 = sb0("m01", [128, 128], F32)
        tri_b = Buf("tri")
        xin = [sb0(f"xin{i}", [128, 1024], F32) for i in range(2)]; xin_b = [Buf(f"oxin{i}") for i in range(2)]
        xbf = [sb0(f"xbf{i}", [128, 1024], BF16) for i in range(2)]; xbf_b = [Buf(f"oxbf{i}") for i in range(2)]
        xT_r = [(sb0(f"xT{i}", [128, 8, GT], BF16), Buf(f"oxT{i}")) for i in range(2)]
        xT, xT_b = xT_r[0]
        xTh = sb0("xTh", [128, 8, 128], BF16); xTh_b = Buf("oxTh")
        smg_r = [(sb0(f"smg{i}", [128, 8, 16], F32), Buf(f"smg{i}")) for i in range(2)]
        smc_r = [(sb0(f"smc{i}", [128, 4, 16], F32), Buf(f"smc{i}")) for i in range(3)]
        smg, smg_b = smg_r[0]
        smc, smc_b = smc_r[0]
        S.dma("sp", otab[:, :], dr["odd_tab"], writes=[otab_b])
        load_bcast_row(c, "sp", ptab[:, 0:16], ptab_b, dr[f"od_a_log{o}"])
        load_bcast_row(c, "sp", ptab[:, 16:32], ptab_b, dr[f"od_dt_bias{o}"])
        load_bcast_row(c, "sp", ptab[:, 32:48], ptab_b, dr[f"od_d_skip{o}"])
        S.dma("sp", wk5[0:4, :], dr[f"od_conv_w{o}"], writes=[wk5_b])
        S.dma("sp", wk5[4:5, :], dr[f"od_conv_b{o}"], writes=[wk5_b])
        S.dma("sp", triu[:, :], dr["triu"], writes=[tri_b])
        S.dma("sp", trisl[:, :], dr["trisl"], writes=[tri_b])
        S.dma("sp", m01[:, :], dr["triu"], writes=[tri_b])
        S.op("act", lambda en: en.activation(out=ptab[:, 0:16], in_=ptab[:, 0:16], func=AF.Exp), reads=[ptab_b], writes=[ptab_b])
        S.op("dve", lambda en: en.tensor_scalar_mul(out=ptab[:, 0:16], in0=ptab[:, 0:16], scalar1=-1.0), reads=[ptab_b], writes=[ptab_b])
        for cc in range(12):
            pt, ptb_ = c.ps.get()
            S.op("pe", lambda en, cc=cc, pt=pt: en.transpose(out=pt[:, 0:5], in_=wk5[0:5, cc * 128:(cc + 1) * 128],
                                                            identity=c.ident_f[0:5, 0:5]), reads=[wk5_b, c.identf_buf], writes=[ptb_])
            S.op("dve", lambda en, cc=cc, pt=pt: en.tensor_scalar_mul(out=cw[:, cc, :], in0=pt[:, 0:5], scalar1=0.5),
                 reads=[ptb_], writes=[cw_b])

        xcnt = [0]

        def load_xT(row0, dstT, dstT_b, slot, src_ap):
            i = xcnt[0] % 2
            xcnt[0] += 1
            S.dma("sp", xin[i][:, :], src_ap[row0:row0 + 128, :], writes=[xin_b[i]])
            emit_xT(c, dstT, dstT_b, slot, xin[i], xin_b[i], xbf[i], xbf_b[i])

        def wload(es, name, col0, ncol):
            t = _sb(c, es, "o_w" + name, [128, 8, ncol], BF16)
            b = Buf("w" + name)
            step = 1024
            for s0 in range(0, ncol, step):
                n = min(step, ncol - s0)
                S.dma("pool", t[:, :, s0:s0 + n], Win_d[:, col0 + s0:col0 + s0 + n].rearrange("(k p) n -> p k n", p=128),
                      writes=[b])
            return t, b

        def rot(d, names, i):
            for nm in names:
                r = d[nm + "_r"]
                d[nm], d[nm + "_b"] = r[i % len(r)]

        def ssd_setup(es):
            d = {}
            sb = lambda name, shape, dt: _sb(c, es, "s_" + name, shape, dt)
            d["Wx"], d["Wx_b"] = wload(es, "xbc", OXBC, 1536 + 16)
            d["cin"] = [sb(f"cin{i}", [128, 3 + GT], F32) for i in range(2)]; d["cin_b"] = [Buf(f"cin{i}") for i in range(2)]
            d["acc"] = [sb(f"acc{i}", [128, GT], F32) for i in range(2)]; d["acc_b"] = [Buf(f"acc{i}") for i in range(2)]
            d["tg"] = [sb(f"tg{i}", [128, GT], F32) for i in range(2)]; d["tg_b"] = [Buf(f"stg{i}") for i in range(2)]
            d["hist"] = sb("hist", [128, 12, 3], F32); d["hist_b"] = [Buf(f"hist{i}") for i in range(12)]
            d["xbcT_r"] = [(sb(f"xbcT{j}", [128, 12, GT], BF16), [Buf(f"xbcT{j}_{i}") for i in range(12)]) for j in range(2)]
            rot(d, ("xbcT",), 0)
            d["Btok_r"] = [(sb(f"Btok{i}", [128, 256], BF16), Buf(f"Btok{i}")) for i in range(2)]
            d["xdte_r"] = [(sb(f"xdte{i}", [128, 1024], BF16), Buf(f"xdte{i}")) for i in range(2)]
            rot(d, ("Btok", "xdte"), 0)
            return d

        def ssd_features(d, xTsrc, xTsrc_b, ncols, halo_mode):
            Wx, Wx_b = d["Wx"], d["Wx_b"]
            for cc in range(12):
                pp, ppb = c.ps.get()
                for k in range(8):
                    S.op("pe", lambda en, k=k, cc=cc, pp=pp: en.matmul(pp[:, 0:ncols], lhsT=Wx[:, k, cc * 128:(cc + 1) * 128],
                                                                       rhs=xTsrc[:, k, 0:ncols], start=(k == 0), stop=(k == 7)),
                         reads=[Wx_b, xTsrc_b], writes=[ppb])
                if halo_mode:
                    S.op("act", lambda en, cc=cc, pp=pp: en.copy(out=d["hist"][:, cc, :], in_=pp[:, ncols - 3:ncols]),
                         reads=[ppb], writes=[d["hist_b"][cc]])
                    continue
                i = cc % 2
                cin, cin_b = d["cin"][i], d["cin_b"][i]
                acc, acc_b = d["acc"][i], d["acc_b"][i]
                tgx, tgx_b = d["tg"][i], d["tg_b"][i]
                S.op("act", lambda en, cin=cin, pp=pp: en.copy(out=cin[:, 3:3 + ncols], in_=pp[:, 0:ncols]), reads=[ppb], writes=[cin_b])
                S.op("act", lambda en, cin=cin, cc=cc: en.copy(out=cin[:, 0:3], in_=d["hist"][:, cc, :]),
                     reads=[d["hist_b"][cc]], writes=[cin_b])
                S.op("act", lambda en, cin=cin, cc=cc: en.copy(out=d["hist"][:, cc, :], in_=cin[:, ncols:ncols + 3]),
                     reads=[cin_b], writes=[d["hist_b"][cc]])
                S.op("dve", lambda en, cin=cin, acc=acc, cc=cc: en.tensor_scalar(
                    out=acc[:, 0:ncols], in0=cin[:, 0:ncols], scalar1=cw[:, cc, 0:1], scalar2=cw[:, cc, 4:5],
                    op0=ALU.mult, op1=ALU.add), reads=[cin_b, cw_b], writes=[acc_b])
                for k in range(1, 4):
                    S.op("dve", lambda en, cin=cin, acc=acc, cc=cc, k=k: en.scalar_tensor_tensor(
                        out=acc[:, 0:ncols], in0=cin[:, k:k + ncols], scalar=cw[:, cc, k:k + 1], in1=acc[:, 0:ncols],
                        op0=ALU.mult, op1=ALU.add), reads=[cin_b, cw_b, acc_b], writes=[acc_b])
                S.op("act", lambda en, acc=acc, tgx=tgx: en.activation(out=tgx[:, 0:ncols], in_=acc[:, 0:ncols], func=AF.Tanh),
                     reads=[acc_b], writes=[tgx_b])
                S.op("dve", lambda en, acc=acc, tgx=tgx, cc=cc: en.scalar_tensor_tensor(
                    out=d["xbcT"][:, cc, 0:ncols], in0=tgx[:, 0:ncols], scalar=1.0, in1=acc[:, 0:ncols],
                    op0=ALU.add, op1=ALU.mult), reads=[tgx_b, acc_b], writes=[d["xbcT_b"][cc]])

        def ssd_dt_group(d):
            Wx, Wx_b = d["Wx"], d["Wx_b"]
            pd, pdb = c.ps.get()
            for t in range(4):
                for k in range(8):
                    S.op("pe", lambda en, k=k, t=t: en.matmul(pd[:, t * 16:(t + 1) * 16], lhsT=xT[:, k, t * 128:(t + 1) * 128],
                                                              rhs=Wx[:, k, 1536:1552], start=(k == 0), stop=(k == 7)),
                         reads=[Wx_b, xT_b], writes=[pdb])
            xr = smg[:, 0:4, :]
            S.op("dve", lambda en: en.tensor_tensor(out=xr, in0=pd[:, 0:64].rearrange("p (t h) -> p t h", t=4),
                                                    in1=ptab[:, 16:32].unsqueeze(1).to_broadcast([128, 4, 16]), op=ALU.add),
                 reads=[pdb, ptab_b], writes=[smg_b])
            ab = smg[:, 4:8, :]
            S.op("act", lambda en: en.activation(out=ab, in_=xr, func=AF.Abs), reads=[smg_b], writes=[smg_b])
            S.op("act", lambda en: en.activation(out=ab, in_=ab, func=AF.Exp, scale=-1.0), reads=[smg_b], writes=[smg_b])
            S.op("act", lambda en: en.activation(out=ab, in_=ab, func=AF.Ln, bias=1.0, scale=1.0), reads=[smg_b], writes=[smg_b])
            S.op("dve", lambda en: en.tensor_scalar_max(out=xr, in0=xr, scalar1=0.0), reads=[smg_b], writes=[smg_b])
            S.op("dve", lambda en: en.tensor_tensor(out=xr, in0=xr, in1=ab, op=ALU.add), reads=[smg_b], writes=[smg_b])
            S.op("dve", lambda en: en.tensor_tensor(out=ab, in0=xr, in1=ptab[:, 0:16].unsqueeze(1).to_broadcast([128, 4, 16]),
                                                    op=ALU.mult), reads=[smg_b, ptab_b], writes=[smg_b])

        def ssd_chunk_scalars(t):
            da = smg[:, 4 + t, :]
            pa, pab = c.ps.get()
            S.op("pe", lambda en: en.matmul(pa[:, 0:16], lhsT=triu[:, :], rhs=da, start=True, stop=True),
                 reads=[tri_b, smg_b], writes=[pab])
            S.op("pe", lambda en: en.matmul(pa[:, 16:32], lhsT=c.ones_f[:, :], rhs=da, start=True, stop=True),
                 reads=[c.ones_buf, smg_b], writes=[pab])
            S.op("act", lambda en: en.copy(out=smc[:, 0, :], in_=pa[:, 0:16]), reads=[pab], writes=[smc_b])
            S.op("act", lambda en: en.activation(out=smc[:, 1, :], in_=pa[:, 0:16], func=AF.Exp), reads=[pab], writes=[smc_b])
            S.op("dve", lambda en: en.tensor_tensor(out=smc[:, 2, :], in0=pa[:, 16:32], in1=smc[:, 0, :], op=ALU.subtract),
                 reads=[pab, smc_b], writes=[smc_b])
            S.op("act", lambda en: en.activation(out=smc[:, 2, :], in_=smc[:, 2, :], func=AF.Exp), reads=[smc_b], writes=[smc_b])
            S.op("dve", lambda en: en.tensor_tensor(out=smc[:, 2, :], in0=smc[:, 2, :], in1=smg[:, t, :], op=ALU.mult),
                 reads=[smc_b, smg_b], writes=[smc_b])
            S.op("act", lambda en: en.activation(out=smc[:, 3, :], in_=pa[:, 16:32], func=AF.Exp), reads=[pab], writes=[smc_b])
            S.op("dve", lambda en: en.tensor_tensor(out=Atot[:, :], in0=Atot[:, :], in1=pa[:, 16:32], op=ALU.add),
                 reads=[pab, Atot_b], writes=[Atot_b])

        def ssd_tok_and_state(d, t, xs_extra=None):
            xbcT, xbcT_b = d["xbcT"], d["xbcT_b"]
            px, pxb = c.ps.get()
            pxv = ps_bf(px)
            for j in range(8):
                S.op("pe", lambda en, j=j: en.transpose(out=pxv[:, j * 128:(j + 1) * 128], in_=xbcT[:, j, t * 128:(t + 1) * 128],
                                                        identity=c.ident_bf[:, :]), reads=[xbcT_b[j], c.ident_buf], writes=[pxb])
            pB, pBb = c.ps.get()
            pBv = ps_bf(pB)
            for j in range(2):
                S.op("pe", lambda en, j=j: en.transpose(out=pBv[:, j * 128:(j + 1) * 128], in_=xbcT[:, 8 + j, t * 128:(t + 1) * 128],
                                                        identity=c.ident_bf[:, :]), reads=[xbcT_b[8 + j], c.ident_buf], writes=[pBb])
            S.op("act", lambda en: en.copy(out=d["Btok"][:, :], in_=pBv[:, 0:256]), reads=[pBb], writes=[d["Btok_b"]])
            xs3 = pxv[:, 0:1024].rearrange("p (h q) -> p h q", h=16)
            S.op("dve", lambda en: en.tensor_tensor(out=d["xdte"][:, :].rearrange("p (h q) -> p h q", h=16), in0=xs3,
                                                    in1=smc[:, 2, :].unsqueeze(2).to_broadcast([128, 16, 64]), op=ALU.mult),
                 reads=[pxb, smc_b], writes=[d["xdte_b"]])
            if xs_extra is not None:
                xs_extra(xs3, pxb)
            pst = [c.ps.get(), c.ps.get()]
            for g in range(2):
                S.op("pe", lambda en, g=g: en.matmul(pst[g][0][:, :], lhsT=d["Btok"][:, g * 128:(g + 1) * 128],
                                                     rhs=d["xdte"][:, g * 512:(g + 1) * 512], start=True, stop=True),
                     reads=[d["Btok_b"], d["xdte_b"]], writes=[pst[g][1]])
            return pst

        def ssd_state_update(d, pst):
            S.op("dve", lambda en: en.tensor_tensor(out=Sst[:, :].rearrange("p (h q) -> p h q", h=16),
                                                    in0=Sst[:, :].rearrange("p (h q) -> p h q", h=16),
                                                    in1=smc[:, 3, :].unsqueeze(2).to_broadcast([128, 16, 64]), op=ALU.mult),
                 reads=[Sst_b, smc_b], writes=[Sst_b])
            for g in range(2):
                S.op("dve", lambda en, g=g: en.tensor_tensor(out=Sst[:, g * 512:(g + 1) * 512], in0=Sst[:, g * 512:(g + 1) * 512],
                                                             in1=pst[g][0][:, :], op=ALU.add), reads=[Sst_b, pst[g][1]], writes=[Sst_b])

        def ret_setup(es, with_q):
            d = {}
            sb = lambda name, shape, dt: _sb(c, es, "r_" + name, shape, dt)
            if with_q:
                t_ = _sb(c, es, "o_wretqg", [128, 8, 1536], BF16)
                b_ = Buf("wretqg")
                S.dma("pool", t_[:, :, 0:512], Win_d[:, ORQ:ORQ + 512].rearrange("(k p) n -> p k n", p=128), writes=[b_])
                S.dma("pool", t_[:, :, 512:1536], Win_d[:, ORG:ORG + 1024].rearrange("(k p) n -> p k n", p=128), writes=[b_])
                d["Wr"], d["Wr_b"] = t_, b_
                d["off"] = {"q": 0, "g": 512}
            else:
                d["Wr"], d["Wr_b"] = wload(es, "ret", ORK, 1536)
                d["off"] = {"k": 0, "v": 512}
            d["rope"] = [sb(f"rope{i}", [128, 128], F32) for i in range(2)]; d["rope_b"] = [Buf(f"rrope{i}") for i in range(2)]
            d["rr_r"] = [([sb(f"rr{j}_{i}", [128, 4, 64], F32) for i in range(2)], [Buf(f"rr{j}_{i}") for i in range(2)]) for j in range(2)]
            d["kr_r"] = [(sb(f"kr{i}", [128, 4, 128], F32), Buf(f"kr{i}")) for i in range(2)]
            d["kp_r"] = [(sb(f"kp{i}", [128, 512], BF16), Buf(f"kp{i}")) for i in range(2)]
            d["vb_r"] = [(sb(f"vb{i}", [128, 1024], BF16), Buf(f"rvb{i}")) for i in range(2)]
            rot(d, ("rr", "kr", "kp", "vb"), 0)
            return d

        def ret_rope(d, psrc, psrc_b, ri, scale_cols, dstt, dst_b):
            s3 = psrc.rearrange("p (h e) -> p h e", h=4)
            rope_t, rope_tb = d["rope"][ri], d["rope_b"][ri]
            cs = rope_t[:, 0:64].unsqueeze(1).to_broadcast([128, 4, 64])
            sn = rope_t[:, 64:128].unsqueeze(1).to_broadcast([128, 4, 64])
            t1, t2 = s3[:, :, 0:64], s3[:, :, 64:128]
            r0, r1 = d["rr"][0], d["rr"][1]
            kr = d["kr"]
            S.op("dve", lambda en: en.tensor_tensor(out=r0[:, :, :], in0=t1, in1=cs, op=ALU.mult), reads=[psrc_b, rope_tb], writes=[d["rr_b"][0]])
            S.op("dve", lambda en: en.tensor_tensor(out=r1[:, :, :], in0=t2, in1=sn, op=ALU.mult), reads=[psrc_b, rope_tb], writes=[d["rr_b"][1]])
            S.op("dve", lambda en: en.tensor_tensor(out=kr[:, :, 0:64], in0=r0[:, :, :], in1=r1[:, :, :], op=ALU.subtract),
                 reads=d["rr_b"], writes=[d["kr_b"]])
            S.op("dve", lambda en: en.tensor_tensor(out=r0[:, :, :], in0=t2, in1=cs, op=ALU.mult), reads=[psrc_b, rope_tb], writes=[d["rr_b"][0]])
            S.op("dve", lambda en: en.tensor_tensor(out=r1[:, :, :], in0=t1, in1=sn, op=ALU.mult), reads=[psrc_b, rope_tb], writes=[d["rr_b"][1]])
            S.op("dve", lambda en: en.tensor_tensor(out=kr[:, :, 64:128], in0=r0[:, :, :], in1=r1[:, :, :], op=ALU.add),
                 reads=d["rr_b"], writes=[d["kr_b"]])
            S.op("dve", lambda en: en.tensor_tensor(out=dstt[:, :].rearrange("p (h e) -> p h e", h=4), in0=kr[:, :, :],
                                                    in1=otab[:, scale_cols[0]:scale_cols[1]].unsqueeze(2).to_broadcast([128, 4, 128]),
                                                    op=ALU.mult), reads=[d["kr_b"], otab_b], writes=[dst_b])

        def ret_kv(d, t, chunk_idx):
            Wr, Wr_b, off = d["Wr"], d["Wr_b"], d["off"]
            ri = chunk_idx % 2
            S.dma("sp", d["rope"][ri][:, :], dr["rope_d"][:, chunk_idx, :], writes=[d["rope_b"][ri]])
            pk, pkb = c.ps.get()
            for k in range(8):
                S.op("pe", lambda en, k=k: en.matmul(pk[:, :], lhsT=xT[:, k, t * 128:(t + 1) * 128],
                                                     rhs=Wr[:, k, off["k"]:off["k"] + 512], start=(k == 0), stop=(k == 7)),
                     reads=[Wr_b, xT_b], writes=[pkb])
            ret_rope(d, pk[:, :], pkb, ri, (4, 8), d["kp"], d["kp_b"])
            for hf in range(2):
                pv, pvb = c.ps.get()
                for k in range(8):
                    S.op("pe", lambda en, k=k, hf=hf, pv=pv: en.matmul(
                        pv[:, :], lhsT=xT[:, k, t * 128:(t + 1) * 128],
                        rhs=Wr[:, k, off["v"] + hf * 512:off["v"] + (hf + 1) * 512], start=(k == 0), stop=(k == 7)),
                        reads=[Wr_b, xT_b], writes=[pvb])
                S.op("act", lambda en, hf=hf, pv=pv: en.copy(out=d["vb"][:, hf * 512:(hf + 1) * 512], in_=pv[:, :]),
                     reads=[pvb], writes=[d["vb_b"]])

        def ret_state_mm(d):
            pkv = [c.ps.get(), c.ps.get()]
            for h in range(4):
                pt, ptb_ = pkv[h // 2]
                S.op("pe", lambda en, h=h, pt=pt: en.matmul(pt[:, (h % 2) * 256:(h % 2) * 256 + 256], lhsT=d["kp"][:, h * 128:(h + 1) * 128],
                                                            rhs=d["vb"][:, h * 256:(h + 1) * 256], start=True, stop=True),
                     reads=[d["kp_b"], d["vb_b"]], writes=[ptb_])
            return pkv

        def ret_state_update(pkv):
            for hf in range(2):
                S.op("dve", lambda en, hf=hf: en.tensor_tensor(out=Rst[:, hf * 512:(hf + 1) * 512], in0=Rst[:, hf * 512:(hf + 1) * 512],
                                                               in1=pkv[hf][0][:, :], op=ALU.add), reads=[Rst_b, pkv[hf][1]], writes=[Rst_b])
            S.op("dve", lambda en: en.tensor_tensor(out=Rst[:, :].rearrange("p (h v) -> p h v", h=4),
                                                    in0=Rst[:, :].rearrange("p (h v) -> p h v", h=4),
                                                    in1=otab[:, 8:12].unsqueeze(2).to_broadcast([128, 4, 256]), op=ALU.mult),
                 reads=[Rst_b, otab_b], writes=[Rst_b])

        def zero_states():
            S.op("dve", lambda en: en.memset(Sst[:, :], 0.0), writes=[Sst_b])
            S.op("dve", lambda en: en.memset(Rst[:, :], 0.0), writes=[Rst_b])
            S.op("dve", lambda en: en.memset(Atot[:, :], 0.0), writes=[Atot_b])

        xbcd_b = [Buf(f"xbcd{g}") for g in range(ngroups)]
        smgd_b = [Buf(f"smgd{g}") for g in range(ngroups)]
        kpd_b = [Buf(f"kpd{i}") for i in range(nchunks)]
        vbd_b = [Buf(f"vbd{i}") for i in range(nchunks)]
        zero_states()
        with ExitStack() as es:
            ds = ssd_setup(es)
            dq = ret_setup(es, False)
            load_xT(0, xTh, xTh_b, 0, halo)
            ssd_features(ds, xTh, xTh_b, 128, True)
            for g in range(ngroups):
                xT, xT_b = xT_r[g % 2]
                smg, smg_b = smg_r[g % 2]
                for t in range(4):
                    load_xT(g * GT + t * 128, xT, xT_b, t, src)
                rot(ds, ("xbcT",), g)
                ssd_features(ds, xT, xT_b, GT, False)
                ssd_dt_group(ds)
                S.dma("sp", c.xbc_d[:, :, g * GT:(g + 1) * GT].rearrange("c p t -> p c t"), ds["xbcT"][:, :, :],
                      reads=ds["xbcT_b"], writes=[xbcd_b[g]])
                S.dma("sp", c.smg_d[g, :, :], smg[:, :, :].rearrange("p a b -> p (a b)"), reads=[smg_b], writes=[smgd_b[g]])
                for t in range(4):
                    ci = g * 4 + t
                    smc, smc_b = smc_r[ci % 3]
                    rot(ds, ("Btok", "xdte"), t)
                    rot(dq, ("rr", "kr", "kp", "vb"), t)
                    ssd_chunk_scalars(t)
                    pst = ssd_tok_and_state(ds, t)
                    ssd_state_update(ds, pst)
                    ret_kv(dq, t, ci)
                    S.dma("sp", c.kp_d[ci * 128:(ci + 1) * 128, :], dq["kp"][:, :], reads=[dq["kp_b"]], writes=[kpd_b[ci]])
                    S.dma("sp", c.vb_d[ci * 128:(ci + 1) * 128, :], dq["vb"][:, :], reads=[dq["vb_b"]], writes=[vbd_b[ci]])
                    pkv = ret_state_mm(dq)
                    ret_state_update(pkv)
        S.barrier()
        (loc_s, loc_r), (all_s, all_r) = c.st_loc[o], c.st_all[o]
        locb = [Buf("loc_s"), Buf("loc_r")]
        allb = [Buf("all_s"), Buf("all_r")]
        S.dma("sp", loc_s[:, 0:1024], Sst[:, :], reads=[Sst_b], writes=[locb[0]])
        S.dma("sp", loc_s[:, 1024:1040], Atot[:, :], reads=[Atot_b], writes=[locb[0]])
        S.dma("sp", loc_r[:, :], Rst[:, :], reads=[Rst_b], writes=[locb[1]])
        for (lo, al, lb, ab_) in ((loc_s, all_s, locb[0], allb[0]), (loc_r, all_r, locb[1], allb[1])):
            if c.use_cc:
                S.op("pool", lambda en, lo=lo, al=al: en.collective_compute(
                    "AllGather", ALU.bypass, replica_groups=[[0, 1, 2, 3], [4, 5, 6, 7]], ins=[lo], outs=[al]),
                    reads=[lb], writes=[ab_])
            else:
                for i in range(4):
                    S.dma("sp", al[i * 128:(i + 1) * 128, :], lo, reads=[lb], writes=[ab_])
        with ExitStack() as es:
            sb = lambda name, shape, dt: _sb(c, es, "c_" + name, shape, dt)
            rec = [sb(f"rec{i}", [128, 1040], F32) for i in range(2)]; rec_b = [Buf(f"rec{i}") for i in range(2)]
            rer = [sb(f"rer{i}", [128, 1024], F32) for i in range(2)]; rer_b = [Buf(f"rer{i}") for i in range(2)]
            cf = sb("cf", [128, 16], F32); cf_b = Buf("cf")
            zero_states()
            for i in range(4):
                rb, rbb = rec[i % 2], rec_b[i % 2]
                rr_, rrb = rer[i % 2], rer_b[i % 2]
                S.dma("sp", rb[:, :], all_s[i * 128:(i + 1) * 128, :], reads=[allb[0]], writes=[rbb])
                S.dma("sp", rr_[:, :], all_r[i * 128:(i + 1) * 128, :], reads=[allb[1]], writes=[rrb])
                S.op("act", lambda en, rb=rb: en.activation(out=cf[:, :], in_=rb[:, 1024:1040], func=AF.Exp), reads=[rbb], writes=[cf_b])
                S.op("dve", lambda en, i=i: en.tensor_scalar(out=cf[:, :], in0=cf[:, :], scalar1=-1.0, scalar2=otab[:, 28 + i:29 + i],
                                                             op0=ALU.add, op1=ALU.mult), reads=[cf_b, otab_b], writes=[cf_b])
                S.op("dve", lambda en: en.tensor_scalar_add(out=cf[:, :], in0=cf[:, :], scalar1=1.0), reads=[cf_b], writes=[cf_b])
                S.op("dve", lambda en: en.tensor_tensor(out=Sst[:, :].rearrange("p (h q) -> p h q", h=16),
                                                        in0=Sst[:, :].rearrange("p (h q) -> p h q", h=16),
                                                        in1=cf[:, :].unsqueeze(2).to_broadcast([128, 16, 64]), op=ALU.mult),
                     reads=[Sst_b, cf_b], writes=[Sst_b])
                S.op("dve", lambda en, i=i, rb=rb: en.scalar_tensor_tensor(out=Sst[:, :], in0=rb[:, 0:1024], scalar=otab[:, 28 + i:29 + i],
                                                                           in1=Sst[:, :], op0=ALU.mult, op1=ALU.add),
                     reads=[rbb, otab_b, Sst_b], writes=[Sst_b])
                S.op("dve", lambda en, i=i, rr_=rr_: en.tensor_tensor(
                    out=rr_[:, :].rearrange("p (h v) -> p h v", h=4), in0=rr_[:, :].rearrange("p (h v) -> p h v", h=4),
                    in1=otab[:, 12 + 4 * i:16 + 4 * i].unsqueeze(2).to_broadcast([128, 4, 256]), op=ALU.mult),
                    reads=[rrb, otab_b], writes=[rrb])
                S.op("dve", lambda en, rr_=rr_: en.tensor_tensor(out=Rst[:, :], in0=Rst[:, :], in1=rr_[:, :], op=ALU.add),
                     reads=[rrb, Rst_b], writes=[Rst_b])
        S.barrier()

        ysT_d = c.ysT_d
        ysd_b = [Buf(f"ysd{i}") for i in range(nchunks)]
        with ExitStack() as es:
            ds = ssd_setup(es)
            sb = lambda name, shape, dt: _sb(c, es, "b_" + name, shape, dt)
            Wz, Wz_b = wload(es, "z", OZ, 1024)
            R2 = lambda nm, shape, dt: [(sb(f"{nm}{i}", shape, dt), Buf(f"{nm}{i}")) for i in range(2)]
            sz_r = R2("sz", [128, 1024], F32)
            thz_r = R2("thz", [128, 1024], F32)
            Sbf_r = R2("Sbf", [128, 1024], BF16)
            cbm_r = R2("cbm", [128, 2, 128], F32)
            Xs_r = R2("Xs", [128, 8, 128], F32)
            ET_r = R2("ET", [128, 8, 128], F32)
            PT_r = [(sb(f"PT{i}", [128, 16, 128], BF16), [Buf(f"PT{i}_{g}") for g in range(2)]) for i in range(2)]
            xdt_r = R2("xdt", [128, 1024], BF16)
            xsD_r = R2("xsD", [128, 1024], BF16)
            yv_r = R2("yv", [128, 1024], F32)
            ysb_r = R2("ysb", [128, 1024], BF16)
            rms_r = R2("rms", [128, 8], F32)
            ysT = [sb(f"ysT{i}", [128, 8, 128], BF16) for i in range(2)]; ysT_b = [Buf(f"ysT{i}") for i in range(2)]
            S.op("dve", lambda en: en.memset(Atot[:, :], 0.0), writes=[Atot_b])
            for g in range(ngroups):
                xT, xT_b = xT_r[g % 2]
                smg, smg_b = smg_r[g % 2]
                for t in range(4):
                    load_xT(g * GT + t * 128, xT, xT_b, t, src)
                rot(ds, ("xbcT",), g)
                S.dma("sp", ds["xbcT"][:, :, :], c.xbc_d[:, :, g * GT:(g + 1) * GT].rearrange("c p t -> p c t"),
                      reads=[xbcd_b[g]], writes=ds["xbcT_b"])
                S.dma("sp", smg[:, :, :].rearrange("p a b -> p (a b)"), c.smg_d[g, :, :], reads=[smgd_b[g]], writes=[smg_b])
                xbcT, xbcT_b = ds["xbcT"], ds["xbcT_b"]
                for t in range(4):
                    ci = g * 4 + t
                    tc0 = t * 128
                    smc, smc_b = smc_r[ci % 3]
                    rot(ds, ("Btok", "xdte"), ci)
                    sz, sz_b = sz_r[ci % 2]; thz, thz_b = thz_r[ci % 2]; Sbf, Sbf_b = Sbf_r[ci % 2]
                    cbm, cbm_b = cbm_r[ci % 2]; PT, PT_b = PT_r[ci % 2]; xdt, xdt_b = xdt_r[ci % 2]
                    xsD, xsD_b = xsD_r[ci % 2]; yv, yv_b = yv_r[ci % 2]; ysb, ysb_b = ysb_r[ci % 2]
                    rms, rms_b = rms_r[ci % 2]
                    ssd_chunk_scalars(t)
                    S.op("act", lambda en: en.copy(out=Sbf[:, :], in_=Sst[:, :]), reads=[Sst_b], writes=[Sbf_b])
                    for hf in range(2):
                        pz, pzb = c.ps.get()
                        for k in range(8):
                            S.op("pe", lambda en, k=k, hf=hf, pz=pz: en.matmul(
                                pz[:, :], lhsT=xT[:, k, tc0:tc0 + 128], rhs=Wz[:, k, hf * 512:(hf + 1) * 512],
                                start=(k == 0), stop=(k == 7)), reads=[Wz_b, xT_b], writes=[pzb])
                        S.op("act", lambda en, hf=hf, pz=pz: en.activation(out=thz[:, hf * 512:(hf + 1) * 512], in_=pz[:, :],
                                                                            func=AF.Tanh, scale=0.5), reads=[pzb], writes=[thz_b])
                        S.op("dve", lambda en, hf=hf, pz=pz: en.scalar_tensor_tensor(
                            out=sz[:, hf * 512:(hf + 1) * 512], in0=thz[:, hf * 512:(hf + 1) * 512], scalar=1.0, in1=pz[:, :],
                            op0=ALU.add, op1=ALU.mult), reads=[thz_b, pzb], writes=[sz_b])
                    pcb, pcbb = c.ps.get()
                    for gg in range(2):
                        S.op("pe", lambda en, gg=gg: en.matmul(pcb[:, gg * 128:(gg + 1) * 128], lhsT=xbcT[:, 8 + gg, tc0:tc0 + 128],
                                                               rhs=xbcT[:, 10 + gg, tc0:tc0 + 128], start=True, stop=True),
                             reads=[xbcT_b[8 + gg], xbcT_b[10 + gg]], writes=[pcbb])
                    S.op("dve", lambda en: en.tensor_tensor(out=cbm[:, :, :], in0=pcb[:, 0:256].rearrange("p (g l) -> p g l", g=2),
                                                            in1=m01[:, :].unsqueeze(1).to_broadcast([128, 2, 128]), op=ALU.mult),
                         reads=[pcbb, tri_b], writes=[cbm_b])

                    def xs_extra(xs3, pxb):
                        S.op("dve", lambda en: en.tensor_tensor(out=xdt[:, :].rearrange("p (h q) -> p h q", h=16), in0=xs3,
                                                                in1=smg[:, t, :].unsqueeze(2).to_broadcast([128, 16, 64]), op=ALU.mult),
                             reads=[pxb, smg_b], writes=[xdt_b])
                        S.op("dve", lambda en: en.tensor_tensor(out=xsD[:, :].rearrange("p (h q) -> p h q", h=16), in0=xs3,
                                                                in1=ptab[:, 32:48].unsqueeze(2).to_broadcast([128, 16, 64]), op=ALU.mult),
                             reads=[pxb, ptab_b], writes=[xsD_b])
                    pst = ssd_tok_and_state(ds, t, xs_extra)
                    for gg in range(2):
                        Xs, Xs_b = Xs_r[gg]
                        ET, ET_b = ET_r[gg]
                        S.op("dve", lambda en, gg=gg: en.tensor_tensor(
                            out=Xs[:, :, :], in0=smg[:, 4 + t, gg * 8:(gg + 1) * 8].unsqueeze(2).to_broadcast([128, 8, 128]),
                            in1=triu[:, :].unsqueeze(1).to_broadcast([128, 8, 128]), op=ALU.mult),
                            reads=[smg_b, tri_b], writes=[Xs_b])
                        pseg = [c.ps.get(), c.ps.get()]
                        for q in range(2):
                            S.op("pe", lambda en, q=q, pseg=pseg: en.matmul(
                                pseg[q][0][:, :], lhsT=trisl[:, :], rhs=Xs[:, q * 4:(q + 1) * 4, :].rearrange("p h l -> p (h l)"),
                                start=True, stop=True), reads=[tri_b, Xs_b], writes=[pseg[q][1]])
                            S.op("act", lambda en, q=q, pseg=pseg: en.activation(
                                out=ET[:, q * 4:(q + 1) * 4, :].rearrange("p h l -> p (h l)"), in_=pseg[q][0][:, :], func=AF.Exp),
                                reads=[pseg[q][1]], writes=[ET_b])
                        S.op("dve", lambda en, gg=gg: en.tensor_tensor(
                            out=PT[:, gg * 8:(gg + 1) * 8, :], in0=ET[:, :, :],
                            in1=cbm[:, gg, :].unsqueeze(1).to_broadcast([128, 8, 128]), op=ALU.mult),
                            reads=[ET_b, cbm_b], writes=[PT_b[gg]])
                    for gg in range(2):
                        po, pob = c.ps.get()
                        S.op("pe", lambda en, gg=gg, po=po: en.matmul(po[:, :], lhsT=xbcT[:, 10 + gg, tc0:tc0 + 128],
                                                                      rhs=Sbf[:, gg * 512:(gg + 1) * 512], start=True, stop=True),
                             reads=[xbcT_b[10 + gg], Sbf_b], writes=[pob])
                        S.op("dve", lambda en, gg=gg, po=po: en.tensor_tensor(
                            out=yv[:, gg * 512:(gg + 1) * 512].rearrange("p (h q) -> p h q", h=8),
                            in0=po[:, :].rearrange("p (h q) -> p h q", h=8),
                            in1=smc[:, 1, gg * 8:(gg + 1) * 8].unsqueeze(2).to_broadcast([128, 8, 64]), op=ALU.mult),
                            reads=[pob, smc_b], writes=[yv_b])
                    ssd_state_update(ds, pst)
                    for gg in range(2):
                        pd_, pdb_ = c.ps.get()
                        S.op("pe", lambda en, gg=gg, pd_=pd_: en.matmul(pd_[:, :], lhsT=c.ident_bf[:, :], rhs=xsD[:, gg * 512:(gg + 1) * 512],
                                                                        start=True, stop=False), reads=[c.ident_buf, xsD_b], writes=[pdb_])
                        for hh in range(8):
                            h = gg * 8 + hh
                            S.op("pe", lambda en, h=h, hh=hh, pd_=pd_: en.matmul(
                                pd_[:, hh * 64:(hh + 1) * 64], lhsT=PT[:, h, :], rhs=xdt[:, h * 64:(h + 1) * 64],
                                start=False, stop=(hh == 7)), reads=[PT_b[gg], xdt_b], writes=[pdb_])
                        S.op("dve", lambda en, gg=gg, pd_=pd_: en.tensor_tensor(out=yv[:, gg * 512:(gg + 1) * 512],
                                                                                in0=yv[:, gg * 512:(gg + 1) * 512], in1=pd_[:, :], op=ALU.add),
                             reads=[yv_b, pdb_], writes=[yv_b])
                    S.op("dve", lambda en: en.scalar_tensor_tensor(out=yv[:, :], in0=yv[:, :], scalar=0.5, in1=sz[:, :],
                                                                   op0=ALU.mult, op1=ALU.mult), reads=[yv_b, sz_b], writes=[yv_b])
                    for gg in range(2):
                        S.op("act", lambda en, gg=gg: en.activation(out=thz[:, gg * 512:(gg + 1) * 512], in_=yv[:, gg * 512:(gg + 1) * 512],
                                                                    func=AF.Square, accum_out=rms[:, gg:gg + 1]),
                             reads=[yv_b], writes=[thz_b, rms_b])
                    S.op("dve", lambda en: en.tensor_scalar(out=rms[:, 2:4], in0=rms[:, 0:2], scalar1=1.0 / 512.0, scalar2=float(EPS),
                                                            op0=ALU.mult, op1=ALU.add), reads=[rms_b], writes=[rms_b])
                    S.op("act", lambda en: en.activation(out=rms[:, 2:4], in_=rms[:, 2:4], func=AF.Sqrt), reads=[rms_b], writes=[rms_b])
                    S.op("dve", lambda en: en.reciprocal(out=rms[:, 2:4], in_=rms[:, 2:4]), reads=[rms_b], writes=[rms_b])
                    S.op("dve", lambda en: en.tensor_tensor(out=ysb[:, :].rearrange("p (g q) -> p g q", g=2),
                                                            in0=yv[:, :].rearrange("p (g q) -> p g q", g=2),
                                                            in1=rms[:, 2:4].unsqueeze(2).to_broadcast([128, 2, 512]), op=ALU.mult),
                         reads=[yv_b, rms_b], writes=[ysb_b])
                    pt, ptb_ = c.ps.get()
                    ptv = ps_bf(pt)
                    for j in range(8):
                        S.op("pe", lambda en, j=j, ptv=ptv: en.transpose(out=ptv[:, j * 128:(j + 1) * 128], in_=ysb[:, j * 128:(j + 1) * 128],
                                                                         identity=c.ident_bf[:, :]), reads=[ysb_b, c.ident_buf], writes=[ptb_])
                    yi = ci % 2
                    S.op("act", lambda en, yi=yi, ptv=ptv: en.copy(out=ysT[yi][:, :, :], in_=ptv[:, :].rearrange("p (j t) -> p j t", j=8)),
                         reads=[ptb_], writes=[ysT_b[yi]])
                    S.dma("sp", ysT_d[:, :, ci * 128:(ci + 1) * 128].rearrange("k p t -> p k t"), ysT[yi][:, :, :],
                          reads=[ysT_b[yi]], writes=[ysd_b[ci]])
        S.barrier()

        with ExitStack() as es:
            dq = ret_setup(es, True)
            sb = lambda name, shape, dt: _sb(c, es, "d_" + name, shape, dt)
            Wr, Wr_b, off = dq["Wr"], dq["Wr_b"], dq["off"]
            Wout = sb("wout", [128, 16, 1024], BF16); Wout_b = Buf("owout")
            ng = sb("ng", [128, 8], F32); ng_b = Buf("ng")
            gng = sb("gng", [128, 1024], F32); gnb = sb("gnb", [128, 1024], F32); gn_b = Buf("gn")
            g_t = sb("g", [128, 1024], F32); b_t = sb("b", [128, 1024], F32); gb_buf = Buf("ogb")
            R2 = lambda nm, shape, dt: [(sb(f"{nm}{i}", shape, dt), Buf(f"{nm}{i}")) for i in range(2)]
            qp_r = R2("qp", [128, 512], BF16)
            qT_r = R2("qT", [128, 4, 128], BF16)
            kT_r = R2("kT", [128, 4, 128], BF16)
            scT_r = R2("scT", [128, 4, 128], BF16)
            yrT_r = R2("yrT", [128, 8, 128], BF16)
            sg = sb("sg", [128, 1024], F32); sg_b = Buf("sg")
            thg = sb("thg", [128, 1024], F32); thg_b = Buf("thg")
            Rbf = sb("Rbf", [128, 1024], BF16); Rbf_b = Buf("Rbf")
            yr = sb("yr", [128, 1024], F32); yr_b = Buf("yr")
            yrb = sb("yrb", [128, 1024], BF16); yrb_b = Buf("yrb")
            ysl = [sb(f"ysl{i}", [128, 8, 128], BF16) for i in range(2)]; ysl_b = [Buf(f"ysl{i}") for i in range(2)]
            gst = sb("gst", [128, 4, 6], F32); gmv = sb("gmv", [128, 4, 4], F32); gs_b = Buf("gs")
            v_r = R2("v", [128, 1024], F32)
            st_r = R2("st", [128, 12], F32)
            mv_r = [sb(f"mv{i}", [128, 4], F32) for i in range(2)]
            xres = sb("xres", [128, 1024], F32); xres_b = Buf("xres")
            S.dma("pool", Wout[:, 0:8, :], dr[f"od_w_out{o}"][0:1024, :].rearrange("(k p) n -> p k n", p=128), writes=[Wout_b])
            S.dma("pool", Wout[:, 8:16, :], dr[f"od_w_out{o}"][1024:2048, :].rearrange("(k p) n -> p k n", p=128), writes=[Wout_b])
            S.dma("sp", ng[:, :], dr[f"od_ssm_norm_g{o}"].rearrange("o (c p) -> p (o c)", p=128), writes=[ng_b], allow_slow_non_contiguous=True)
            load_bcast_row(c, "sp", gng, gn_b, dr[f"od_ret_gn_g{o}"])
            load_bcast_row(c, "sp", gnb, gn_b, dr[f"od_ret_gn_b{o}"])
            load_bcast_row(c, "sp", g_t, gb_buf, dr[f"ln_mix_g{layer}"])
            load_bcast_row(c, "sp", b_t, gb_buf, dr[f"ln_mix_b{layer}"])
            for kc in range(8):
                S.op("dve", lambda en, kc=kc: en.tensor_scalar_mul(out=Wout[:, kc, :], in0=Wout[:, kc, :], scalar1=ng[:, kc:kc + 1]),
                     reads=[Wout_b, ng_b], writes=[Wout_b])
            for g in range(ngroups):
                xT, xT_b = xT_r[g % 2]
                for t in range(4):
                    load_xT(g * GT + t * 128, xT, xT_b, t, src)
                for t in range(4):
                    ci = g * 4 + t
                    tc0 = t * 128
                    ri = ci % 2
                    yi = ci % 2
                    rot(dq, ("rr", "kr", "kp", "vb"), ci)
                    qp, qp_b = qp_r[ci % 2]; qT, qT_b = qT_r[ci % 2]; kT, kT_b = kT_r[ci % 2]
                    scT, scT_b = scT_r[ci % 2]; yrT, yrT_b = yrT_r[ci % 2]
                    S.dma("sp", ysl[yi][:, :, :], ysT_d[:, :, ci * 128:(ci + 1) * 128].rearrange("k p t -> p k t"),
                          reads=[ysd_b[ci]], writes=[ysl_b[yi]])
                    S.dma("sp", dq["rope"][ri][:, :], dr["rope_d"][:, ci, :], writes=[dq["rope_b"][ri]])
                    S.dma("sp", dq["kp"][:, :], c.kp_d[ci * 128:(ci + 1) * 128, :], reads=[kpd_b[ci]], writes=[dq["kp_b"]])
                    S.dma("sp", dq["vb"][:, :], c.vb_d[ci * 128:(ci + 1) * 128, :], reads=[vbd_b[ci]], writes=[dq["vb_b"]])
                    pq, pqb = c.ps.get()
                    for k in range(8):
                        S.op("pe", lambda en, k=k: en.matmul(pq[:, :], lhsT=xT[:, k, tc0:tc0 + 128], rhs=Wr[:, k, 0:512],
                                                             start=(k == 0), stop=(k == 7)), reads=[Wr_b, xT_b], writes=[pqb])
                    ret_rope(dq, pq[:, :], pqb, ri, (0, 4), qp, qp_b)
                    for (srcp, srcp_b, dT, dT_b) in ((qp, qp_b, qT, qT_b), (dq["kp"], dq["kp_b"], kT, kT_b)):
                        pt, ptb_ = c.ps.get()
                        ptv = ps_bf(pt)
                        for j in range(4):
                            S.op("pe", lambda en, j=j, ptv=ptv, srcp=srcp: en.transpose(
                                out=ptv[:, j * 128:(j + 1) * 128], in_=srcp[:, j * 128:(j + 1) * 128], identity=c.ident_bf[:, :]),
                                reads=[srcp_b, c.ident_buf], writes=[ptb_])
                        S.op("act", lambda en, ptv=ptv, dT=dT: en.copy(out=dT[:, :, :], in_=ptv[:, 0:512].rearrange("p (j t) -> p j t", j=4)),
                             reads=[ptb_], writes=[dT_b])
                    for hf in range(2):
                        pg, pgb = c.ps.get()
                        for k in range(8):
                            S.op("pe", lambda en, k=k, hf=hf, pg=pg: en.matmul(
                                pg[:, :], lhsT=xT[:, k, tc0:tc0 + 128], rhs=Wr[:, k, 512 + hf * 512:512 + (hf + 1) * 512],
                                start=(k == 0), stop=(k == 7)), reads=[Wr_b, xT_b], writes=[pgb])
                        S.op("act", lambda en, hf=hf, pg=pg: en.activation(out=thg[:, hf * 512:(hf + 1) * 512], in_=pg[:, :],
                                                                            func=AF.Tanh, scale=0.5), reads=[pgb], writes=[thg_b])
                        S.op("dve", lambda en, hf=hf, pg=pg: en.scalar_tensor_tensor(
                            out=sg[:, hf * 512:(hf + 1) * 512], in0=thg[:, hf * 512:(hf + 1) * 512], scalar=1.0, in1=pg[:, :],
                            op0=ALU.add, op1=ALU.mult), reads=[thg_b, pgb], writes=[sg_b])
                    psc, pscb = c.ps.get()
                    for h in range(4):
                        S.op("pe", lambda en, h=h: en.matmul(psc[:, h * 128:(h + 1) * 128], lhsT=kT[:, h, :], rhs=qT[:, h, :],
                                                             start=True, stop=True), reads=[kT_b, qT_b], writes=[pscb])
                    S.op("dve", lambda en: en.tensor_tensor(out=scT[:, :, :], in0=psc[:, :].rearrange("p (h l) -> p h l", h=4),
                                                            in1=m01[:, :].unsqueeze(1).to_broadcast([128, 4, 128]), op=ALU.mult),
                         reads=[pscb, tri_b], writes=[scT_b])
                    S.op("act", lambda en: en.copy(out=Rbf[:, :], in_=Rst[:, :]), reads=[Rst_b], writes=[Rbf_b])
                    py = [c.ps.get(), c.ps.get()]
                    for h in range(4):
                        pt, ptb_ = py[h // 2]
                        cs_ = slice((h % 2) * 256, (h % 2) * 256 + 256)
                        S.op("pe", lambda en, h=h, pt=pt, cs_=cs_: en.matmul(pt[:, cs_], lhsT=scT[:, h, :], rhs=dq["vb"][:, h * 256:(h + 1) * 256],
                                                                             start=True, stop=False), reads=[scT_b, dq["vb_b"]], writes=[ptb_])
                        S.op("pe", lambda en, h=h, pt=pt, cs_=cs_: en.matmul(pt[:, cs_], lhsT=qT[:, h, :], rhs=Rbf[:, h * 256:(h + 1) * 256],
                                                                             start=False, stop=True), reads=[qT_b, Rbf_b], writes=[ptb_])
                    pkv = ret_state_mm(dq)
                    ret_state_update(pkv)
                    for h in range(4):
                        pt, ptb_ = py[h // 2]
                        cs_ = slice((h % 2) * 256, (h % 2) * 256 + 256)
                        S.op("dve", lambda en, h=h, pt=pt, cs_=cs_: en.bn_stats(out=gst[:, h, :], in_=pt[:, cs_]), reads=[ptb_], writes=[gs_b])
                        S.op("dve", lambda en, h=h: en.bn_aggr(out=gmv[:, h, 0:2], in_=gst[:, h, :]), reads=[gs_b], writes=[gs_b])
                    S.op("dve", lambda en: en.tensor_scalar_add(out=gmv[:, :, 2:3], in0=gmv[:, :, 1:2], scalar1=float(EPS)), reads=[gs_b], writes=[gs_b])
                    S.op("act", lambda en: en.activation(out=gmv[:, :, 2:3], in_=gmv[:, :, 2:3], func=AF.Sqrt), reads=[gs_b], writes=[gs_b])
                    S.op("dve", lambda en: en.reciprocal(out=gmv[:, :, 2:3], in_=gmv[:, :, 2:3]), reads=[gs_b], writes=[gs_b])
                    S.op("dve", lambda en: en.scalar_tensor_tensor(out=gmv[:, :, 3:4], in0=gmv[:, :, 0:1], scalar=-1.0, in1=gmv[:, :, 2:3],
                                                                   op0=ALU.mult, op1=ALU.mult), reads=[gs_b], writes=[gs_b])
                    for h in range(4):
                        pt, ptb_ = py[h // 2]
                        cs_ = slice((h % 2) * 256, (h % 2) * 256 + 256)
                        S.op("act", lambda en, h=h, pt=pt, cs_=cs_: en.activation(out=yr[:, h * 256:(h + 1) * 256], in_=pt[:, cs_], func=AF.Identity,
                                                                                  bias=gmv[:, h, 3:4], scale=gmv[:, h, 2:3]),
                             reads=[ptb_, gs_b], writes=[yr_b])
                    S.op("dve", lambda en: en.tensor_tensor(out=yr[:, :], in0=yr[:, :], in1=gng[:, :], op=ALU.mult), reads=[yr_b, gn_b], writes=[yr_b])
                    S.op("dve", lambda en: en.tensor_tensor(out=yr[:, :], in0=yr[:, :], in1=gnb[:, :], op=ALU.add), reads=[yr_b, gn_b], writes=[yr_b])
                    S.op("dve", lambda en: en.scalar_tensor_tensor(out=yrb[:, :], in0=yr[:, :], scalar=0.5, in1=sg[:, :],
                                                                   op0=ALU.mult, op1=ALU.mult), reads=[yr_b, sg_b], writes=[yrb_b])
                    pt, ptb_ = c.ps.get()
                    ptv = ps_bf(pt)
                    for j in range(8):
                        S.op("pe", lambda en, j=j, ptv=ptv: en.transpose(out=ptv[:, j * 128:(j + 1) * 128], in_=yrb[:, j * 128:(j + 1) * 128],
                                                                         identity=c.ident_bf[:, :]), reads=[yrb_b, c.ident_buf], writes=[ptb_])
                    S.op("act", lambda en, ptv=ptv: en.copy(out=yrT[:, :, :], in_=ptv[:, :].rearrange("p (j t) -> p j t", j=8)),
                         reads=[ptb_], writes=[yrT_b])
                    halves = [c.ps.get(), c.ps.get()]
                    for kc in range(16):
                        lt = ysl[yi][:, kc, :] if kc < 8 else yrT[:, kc - 8, :]
                        lb = ysl_b[yi] if kc < 8 else yrT_b
                        for hf in range(2):
                            ph, phb = halves[hf]
                            S.op("pe", lambda en, ph=ph, kc=kc, hf=hf, lt=lt: en.matmul(
                                ph[:, :], lhsT=lt, rhs=Wout[:, kc, hf * 512:(hf + 1) * 512], start=(kc == 0), stop=(kc == 15)),
                                reads=[lb, Wout_b], writes=[phb])
                    r0 = g * GT + t * 128
                    S.dma("sp", xres[:, :], src[r0:r0 + 128, :], writes=[xres_b])
                    xi = ci % 2
                    v, v_b = v_r[xi]
                    st, smm_b = st_r[xi]
                    mv = mv_r[xi]
                    emit_ln_epilogue(c, (v, v_b, st, mv, smm_b), halves, xres, xres_b, g_t, b_t, gb_buf, v, v_b)
                    outs.append(S.dma("sp", dst[r0:r0 + 128, :], v[:, :], reads=[v_b]))
    S.barrier()
    return outs


def emit_halo_exchange(c, xsrc, NT, hidx):
    S, nc = c.S, c.nc
    hl_loc = nc.dram_tensor(f"hl_loc{hidx}", [128, D], F32, kind="Internal").ap()
    hl_all = nc.dram_tensor(f"hl_all{hidx}", [4 * 128, D], F32, kind="Internal").ap()
    halo_d = nc.dram_tensor(f"halo_d{hidx}", [128, D], F32, kind="Internal").ap()
    lb, ab_, hb_ = Buf("hl_loc"), Buf("hl_all"), Buf("halo_d")
    with ExitStack() as es:
        sb = lambda name, shape, dt: _sb(c, es, "h_" + name, shape, dt)
        t0 = sb("t0", [128, D], F32); t0_b = Buf("ht0")
        rec = [sb(f"rec{i}", [128, D], F32) for i in range(2)]; rec_b = [Buf(f"hrec{i}") for i in range(2)]
        acc = sb("acc", [128, D], F32); acc_b = Buf("hacc")
        hsel = sb("hsel", [128, 4], F32); hsel_b = Buf("hsel")
        S.dma("sp", hsel[:, :], c.dram["hsel"], writes=[hsel_b])
        S.dma("sp", t0[:, :], xsrc[NT - 128:NT, :], writes=[t0_b])
        S.dma("sp", hl_loc, t0[:, :], reads=[t0_b], writes=[lb])
        if c.use_cc:
            S.op("pool", lambda en: en.collective_compute("AllGather", ALU.bypass, replica_groups=[[0, 1, 2, 3], [4, 5, 6, 7]],
                                                          ins=[hl_loc], outs=[hl_all]), reads=[lb], writes=[ab_])
        else:
            for i in range(4):
                S.dma("sp", hl_all[i * 128:(i + 1) * 128, :], hl_loc, reads=[lb], writes=[ab_])
        for i in range(4):
            rb, rbb = rec[i % 2], rec_b[i % 2]
            S.dma("sp", rb[:, :], hl_all[i * 128:(i + 1) * 128, :], reads=[ab_], writes=[rbb])
            if i == 0:
                S.op("dve", lambda en, rb=rb: en.tensor_scalar_mul(out=acc[:, :], in0=rb[:, :], scalar1=hsel[:, 0:1]),
                     reads=[rbb, hsel_b], writes=[acc_b])
            else:
                S.op("dve", lambda en, rb=rb, i=i: en.scalar_tensor_tensor(out=acc[:, :], in0=rb[:, :], scalar=hsel[:, i:i + 1],
                                                                           in1=acc[:, :], op0=ALU.mult, op1=ALU.add),
                     reads=[rbb, hsel_b, acc_b], writes=[acc_b])
        S.dma("sp", halo_d, acc[:, :], reads=[acc_b], writes=[hb_])
    S.barrier()
    return halo_d


EVEN_IN = 1792
ODD_IN = 5648

PER_LAYER_SHAPES = {
    "ln_mix_g": [1, D], "ln_mix_b": [1, D], "ln_ffn_g": [1, D], "ln_ffn_b": [1, D],
    "ffn_w_gate": [D, FH], "ffn_w_up": [D, FH], "ffn_w_down": [FH, D],
}
EVEN_SHAPES = {
    "ev_w_in": [D, EVEN_IN], "ev_sinks": [1, 8], "ev_dw_w": [31, 512], "ev_dw_b": [1, 512],
    "ev_cn_g": [1, 512], "ev_cn_b": [1, 512], "ev_w_out": [D, D],
}
ODD_SHAPES = {
    "od_w_in": [D, ODD_IN], "od_conv_w": [4, 1536], "od_conv_b": [1, 1536], "od_dt_bias": [1, 16],
    "od_a_log": [1, 16], "od_d_skip": [1, 16], "od_ssm_norm_g": [1, D], "od_ret_gn_g": [1, D],
    "od_ret_gn_b": [1, D], "od_w_out": [2 * D, D],
}


def stage_inputs(stages):
    need = {"x": None, "ident": [128, 128], "ones": [128, 128]}
    for kind, idx in stages:
        if kind == "ffn":
            for k in ("ln_ffn_g", "ln_ffn_b", "ffn_w_gate", "ffn_w_up", "ffn_w_down"):
                need[f"{k}{idx}"] = PER_LAYER_SHAPES[k]
        elif kind == "even":
            layer = 2 * idx
            for k in ("ln_mix_g", "ln_mix_b"):
                need[f"{k}{layer}"] = PER_LAYER_SHAPES[k]
            for k, s in EVEN_SHAPES.items():
                need[f"{k}{idx}"] = s
            need["amask"] = [128, 256]
            need["amask0"] = [128, 256]
            need["rope_a"] = None
            need["halo"] = [128, D]
        elif kind == "odd":
            layer = 2 * idx + 1
            for k in ("ln_mix_g", "ln_mix_b"):
                need[f"{k}{layer}"] = PER_LAYER_SHAPES[k]
            for k, s in ODD_SHAPES.items():
                need[f"{k}{idx}"] = s
            need["halo"] = [128, D]
            need["odd_tab"] = [128, 32]
            need["triu"] = [128, 128]
            need["trisl"] = [128, 128]
            need["rope_d"] = None
    if sum(1 for k, _ in stages if k in ("even", "odd")) > 1:
        need["hsel"] = [128, 4]
    return need


def build_program(NT, stages, use_cc=True):
    nc = bass.Bass("TRN2", target_bir_lowering=False)
    c = Ctx()
    c.nc = nc
    c.NT = NT
    c.use_cc = use_cc
    c.st_loc = {}
    c.st_all = {}
    for kind, idx in stages:
        if kind == "odd":
            c.st_loc[idx] = (nc.dram_tensor(f"loc_s{idx}", [128, 1040], F32, kind="Internal").ap(),
                             nc.dram_tensor(f"loc_r{idx}", [128, 1024], F32, kind="Internal").ap())
            c.st_all[idx] = (nc.dram_tensor(f"all_s{idx}", [4 * 128, 1040], F32, kind="Internal").ap(),
                             nc.dram_tensor(f"all_r{idx}", [4 * 128, 1024], F32, kind="Internal").ap())
            if not hasattr(c, "ysT_d"):
                c.ysT_d = nc.dram_tensor("ysT_d", [8, 128, NT], BF16, kind="Internal").ap()
                c.xbc_d = nc.dram_tensor("xbc_d", [12, 128, NT], BF16, kind="Internal").ap()
                c.smg_d = nc.dram_tensor("smg_d", [NT // 512, 128, 128], F32, kind="Internal").ap()
                c.kp_d = nc.dram_tensor("kp_d", [NT, 512], BF16, kind="Internal").ap()
                c.vb_d = nc.dram_tensor("vb_d", [NT, 1024], BF16, kind="Internal").ap()
    es = ExitStack()
    c.es = es
    c.dram = {}
    need = stage_inputs(stages)
    need["x"] = [NT, D]
    if "rope_a" in need:
        need["rope_a"] = [128, NT // 128 + 1, 16]
    if "rope_d" in need:
        need["rope_d"] = [128, NT // 128, 128]
    for name, shape in need.items():
        c.dram[name] = nc.dram_tensor(name, list(shape), F32, kind="ExternalInput").ap()
    y = nc.dram_tensor("y", [NT, D], F32, kind="ExternalOutput").ap()
    xa = nc.dram_tensor("xa", [NT, D], F32, kind="Internal").ap()
    xb = nc.dram_tensor("xb", [NT, D], F32, kind="Internal").ap()
    with es:
        c.S = Sched(nc, es)
        c.ps = PsumPool(c)
        c.ident_f = _sb(c, es, "ident_f", [128, 128], F32)
        c.ident_bf = _sb(c, es, "ident_bf", [128, 128], BF16)
        c.ones_f = _sb(c, es, "ones_f", [128, 128], F32)
        c.ident_buf = Buf("ident")
        c.identf_buf = Buf("identf")
        c.ones_buf = Buf("ones")
        c.S.dma("sp", c.ident_f[:, :], c.dram["ident"], writes=[c.identf_buf])
        c.S.dma("sp", c.ones_f[:, :], c.dram["ones"], writes=[c.ones_buf])
        c.S.op("dve", lambda e: e.tensor_copy(out=c.ident_bf[:, :], in_=c.ident_f[:, :]), reads=[c.identf_buf],
               writes=[c.ident_buf])
        cur = c.dram["x"]
        bufs = [xa, xb]
        outs = []
        nmix = 0
        for si, (kind, idx) in enumerate(stages):
            last = si == len(stages) - 1
            dst = y if last else bufs[si % 2]
            if kind in ("even", "odd"):
                halo = c.dram["halo"] if nmix == 0 else emit_halo_exchange(c, cur, NT, nmix)
                nmix += 1
            if kind == "ffn":
                outs = emit_ffn(c, idx, cur, dst, NT)
            elif kind == "even":
                outs = emit_even(c, idx, 2 * idx, cur, dst, halo, NT)
            elif kind == "odd":
                outs = emit_odd(c, idx, 2 * idx + 1, cur, dst, halo, NT)
            else:
                raise NotImplementedError(kind)
            cur = dst
        c.S.emit(final_ops=outs)
    return nc


ROPE_THETA = 500000.0


def rope_table_a(pos0, NT):
    nt = NT // 128 + 1
    pos = (pos0 - 128 + np.arange(nt * 128)).astype(np.float32)
    inv = np.power(np.float32(ROPE_THETA), -np.arange(8, dtype=np.float32) / np.float32(8)).astype(np.float32)
    ang = (pos[:, None] * inv[None, :]).astype(np.float32)
    tab = np.concatenate([np.cos(ang), np.sin(ang)], axis=1).astype(np.float32)
    return np.ascontiguousarray(tab.reshape(nt, 128, 16).transpose(1, 0, 2))


RET_THETA = 10000.0


def rope_table_d(pos0, NT):
    nt = NT // 128
    pos = (pos0 + np.arange(nt * 128)).astype(np.float32)
    inv = (1.0 / np.power(np.float32(RET_THETA), np.linspace(0.0, 1.0, 64, dtype=np.float32))).astype(np.float32)
    ang = (pos[:, None] * inv[None, :]).astype(np.float32)
    tab = np.concatenate([np.cos(ang), np.sin(ang)], axis=1).astype(np.float32)
    return np.ascontiguousarray(tab.reshape(nt, 128, 128).transpose(1, 0, 2))


def odd_table(r, NT):
    h = np.arange(4, dtype=np.float64)
    lg = np.log(1.0 - np.power(2.0, -5.0 - h))
    l = np.arange(128, dtype=np.float64)[:, None]
    t = np.zeros((128, 32), np.float64)
    t[:, 0:4] = np.exp(lg[None, :] * (l + 1.0))
    t[:, 4:8] = np.exp(-lg[None, :] * (l + 1.0)) * (128.0 ** -0.5)
    t[:, 8:12] = np.exp(lg * 128.0)[None, :]
    for i in range(4):
        if i < r:
            t[:, 12 + 4 * i:16 + 4 * i] = np.exp(lg * float(NT * (r - 1 - i)))[None, :]
            t[:, 28 + i] = 1.0
    return t.astype(np.float32)


def attn_masks(first):
    i = np.arange(128)[:, None]
    j = np.arange(256)[None, :]
    valid = (j > i) & (j <= i + 128)
    m = np.where(valid, 0.0, A_MASK_NEG).astype(np.float32)
    m0 = m.copy()
    if first:
        m0[:, :128] = A_MASK_NEG
    return m, m0


def make_in_map(inp, x_shard, halo, cidx, NT, stages):
    need = stage_inputs(stages)
    r = cidx % 4
    m = {}
    for name in need:
        if name == "x":
            m[name] = np.ascontiguousarray(x_shard, dtype=np.float32)
        elif name == "ident":
            m[name] = np.eye(128, dtype=np.float32)
        elif name == "ones":
            m[name] = np.ones((128, 128), dtype=np.float32)
        elif name == "halo":
            m[name] = np.ascontiguousarray(halo, dtype=np.float32)
        elif name == "amask":
            m[name] = attn_masks(False)[0]
        elif name == "amask0":
            m[name] = attn_masks(r == 0)[1]
        elif name == "rope_a":
            m[name] = rope_table_a(r * NT, NT)
        elif name == "rope_d":
            m[name] = rope_table_d(r * NT, NT)
        elif name == "odd_tab":
            m[name] = odd_table(r, NT)
        elif name == "hsel":
            hs = np.zeros((128, 4), np.float32)
            if r > 0:
                hs[:, r - 1] = 1.0
            m[name] = hs
        elif name == "triu":
            m[name] = np.triu(np.ones((128, 128), np.float32))
        elif name == "trisl":
            m[name] = np.tril(np.ones((128, 128), np.float32), -1).T.copy().T if False else (np.arange(128)[:, None] > np.arange(128)[None, :]).astype(np.float32)
        else:
            base = name.rstrip("0123456789")
            idx = int(name[len(base):])
            m[name] = np.ascontiguousarray(np.asarray(inp[base][idx], dtype=np.float32).reshape(need[name]))
    return m


SEQ = 16384
BATCH = 2
FUSED = True
ALL_STAGES = [("even", 0), ("ffn", 0), ("odd", 0), ("ffn", 1), ("even", 1), ("ffn", 2), ("odd", 1), ("ffn", 3)]


def run_stages(inp, x, stages):
    S_ = x.shape[1]
    NT = S_ // 4
    nc = build_program(NT, stages)
    in_maps = []
    for cidx in range(NCORES):
        b, r = cidx // 4, cidx % 4
        halo = x[b, r * NT - 128:r * NT] if r > 0 else np.zeros((128, D), np.float32)
        in_maps.append(make_in_map(inp, x[b, r * NT:(r + 1) * NT], halo, cidx, NT, stages))
    res = run_bass_kernel_spmd(nc, in_maps, core_ids=list(range(NCORES)))
    out = np.empty_like(x)
    for cidx in range(NCORES):
        b, r = cidx // 4, cidx % 4
        out[b, r * NT:(r + 1) * NT] = res.results[cidx]["y"]
    return out


def kernel(**inputs):
    inp = {k: np.asarray(v) for k, v in inputs.items()}
    x = np.ascontiguousarray(inp["x"], dtype=np.float32)
    if FUSED:
        return run_stages(inp, x, ALL_STAGES)
    for li in range(DEPTH):
        x = run_stages(inp, x, ALL_STAGES[2 * li:2 * li + 2])
    return x
```

```python
import numpy as np
import os as _os_env
from contextlib import ExitStack
import concourse.bass as bass
import concourse.mybir as mybir
from concourse.bass_utils import run_bass_kernel_spmd

F32 = mybir.dt.float32
BF16 = mybir.dt.bfloat16
ALU = mybir.AluOpType
AF = mybir.ActivationFunctionType

D = 1024
FH = 2816
DEPTH = 4
ALPHA = (2 * DEPTH) ** 0.25
EPS = 1e-5
NCORES = 8


class Buf:
    __slots__ = ("name", "w", "rs")

    def __init__(self, name):
        self.name = name
        self.w = None
        self.rs = []


class Op:
    __slots__ = ("stream", "fn", "deps", "adeps", "needed", "sem", "val", "is_dma", "idx", "seg", "cost", "lat", "dq_index")


class _Rec:
    def __init__(self):
        self.call = None

    def __getattr__(self, name):
        def f(*a, **kw):
            assert self.call is None, "one engine instruction per op"
            self.call = (name, a, kw)
            return None
        return f


_NOWAR = bool(_os_env.environ.get('NOWAR'))
ATTACH_WAIT = _os_env.environ.get('ATTACH_WAIT', '1') == '1'


class Sched:
    STRICT = tuple(x for x in _os_env.environ.get("STRICT_ENG", "act,dve,pool").split(",") if x)
    NDMA = 6
    CHAIN = tuple(x for x in _os_env.environ.get("CHAIN_ENG", "act").split(",") if x)
    REORDER = _os_env.environ.get("REORDER", "1") == "1"

    def __init__(self, nc, es):
        self.nc = nc
        self.streams = {k: [] for k in ("pe", "act", "dve", "pool", "sp")}
        self.sem = {k: es.enter_context(nc.semaphore("pg_" + k)) for k in self.streams}
        self.dsem = {q: [es.enter_context(nc.semaphore(f"dq_{q}{i}")) for i in range(self.NDMA)]
                     for q in ("sp", "pool", "act")}
        self.dcnt = {q: 0 for q in self.dsem}
        self.dring = {q: [None] * self.NDMA for q in self.dsem}
        self.seg = 0
        self.all_ops = []
        self.last_on = {}

    @staticmethod
    def _cost(stream, name, a, kw, is_dma):
        def fsz(ap):
            sh = ap.shape
            n = 1
            for s_ in sh[1:]:
                n *= int(s_)
            return n
        try:
            if is_dma:
                o = kw.get("out")
                nbytes = fsz(o) * int(o.shape[0]) * 4
                return (1000.0 if stream == "pool" else 100.0), 2000.0 + nbytes / 150.0
            if name == "collective_compute":
                return 1000.0, 60000.0
            if stream == "pe":
                if name == "transpose":
                    return 110.0, 300.0
                rhs = kw.get("rhs")
                n = fsz(rhs)
                mult = 4.0 if rhs.dtype == F32 else 1.0
                return mult * (40.0 + 0.47 * n), 300.0
            o = kw.get("out", a[0] if a else None)
            f = fsz(o) if o is not None else 64
            if stream == "act":
                return 170.0 + 0.9 * f, 300.0
            if stream == "dve":
                return 70.0 + 0.66 * f, 300.0
            return 200.0 + 2.0 * f, 200.0
        except Exception:
            return 300.0, 200.0

    def _add(self, stream, fn, reads, writes, is_dma=False, extra=()):
        op = Op()
        op.stream = stream
        rec = _Rec()
        fn(rec)
        name_, a_, kw_ = rec.call
        op.fn = lambda e, name_=name_, a_=a_, kw_=kw_: getattr(e, name_)(*a_, **kw_)
        op.is_dma = is_dma
        op.needed = False
        op.sem = None
        op.val = 0
        op.seg = self.seg
        op.cost, op.lat = self._cost(stream, name_, a_, kw_, is_dma)
        deps = []
        for b in reads:
            if b.w is not None:
                deps.append(b.w)
        for b in writes:
            if b.w is not None:
                deps.append(b.w)
            if not _NOWAR:
                deps.extend(b.rs)
        deps.extend(extra)
        seen = set()
        dd = []
        for d in deps:
            if id(d) in seen:
                continue
            seen.add(id(d))
            dd.append(d)
        if stream in self.CHAIN and self.last_on.get(stream) is not None and self.last_on[stream].seg == self.seg:
            lo = self.last_on[stream]
            if id(lo) not in seen:
                dd.append(lo)
        self.last_on[stream] = op
        op.adeps = dd
        for b in reads:
            b.rs.append(op)
        for b in writes:
            b.w = op
            b.rs = []
        op.idx = len(self.all_ops)
        self.all_ops.append(op)
        return op

    def op(self, stream, fn, reads=(), writes=()):
        return self._add(stream, fn, reads, writes)

    def dma(self, q, out, in_, reads=(), writes=(), **kw):
        i = self.dcnt[q]
        self.dcnt[q] += 1
        slot = i % self.NDMA
        prev = self.dring[q][slot]
        extra = (prev,) if prev is not None else ()
        op = self._add(q, lambda e: e.dma_start(out=out, in_=in_, **kw), reads, writes, is_dma=True, extra=extra)
        op.sem = self.dsem[q][slot]
        op.val = 16 * (i // self.NDMA + 1)
        op.needed = True
        op.dq_index = i
        self.dring[q][slot] = op
        return op

    def barrier(self):
        self.seg += 1

    def _schedule(self):
        import heapq
        streams = {k: [] for k in self.streams}
        ops = self.all_ops
        if not self.REORDER:
            self.est_ns = 0.0
            cur_seg = 0
            fence = []
            first = {k: False for k in self.streams}
            last_dma = {q: {} for q in self.dsem}
            for op in ops:
                if op.seg != cur_seg:
                    cur_seg = op.seg
                    fence = []
                    for k, lst in streams.items():
                        for o2 in reversed(lst):
                            if not o2.is_dma:
                                fence.append(o2)
                                break
                    for q in last_dma:
                        fence.extend(last_dma[q].values())
                    first = {k: True for k in self.streams}
                if first[op.stream]:
                    first[op.stream] = False
                    op.adeps = list(op.adeps) + [f for f in fence if f is not op]
                streams[op.stream].append(op)
                if op.is_dma:
                    last_dma[op.stream][op.dq_index % self.NDMA] = op
            return streams
        nseg = self.seg + 1
        by_seg = [[] for _ in range(nseg)]
        for op in ops:
            by_seg[op.seg].append(op)
        fin = {}
        free = {k: 0.0 for k in self.streams}
        last_dma = {q: {} for q in self.dsem}
        fence = []
        tnow = 0.0
        for s in range(nseg):
            seg_ops = by_seg[s]
            if not seg_ops:
                continue
            first_in_stream = {k: True for k in self.streams}
            nun = {}
            users = {}
            for op in seg_ops:
                c = 0
                for d in op.adeps:
                    if d.seg == s:
                        c += 1
                        users.setdefault(id(d), []).append(op)
                nun[id(op)] = c
            ready = {k: [] for k in self.streams}

            def est_ready(op):
                t = tnow
                for d in op.adeps:
                    if d.seg == s:
                        f = fin[id(d)]
                        if d.stream != op.stream or d.is_dma:
                            f += d.lat
                        elif op.stream in self.STRICT:
                            f += 200.0
                        t = max(t, f)
                return t
            for op in seg_ops:
                if nun[id(op)] == 0:
                    heapq.heappush(ready[op.stream], (est_ready(op), op.idx, op))
            nleft = len(seg_ops)
            while nleft:
                best = None
                for k in self.streams:
                    if not ready[k]:
                        continue
                    if self.REORDER:
                        cand = None
                        tmp = []
                        while ready[k] and ready[k][0][0] <= free[k]:
                            tmp.append(heapq.heappop(ready[k]))
                        if tmp:
                            cand = min(tmp, key=lambda x: x[1])
                            for x in tmp:
                                if x is not cand:
                                    heapq.heappush(ready[k], x)
                            st = free[k]
                        else:
                            cand = heapq.heappop(ready[k])
                            st = cand[0]
                    else:
                        cand = min(ready[k], key=lambda x: x[1])
                        ready[k].remove(cand)
                        heapq.heapify(ready[k])
                        st = max(cand[0], free[k])
                    if best is None or (st, cand[1]) < (best[0], best[1][1]):
                        if best is not None:
                            heapq.heappush(ready[best[2]], best[1])
                        best = (st, cand, k)
                    else:
                        heapq.heappush(ready[k], cand)
                st, cand, k = best
                op = cand[2]
                if first_in_stream[k]:
                    first_in_stream[k] = False
                    if fence:
                        op.adeps = list(op.adeps) + [f for f in fence if f is not op]
                free[k] = st + op.cost
                fin[id(op)] = st + op.cost
                streams[k].append(op)
                if op.is_dma:
                    last_dma[k][op.dq_index % self.NDMA] = op
                nleft -= 1
                for u in users.get(id(op), ()):
                    nun[id(u)] -= 1
                    if nun[id(u)] == 0:
                        heapq.heappush(ready[u.stream], (est_ready(u), u.idx, u))
            fence = []
            for k, lst in streams.items():
                for op in reversed(lst):
                    if not op.is_dma:
                        fence.append(op)
                        break
            for q in last_dma:
                fence.extend(last_dma[q].values())
            tnow = max(list(free.values()) + [fin[id(f)] + f.lat for f in fence])
            for k in free:
                free[k] = tnow
        self.est_ns = max(free.values())
        return streams

    def emit(self, final_ops=()):
        streams = self._schedule()
        self.streams = streams
        for k, lst in streams.items():
            for op in lst:
                dd = []
                for d in op.adeps:
                    if (not d.is_dma) and d.stream == k and k not in self.STRICT:
                        continue
                    dd.append(d)
                    d.needed = True
                op.deps = dd
        for k, lst in streams.items():
            cnt = 0
            for op in lst:
                if op.is_dma:
                    continue
                if op.needed:
                    cnt += 1
                    op.sem = self.sem[k]
                    op.val = cnt
        print('SCHED ops', {k: len(v) for k, v in streams.items()}, 'semmax',
              {k: max([o.val for o in v if not o.is_dma] + [0]) for k, v in streams.items()},
              'est_ms', round(self.est_ns / 1e6, 3), 'busy_ms', {k: round(sum(o.cost for o in v) / 1e6, 3) for k, v in streams.items()}, flush=True)
        with self.nc.Block() as block:
            def run(k):
                def body(e):
                    waited = {}

                    def wait(d):
                        key = id(d.sem)
                        if waited.get(key, 0) < d.val:
                            e.wait_ge(d.sem, d.val)
                            waited[key] = d.val
                    for op in streams[k]:
                        pend = []
                        for d in op.deps:
                            key = id(d.sem)
                            if waited.get(key, 0) < d.val:
                                pend = [p for p in pend if id(p.sem) != key or p.val > d.val]
                                if not any(id(p.sem) == key for p in pend):
                                    pend.append(d)
                        attach = None
                        if ATTACH_WAIT and pend and not op.is_dma and k in ('pe', 'act', 'dve'):
                            attach = pend.pop()
                        for d in pend:
                            wait(d)
                        ins = op.fn(e)
                        if attach is not None:
                            ins._wait_ge(attach.sem, attach.val)
                            waited[id(attach.sem)] = attach.val
                        if op.is_dma:
                            ins.then_inc(op.sem, 16)
                        elif op.needed:
                            ins.then_inc(op.sem, 1)
                    if k == "sp":
                        for d in final_ops:
                            wait(d)
                return body
            block.tensor(run("pe"))
            block.scalar(run("act"))
            block.vector(run("dve"))
            block.gpsimd(run("pool"))
            block.sync(run("sp"))


class Ctx:
    pass


_UID = [0]


def _sb(c, es, name, shape, dt):
    _UID[0] += 1
    return es.enter_context(c.nc.sbuf_tensor(f"{name}_{_UID[0]}", list(shape), dt))


class PsumPool:
    def __init__(self, c, n=8):
        self.t = [c.es.enter_context(c.nc.psum_tensor(f"ps{i}", [128, 512], F32)) for i in range(n)]
        self.b = [Buf(f"ps{i}") for i in range(n)]
        self.i = 0
        self.n = n

    def get(self):
        i = self.i
        self.i = (self.i + 1) % self.n
        return self.t[i], self.b[i]


def load_bcast_row(c, q, dst_tile, dst_buf, dram_ap_row):
    n = dst_tile.shape[-1]
    return c.S.dma(q, dst_tile[:, :], dram_ap_row.to_broadcast([128, n]), writes=[dst_buf])


def emit_xT(c, xT, xT_buf, tslot, x_tile, x_buf, xbf, xbf_buf):
    S = c.S
    S.op("act", lambda e: e.copy(out=xbf[:, :], in_=x_tile[:, :]), reads=[x_buf], writes=[xbf_buf])
    pt, pb = c.ps.get()
    ptb = pt[:, :].bitcast(BF16)
    for k in range(8):
        S.op("pe", lambda e, k=k: e.transpose(out=ptb[:, k * 128:(k + 1) * 128], in_=xbf[:, k * 128:(k + 1) * 128],
                                               identity=c.ident_bf[:, :]),
             reads=[xbf_buf, c.ident_buf], writes=[pb])
    S.op("dve", lambda e: e.tensor_copy(out=xT[:, :, tslot * 128:(tslot + 1) * 128],
                                        in_=ptb.rearrange("p (k t) -> p k t", k=8)),
         reads=[pb], writes=[xT_buf])


def emit_ln_epilogue(c, es_bufs, ps_halves, x_old, x_old_buf, g_t, b_t, gb_buf, out_tile, out_buf):
    S = c.S
    v, v_buf, st, mv, sm_buf = es_bufs
    for hf in range(2):
        pt, pb = ps_halves[hf]
        S.op("dve", lambda e, hf=hf, pt=pt: e.scalar_tensor_tensor(
            out=v[:, hf * 512:(hf + 1) * 512], in0=x_old[:, hf * 512:(hf + 1) * 512], scalar=float(ALPHA),
            in1=pt[:, :], op0=ALU.mult, op1=ALU.add), reads=[x_old_buf, pb], writes=[v_buf])
    ln_core(c, v, v_buf, st, mv, sm_buf, g_t, b_t, gb_buf, out_tile, out_buf)


def ln_core(c, v, v_buf, st, mv, sm_buf, g_t, b_t, gb_buf, out_tile, out_buf):
    S = c.S
    for hf in range(2):
        S.op("dve", lambda e, hf=hf: e.bn_stats(out=st[:, hf * 6:(hf + 1) * 6], in_=v[:, hf * 512:(hf + 1) * 512]),
             reads=[v_buf], writes=[sm_buf])
    S.op("dve", lambda e: e.bn_aggr(out=mv[:, 0:2], in_=st[:, 0:12]), reads=[sm_buf], writes=[sm_buf])
    S.op("dve", lambda e: e.tensor_scalar_add(out=mv[:, 2:3], in0=mv[:, 1:2], scalar1=float(EPS)),
         reads=[sm_buf], writes=[sm_buf])
    S.op("act", lambda e: e.activation(out=mv[:, 2:3], in_=mv[:, 2:3], func=AF.Sqrt), reads=[sm_buf], writes=[sm_buf])
    S.op("dve", lambda e: e.reciprocal(out=mv[:, 2:3], in_=mv[:, 2:3]), reads=[sm_buf], writes=[sm_buf])
    S.op("dve", lambda e: e.scalar_tensor_tensor(out=mv[:, 3:4], in0=mv[:, 0:1], scalar=-1.0, in1=mv[:, 2:3],
                                                 op0=ALU.mult, op1=ALU.mult), reads=[sm_buf], writes=[sm_buf])
    S.op("act", lambda e: e.activation(out=v[:, :], in_=v[:, :], func=AF.Identity, bias=mv[:, 3:4], scale=mv[:, 2:3]),
         reads=[v_buf, sm_buf], writes=[v_buf])
    S.op("dve", lambda e: e.tensor_tensor(out=v[:, :], in0=v[:, :], in1=g_t[:, :], op=ALU.mult),
         reads=[v_buf, gb_buf], writes=[v_buf])
    S.op("dve", lambda e: e.tensor_tensor(out=out_tile[:, :], in0=v[:, :], in1=b_t[:, :], op=ALU.add),
         reads=[v_buf, gb_buf], writes=[out_buf])


def emit_ffn(c, layer, src, dst, NT):
    S, nc = c.S, c.nc
    T = min(1024, NT)
    ntile = T // 128
    nblk = T // 512
    outs = []
    with ExitStack() as es:
        xT = _sb(c, es, "f_xT", [128, 8, T], BF16)
        xT_buf = Buf("f_xT")
        hT = _sb(c, es, "f_hT", [128, 22, T], BF16)
        hT_bufs = [Buf(f"f_hT{j}") for j in range(22)]
        NWB = 2
        wg = [_sb(c, es, f"f_wg{i}", [128, 8, 512], BF16) for i in range(NWB)]
        wu = [_sb(c, es, f"f_wu{i}", [128, 8, 512], BF16) for i in range(NWB)]
        wg_b = [Buf(f"f_wg{i}") for i in range(NWB)]
        wu_b = [Buf(f"f_wu{i}") for i in range(NWB)]
        wd = _sb(c, es, "f_wd", [128, 22, 1024], BF16)
        jgroups = [(j0, min(4, 22 - j0)) for j0 in range(0, 22, 4)]
        wd_b = [Buf(f"f_wd{g}") for g in range(len(jgroups))]
        NXB = 2
        xin = [_sb(c, es, f"f_xin{i}", [128, 1024], F32) for i in range(NXB)]
        xin_b = [Buf(f"f_xin{i}") for i in range(NXB)]
        xbf = [_sb(c, es, f"f_xbf{i}", [128, 1024], BF16) for i in range(NXB)]
        xbf_b = [Buf(f"f_xbf{i}") for i in range(NXB)]
        sg = [_sb(c, es, f"f_sg{i}", [128, 512], F32) for i in range(2)]
        sg_b = [Buf(f"f_sg{i}") for i in range(2)]
        v = [_sb(c, es, f"f_v{i}", [128, 1024], F32) for i in range(2)]
        v_b = [Buf(f"f_v{i}") for i in range(2)]
        st = [_sb(c, es, f"f_st{i}", [128, 12], F32) for i in range(2)]
        mv = [_sb(c, es, f"f_mv{i}", [128, 4], F32) for i in range(2)]
        sm_b = [Buf(f"f_sm{i}") for i in range(2)]
        xo = [_sb(c, es, f"f_xo{i}", [128, 1024], F32) for i in range(2)]
        xo_b = [Buf(f"f_xo{i}") for i in range(2)]
        g_t = _sb(c, es, "f_g", [128, 1024], F32)
        b_t = _sb(c, es, "f_b", [128, 1024], F32)
        gb_buf = Buf("f_gb")
        load_bcast_row(c, "sp", g_t, gb_buf, c.dram[f"ln_ffn_g{layer}"])
        load_bcast_row(c, "sp", b_t, gb_buf, c.dram[f"ln_ffn_b{layer}"])
        Wg = c.dram[f"ffn_w_gate{layer}"]
        Wu = c.dram[f"ffn_w_up{layer}"]
        Wd = c.dram[f"ffn_w_down{layer}"]
        xcnt = 0
        wcnt = 0
        ecnt = 0
        first = True
        for g0 in range(0, NT, T):
            for t in range(ntile):
                i = xcnt % NXB
                xcnt += 1
                r0 = g0 + t * 128
                S.dma("sp", xin[i][:, :], src[r0:r0 + 128, :], writes=[xin_b[i]])
                emit_xT(c, xT, xT_buf, t, xin[i], xin_b[i], xbf[i], xbf_b[i])
            for gi, (j0, nj) in enumerate(jgroups):
                wi = wcnt % NWB
                wcnt += 1
                c0 = j0 * 128
                ncol = nj * 128
                S.dma("pool", wg[wi][:, :, 0:ncol], Wg[:, c0:c0 + ncol].rearrange("(k p) n -> p k n", p=128),
                      writes=[wg_b[wi]])
                S.dma("pool", wu[wi][:, :, 0:ncol], Wu[:, c0:c0 + ncol].rearrange("(k p) n -> p k n", p=128),
                      writes=[wu_b[wi]])
                if first:
                    S.dma("pool", wd[:, j0:j0 + nj, :], Wd[c0:c0 + ncol, :].rearrange("(j p) n -> p j n", p=128),
                          writes=[wd_b[gi]])
                for jj in range(nj):
                    j = j0 + jj
                    for nb in range(nblk):
                        pg, pgb = c.ps.get()
                        pu, pub = c.ps.get()
                        for k in range(8):
                            S.op("pe", lambda e, k=k, pg=pg, jj=jj, nb=nb, wi=wi: e.matmul(
                                pg[:, :], lhsT=wg[wi][:, k, jj * 128:(jj + 1) * 128], rhs=xT[:, k, nb * 512:(nb + 1) * 512],
                                start=(k == 0), stop=(k == 7)), reads=[wg_b[wi], xT_buf], writes=[pgb])
                        for k in range(8):
                            S.op("pe", lambda e, k=k, pu=pu, jj=jj, nb=nb, wi=wi: e.matmul(
                                pu[:, :], lhsT=wu[wi][:, k, jj * 128:(jj + 1) * 128], rhs=xT[:, k, nb * 512:(nb + 1) * 512],
                                start=(k == 0), stop=(k == 7)), reads=[wu_b[wi], xT_buf], writes=[pub])
                        si = (j * nblk + nb) % 2
                        S.op("act", lambda e, si=si, pg=pg: e.activation(out=sg[si][:, :], in_=pg[:, :], func=AF.Silu),
                             reads=[pgb], writes=[sg_b[si]])
                        S.op("dve", lambda e, si=si, pu=pu, j=j, nb=nb: e.tensor_tensor(
                            out=hT[:, j, nb * 512:(nb + 1) * 512], in0=sg[si][:, :], in1=pu[:, :], op=ALU.mult),
                            reads=[sg_b[si], pub], writes=[hT_bufs[j]])
            first = False
            for t in range(ntile):
                halves = [c.ps.get(), c.ps.get()]
                for j in range(22):
                    for hf in range(2):
                        ph, phb = halves[hf]
                        S.op("pe", lambda e, ph=ph, j=j, hf=hf, t=t: e.matmul(
                            ph[:, :], lhsT=hT[:, j, t * 128:(t + 1) * 128], rhs=wd[:, j, hf * 512:(hf + 1) * 512],
                            start=(j == 0), stop=(j == 21)), reads=[hT_bufs[j], wd_b[j // 4]], writes=[phb])
                i = ecnt % 2
                ecnt += 1
                r0 = g0 + t * 128
                S.dma("sp", xin[i][:, :], src[r0:r0 + 128, :], writes=[xin_b[i]])
                emit_ln_epilogue(c, (v[i], v_b[i], st[i], mv[i], sm_b[i]), halves, xin[i], xin_b[i],
                                 g_t, b_t, gb_buf, xo[i], xo_b[i])
                outs.append(S.dma("sp", dst[r0:r0 + 128, :], xo[i][:, :], reads=[xo_b[i]]))
    S.barrier()
    return outs


A_MASK_NEG = -30000.0
import os as _os
DBG_STOP = int(_os.environ.get("DBG_STOP", "0"))


class _Stop(Exception):
    pass


def _chk(n):
    if DBG_STOP == n:
        raise _Stop()


def ps_bf(pt):
    return pt[:, :].bitcast(BF16)


def emit_even(c, e, layer, src, dst, halo, NT):
    S, nc = c.S, c.nc
    GT = 512
    ngroups = NT // GT
    outs = []
    dr = c.dram
    with ExitStack() as es:
      try:
          sb = lambda name, shape, dt: _sb(c, es, "e_" + name, shape, dt)
          Win = sb("win", [128, 8, 1792], BF16); Win_b = Buf("win")
          Wout = sb("wout", [128, 8, 1024], BF16); Wout_b = Buf("wout")
          dg = sb("dg", [128, 4, 31, 128], BF16); dg_b = Buf("dg")
          wk = sb("wk", [31, 512], F32); wk_b = Buf("wk")
          wcol = sb("wcol", [128, 4, 32], F32); wcol_b = Buf("wcol")
          cvec = sb("cvec", [128, 16], F32); cvec_b = Buf("cvec")
          sinks = sb("sinks", [128, 8], F32); sinks_b = Buf("sinks")
          amask = sb("amask", [128, 256], F32); amask0 = sb("amask0", [128, 256], F32); am_b = Buf("amask")
          rope = sb("rope", [128, NT // 128 + 1, 16], F32); rope_b = Buf("rope")
          g_t = sb("g", [128, 1024], F32); b_t = sb("b", [128, 1024], F32); gb_buf = Buf("gb")
          xT = sb("xT", [128, 8, GT], BF16); xT_b = Buf("xT")
          xTh = sb("xTh", [128, 8, 128], BF16); xTh_b = Buf("xTh")
          hbuf = [sb(f"hbuf{i}", [128, 4, 32 + GT], BF16) for i in range(2)]
          hb_body = [Buf(f"hb{i}") for i in range(2)]
          hb_pre = [Buf(f"hp{i}") for i in range(2)]
          cv = sb("cv", [128, 4, GT], F32); cv_b = [Buf(f"cv{i}") for i in range(4)]
          sq = sb("sq", [128, GT], F32); sq_b = Buf("sq")
          tg = [sb(f"tg{i}", [128, GT], F32) for i in range(2)]; tg_b = [Buf(f"tg{i}") for i in range(2)]
          mean = sb("mean", [128, GT], F32); msq = sb("msq", [128, GT], F32); rstd = sb("rstd", [128, GT], F32)
          stat_b = Buf("stat")
          ta = [sb(f"ta{i}", [128, GT], F32) for i in range(2)]; ta_b = [Buf(f"ta{i}") for i in range(2)]
          yT = sb("yT", [128, 8, GT], BF16); yT_b = [Buf(f"yT{i}") for i in range(8)]
          xin = [sb(f"xin{i}", [128, 1024], F32) for i in range(2)]; xin_b = [Buf(f"xin{i}") for i in range(2)]
          xbf = [sb(f"xbf{i}", [128, 1024], BF16) for i in range(2)]; xbf_b = [Buf(f"xbf{i}") for i in range(2)]
          qb = sb("qb", [128, 8, 64], BF16); qb_b = Buf("qb")
          kb = sb("kb", [128, 2, 64], BF16); kb_b = Buf("kb")
          rt = [sb(f"rt{i}", [128, 8, 8], F32) for i in range(2)]; rt_b = [Buf(f"rt{i}") for i in range(2)]
          NR = 3
          vb = [sb(f"vb{i}", [128, 128], BF16) for i in range(NR)]; vb_b = [Buf(f"vb{i}") for i in range(NR)]
          kT = [sb(f"kT{i}", [128, 128], BF16) for i in range(NR)]; kT_b = [Buf(f"kT{i}") for i in range(NR)]
          qT = sb("qT", [128, 4, 128], BF16); qT_b = Buf("qT")
          sm = sb("sm", [128, 8, 256], F32); sm_b = [Buf(f"sm{i}") for i in range(4)]
          pb = sb("pb", [128, 8, 256], BF16); pb_b = Buf("pb")
          pT = sb("pT", [128, 16, 128], BF16); pT_b = [Buf(f"pT{i}") for i in range(2)]
          att = sb("att", [128, 40], F32); att_b = Buf("att")
          ob = sb("ob", [128, 8, 64], BF16); ob_b = Buf("ob")
          v = [sb(f"v{i}", [128, 1024], F32) for i in range(2)]; v_b = [Buf(f"v{i}") for i in range(2)]
          st = [sb(f"st{i}", [128, 12], F32) for i in range(2)]
          mv = [sb(f"mv{i}", [128, 4], F32) for i in range(2)]; smm_b = [Buf(f"smm{i}") for i in range(2)]
          xo = [sb(f"xo{i}", [128, 1024], F32) for i in range(2)]; xo_b = [Buf(f"xo{i}") for i in range(2)]

          S.dma("pool", Win[:, :, :], dr[f"ev_w_in{e}"].rearrange("(k p) n -> p k n", p=128), writes=[Win_b])
          S.dma("pool", Wout[:, :, :], dr[f"ev_w_out{e}"].rearrange("(k p) n -> p k n", p=128), writes=[Wout_b])
          S.dma("sp", wk[:, :], dr[f"ev_dw_w{e}"], writes=[wk_b])
          S.dma("sp", cvec[:, 0:4], dr[f"ev_dw_b{e}"].rearrange("o (c p) -> p (o c)", p=128), writes=[cvec_b], allow_slow_non_contiguous=True)
          S.dma("sp", cvec[:, 4:8], dr[f"ev_cn_g{e}"].rearrange("o (c p) -> p (o c)", p=128), writes=[cvec_b], allow_slow_non_contiguous=True)
          S.dma("sp", cvec[:, 8:12], dr[f"ev_cn_b{e}"].rearrange("o (c p) -> p (o c)", p=128), writes=[cvec_b], allow_slow_non_contiguous=True)
          load_bcast_row(c, "sp", sinks, sinks_b, dr[f"ev_sinks{e}"])
          S.dma("sp", amask[:, :], dr["amask"], writes=[am_b])
          S.dma("sp", amask0[:, :], dr["amask0"], writes=[am_b])
          S.dma("sp", rope[:, :, :], dr["rope_a"], writes=[rope_b])
          load_bcast_row(c, "sp", g_t, gb_buf, dr[f"ln_mix_g{layer}"])
          load_bcast_row(c, "sp", b_t, gb_buf, dr[f"ln_mix_b{layer}"])
          S.op("dve", lambda en: en.tensor_scalar_mul(out=cvec[:, 4:12], in0=cvec[:, 4:12], scalar1=0.5),
               reads=[cvec_b], writes=[cvec_b])
          for cc in range(4):
              pt, ptb_ = c.ps.get()
              S.op("pe", lambda en, cc=cc, pt=pt: en.transpose(out=pt[:, 0:31], in_=wk[0:31, cc * 128:(cc + 1) * 128],
                                                              identity=c.ident_f[0:31, 0:31]),
                   reads=[wk_b, c.identf_buf], writes=[ptb_])
              S.op("dve", lambda en, cc=cc, pt=pt: en.tensor_copy(out=wcol[:, cc, 0:31], in_=pt[:, 0:31]),
                   reads=[ptb_], writes=[wcol_b])
          for cc in range(4):
              for k in range(31):
                  S.op("dve", lambda en, cc=cc, k=k: en.tensor_scalar(
                      out=dg[:, cc, k, :], in0=c.ident_f[:, :], scalar1=wcol[:, cc, k:k + 1], scalar2=0.5,
                      op0=ALU.mult, op1=ALU.mult), reads=[wcol_b, c.identf_buf], writes=[dg_b])

          _chk(1)
          xcnt = [0]

          def load_xT(row0, dstT, dstT_b, slot, src_ap):
              i = xcnt[0] % 2
              xcnt[0] += 1
              S.dma("sp", xin[i][:, :], src_ap[row0:row0 + 128, :], writes=[xin_b[i]])
              emit_xT(c, dstT, dstT_b, slot, xin[i], xin_b[i], xbf[i], xbf_b[i])

          def glu_chunk(xTsrc, xTsrc_b, ncols, hb, hb_buf, col0):
              for cc in range(4):
                  pa, pab = c.ps.get()
                  pg, pgb = c.ps.get()
                  for k in range(8):
                      S.op("pe", lambda en, k=k, cc=cc, pa=pa: en.matmul(
                          pa[:, 0:ncols], lhsT=Win[:, k, 768 + cc * 128:768 + (cc + 1) * 128], rhs=xTsrc[:, k, 0:ncols],
                          start=(k == 0), stop=(k == 7)), reads=[Win_b, xTsrc_b], writes=[pab])
                  for k in range(8):
                      S.op("pe", lambda en, k=k, cc=cc, pg=pg: en.matmul(
                          pg[:, 0:ncols], lhsT=Win[:, k, 1280 + cc * 128:1280 + (cc + 1) * 128], rhs=xTsrc[:, k, 0:ncols],
                          start=(k == 0), stop=(k == 7)), reads=[Win_b, xTsrc_b], writes=[pgb])
                  ti = cc % 2
                  S.op("act", lambda en, ti=ti, pg=pg: en.activation(out=tg[ti][:, 0:ncols], in_=pg[:, 0:ncols],
                                                                      func=AF.Tanh, scale=0.5),
                       reads=[pgb], writes=[tg_b[ti]])
                  S.op("dve", lambda en, ti=ti, pa=pa, cc=cc: en.scalar_tensor_tensor(
                      out=hb[:, cc, col0:col0 + ncols], in0=tg[ti][:, 0:ncols], scalar=1.0, in1=pa[:, 0:ncols],
                      op0=ALU.add, op1=ALU.mult), reads=[tg_b[ti], pab], writes=[hb_buf])

          def kv_tile(xTsrc, xTsrc_b, col0, ring_i, tile_idx, with_q):
              pkv, pkvb = c.ps.get()
              for k in range(8):
                  S.op("pe", lambda en, k=k: en.matmul(pkv[:, 0:256], lhsT=xTsrc[:, k, col0:col0 + 128],
                                                       rhs=Win[:, k, 512:768], start=(k == 0), stop=(k == 7)),
                       reads=[Win_b, xTsrc_b], writes=[pkvb])
              if with_q:
                  pq, pqb = c.ps.get()
                  for k in range(8):
                      S.op("pe", lambda en, k=k: en.matmul(pq[:, :], lhsT=xTsrc[:, k, col0:col0 + 128],
                                                           rhs=Win[:, k, 0:512], start=(k == 0), stop=(k == 7)),
                           reads=[Win_b, xTsrc_b], writes=[pqb])

              def rope_apply(s3, psrc_b, dstt, dst_b, hs):
                  nh = len(hs)
                  H = int(np.prod(hs))
                  full = [128] + list(hs)
                  cs = rope[:, tile_idx, 0:8]
                  sn = rope[:, tile_idx, 8:16]
                  for _ in range(nh):
                      cs = cs.unsqueeze(1)
                      sn = sn.unsqueeze(1)
                  cs = cs.to_broadcast(full + [8])
                  sn = sn.to_broadcast(full + [8])
                  if nh == 1:
                      r0, r1 = rt[0][:, 0:H, :], rt[1][:, 0:H, :]
                  else:
                      r0 = rt[0][:, 0:H, :].rearrange("p (a b) d -> p a b d", a=hs[0])
                      r1 = rt[1][:, 0:H, :].rearrange("p (a b) d -> p a b d", a=hs[0])
                  sl = (slice(None),) * (1 + nh)
                  t1, t2 = s3[sl + (slice(0, 8),)], s3[sl + (slice(8, 16),)]
                  S.op("dve", lambda en: en.tensor_tensor(out=r0, in0=t1, in1=cs, op=ALU.mult),
                       reads=[psrc_b, rope_b], writes=[rt_b[0]])
                  S.op("dve", lambda en: en.tensor_tensor(out=r1, in0=t2, in1=sn, op=ALU.mult),
                       reads=[psrc_b, rope_b], writes=[rt_b[1]])
                  S.op("dve", lambda en: en.tensor_tensor(out=dstt[sl + (slice(0, 8),)], in0=r0, in1=r1, op=ALU.subtract),
                       reads=[rt_b[0], rt_b[1]], writes=[dst_b])
                  S.op("dve", lambda en: en.tensor_tensor(out=r0, in0=t2, in1=cs, op=ALU.mult),
                       reads=[psrc_b, rope_b], writes=[rt_b[0]])
                  S.op("dve", lambda en: en.tensor_tensor(out=r1, in0=t1, in1=sn, op=ALU.mult),
                       reads=[psrc_b, rope_b], writes=[rt_b[1]])
                  S.op("dve", lambda en: en.tensor_tensor(out=dstt[sl + (slice(8, 16),)], in0=r0, in1=r1, op=ALU.add),
                       reads=[rt_b[0], rt_b[1]], writes=[dst_b])
                  S.op("act", lambda en: en.copy(out=dstt[sl + (slice(16, 64),)], in_=s3[sl + (slice(16, 64),)]),
                       reads=[psrc_b], writes=[dst_b])

              rope_apply(pkv[:, 0:128].rearrange("p (h d) -> p h d", h=2), pkvb, kb[:, :, :], kb_b, [2])
              S.op("act", lambda en: en.copy(out=vb[ring_i][:, :], in_=pkv[:, 128:256]), reads=[pkvb], writes=[vb_b[ring_i]])
              pt, ptb_ = c.ps.get()
              ptb = ps_bf(pt)
              S.op("pe", lambda en: en.transpose(out=ptb[:, 0:128], in_=kb[:, :, :].rearrange("p h d -> p (h d)"),
                                                 identity=c.ident_bf[:, :]), reads=[kb_b, c.ident_buf], writes=[ptb_])
              S.op("act", lambda en: en.copy(out=kT[ring_i][:, :], in_=ptb[:, 0:128]), reads=[ptb_], writes=[kT_b[ring_i]])
              if with_q:
                  rope_apply(pq[:, :].rearrange("p (g j d) -> p g j d", g=2, j=4), pqb,
                             qb[:, :, :].rearrange("p (j g) d -> p g j d", g=2), qb_b, [2, 4])
                  pt2, pt2b_ = c.ps.get()
                  pt2b = ps_bf(pt2)
                  qflat = qb[:, :, :].rearrange("p h d -> p (h d)")
                  for j in range(4):
                      S.op("pe", lambda en, j=j: en.transpose(out=pt2b[:, j * 128:(j + 1) * 128],
                                                              in_=qflat[:, j * 128:(j + 1) * 128], identity=c.ident_bf[:, :]),
                           reads=[qb_b, c.ident_buf], writes=[pt2b_])
                  S.op("dve", lambda en: en.tensor_copy(out=qT[:, :, :], in_=pt2b[:, 0:512].rearrange("p (j t) -> p j t", j=4)),
                       reads=[pt2b_], writes=[qT_b])

          _chk(2)
          load_xT(0, xTh, xTh_b, 0, halo)
          glu_chunk(xTh, xTh_b, 128, hbuf[1], hb_body[1], 32 + GT - 128)
          kv_tile(xTh, xTh_b, 0, (NR - 1), 0, False)
          _chk(3)
          blk_global = 0
          for g in range(ngroups):
              hb = hbuf[g % 2]
              hprev = hbuf[(g + 1) % 2]
              for t in range(4):
                  load_xT(g * GT + t * 128, xT, xT_b, t, src)
              S.op("act", lambda en, hb=hb, hprev=hprev: en.copy(out=hb[:, :, 2:32], in_=hprev[:, :, GT + 2:GT + 32]),
                   reads=[hb_body[(g + 1) % 2]], writes=[hb_pre[g % 2]])
              glu_chunk(xT, xT_b, GT, hb, hb_body[g % 2], 32)
              _chk(4)
              for cc in range(4):
                  pc, pcb = c.ps.get()
                  for k in range(31):
                      S.op("pe", lambda en, cc=cc, k=k, pc=pc, hb=hb: en.matmul(
                          pc[:, :], lhsT=dg[:, cc, k, :], rhs=hb[:, cc, 2 + k:2 + k + GT], start=(k == 0), stop=(k == 30)),
                          reads=[dg_b, hb_body[g % 2], hb_pre[g % 2]], writes=[pcb])
                  S.op("act", lambda en, cc=cc, pc=pc: en.activation(out=cv[:, cc, :], in_=pc[:, :], func=AF.Identity,
                                                                      bias=cvec[:, cc:cc + 1], scale=1.0),
                       reads=[pcb, cvec_b], writes=[cv_b[cc]])
              _chk(5)
              p1, p1b = c.ps.get()
              p2, p2b = c.ps.get()
              for cc in range(4):
                  S.op("pe", lambda en, cc=cc: en.matmul(p1[:, :], lhsT=c.ones_f[:, :], rhs=cv[:, cc, :],
                                                         start=(cc == 0), stop=(cc == 3)),
                       reads=[cv_b[cc], c.ones_buf], writes=[p1b])
              for cc in range(4):
                  S.op("act", lambda en, cc=cc: en.activation(out=sq[:, :], in_=cv[:, cc, :], func=AF.Square),
                       reads=[cv_b[cc]], writes=[sq_b])
                  S.op("pe", lambda en, cc=cc: en.matmul(p2[:, :], lhsT=c.ones_f[:, :], rhs=sq[:, :],
                                                         start=(cc == 0), stop=(cc == 3)),
                       reads=[sq_b, c.ones_buf], writes=[p2b])
              S.op("dve", lambda en: en.tensor_scalar_mul(out=mean[:, :], in0=p1[:, :], scalar1=1.0 / 512.0),
                   reads=[p1b], writes=[stat_b])
              S.op("dve", lambda en: en.tensor_tensor(out=msq[:, :], in0=mean[:, :], in1=mean[:, :], op=ALU.mult),
                   reads=[stat_b], writes=[stat_b])
              S.op("dve", lambda en: en.scalar_tensor_tensor(out=rstd[:, :], in0=p2[:, :], scalar=1.0 / 512.0, in1=msq[:, :],
                                                             op0=ALU.mult, op1=ALU.subtract), reads=[p2b, stat_b], writes=[stat_b])
              S.op("dve", lambda en: en.tensor_scalar_add(out=rstd[:, :], in0=rstd[:, :], scalar1=float(EPS)),
                   reads=[stat_b], writes=[stat_b])
              S.op("act", lambda en: en.activation(out=rstd[:, :], in_=rstd[:, :], func=AF.Sqrt), reads=[stat_b], writes=[stat_b])
              S.op("dve", lambda en: en.reciprocal(out=rstd[:, :], in_=rstd[:, :]), reads=[stat_b], writes=[stat_b])
              for cc in range(4):
                  i = cc % 2
                  S.op("dve", lambda en, cc=cc, i=i: en.tensor_tensor(out=ta[i][:, :], in0=cv[:, cc, :], in1=mean[:, :],
                                                                     op=ALU.subtract), reads=[cv_b[cc], stat_b], writes=[ta_b[i]])
                  S.op("dve", lambda en, i=i: en.tensor_tensor(out=ta[i][:, :], in0=ta[i][:, :], in1=rstd[:, :], op=ALU.mult),
                       reads=[ta_b[i], stat_b], writes=[ta_b[i]])
                  S.op("act", lambda en, cc=cc, i=i: en.activation(out=ta[i][:, :], in_=ta[i][:, :], func=AF.Identity,
                                                                    bias=cvec[:, 8 + cc:9 + cc], scale=cvec[:, 4 + cc:5 + cc]),
                       reads=[ta_b[i], cvec_b], writes=[ta_b[i]])
                  S.op("act", lambda en, i=i: en.activation(out=tg[i][:, :], in_=ta[i][:, :], func=AF.Tanh),
                       reads=[ta_b[i]], writes=[tg_b[i]])
                  S.op("dve", lambda en, cc=cc, i=i: en.scalar_tensor_tensor(
                      out=yT[:, 4 + cc, :], in0=tg[i][:, :], scalar=1.0, in1=ta[i][:, :], op0=ALU.add, op1=ALU.mult),
                      reads=[tg_b[i], ta_b[i]], writes=[yT_b[4 + cc]])
              _chk(6)
              for t in range(4):
                  bi = blk_global
                  blk_global += 1
                  cur = bi % NR
                  prv = (bi - 1) % NR
                  kv_tile(xT, xT_b, t * 128, cur, bi + 1, True)
                  msk = amask0 if bi == 0 else amask
                  for bank in range(4):
                      pscr, pscb = c.ps.get()
                      for hh in range(2):
                          h = bank * 2 + hh
                          gk = h // 4
                          lq = qT[gk * 64:gk * 64 + 64, h % 4, :]
                          S.op("pe", lambda en, pscr=pscr, hh=hh, lq=lq, gk=gk, prv=prv: en.matmul(
                              pscr[:, hh * 256:hh * 256 + 128], lhsT=lq, rhs=kT[prv][gk * 64:gk * 64 + 64, :],
                              start=True, stop=True), reads=[qT_b, kT_b[prv]], writes=[pscb])
                          S.op("pe", lambda en, pscr=pscr, hh=hh, lq=lq, gk=gk, cur=cur: en.matmul(
                              pscr[:, hh * 256 + 128:hh * 256 + 256], lhsT=lq, rhs=kT[cur][gk * 64:gk * 64 + 64, :],
                              start=True, stop=True), reads=[qT_b, kT_b[cur]], writes=[pscb])
                      S.op("dve", lambda en, bank=bank, pscr=pscr, msk=msk: en.tensor_tensor(
                          out=sm[:, bank * 2:bank * 2 + 2, :], in0=pscr[:, :].rearrange("p (h k) -> p h k", h=2),
                          in1=msk[:, :].unsqueeze(1).to_broadcast([128, 2, 256]), op=ALU.add),
                          reads=[pscb, am_b], writes=[sm_b[bank]])
                  S.op("dve", lambda en: en.tensor_reduce(out=att[:, 0:8], in_=sm[:, :, :], axis=mybir.AxisListType.X, op=ALU.max),
                       reads=sm_b, writes=[att_b])
                  S.op("dve", lambda en: en.scalar_tensor_tensor(out=att[:, 0:8], in0=att[:, 0:8], scalar=0.125, in1=sinks[:, :],
                                                                 op0=ALU.mult, op1=ALU.max), reads=[att_b, sinks_b], writes=[att_b])
                  S.op("dve", lambda en: en.tensor_scalar_mul(out=att[:, 8:16], in0=att[:, 0:8], scalar1=-1.0),
                       reads=[att_b], writes=[att_b])
                  for h in range(8):
                      S.op("act", lambda en, h=h: en.activation(out=pb[:, h, :], in_=sm[:, h, :], func=AF.Exp,
                                                                bias=att[:, 8 + h:9 + h], scale=0.125, accum_out=att[:, 16 + h:17 + h]),
                           reads=[sm_b[h // 2], att_b], writes=[pb_b, att_b])
                  S.op("dve", lambda en: en.tensor_tensor(out=att[:, 24:32], in0=sinks[:, :], in1=att[:, 0:8], op=ALU.subtract),
                       reads=[att_b, sinks_b], writes=[att_b])
                  S.op("act", lambda en: en.activation(out=att[:, 24:32], in_=att[:, 24:32], func=AF.Exp), reads=[att_b], writes=[att_b])
                  S.op("dve", lambda en: en.tensor_tensor(out=att[:, 32:40], in0=att[:, 16:24], in1=att[:, 24:32], op=ALU.add),
                       reads=[att_b], writes=[att_b])
                  S.op("dve", lambda en: en.reciprocal(out=att[:, 32:40], in_=att[:, 32:40]), reads=[att_b], writes=[att_b])
                  for half2 in range(2):
                      ptt, pttb_ = c.ps.get()
                      pttb = ps_bf(ptt)
                      for j in range(8):
                          idx = half2 * 8 + j
                          h, hf = idx // 2, idx % 2
                          S.op("pe", lambda en, j=j, h=h, hf=hf, pttb=pttb: en.transpose(
                              out=pttb[:, j * 128:(j + 1) * 128], in_=pb[:, h, hf * 128:(hf + 1) * 128], identity=c.ident_bf[:, :]),
                              reads=[pb_b, c.ident_buf], writes=[pttb_])
                      eng = "act" if half2 == 0 else "dve"
                      if eng == "act":
                          S.op("act", lambda en, half2=half2, pttb=pttb: en.copy(
                              out=pT[:, half2 * 8:half2 * 8 + 8, :], in_=pttb[:, :].rearrange("p (j t) -> p j t", j=8)),
                              reads=[pttb_], writes=[pT_b[half2]])
                      else:
                          S.op("dve", lambda en, half2=half2, pttb=pttb: en.tensor_copy(
                              out=pT[:, half2 * 8:half2 * 8 + 8, :], in_=pttb[:, :].rearrange("p (j t) -> p j t", j=8)),
                              reads=[pttb_], writes=[pT_b[half2]])
                  po, pob = c.ps.get()
                  for h in range(8):
                      gk = h // 4
                      S.op("pe", lambda en, h=h, gk=gk, prv=prv: en.matmul(po[:, h * 64:(h + 1) * 64], lhsT=pT[:, h * 2, :],
                                                                             rhs=vb[prv][:, gk * 64:(gk + 1) * 64], start=True, stop=False),
                           reads=[pT_b[h // 4], vb_b[prv]], writes=[pob])
                      S.op("pe", lambda en, h=h, gk=gk, cur=cur: en.matmul(po[:, h * 64:(h + 1) * 64], lhsT=pT[:, h * 2 + 1, :],
                                                                             rhs=vb[cur][:, gk * 64:(gk + 1) * 64], start=False, stop=True),
                           reads=[pT_b[h // 4], vb_b[cur]], writes=[pob])
                  S.op("dve", lambda en: en.tensor_tensor(out=ob[:, :, :], in0=po[:, :].rearrange("p (h d) -> p h d", h=8),
                                                          in1=att[:, 32:40].unsqueeze(2).to_broadcast([128, 8, 64]), op=ALU.mult),
                       reads=[pob, att_b], writes=[ob_b])
                  pt3, pt3b_ = c.ps.get()
                  pt3b = ps_bf(pt3)
                  oflat = ob[:, :, :].rearrange("p h d -> p (h d)")
                  for j in range(4):
                      S.op("pe", lambda en, j=j: en.transpose(out=pt3b[:, j * 128:(j + 1) * 128], in_=oflat[:, j * 128:(j + 1) * 128],
                                                              identity=c.ident_bf[:, :]), reads=[ob_b, c.ident_buf], writes=[pt3b_])
                  S.op("act", lambda en, t=t: en.copy(out=yT[:, 0:4, t * 128:(t + 1) * 128],
                                                      in_=pt3b[:, 0:512].rearrange("p (j t) -> p j t", j=4)),
                       reads=[pt3b_], writes=yT_b[0:4])
              _chk(7)
              for t in range(4):
                  halves = [c.ps.get(), c.ps.get()]
                  for kc in range(8):
                      for hf in range(2):
                          ph, phb = halves[hf]
                          S.op("pe", lambda en, ph=ph, kc=kc, hf=hf, t=t: en.matmul(
                              ph[:, :], lhsT=yT[:, kc, t * 128:(t + 1) * 128], rhs=Wout[:, kc, hf * 512:(hf + 1) * 512],
                              start=(kc == 0), stop=(kc == 7)), reads=[yT_b[kc], Wout_b], writes=[phb])
                  i = xcnt[0] % 2
                  xcnt[0] += 1
                  r0 = g * GT + t * 128
                  S.dma("sp", xin[i][:, :], src[r0:r0 + 128, :], writes=[xin_b[i]])
                  emit_ln_epilogue(c, (v[i], v_b[i], st[i], mv[i], smm_b[i]), halves, xin[i], xin_b[i],
                                   g_t, b_t, gb_buf, xo[i], xo_b[i])
                  outs.append(S.dma("sp", dst[r0:r0 + 128, :], xo[i][:, :], reads=[xo_b[i]]))
      except _Stop:
        pass
    S.barrier()
    return outs


OZ, OXBC, ODT, ORQ, ORK, ORV, ORG = 0, 1024, 2560, 2576, 3088, 3600, 4624
ST_W = 2064


def emit_odd(c, o, layer, src, dst, halo, NT):
    S, nc = c.S, c.nc
    GT = 512
    ngroups = NT // GT
    nchunks = NT // 128
    dr = c.dram
    outs = []
    Win_d = dr[f"od_w_in{o}"]
    with ExitStack() as es0:
        sb0 = lambda name, shape, dt: _sb(c, es0, "o_" + name, shape, dt)
        Sst = sb0("Sst", [128, 1024], F32); Sst_b = Buf("Sst")
        Rst = sb0("Rst", [128, 1024], F32); Rst_b = Buf("Rst")
        Atot = sb0("Atot", [128, 16], F32); Atot_b = Buf("Atot")
        otab = sb0("otab", [128, 32], F32); otab_b = Buf("otab")
        ptab = sb0("ptab", [128, 64], F32); ptab_b = Buf("ptab")
        wk5 = sb0("wk5", [5, 1536], F32); wk5_b = Buf("wk5")
        cw = sb0("cw", [128, 12, 5], F32); cw_b = Buf("cw")
        triu = sb0("triu", [128, 128], F32)
        trisl = sb0("trisl", [128, 128], F32)
        m01 = sb0("m01", [128, 128], F32)
        tri_b = Buf("tri")
        xin = [sb0(f"xin{i}", [128, 1024], F32) for i in range(2)]; xin_b = [Buf(f"oxin{i}") for i in range(2)]
        xbf = [sb0(f"xbf{i}", [128, 1024], BF16) for i in range(2)]; xbf_b = [Buf(f"oxbf{i}") for i in range(2)]
        xT_r = [(sb0(f"xT{i}", [128, 8, GT], BF16), Buf(f"oxT{i}")) for i in range(2)]
        xT, xT_b = xT_r[0]
        xTh = sb0("xTh", [128, 8, 128], BF16); xTh_b = Buf("oxTh")
        smg_r = [(sb0(f"smg{i}", [128, 8, 16], F32), Buf(f"smg{i}")) for i in range(2)]
        smc_r = [(sb0(f"smc{i}", [128, 4, 16], F32), Buf(f"smc{i}")) for i in range(3)]
        smg, smg_b = smg_r[0]
        smc, smc_b = smc_r[0]
        S.dma("sp", otab[:, :], dr["odd_tab"], writes=[otab_b])
        load_bcast_row(c, "sp", ptab[:, 0:16], ptab_b, dr[f"od_a_log{o}"])
        load_bcast_row(c, "sp", ptab[:, 16:32], ptab_b, dr[f"od_dt_bias{o}"])
        load_bcast_row(c, "sp", ptab[:, 32:48], ptab_b, dr[f"od_d_skip{o}"])
        S.dma("sp", wk5[0:4, :], dr[f"od_conv_w{o}"], writes=[wk5_b])
        S.dma("sp", wk5[4:5, :], dr[f"od_conv_b{o}"], writes=[wk5_b])
        S.dma("sp", triu[:, :], dr["triu"], writes=[tri_b])
        S.dma("sp", trisl[:, :], dr["trisl"], writes=[tri_b])
        S.dma("sp", m01[:, :], dr["triu"], writes=[tri_b])
        S.op("act", lambda en: en.activation(out=ptab[:, 0:16], in_=ptab[:, 0:16], func=AF.Exp), reads=[ptab_b], writes=[ptab_b])
        S.op("dve", lambda en: en.tensor_scalar_mul(out=ptab[:, 0:16], in0=ptab[:, 0:16], scalar1=-1.0), reads=[ptab_b], writes=[ptab_b])
        for cc in range(12):
            pt, ptb_ = c.ps.get()
            S.op("pe", lambda en, cc=cc, pt=pt: en.transpose(out=pt[:, 0:5], in_=wk5[0:5, cc * 128:(cc + 1) * 128],
                                                            identity=c.ident_f[0:5, 0:5]), reads=[wk5_b, c.identf_buf], writes=[ptb_])
            S.op("dve", lambda en, cc=cc, pt=pt: en.tensor_scalar_mul(out=cw[:, cc, :], in0=pt[:, 0:5], scalar1=0.5),
                 reads=[ptb_], writes=[cw_b])

        xcnt = [0]

        def load_xT(row0, dstT, dstT_b, slot, src_ap):
            i = xcnt[0] % 2
            xcnt[0] += 1
            S.dma("sp", xin[i][:, :], src_ap[row0:row0 + 128, :], writes=[xin_b[i]])
            emit_xT(c, dstT, dstT_b, slot, xin[i], xin_b[i], xbf[i], xbf_b[i])

        def wload(es, name, col0, ncol):
            t = _sb(c, es, "o_w" + name, [128, 8, ncol], BF16)
            b = Buf("w" + name)
            step = 1024
            for s0 in range(0, ncol, step):
                n = min(step, ncol - s0)
                S.dma("pool", t[:, :, s0:s0 + n], Win_d[:, col0 + s0:col0 + s0 + n].rearrange("(k p) n -> p k n", p=128),
                      writes=[b])
            return t, b

        def rot(d, names, i):
            for nm in names:
                r = d[nm + "_r"]
                d[nm], d[nm + "_b"] = r[i % len(r)]

        def ssd_setup(es):
            d = {}
            sb = lambda name, shape, dt: _sb(c, es, "s_" + name, shape, dt)
            d["Wx"], d["Wx_b"] = wload(es, "xbc", OXBC, 1536 + 16)
            d["cin"] = [sb(f"cin{i}", [128, 3 + GT], F32) for i in range(2)]; d["cin_b"] = [Buf(f"cin{i}") for i in range(2)]
            d["acc"] = [sb(f"acc{i}", [128, GT], F32) for i in range(2)]; d["acc_b"] = [Buf(f"acc{i}") for i in range(2)]
            d["tg"] = [sb(f"tg{i}", [128, GT], F32) for i in range(2)]; d["tg_b"] = [Buf(f"stg{i}") for i in range(2)]
            d["hist"] = sb("hist", [128, 12, 3], F32); d["hist_b"] = [Buf(f"hist{i}") for i in range(12)]
            d["xbcT_r"] = [(sb(f"xbcT{j}", [128, 12, GT], BF16), [Buf(f"xbcT{j}_{i}") for i in range(12)]) for j in range(2)]
            rot(d, ("xbcT",), 0)
            d["Btok_r"] = [(sb(f"Btok{i}", [128, 256], BF16), Buf(f"Btok{i}")) for i in range(2)]
            d["xdte_r"] = [(sb(f"xdte{i}", [128, 1024], BF16), Buf(f"xdte{i}")) for i in range(2)]
            rot(d, ("Btok", "xdte"), 0)
            return d

        def ssd_features(d, xTsrc, xTsrc_b, ncols, halo_mode):
            Wx, Wx_b = d["Wx"], d["Wx_b"]
            for cc in range(12):
                pp, ppb = c.ps.get()
                for k in range(8):
                    S.op("pe", lambda en, k=k, cc=cc, pp=pp: en.matmul(pp[:, 0:ncols], lhsT=Wx[:, k, cc * 128:(cc + 1) * 128],
                                                                       rhs=xTsrc[:, k, 0:ncols], start=(k == 0), stop=(k == 7)),
                         reads=[Wx_b, xTsrc_b], writes=[ppb])
                if halo_mode:
                    S.op("act", lambda en, cc=cc, pp=pp: en.copy(out=d["hist"][:, cc, :], in_=pp[:, ncols - 3:ncols]),
                         reads=[ppb], writes=[d["hist_b"][cc]])
                    continue
                i = cc % 2
                cin, cin_b = d["cin"][i], d["cin_b"][i]
                acc, acc_b = d["acc"][i], d["acc_b"][i]
                tgx, tgx_b = d["tg"][i], d["tg_b"][i]
                S.op("act", lambda en, cin=cin, pp=pp: en.copy(out=cin[:, 3:3 + ncols], in_=pp[:, 0:ncols]), reads=[ppb], writes=[cin_b])
                S.op("act", lambda en, cin=cin, cc=cc: en.copy(out=cin[:, 0:3], in_=d["hist"][:, cc, :]),
                     reads=[d["hist_b"][cc]], writes=[cin_b])
                S.op("act", lambda en, cin=cin, cc=cc: en.copy(out=d["hist"][:, cc, :], in_=cin[:, ncols:ncols + 3]),
                     reads=[cin_b], writes=[d["hist_b"][cc]])
                S.op("dve", lambda en, cin=cin, acc=acc, cc=cc: en.tensor_scalar(
                    out=acc[:, 0:ncols], in0=cin[:, 0:ncols], scalar1=cw[:, cc, 0:1], scalar2=cw[:, cc, 4:5],
                    op0=ALU.mult, op1=ALU.add), reads=[cin_b, cw_b], writes=[acc_b])
                for k in range(1, 4):
                    S.op("dve", lambda en, cin=cin, acc=acc, cc=cc, k=k: en.scalar_tensor_tensor(
                        out=acc[:, 0:ncols], in0=cin[:, k:k + ncols], scalar=cw[:, cc, k:k + 1], in1=acc[:, 0:ncols],
                        op0=ALU.mult, op1=ALU.add), reads=[cin_b, cw_b, acc_b], writes=[acc_b])
                S.op("act", lambda en, acc=acc, tgx=tgx: en.activation(out=tgx[:, 0:ncols], in_=acc[:, 0:ncols], func=AF.Tanh),
                     reads=[acc_b], writes=[tgx_b])
                S.op("dve", lambda en, acc=acc, tgx=tgx, cc=cc: en.scalar_tensor_tensor(
                    out=d["xbcT"][:, cc, 0:ncols], in0=tgx[:, 0:ncols], scalar=1.0, in1=acc[:, 0:ncols],
                    op0=ALU.add, op1=ALU.mult), reads=[tgx_b, acc_b], writes=[d["xbcT_b"][cc]])

        def ssd_dt_group(d):
            Wx, Wx_b = d["Wx"], d["Wx_b"]
            pd, pdb = c.ps.get()
            for t in range(4):
                for k in range(8):
                    S.op("pe", lambda en, k=k, t=t: en.matmul(pd[:, t * 16:(t + 1) * 16], lhsT=xT[:, k, t * 128:(t + 1) * 128],
                                                              rhs=Wx[:, k, 1536:1552], start=(k == 0), stop=(k == 7)),
                         reads=[Wx_b, xT_b], writes=[pdb])
            xr = smg[:, 0:4, :]
            S.op("dve", lambda en: en.tensor_tensor(out=xr, in0=pd[:, 0:64].rearrange("p (t h) -> p t h", t=4),
                                                    in1=ptab[:, 16:32].unsqueeze(1).to_broadcast([128, 4, 16]), op=ALU.add),
                 reads=[pdb, ptab_b], writes=[smg_b])
            ab = smg[:, 4:8, :]
            S.op("act", lambda en: en.activation(out=ab, in_=xr, func=AF.Abs), reads=[smg_b], writes=[smg_b])
            S.op("act", lambda en: en.activation(out=ab, in_=ab, func=AF.Exp, scale=-1.0), reads=[smg_b], writes=[smg_b])
            S.op("act", lambda en: en.activation(out=ab, in_=ab, func=AF.Ln, bias=1.0, scale=1.0), reads=[smg_b], writes=[smg_b])
            S.op("dve", lambda en: en.tensor_scalar_max(out=xr, in0=xr, scalar1=0.0), reads=[smg_b], writes=[smg_b])
            S.op("dve", lambda en: en.tensor_tensor(out=xr, in0=xr, in1=ab, op=ALU.add), reads=[smg_b], writes=[smg_b])
            S.op("dve", lambda en: en.tensor_tensor(out=ab, in0=xr, in1=ptab[:, 0:16].unsqueeze(1).to_broadcast([128, 4, 16]),
                                                    op=ALU.mult), reads=[smg_b, ptab_b], writes=[smg_b])

        def ssd_chunk_scalars(t):
            da = smg[:, 4 + t, :]
            pa, pab = c.ps.get()
            S.op("pe", lambda en: en.matmul(pa[:, 0:16], lhsT=triu[:, :], rhs=da, start=True, stop=True),
                 reads=[tri_b, smg_b], writes=[pab])
            S.op("pe", lambda en: en.matmul(pa[:, 16:32], lhsT=c.ones_f[:, :], rhs=da, start=True, stop=True),
                 reads=[c.ones_buf, smg_b], writes=[pab])
            S.op("act", lambda en: en.copy(out=smc[:, 0, :], in_=pa[:, 0:16]), reads=[pab], writes=[smc_b])
            S.op("act", lambda en: en.activation(out=smc[:, 1, :], in_=pa[:, 0:16], func=AF.Exp), reads=[pab], writes=[smc_b])
            S.op("dve", lambda en: en.tensor_tensor(out=smc[:, 2, :], in0=pa[:, 16:32], in1=smc[:, 0, :], op=ALU.subtract),
                 reads=[pab, smc_b], writes=[smc_b])
            S.op("act", lambda en: en.activation(out=smc[:, 2, :], in_=smc[:, 2, :], func=AF.Exp), reads=[smc_b], writes=[smc_b])
            S.op("dve", lambda en: en.tensor_tensor(out=smc[:, 2, :], in0=smc[:, 2, :], in1=smg[:, t, :], op=ALU.mult),
                 reads=[smc_b, smg_b], writes=[smc_b])
            S.op("act", lambda en: en.activation(out=smc[:, 3, :], in_=pa[:, 16:32], func=AF.Exp), reads=[pab], writes=[smc_b])
            S.op("dve", lambda en: en.tensor_tensor(out=Atot[:, :], in0=Atot[:, :], in1=pa[:, 16:32], op=ALU.add),
                 reads=[pab, Atot_b], writes=[Atot_b])

        def ssd_tok_and_state(d, t, xs_extra=None):
            xbcT, xbcT_b = d["xbcT"], d["xbcT_b"]
            px, pxb = c.ps.get()
            pxv = ps_bf(px)
            for j in range(8):
                S.op("pe", lambda en, j=j: en.transpose(out=pxv[:, j * 128:(j + 1) * 128], in_=xbcT[:, j, t * 128:(t + 1) * 128],
                                                        identity=c.ident_bf[:, :]), reads=[xbcT_b[j], c.ident_buf], writes=[pxb])
            pB, pBb = c.ps.get()
            pBv = ps_bf(pB)
            for j in range(2):
                S.op("pe", lambda en, j=j: en.transpose(out=pBv[:, j * 128:(j + 1) * 128], in_=xbcT[:, 8 + j, t * 128:(t + 1) * 128],
                                                        identity=c.ident_bf[:, :]), reads=[xbcT_b[8 + j], c.ident_buf], writes=[pBb])
            S.op("act", lambda en: en.copy(out=d["Btok"][:, :], in_=pBv[:, 0:256]), reads=[pBb], writes=[d["Btok_b"]])
            xs3 = pxv[:, 0:1024].rearrange("p (h q) -> p h q", h=16)
            S.op("dve", lambda en: en.tensor_tensor(out=d["xdte"][:, :].rearrange("p (h q) -> p h q", h=16), in0=xs3,
                                                    in1=smc[:, 2, :].unsqueeze(2).to_broadcast([128, 16, 64]), op=ALU.mult),
                 reads=[pxb, smc_b], writes=[d["xdte_b"]])
            if xs_extra is not None:
                xs_extra(xs3, pxb)
            pst = [c.ps.get(), c.ps.get()]
            for g in range(2):
                S.op("pe", lambda en, g=g: en.matmul(pst[g][0][:, :], lhsT=d["Btok"][:, g * 128:(g + 1) * 128],
                                                     rhs=d["xdte"][:, g * 512:(g + 1) * 512], start=True, stop=True),
                     reads=[d["Btok_b"], d["xdte_b"]], writes=[pst[g][1]])
            return pst

        def ssd_state_update(d, pst):
            S.op("dve", lambda en: en.tensor_tensor(out=Sst[:, :].rearrange("p (h q) -> p h q", h=16),
                                                    in0=Sst[:, :].rearrange("p (h q) -> p h q", h=16),
                                                    in1=smc[:, 3, :].unsqueeze(2).to_broadcast([128, 16, 64]), op=ALU.mult),
                 reads=[Sst_b, smc_b], writes=[Sst_b])
            for g in range(2):
                S.op("dve", lambda en, g=g: en.tensor_tensor(out=Sst[:, g * 512:(g + 1) * 512], in0=Sst[:, g * 512:(g + 1) * 512],
                                                             in1=pst[g][0][:, :], op=ALU.add), reads=[Sst_b, pst[g][1]], writes=[Sst_b])

        def ret_setup(es, with_q):
            d = {}
            sb = lambda name, shape, dt: _sb(c, es, "r_" + name, shape, dt)
            if with_q:
                t_ = _sb(c, es, "o_wretqg", [128, 8, 1536], BF16)
                b_ = Buf("wretqg")
                S.dma("pool", t_[:, :, 0:512], Win_d[:, ORQ:ORQ + 512].rearrange("(k p) n -> p k n", p=128), writes=[b_])
                S.dma("pool", t_[:, :, 512:1536], Win_d[:, ORG:ORG + 1024].rearrange("(k p) n -> p k n", p=128), writes=[b_])
                d["Wr"], d["Wr_b"] = t_, b_
                d["off"] = {"q": 0, "g": 512}
            else:
                d["Wr"], d["Wr_b"] = wload(es, "ret", ORK, 1536)
                d["off"] = {"k": 0, "v": 512}
            d["rope"] = [sb(f"rope{i}", [128, 128], F32) for i in range(2)]; d["rope_b"] = [Buf(f"rrope{i}") for i in range(2)]
            d["rr_r"] = [([sb(f"rr{j}_{i}", [128, 4, 64], F32) for i in range(2)], [Buf(f"rr{j}_{i}") for i in range(2)]) for j in range(2)]
            d["kr_r"] = [(sb(f"kr{i}", [128, 4, 128], F32), Buf(f"kr{i}")) for i in range(2)]
            d["kp_r"] = [(sb(f"kp{i}", [128, 512], BF16), Buf(f"kp{i}")) for i in range(2)]
            d["vb_r"] = [(sb(f"vb{i}", [128, 1024], BF16), Buf(f"rvb{i}")) for i in range(2)]
            rot(d, ("rr", "kr", "kp", "vb"), 0)
            return d

        def ret_rope(d, psrc, psrc_b, ri, scale_cols, dstt, dst_b):
            s3 = psrc.rearrange("p (h e) -> p h e", h=4)
            rope_t, rope_tb = d["rope"][ri], d["rope_b"][ri]
            cs = rope_t[:, 0:64].unsqueeze(1).to_broadcast([128, 4, 64])
            sn = rope_t[:, 64:128].unsqueeze(1).to_broadcast([128, 4, 64])
            t1, t2 = s3[:, :, 0:64], s3[:, :, 64:128]
            r0, r1 = d["rr"][0], d["rr"][1]
            kr = d["kr"]
            S.op("dve", lambda en: en.tensor_tensor(out=r0[:, :, :], in0=t1, in1=cs, op=ALU.mult), reads=[psrc_b, rope_tb], writes=[d["rr_b"][0]])
            S.op("dve", lambda en: en.tensor_tensor(out=r1[:, :, :], in0=t2, in1=sn, op=ALU.mult), reads=[psrc_b, rope_tb], writes=[d["rr_b"][1]])
            S.op("dve", lambda en: en.tensor_tensor(out=kr[:, :, 0:64], in0=r0[:, :, :], in1=r1[:, :, :], op=ALU.subtract),
                 reads=d["rr_b"], writes=[d["kr_b"]])
            S.op("dve", lambda en: en.tensor_tensor(out=r0[:, :, :], in0=t2, in1=cs, op=ALU.mult), reads=[psrc_b, rope_tb], writes=[d["rr_b"][0]])
            S.op("dve", lambda en: en.tensor_tensor(out=r1[:, :, :], in0=t1, in1=sn, op=ALU.mult), reads=[psrc_b, rope_tb], writes=[d["rr_b"][1]])
            S.op("dve", lambda en: en.tensor_tensor(out=kr[:, :, 64:128], in0=r0[:, :, :], in1=r1[:, :, :], op=ALU.add),
                 reads=d["rr_b"], writes=[d["kr_b"]])
            S.op("dve", lambda en: en.tensor_tensor(out=dstt[:, :].rearrange("p (h e) -> p h e", h=4), in0=kr[:, :, :],
                                                    in1=otab[:, scale_cols[0]:scale_cols[1]].unsqueeze(2).to_broadcast([128, 4, 128]),
                                                    op=ALU.mult), reads=[d["kr_b"], otab_b], writes=[dst_b])

        def ret_kv(d, t, chunk_idx):
            Wr, Wr_b, off = d["Wr"], d["Wr_b"], d["off"]
            ri = chunk_idx % 2
            S.dma("sp", d["rope"][ri][:, :], dr["rope_d"][:, chunk_idx, :], writes=[d["rope_b"][ri]])
            pk, pkb = c.ps.get()
            for k in range(8):
                S.op("pe", lambda en, k=k: en.matmul(pk[:, :], lhsT=xT[:, k, t * 128:(t + 1) * 128],
                                                     rhs=Wr[:, k, off["k"]:off["k"] + 512], start=(k == 0), stop=(k == 7)),
                     reads=[Wr_b, xT_b], writes=[pkb])
            ret_rope(d, pk[:, :], pkb, ri, (4, 8), d["kp"], d["kp_b"])
            for hf in range(2):
                pv, pvb = c.ps.get()
                for k in range(8):
                    S.op("pe", lambda en, k=k, hf=hf, pv=pv: en.matmul(
                        pv[:, :], lhsT=xT[:, k, t * 128:(t + 1) * 128],
                        rhs=Wr[:, k, off["v"] + hf * 512:off["v"] + (hf + 1) * 512], start=(k == 0), stop=(k == 7)),
                        reads=[Wr_b, xT_b], writes=[pvb])
                S.op("act", lambda en, hf=hf, pv=pv: en.copy(out=d["vb"][:, hf * 512:(hf + 1) * 512], in_=pv[:, :]),
                     reads=[pvb], writes=[d["vb_b"]])

        def ret_state_mm(d):
            pkv = [c.ps.get(), c.ps.get()]
            for h in range(4):
                pt, ptb_ = pkv[h // 2]
                S.op("pe", lambda en, h=h, pt=pt: en.matmul(pt[:, (h % 2) * 256:(h % 2) * 256 + 256], lhsT=d["kp"][:, h * 128:(h + 1) * 128],
                                                            rhs=d["vb"][:, h * 256:(h + 1) * 256], start=True, stop=True),
                     reads=[d["kp_b"], d["vb_b"]], writes=[ptb_])
            return pkv

        def ret_state_update(pkv):
            for hf in range(2):
                S.op("dve", lambda en, hf=hf: en.tensor_tensor(out=Rst[:, hf * 512:(hf + 1) * 512], in0=Rst[:, hf * 512:(hf + 1) * 512],
                                                               in1=pkv[hf][0][:, :], op=ALU.add), reads=[Rst_b, pkv[hf][1]], writes=[Rst_b])
            S.op("dve", lambda en: en.tensor_tensor(out=Rst[:, :].rearrange("p (h v) -> p h v", h=4),
                                                    in0=Rst[:, :].rearrange("p (h v) -> p h v", h=4),
                                                    in1=otab[:, 8:12].unsqueeze(2).to_broadcast([128, 4, 256]), op=ALU.mult),
                 reads=[Rst_b, otab_b], writes=[Rst_b])

        def zero_states():
            S.op("dve", lambda en: en.memset(Sst[:, :], 0.0), writes=[Sst_b])
            S.op("dve", lambda en: en.memset(Rst[:, :], 0.0), writes=[Rst_b])
            S.op("dve", lambda en: en.memset(Atot[:, :], 0.0), writes=[Atot_b])

        xbcd_b = [Buf(f"xbcd{g}") for g in range(ngroups)]
        smgd_b = [Buf(f"smgd{g}") for g in range(ngroups)]
        kpd_b = [Buf(f"kpd{i}") for i in range(nchunks)]
        vbd_b = [Buf(f"vbd{i}") for i in range(nchunks)]
        zero_states()
        with ExitStack() as es:
            ds = ssd_setup(es)
            dq = ret_setup(es, False)
            load_xT(0, xTh, xTh_b, 0, halo)
            ssd_features(ds, xTh, xTh_b, 128, True)
            for g in range(ngroups):
                xT, xT_b = xT_r[g % 2]
                smg, smg_b = smg_r[g % 2]
                for t in range(4):
                    load_xT(g * GT + t * 128, xT, xT_b, t, src)
                rot(ds, ("xbcT",), g)
                ssd_features(ds, xT, xT_b, GT, False)
                ssd_dt_group(ds)
                S.dma("sp", c.xbc_d[:, :, g * GT:(g + 1) * GT].rearrange("c p t -> p c t"), ds["xbcT"][:, :, :],
                      reads=ds["xbcT_b"], writes=[xbcd_b[g]])
                S.dma("sp", c.smg_d[g, :, :], smg[:, :, :].rearrange("p a b -> p (a b)"), reads=[smg_b], writes=[smgd_b[g]])
                for t in range(4):
                    ci = g * 4 + t
                    smc, smc_b = smc_r[ci % 3]
                    rot(ds, ("Btok", "xdte"), t)
                    rot(dq, ("rr", "kr", "kp", "vb"), t)
                    ssd_chunk_scalars(t)
                    pst = ssd_tok_and_state(ds, t)
                    ssd_state_update(ds, pst)
                    ret_kv(dq, t, ci)
                    S.dma("sp", c.kp_d[ci * 128:(ci + 1) * 128, :], dq["kp"][:, :], reads=[dq["kp_b"]], writes=[kpd_b[ci]])
                    S.dma("sp", c.vb_d[ci * 128:(ci + 1) * 128, :], dq["vb"][:, :], reads=[dq["vb_b"]], writes=[vbd_b[ci]])
                    pkv = ret_state_mm(dq)
                    ret_state_update(pkv)
        S.barrier()
        (loc_s, loc_r), (all_s, all_r) = c.st_loc[o], c.st_all[o]
        locb = [Buf("loc_s"), Buf("loc_r")]
        allb = [Buf("all_s"), Buf("all_r")]
        S.dma("sp", loc_s[:, 0:1024], Sst[:, :], reads=[Sst_b], writes=[locb[0]])
        S.dma("sp", loc_s[:, 1024:1040], Atot[:, :], reads=[Atot_b], writes=[locb[0]])
        S.dma("sp", loc_r[:, :], Rst[:, :], reads=[Rst_b], writes=[locb[1]])
        for (lo, al, lb, ab_) in ((loc_s, all_s, locb[0], allb[0]), (loc_r, all_r, locb[1], allb[1])):
            if c.use_cc:
                S.op("pool", lambda en, lo=lo, al=al: en.collective_compute(
                    "AllGather", ALU.bypass, replica_groups=[[0, 1, 2, 3], [4, 5, 6, 7]], ins=[lo], outs=[al]),
                    reads=[lb], writes=[ab_])
            else:
                for i in range(4):
                    S.dma("sp", al[i * 128:(i + 1) * 128, :], lo, reads=[lb], writes=[ab_])
        with ExitStack() as es:
            sb = lambda name, shape, dt: _sb(c, es, "c_" + name, shape, dt)
            rec = [sb(f"rec{i}", [128, 1040], F32) for i in range(2)]; rec_b = [Buf(f"rec{i}") for i in range(2)]
            rer = [sb(f"rer{i}", [128, 1024], F32) for i in range(2)]; rer_b = [Buf(f"rer{i}") for i in range(2)]
            cf = sb("cf", [128, 16], F32); cf_b = Buf("cf")
            zero_states()
            for i in range(4):
                rb, rbb = rec[i % 2], rec_b[i % 2]
                rr_, rrb = rer[i % 2], rer_b[i % 2]
                S.dma("sp", rb[:, :], all_s[i * 128:(i + 1) * 128, :], reads=[allb[0]], writes=[rbb])
                S.dma("sp", rr_[:, :], all_r[i * 128:(i + 1) * 128, :], reads=[allb[1]], writes=[rrb])
                S.op("act", lambda en, rb=rb: en.activation(out=cf[:, :], in_=rb[:, 1024:1040], func=AF.Exp), reads=[rbb], writes=[cf_b])
                S.op("dve", lambda en, i=i: en.tensor_scalar(out=cf[:, :], in0=cf[:, :], scalar1=-1.0, scalar2=otab[:, 28 + i:29 + i],
                                                             op0=ALU.add, op1=ALU.mult), reads=[cf_b, otab_b], writes=[cf_b])
                S.op("dve", lambda en: en.tensor_scalar_add(out=cf[:, :], in0=cf[:, :], scalar1=1.0), reads=[cf_b], writes=[cf_b])
                S.op("dve", lambda en: en.tensor_tensor(out=Sst[:, :].rearrange("p (h q) -> p h q", h=16),
                                                        in0=Sst[:, :].rearrange("p (h q) -> p h q", h=16),
                                                        in1=cf[:, :].unsqueeze(2).to_broadcast([128, 16, 64]), op=ALU.mult),
                     reads=[Sst_b, cf_b], writes=[Sst_b])
                S.op("dve", lambda en, i=i, rb=rb: en.scalar_tensor_tensor(out=Sst[:, :], in0=rb[:, 0:1024], scalar=otab[:, 28 + i:29 + i],
                                                                           in1=Sst[:, :], op0=ALU.mult, op1=ALU.add),
                     reads=[rbb, otab_b, Sst_b], writes=[Sst_b])
                S.op("dve", lambda en, i=i, rr_=rr_: en.tensor_tensor(
                    out=rr_[:, :].rearrange("p (h v) -> p h v", h=4), in0=rr_[:, :].rearrange("p (h v) -> p h v", h=4),
                    in1=otab[:, 12 + 4 * i:16 + 4 * i].unsqueeze(2).to_broadcast([128, 4, 256]), op=ALU.mult),
                    reads=[rrb, otab_b], writes=[rrb])
                S.op("dve", lambda en, rr_=rr_: en.tensor_tensor(out=Rst[:, :], in0=Rst[:, :], in1=rr_[:, :], op=ALU.add),
                     reads=[rrb, Rst_b], writes=[Rst_b])
        S.barrier()

        ysT_d = c.ysT_d
        ysd_b = [Buf(f"ysd{i}") for i in range(nchunks)]
        with ExitStack() as es:
            ds = ssd_setup(es)
            sb = lambda name, shape, dt: _sb(c, es, "b_" + name, shape, dt)
            Wz, Wz_b = wload(es, "z", OZ, 1024)
            R2 = lambda nm, shape, dt: [(sb(f"{nm}{i}", shape, dt), Buf(f"{nm}{i}")) for i in range(2)]
            sz_r = R2("sz", [128, 1024], F32)
            thz_r = R2("thz", [128, 1024], F32)
            Sbf_r = R2("Sbf", [128, 1024], BF16)
            cbm_r = R2("cbm", [128, 2, 128], F32)
            Xs_r = R2("Xs", [128, 8, 128], F32)
            ET_r = R2("ET", [128, 8, 128], F32)
            PT_r = [(sb(f"PT{i}", [128, 16, 128], BF16), [Buf(f"PT{i}_{g}") for g in range(2)]) for i in range(2)]
            xdt_r = R2("xdt", [128, 1024], BF16)
            xsD_r = R2("xsD", [128, 1024], BF16)
            yv_r = R2("yv", [128, 1024], F32)
            ysb_r = R2("ysb", [128, 1024], BF16)
            rms_r = R2("rms", [128, 8], F32)
            ysT = [sb(f"ysT{i}", [128, 8, 128], BF16) for i in range(2)]; ysT_b = [Buf(f"ysT{i}") for i in range(2)]
            S.op("dve", lambda en: en.memset(Atot[:, :], 0.0), writes=[Atot_b])
            for g in range(ngroups):
                xT, xT_b = xT_r[g % 2]
                smg, smg_b = smg_r[g % 2]
                for t in range(4):
                    load_xT(g * GT + t * 128, xT, xT_b, t, src)
                rot(ds, ("xbcT",), g)
                S.dma("sp", ds["xbcT"][:, :, :], c.xbc_d[:, :, g * GT:(g + 1) * GT].rearrange("c p t -> p c t"),
                      reads=[xbcd_b[g]], writes=ds["xbcT_b"])
                S.dma("sp", smg[:, :, :].rearrange("p a b -> p (a b)"), c.smg_d[g, :, :], reads=[smgd_b[g]], writes=[smg_b])
                xbcT, xbcT_b = ds["xbcT"], ds["xbcT_b"]
                for t in range(4):
                    ci = g * 4 + t
                    tc0 = t * 128
                    smc, smc_b = smc_r[ci % 3]
                    rot(ds, ("Btok", "xdte"), ci)
                    sz, sz_b = sz_r[ci % 2]; thz, thz_b = thz_r[ci % 2]; Sbf, Sbf_b = Sbf_r[ci % 2]
                    cbm, cbm_b = cbm_r[ci % 2]; PT, PT_b = PT_r[ci % 2]; xdt, xdt_b = xdt_r[ci % 2]
                    xsD, xsD_b = xsD_r[ci % 2]; yv, yv_b = yv_r[ci % 2]; ysb, ysb_b = ysb_r[ci % 2]
                    rms, rms_b = rms_r[ci % 2]
                    ssd_chunk_scalars(t)
                    S.op("act", lambda en: en.copy(out=Sbf[:, :], in_=Sst[:, :]), reads=[Sst_b], writes=[Sbf_b])
                    for hf in range(2):
                        pz, pzb = c.ps.get()
                        for k in range(8):
                            S.op("pe", lambda en, k=k, hf=hf, pz=pz: en.matmul(
                                pz[:, :], lhsT=xT[:, k, tc0:tc0 + 128], rhs=Wz[:, k, hf * 512:(hf + 1) * 512],
                                start=(k == 0), stop=(k == 7)), reads=[Wz_b, xT_b], writes=[pzb])
                        S.op("act", lambda en, hf=hf, pz=pz: en.activation(out=thz[:, hf * 512:(hf + 1) * 512], in_=pz[:, :],
                                                                            func=AF.Tanh, scale=0.5), reads=[pzb], writes=[thz_b])
                        S.op("dve", lambda en, hf=hf, pz=pz: en.scalar_tensor_tensor(
                            out=sz[:, hf * 512:(hf + 1) * 512], in0=thz[:, hf * 512:(hf + 1) * 512], scalar=1.0, in1=pz[:, :],
                            op0=ALU.add, op1=ALU.mult), reads=[thz_b, pzb], writes=[sz_b])
                    pcb, pcbb = c.ps.get()
                    for gg in range(2):
                        S.op("pe", lambda en, gg=gg: en.matmul(pcb[:, gg * 128:(gg + 1) * 128], lhsT=xbcT[:, 8 + gg, tc0:tc0 + 128],
                                                               rhs=xbcT[:, 10 + gg, tc0:tc0 + 128], start=True, stop=True),
                             reads=[xbcT_b[8 + gg], xbcT_b[10 + gg]], writes=[pcbb])
                    S.op("dve", lambda en: en.tensor_tensor(out=cbm[:, :, :], in0=pcb[:, 0:256].rearrange("p (g l) -> p g l", g=2),
                                                            in1=m01[:, :].unsqueeze(1).to_broadcast([128, 2, 128]), op=ALU.mult),
                         reads=[pcbb, tri_b], writes=[cbm_b])

                    def xs_extra(xs3, pxb):
                        S.op("dve", lambda en: en.tensor_tensor(out=xdt[:, :].rearrange("p (h q) -> p h q", h=16), in0=xs3,
                                                                in1=smg[:, t, :].unsqueeze(2).to_broadcast([128, 16, 64]), op=ALU.mult),
                             reads=[pxb, smg_b], writes=[xdt_b])
                        S.op("dve", lambda en: en.tensor_tensor(out=xsD[:, :].rearrange("p (h q) -> p h q", h=16), in0=xs3,
                                                                in1=ptab[:, 32:48].unsqueeze(2).to_broadcast([128, 16, 64]), op=ALU.mult),
                             reads=[pxb, ptab_b], writes=[xsD_b])
                    pst = ssd_tok_and_state(ds, t, xs_extra)
                    for gg in range(2):
                        Xs, Xs_b = Xs_r[gg]
                        ET, ET_b = ET_r[gg]
                        S.op("dve", lambda en, gg=gg: en.tensor_tensor(
                            out=Xs[:, :, :], in0=smg[:, 4 + t, gg * 8:(gg + 1) * 8].unsqueeze(2).to_broadcast([128, 8, 128]),
                            in1=triu[:, :].unsqueeze(1).to_broadcast([128, 8, 128]), op=ALU.mult),
                            reads=[smg_b, tri_b], writes=[Xs_b])
                        pseg = [c.ps.get(), c.ps.get()]
                        for q in range(2):
                            S.op("pe", lambda en, q=q, pseg=pseg: en.matmul(
                                pseg[q][0][:, :], lhsT=trisl[:, :], rhs=Xs[:, q * 4:(q + 1) * 4, :].rearrange("p h l -> p (h l)"),
                                start=True, stop=True), reads=[tri_b, Xs_b], writes=[pseg[q][1]])
                            S.op("act", lambda en, q=q, pseg=pseg: en.activation(
                                out=ET[:, q * 4:(q + 1) * 4, :].rearrange("p h l -> p (h l)"), in_=pseg[q][0][:, :], func=AF.Exp),
                                reads=[pseg[q][1]], writes=[ET_b])
                        S.op("dve", lambda en, gg=gg: en.tensor_tensor(
                            out=PT[:, gg * 8:(gg + 1) * 8, :], in0=ET[:, :, :],
                            in1=cbm[:, gg, :].unsqueeze(1).to_broadcast([128, 8, 128]), op=ALU.mult),
                            reads=[ET_b, cbm_b], writes=[PT_b[gg]])
                    for gg in range(2):
                        po, pob = c.ps.get()
                        S.op("pe", lambda en, gg=gg, po=po: en.matmul(po[:, :], lhsT=xbcT[:, 10 + gg, tc0:tc0 + 128],
                                                                      rhs=Sbf[:, gg * 512:(gg + 1) * 512], start=True, stop=True),
                             reads=[xbcT_b[10 + gg], Sbf_b], writes=[pob])
                        S.op("dve", lambda en, gg=gg, po=po: en.tensor_tensor(
                            out=yv[:, gg * 512:(gg + 1) * 512].rearrange("p (h q) -> p h q", h=8),
                            in0=po[:, :].rearrange("p (h q) -> p h q", h=8),
                            in1=smc[:, 1, gg * 8:(gg + 1) * 8].unsqueeze(2).to_broadcast([128, 8, 64]), op=ALU.mult),
                            reads=[pob, smc_b], writes=[yv_b])
                    ssd_state_update(ds, pst)
                    for gg in range(2):
                        pd_, pdb_ = c.ps.get()
                        S.op("pe", lambda en, gg=gg, pd_=pd_: en.matmul(pd_[:, :], lhsT=c.ident_bf[:, :], rhs=xsD[:, gg * 512:(gg + 1) * 512],
                                                                        start=True, stop=False), reads=[c.ident_buf, xsD_b], writes=[pdb_])
                        for hh in range(8):
                            h = gg * 8 + hh
                            S.op("pe", lambda en, h=h, hh=hh, pd_=pd_: en.matmul(
                                pd_[:, hh * 64:(hh + 1) * 64], lhsT=PT[:, h, :], rhs=xdt[:, h * 64:(h + 1) * 64],
                                start=False, stop=(hh == 7)), reads=[PT_b[gg], xdt_b], writes=[pdb_])
                        S.op("dve", lambda en, gg=gg, pd_=pd_: en.tensor_tensor(out=yv[:, gg * 512:(gg + 1) * 512],
                                                                                in0=yv[:, gg * 512:(gg + 1) * 512], in1=pd_[:, :], op=ALU.add),
                             reads=[yv_b, pdb_], writes=[yv_b])
                    S.op("dve", lambda en: en.scalar_tensor_tensor(out=yv[:, :], in0=yv[:, :], scalar=0.5, in1=sz[:, :],
                                                                   op0=ALU.mult, op1=ALU.mult), reads=[yv_b, sz_b], writes=[yv_b])
                    for gg in range(2):
                        S.op("act", lambda en, gg=gg: en.activation(out=thz[:, gg * 512:(gg + 1) * 512], in_=yv[:, gg * 512:(gg + 1) * 512],
                                                                    func=AF.Square, accum_out=rms[:, gg:gg + 1]),
                             reads=[yv_b], writes=[thz_b, rms_b])
                    S.op("dve", lambda en: en.tensor_scalar(out=rms[:, 2:4], in0=rms[:, 0:2], scalar1=1.0 / 512.0, scalar2=float(EPS),
                                                            op0=ALU.mult, op1=ALU.add), reads=[rms_b], writes=[rms_b])
                    S.op("act", lambda en: en.activation(out=rms[:, 2:4], in_=rms[:, 2:4], func=AF.Sqrt), reads=[rms_b], writes=[rms_b])
                    S.op("dve", lambda en: en.reciprocal(out=rms[:, 2:4], in_=rms[:, 2:4]), reads=[rms_b], writes=[rms_b])
                    S.op("dve", lambda en: en.tensor_tensor(out=ysb[:, :].rearrange("p (g q) -> p g q", g=2),
                                                            in0=yv[:, :].rearrange("p (g q) -> p g q", g=2),
                                                            in1=rms[:, 2:4].unsqueeze(2).to_broadcast([128, 2, 512]), op=ALU.mult),
                         reads=[yv_b, rms_b], writes=[ysb_b])
                    pt, ptb_ = c.ps.get()
                    ptv = ps_bf(pt)
                    for j in range(8):
                        S.op("pe", lambda en, j=j, ptv=ptv: en.transpose(out=ptv[:, j * 128:(j + 1) * 128], in_=ysb[:, j * 128:(j + 1) * 128],
                                                                         identity=c.ident_bf[:, :]), reads=[ysb_b, c.ident_buf], writes=[ptb_])
                    yi = ci % 2
                    S.op("act", lambda en, yi=yi, ptv=ptv: en.copy(out=ysT[yi][:, :, :], in_=ptv[:, :].rearrange("p (j t) -> p j t", j=8)),
                         reads=[ptb_], writes=[ysT_b[yi]])
                    S.dma("sp", ysT_d[:, :, ci * 128:(ci + 1) * 128].rearrange("k p t -> p k t"), ysT[yi][:, :, :],
                          reads=[ysT_b[yi]], writes=[ysd_b[ci]])
        S.barrier()

        with ExitStack() as es:
            dq = ret_setup(es, True)
            sb = lambda name, shape, dt: _sb(c, es, "d_" + name, shape, dt)
            Wr, Wr_b, off = dq["Wr"], dq["Wr_b"], dq["off"]
            Wout = sb("wout", [128, 16, 1024], BF16); Wout_b = Buf("owout")
            ng = sb("ng", [128, 8], F32); ng_b = Buf("ng")
            gng = sb("gng", [128, 1024], F32); gnb = sb("gnb", [128, 1024], F32); gn_b = Buf("gn")
            g_t = sb("g", [128, 1024], F32); b_t = sb("b", [128, 1024], F32); gb_buf = Buf("ogb")
            R2 = lambda nm, shape, dt: [(sb(f"{nm}{i}", shape, dt), Buf(f"{nm}{i}")) for i in range(2)]
            qp_r = R2("qp", [128, 512], BF16)
            qT_r = R2("qT", [128, 4, 128], BF16)
            kT_r = R2("kT", [128, 4, 128], BF16)
            scT_r = R2("scT", [128, 4, 128], BF16)
            yrT_r = R2("yrT", [128, 8, 128], BF16)
            sg = sb("sg", [128, 1024], F32); sg_b = Buf("sg")
            thg = sb("thg", [128, 1024], F32); thg_b = Buf("thg")
            Rbf = sb("Rbf", [128, 1024], BF16); Rbf_b = Buf("Rbf")
            yr = sb("yr", [128, 1024], F32); yr_b = Buf("yr")
            yrb = sb("yrb", [128, 1024], BF16); yrb_b = Buf("yrb")
            ysl = [sb(f"ysl{i}", [128, 8, 128], BF16) for i in range(2)]; ysl_b = [Buf(f"ysl{i}") for i in range(2)]
            gst = sb("gst", [128, 4, 6], F32); gmv = sb("gmv", [128, 4, 4], F32); gs_b = Buf("gs")
            v_r = R2("v", [128, 1024], F32)
            st_r = R2("st", [128, 12], F32)
            mv_r = [sb(f"mv{i}", [128, 4], F32) for i in range(2)]
            xres = sb("xres", [128, 1024], F32); xres_b = Buf("xres")
            S.dma("pool", Wout[:, 0:8, :], dr[f"od_w_out{o}"][0:1024, :].rearrange("(k p) n -> p k n", p=128), writes=[Wout_b])
            S.dma("pool", Wout[:, 8:16, :], dr[f"od_w_out{o}"][1024:2048, :].rearrange("(k p) n -> p k n", p=128), writes=[Wout_b])
            S.dma("sp", ng[:, :], dr[f"od_ssm_norm_g{o}"].rearrange("o (c p) -> p (o c)", p=128), writes=[ng_b], allow_slow_non_contiguous=True)
            load_bcast_row(c, "sp", gng, gn_b, dr[f"od_ret_gn_g{o}"])
            load_bcast_row(c, "sp", gnb, gn_b, dr[f"od_ret_gn_b{o}"])
            load_bcast_row(c, "sp", g_t, gb_buf, dr[f"ln_mix_g{layer}"])
            load_bcast_row(c, "sp", b_t, gb_buf, dr[f"ln_mix_b{layer}"])
            for kc in range(8):
                S.op("dve", lambda en, kc=kc: en.tensor_scalar_mul(out=Wout[:, kc, :], in0=Wout[:, kc, :], scalar1=ng[:, kc:kc + 1]),
                     reads=[Wout_b, ng_b], writes=[Wout_b])
            for g in range(ngroups):
                xT, xT_b = xT_r[g % 2]
                for t in range(4):
                    load_xT(g * GT + t * 128, xT, xT_b, t, src)
                for t in range(4):
                    ci = g * 4 + t
                    tc0 = t * 128
                    ri = ci % 2
                    yi = ci % 2
                    rot(dq, ("rr", "kr", "kp", "vb"), ci)
                    qp, qp_b = qp_r[ci % 2]; qT, qT_b = qT_r[ci % 2]; kT, kT_b = kT_r[ci % 2]
                    scT, scT_b = scT_r[ci % 2]; yrT, yrT_b = yrT_r[ci % 2]
                    S.dma("sp", ysl[yi][:, :, :], ysT_d[:, :, ci * 128:(ci + 1) * 128].rearrange("k p t -> p k t"),
                          reads=[ysd_b[ci]], writes=[ysl_b[yi]])
                    S.dma("sp", dq["rope"][ri][:, :], dr["rope_d"][:, ci, :], writes=[dq["rope_b"][ri]])
                    S.dma("sp", dq["kp"][:, :], c.kp_d[ci * 128:(ci + 1) * 128, :], reads=[kpd_b[ci]], writes=[dq["kp_b"]])
                    S.dma("sp", dq["vb"][:, :], c.vb_d[ci * 128:(ci + 1) * 128, :], reads=[vbd_b[ci]], writes=[dq["vb_b"]])
                    pq, pqb = c.ps.get()
                    for k in range(8):
                        S.op("pe", lambda en, k=k: en.matmul(pq[:, :], lhsT=xT[:, k, tc0:tc0 + 128], rhs=Wr[:, k, 0:512],
                                                             start=(k == 0), stop=(k == 7)), reads=[Wr_b, xT_b], writes=[pqb])
                    ret_rope(dq, pq[:, :], pqb, ri, (0, 4), qp, qp_b)
                    for (srcp, srcp_b, dT, dT_b) in ((qp, qp_b, qT, qT_b), (dq["kp"], dq["kp_b"], kT, kT_b)):
                        pt, ptb_ = c.ps.get()
                        ptv = ps_bf(pt)
                        for j in range(4):
                            S.op("pe", lambda en, j=j, ptv=ptv, srcp=srcp: en.transpose(
                                out=ptv[:, j * 128:(j + 1) * 128], in_=srcp[:, j * 128:(j + 1) * 128], identity=c.ident_bf[:, :]),
                                reads=[srcp_b, c.ident_buf], writes=[ptb_])
                        S.op("act", lambda en, ptv=ptv, dT=dT: en.copy(out=dT[:, :, :], in_=ptv[:, 0:512].rearrange("p (j t) -> p j t", j=4)),
                             reads=[ptb_], writes=[dT_b])
                    for hf in range(2):
                        pg, pgb = c.ps.get()
                        for k in range(8):
                            S.op("pe", lambda en, k=k, hf=hf, pg=pg: en.matmul(
                                pg[:, :], lhsT=xT[:, k, tc0:tc0 + 128], rhs=Wr[:, k, 512 + hf * 512:512 + (hf + 1) * 512],
                                start=(k == 0), stop=(k == 7)), reads=[Wr_b, xT_b], writes=[pgb])
                        S.op("act", lambda en, hf=hf, pg=pg: en.activation(out=thg[:, hf * 512:(hf + 1) * 512], in_=pg[:, :],
                                                                            func=AF.Tanh, scale=0.5), reads=[pgb], writes=[thg_b])
                        S.op("dve", lambda en, hf=hf, pg=pg: en.scalar_tensor_tensor(
                            out=sg[:, hf * 512:(hf + 1) * 512], in0=thg[:, hf * 512:(hf + 1) * 512], scalar=1.0, in1=pg[:, :],
                            op0=ALU.add, op1=ALU.mult), reads=[thg_b, pgb], writes=[sg_b])
                    psc, pscb = c.ps.get()
                    for h in range(4):
                        S.op("pe", lambda en, h=h: en.matmul(psc[:, h * 128:(h + 1) * 128], lhsT=kT[:, h, :], rhs=qT[:, h, :],
                                                             start=True, stop=True), reads=[kT_b, qT_b], writes=[pscb])
                    S.op("dve", lambda en: en.tensor_tensor(out=scT[:, :, :], in0=psc[:, :].rearrange("p (h l) -> p h l", h=4),
                                                            in1=m01[:, :].unsqueeze(1).to_broadcast([128, 4, 128]), op=ALU.mult),
                         reads=[pscb, tri_b], writes=[scT_b])
                    S.op("act", lambda en: en.copy(out=Rbf[:, :], in_=Rst[:, :]), reads=[Rst_b], writes=[Rbf_b])
                    py = [c.ps.get(), c.ps.get()]
                    for h in range(4):
                        pt, ptb_ = py[h // 2]
                        cs_ = slice((h % 2) * 256, (h % 2) * 256 + 256)
                        S.op("pe", lambda en, h=h, pt=pt, cs_=cs_: en.matmul(pt[:, cs_], lhsT=scT[:, h, :], rhs=dq["vb"][:, h * 256:(h + 1) * 256],
                                                                             start=True, stop=False), reads=[scT_b, dq["vb_b"]], writes=[ptb_])
                        S.op("pe", lambda en, h=h, pt=pt, cs_=cs_: en.matmul(pt[:, cs_], lhsT=qT[:, h, :], rhs=Rbf[:, h * 256:(h + 1) * 256],
                                                                             start=False, stop=True), reads=[qT_b, Rbf_b], writes=[ptb_])
                    pkv = ret_state_mm(dq)
                    ret_state_update(pkv)
                    for h in range(4):
                        pt, ptb_ = py[h // 2]
                        cs_ = slice((h % 2) * 256, (h % 2) * 256 + 256)
                        S.op("dve", lambda en, h=h, pt=pt, cs_=cs_: en.bn_stats(out=gst[:, h, :], in_=pt[:, cs_]), reads=[ptb_], writes=[gs_b])
                        S.op("dve", lambda en, h=h: en.bn_aggr(out=gmv[:, h, 0:2], in_=gst[:, h, :]), reads=[gs_b], writes=[gs_b])
                    S.op("dve", lambda en: en.tensor_scalar_add(out=gmv[:, :, 2:3], in0=gmv[:, :, 1:2], scalar1=float(EPS)), reads=[gs_b], writes=[gs_b])
                    S.op("act", lambda en: en.activation(out=gmv[:, :, 2:3], in_=gmv[:, :, 2:3], func=AF.Sqrt), reads=[gs_b], writes=[gs_b])
                    S.op("dve", lambda en: en.reciprocal(out=gmv[:, :, 2:3], in_=gmv[:, :, 2:3]), reads=[gs_b], writes=[gs_b])
                    S.op("dve", lambda en: en.scalar_tensor_tensor(out=gmv[:, :, 3:4], in0=gmv[:, :, 0:1], scalar=-1.0, in1=gmv[:, :, 2:3],
                                                                   op0=ALU.mult, op1=ALU.mult), reads=[gs_b], writes=[gs_b])
                    for h in range(4):
                        pt, ptb_ = py[h // 2]
                        cs_ = slice((h % 2) * 256, (h % 2) * 256 + 256)
                        S.op("act", lambda en, h=h, pt=pt, cs_=cs_: en.activation(out=yr[:, h * 256:(h + 1) * 256], in_=pt[:, cs_], func=AF.Identity,
                                                                                  bias=gmv[:, h, 3:4], scale=gmv[:, h, 2:3]),
                             reads=[ptb_, gs_b], writes=[yr_b])
                    S.op("dve", lambda en: en.tensor_tensor(out=yr[:, :], in0=yr[:, :], in1=gng[:, :], op=ALU.mult), reads=[yr_b, gn_b], writes=[yr_b])
                    S.op("dve", lambda en: en.tensor_tensor(out=yr[:, :], in0=yr[:, :], in1=gnb[:, :], op=ALU.add), reads=[yr_b, gn_b], writes=[yr_b])
                    S.op("dve", lambda en: en.scalar_tensor_tensor(out=yrb[:, :], in0=yr[:, :], scalar=0.5, in1=sg[:, :],
                                                                   op0=ALU.mult, op1=ALU.mult), reads=[yr_b, sg_b], writes=[yrb_b])
                    pt, ptb_ = c.ps.get()
                    ptv = ps_bf(pt)
                    for j in range(8):
                        S.op("pe", lambda en, j=j, ptv=ptv: en.transpose(out=ptv[:, j * 128:(j + 1) * 128], in_=yrb[:, j * 128:(j + 1) * 128],
                                                                         identity=c.ident_bf[:, :]), reads=[yrb_b, c.ident_buf], writes=[ptb_])
                    S.op("act", lambda en, ptv=ptv: en.copy(out=yrT[:, :, :], in_=ptv[:, :].rearrange("p (j t) -> p j t", j=8)),
                         reads=[ptb_], writes=[yrT_b])
                    halves = [c.ps.get(), c.ps.get()]
                    for kc in range(16):
                        lt = ysl[yi][:, kc, :] if kc < 8 else yrT[:, kc - 8, :]
                        lb = ysl_b[yi] if kc < 8 else yrT_b
                        for hf in range(2):
                            ph, phb = halves[hf]
                            S.op("pe", lambda en, ph=ph, kc=kc, hf=hf, lt=lt: en.matmul(
                                ph[:, :], lhsT=lt, rhs=Wout[:, kc, hf * 512:(hf + 1) * 512], start=(kc == 0), stop=(kc == 15)),
                                reads=[lb, Wout_b], writes=[phb])
                    r0 = g * GT + t * 128
                    S.dma("sp", xres[:, :], src[r0:r0 + 128, :], writes=[xres_b])
                    xi = ci % 2
                    v, v_b = v_r[xi]
                    st, smm_b = st_r[xi]
                    mv = mv_r[xi]
                    emit_ln_epilogue(c, (v, v_b, st, mv, smm_b), halves, xres, xres_b, g_t, b_t, gb_buf, v, v_b)
                    outs.append(S.dma("sp", dst[r0:r0 + 128, :], v[:, :], reads=[v_b]))
    S.barrier()
    return outs


def emit_halo_exchange(c, xsrc, NT, hidx):
    S, nc = c.S, c.nc
    hl_loc = nc.dram_tensor(f"hl_loc{hidx}", [128, D], F32, kind="Internal").ap()
    hl_all = nc.dram_tensor(f"hl_all{hidx}", [4 * 128, D], F32, kind="Internal").ap()
    halo_d = nc.dram_tensor(f"halo_d{hidx}", [128, D], F32, kind="Internal").ap()
    lb, ab_, hb_ = Buf("hl_loc"), Buf("hl_all"), Buf("halo_d")
    with ExitStack() as es:
        sb = lambda name, shape, dt: _sb(c, es, "h_" + name, shape, dt)
        t0 = sb("t0", [128, D], F32); t0_b = Buf("ht0")
        rec = [sb(f"rec{i}", [128, D], F32) for i in range(2)]; rec_b = [Buf(f"hrec{i}") for i in range(2)]
        acc = sb("acc", [128, D], F32); acc_b = Buf("hacc")
        hsel = sb("hsel", [128, 4], F32); hsel_b = Buf("hsel")
        S.dma("sp", hsel[:, :], c.dram["hsel"], writes=[hsel_b])
        S.dma("sp", t0[:, :], xsrc[NT - 128:NT, :], writes=[t0_b])
        S.dma("sp", hl_loc, t0[:, :], reads=[t0_b], writes=[lb])
        if c.use_cc:
            S.op("pool", lambda en: en.collective_compute("AllGather", ALU.bypass, replica_groups=[[0, 1, 2, 3], [4, 5, 6, 7]],
                                                          ins=[hl_loc], outs=[hl_all]), reads=[lb], writes=[ab_])
        else:
            for i in range(4):
                S.dma("sp", hl_all[i * 128:(i + 1) * 128, :], hl_loc, reads=[lb], writes=[ab_])
        for i in range(4):
            rb, rbb = rec[i % 2], rec_b[i % 2]
            S.dma("sp", rb[:, :], hl_all[i * 128:(i + 1) * 128, :], reads=[ab_], writes=[rbb])
            if i == 0:
                S.op("dve", lambda en, rb=rb: en.tensor_scalar_mul(out=acc[:, :], in0=rb[:, :], scalar1=hsel[:, 0:1]),
                     reads=[rbb, hsel_b], writes=[acc_b])
            else:
                S.op("dve", lambda en, rb=rb, i=i: en.scalar_tensor_tensor(out=acc[:, :], in0=rb[:, :], scalar=hsel[:, i:i + 1],
                                                                           in1=acc[:, :], op0=ALU.mult, op1=ALU.add),
                     reads=[rbb, hsel_b, acc_b], writes=[acc_b])
        S.dma("sp", halo_d, acc[:, :], reads=[acc_b], writes=[hb_])
    S.barrier()
    return halo_d


EVEN_IN = 1792
ODD_IN = 5648

PER_LAYER_SHAPES = {
    "ln_mix_g": [1, D], "ln_mix_b": [1, D], "ln_ffn_g": [1, D], "ln_ffn_b": [1, D],
    "ffn_w_gate": [D, FH], "ffn_w_up": [D, FH], "ffn_w_down": [FH, D],
}
EVEN_SHAPES = {
    "ev_w_in": [D, EVEN_IN], "ev_sinks": [1, 8], "ev_dw_w": [31, 512], "ev_dw_b": [1, 512],
    "ev_cn_g": [1, 512], "ev_cn_b": [1, 512], "ev_w_out": [D, D],
}
ODD_SHAPES = {
    "od_w_in": [D, ODD_IN], "od_conv_w": [4, 1536], "od_conv_b": [1, 1536], "od_dt_bias": [1, 16],
    "od_a_log": [1, 16], "od_d_skip": [1, 16], "od_ssm_norm_g": [1, D], "od_ret_gn_g": [1, D],
    "od_ret_gn_b": [1, D], "od_w_out": [2 * D, D],
}


def stage_inputs(stages):
    need = {"x": None, "ident": [128, 128], "ones": [128, 128]}
    for kind, idx in stages:
        if kind == "ffn":
            for k in ("ln_ffn_g", "ln_ffn_b", "ffn_w_gate", "ffn_w_up", "ffn_w_down"):
                need[f"{k}{idx}"] = PER_LAYER_SHAPES[k]
        elif kind == "even":
            layer = 2 * idx
            for k in ("ln_mix_g", "ln_mix_b"):
                need[f"{k}{layer}"] = PER_LAYER_SHAPES[k]
            for k, s in EVEN_SHAPES.items():
                need[f"{k}{idx}"] = s
            need["amask"] = [128, 256]
            need["amask0"] = [128, 256]
            need["rope_a"] = None
            need["halo"] = [128, D]
        elif kind == "odd":
            layer = 2 * idx + 1
            for k in ("ln_mix_g", "ln_mix_b"):
                need[f"{k}{layer}"] = PER_LAYER_SHAPES[k]
            for k, s in ODD_SHAPES.items():
                need[f"{k}{idx}"] = s
            need["halo"] = [128, D]
            need["odd_tab"] = [128, 32]
            need["triu"] = [128, 128]
            need["trisl"] = [128, 128]
            need["rope_d"] = None
    if sum(1 for k, _ in stages if k in ("even", "odd")) > 1:
        need["hsel"] = [128, 4]
    return need


def build_program(NT, stages, use_cc=True):
    nc = bass.Bass("TRN2", target_bir_lowering=False)
    c = Ctx()
    c.nc = nc
    c.NT = NT
    c.use_cc = use_cc
    c.st_loc = {}
    c.st_all = {}
    for kind, idx in stages:
        if kind == "odd":
            c.st_loc[idx] = (nc.dram_tensor(f"loc_s{idx}", [128, 1040], F32, kind="Internal").ap(),
                             nc.dram_tensor(f"loc_r{idx}", [128, 1024], F32, kind="Internal").ap())
            c.st_all[idx] = (nc.dram_tensor(f"all_s{idx}", [4 * 128, 1040], F32, kind="Internal").ap(),
                             nc.dram_tensor(f"all_r{idx}", [4 * 128, 1024], F32, kind="Internal").ap())
            if not hasattr(c, "ysT_d"):
                c.ysT_d = nc.dram_tensor("ysT_d", [8, 128, NT], BF16, kind="Internal").ap()
                c.xbc_d = nc.dram_tensor("xbc_d", [12, 128, NT], BF16, kind="Internal").ap()
                c.smg_d = nc.dram_tensor("smg_d", [NT // 512, 128, 128], F32, kind="Internal").ap()
                c.kp_d = nc.dram_tensor("kp_d", [NT, 512], BF16, kind="Internal").ap()
                c.vb_d = nc.dram_tensor("vb_d", [NT, 1024], BF16, kind="Internal").ap()
    es = ExitStack()
    c.es = es
    c.dram = {}
    need = stage_inputs(stages)
    need["x"] = [NT, D]
    if "rope_a" in need:
        need["rope_a"] = [128, NT // 128 + 1, 16]
    if "rope_d" in need:
        need["rope_d"] = [128, NT // 128, 128]
    for name, shape in need.items():
        c.dram[name] = nc.dram_tensor(name, list(shape), F32, kind="ExternalInput").ap()
    y = nc.dram_tensor("y", [NT, D], F32, kind="ExternalOutput").ap()
    xa = nc.dram_tensor("xa", [NT, D], F32, kind="Internal").ap()
    xb = nc.dram_tensor("xb", [NT, D], F32, kind="Internal").ap()
    with es:
        c.S = Sched(nc, es)
        c.ps = PsumPool(c)
        c.ident_f = _sb(c, es, "ident_f", [128, 128], F32)
        c.ident_bf = _sb(c, es, "ident_bf", [128, 128], BF16)
        c.ones_f = _sb(c, es, "ones_f", [128, 128], F32)
        c.ident_buf = Buf("ident")
        c.identf_buf = Buf("identf")
        c.ones_buf = Buf("ones")
        c.S.dma("sp", c.ident_f[:, :], c.dram["ident"], writes=[c.identf_buf])
        c.S.dma("sp", c.ones_f[:, :], c.dram["ones"], writes=[c.ones_buf])
        c.S.op("dve", lambda e: e.tensor_copy(out=c.ident_bf[:, :], in_=c.ident_f[:, :]), reads=[c.identf_buf],
               writes=[c.ident_buf])
        cur = c.dram["x"]
        bufs = [xa, xb]
        outs = []
        nmix = 0
        for si, (kind, idx) in enumerate(stages):
            last = si == len(stages) - 1
            dst = y if last else bufs[si % 2]
            if kind in ("even", "odd"):
                halo = c.dram["halo"] if nmix == 0 else emit_halo_exchange(c, cur, NT, nmix)
                nmix += 1
            if kind == "ffn":
                outs = emit_ffn(c, idx, cur, dst, NT)
            elif kind == "even":
                outs = emit_even(c, idx, 2 * idx, cur, dst, halo, NT)
            elif kind == "odd":
                outs = emit_odd(c, idx, 2 * idx + 1, cur, dst, halo, NT)
            else:
                raise NotImplementedError(kind)
            cur = dst
        c.S.emit(final_ops=outs)
    return nc


ROPE_THETA = 500000.0


def rope_table_a(pos0, NT):
    nt = NT // 128 + 1
    pos = (pos0 - 128 + np.arange(nt * 128)).astype(np.float32)
    inv = np.power(np.float32(ROPE_THETA), -np.arange(8, dtype=np.float32) / np.float32(8)).astype(np.float32)
    ang = (pos[:, None] * inv[None, :]).astype(np.float32)
    tab = np.concatenate([np.cos(ang), np.sin(ang)], axis=1).astype(np.float32)
    return np.ascontiguousarray(tab.reshape(nt, 128, 16).transpose(1, 0, 2))


RET_THETA = 10000.0


def rope_table_d(pos0, NT):
    nt = NT // 128
    pos = (pos0 + np.arange(nt * 128)).astype(np.float32)
    inv = (1.0 / np.power(np.float32(RET_THETA), np.linspace(0.0, 1.0, 64, dtype=np.float32))).astype(np.float32)
    ang = (pos[:, None] * inv[None, :]).astype(np.float32)
    tab = np.concatenate([np.cos(ang), np.sin(ang)], axis=1).astype(np.float32)
    return np.ascontiguousarray(tab.reshape(nt, 128, 128).transpose(1, 0, 2))


def odd_table(r, NT):
    h = np.arange(4, dtype=np.float64)
    lg = np.log(1.0 - np.power(2.0, -5.0 - h))
    l = np.arange(128, dtype=np.float64)[:, None]
    t = np.zeros((128, 32), np.float64)
    t[:, 0:4] = np.exp(lg[None, :] * (l + 1.0))
    t[:, 4:8] = np.exp(-lg[None, :] * (l + 1.0)) * (128.0 ** -0.5)
    t[:, 8:12] = np.exp(lg * 128.0)[None, :]
    for i in range(4):
        if i < r:
            t[:, 12 + 4 * i:16 + 4 * i] = np.exp(lg * float(NT * (r - 1 - i)))[None, :]
            t[:, 28 + i] = 1.0
    return t.astype(np.float32)


def attn_masks(first):
    i = np.arange(128)[:, None]
    j = np.arange(256)[None, :]
    valid = (j > i) & (j <= i + 128)
    m = np.where(valid, 0.0, A_MASK_NEG).astype(np.float32)
    m0 = m.copy()
    if first:
        m0[:, :128] = A_MASK_NEG
    return m, m0


def make_in_map(inp, x_shard, halo, cidx, NT, stages):
    need = stage_inputs(stages)
    r = cidx % 4
    m = {}
    for name in need:
        if name == "x":
            m[name] = np.ascontiguousarray(x_shard, dtype=np.float32)
        elif name == "ident":
            m[name] = np.eye(128, dtype=np.float32)
        elif name == "ones":
            m[name] = np.ones((128, 128), dtype=np.float32)
        elif name == "halo":
            m[name] = np.ascontiguousarray(halo, dtype=np.float32)
        elif name == "amask":
            m[name] = attn_masks(False)[0]
        elif name == "amask0":
            m[name] = attn_masks(r == 0)[1]
        elif name == "rope_a":
            m[name] = rope_table_a(r * NT, NT)
        elif name == "rope_d":
            m[name] = rope_table_d(r * NT, NT)
        elif name == "odd_tab":
            m[name] = odd_table(r, NT)
        elif name == "hsel":
            hs = np.zeros((128, 4), np.float32)
            if r > 0:
                hs[:, r - 1] = 1.0
            m[name] = hs
        elif name == "triu":
            m[name] = np.triu(np.ones((128, 128), np.float32))
        elif name == "trisl":
            m[name] = np.tril(np.ones((128, 128), np.float32), -1).T.copy().T if False else (np.arange(128)[:, None] > np.arange(128)[None, :]).astype(np.float32)
        else:
            base = name.rstrip("0123456789")
            idx = int(name[len(base):])
            m[name] = np.ascontiguousarray(np.asarray(inp[base][idx], dtype=np.float32).reshape(need[name]))
    return m


SEQ = 16384
BATCH = 2
FUSED = True
ALL_STAGES = [("even", 0), ("ffn", 0), ("odd", 0), ("ffn", 1), ("even", 1), ("ffn", 2), ("odd", 1), ("ffn", 3)]


def run_stages(inp, x, stages):
    S_ = x.shape[1]
    NT = S_ // 4
    nc = build_program(NT, stages)
    in_maps = []
    for cidx in range(NCORES):
        b, r = cidx // 4, cidx % 4
        halo = x[b, r * NT - 128:r * NT] if r > 0 else np.zeros((128, D), np.float32)
        in_maps.append(make_in_map(inp, x[b, r * NT:(r + 1) * NT], halo, cidx, NT, stages))
    res = run_bass_kernel_spmd(nc, in_maps, core_ids=list(range(NCORES)))
    out = np.empty_like(x)
    for cidx in range(NCORES):
        b, r = cidx // 4, cidx % 4
        out[b, r * NT:(r + 1) * NT] = res.results[cidx]["y"]
    return out


def kernel(**inputs):
    inp = {k: np.asarray(v) for k, v in inputs.items()}
    x = np.ascontiguousarray(inp["x"], dtype=np.float32)
    if FUSED:
        return run_stages(inp, x, ALL_STAGES)
    for li in range(DEPTH):
        x = run_stages(inp, x, ALL_STAGES[2 * li:2 * li + 2])
    return x
```

```python
import numpy as np
import os as _os_env
from contextlib import ExitStack
import concourse.bass as bass
import concourse.mybir as mybir
from concourse.bass_utils import run_bass_kernel_spmd

F32 = mybir.dt.float32
BF16 = mybir.dt.bfloat16
ALU = mybir.AluOpType
AF = mybir.ActivationFunctionType

D = 1024
FH = 2816
DEPTH = 4
ALPHA = (2 * DEPTH) ** 0.25
EPS = 1e-5
NCORES = 8


class Buf:
    __slots__ = ("name", "w", "rs")

    def __init__(self, name):
        self.name = name
        self.w = None
        self.rs = []


class Op:
    __slots__ = ("stream", "fn", "deps", "adeps", "needed", "sem", "val", "is_dma", "idx", "seg", "cost", "lat", "dq_index")


class _Rec:
    def __init__(self):
        self.call = None

    def __getattr__(self, name):
        def f(*a, **kw):
            assert self.call is None, "one engine instruction per op"
            self.call = (name, a, kw)
            return None
        return f


_NOWAR = bool(_os_env.environ.get('NOWAR'))
ATTACH_WAIT = _os_env.environ.get('ATTACH_WAIT', '1') == '1'
ACT_SMART = _os_env.environ.get('ACT_SMART', '0') == '1'


class Sched:
    STRICT = tuple(x for x in _os_env.environ.get("STRICT_ENG", "act,dve,pool").split(",") if x)
    NDMA = 10
    CHAIN = tuple(x for x in _os_env.environ.get("CHAIN_ENG", "act").split(",") if x)
    REORDER = _os_env.environ.get("REORDER", "1") == "1"

    def __init__(self, nc, es):
        self.nc = nc
        self.streams = {k: [] for k in ("pe", "act", "dve", "pool", "sp")}
        self.sem = {k: es.enter_context(nc.semaphore("pg_" + k)) for k in self.streams}
        self.dsem = {q: [es.enter_context(nc.semaphore(f"dq_{q}{i}")) for i in range(self.NDMA)]
                     for q in ("sp", "pool", "act")}
        self.dcnt = {q: 0 for q in self.dsem}
        self.dring = {q: [None] * self.NDMA for q in self.dsem}
        self.seg = 0
        self.all_ops = []
        self.last_on = {}
        self.act_cls = -1

    @staticmethod
    def _cost(stream, name, a, kw, is_dma):
        def fsz(ap):
            sh = ap.shape
            n = 1
            for s_ in sh[1:]:
                n *= int(s_)
            return n
        try:
            if is_dma:
                o = kw.get("out")
                nbytes = fsz(o) * int(o.shape[0]) * 4
                return (1000.0 if stream == "pool" else 100.0), 2000.0 + nbytes / 150.0
            if name == "collective_compute":
                return 1000.0, 60000.0
            if stream == "pe":
                if name == "transpose":
                    return 110.0, 250.0
                rhs = kw.get("rhs")
                n = fsz(rhs)
                mult = 4.0 if rhs.dtype == F32 else 1.0
                return mult * (40.0 + 0.47 * n), 250.0
            o = kw.get("out", a[0] if a else None)
            f = fsz(o) if o is not None else 64
            if stream == "act":
                return 170.0 + 0.9 * f, 250.0
            if stream == "dve":
                return 70.0 + 0.66 * f, 250.0
            return 200.0 + 2.0 * f, 200.0
        except Exception:
            return 300.0, 200.0

    def _add(self, stream, fn, reads, writes, is_dma=False, extra=()):
        op = Op()
        op.stream = stream
        rec = _Rec()
        fn(rec)
        name_, a_, kw_ = rec.call
        op.fn = lambda e, name_=name_, a_=a_, kw_=kw_: getattr(e, name_)(*a_, **kw_)
        op.is_dma = is_dma
        op.needed = False
        op.sem = None
        op.val = 0
        op.seg = self.seg
        op.cost, op.lat = self._cost(stream, name_, a_, kw_, is_dma)
        deps = []
        for b in reads:
            if b.w is not None:
                deps.append(b.w)
        for b in writes:
            if b.w is not None:
                deps.append(b.w)
            if not _NOWAR:
                deps.extend(b.rs)
        deps.extend(extra)
        seen = set()
        dd = []
        for d in deps:
            if id(d) in seen:
                continue
            seen.add(id(d))
            dd.append(d)
        if stream in self.CHAIN and self.last_on.get(stream) is not None and self.last_on[stream].seg == self.seg:
            lo = self.last_on[stream]
            chain = True
            if ACT_SMART and stream == "act":
                f_ = kw_.get("func", None)
                cls = 0
                if f_ in (AF.Exp, AF.Tanh):
                    cls = 1
                elif f_ == AF.Sqrt:
                    cls = 2
                elif f_ == AF.Ln:
                    cls = 3
                chain = cls != 0 and cls != self.act_cls
                if cls != 0:
                    self.act_cls = cls
            if chain and id(lo) not in seen:
                dd.append(lo)
        self.last_on[stream] = op
        op.adeps = dd
        for b in reads:
            b.rs.append(op)
        for b in writes:
            b.w = op
            b.rs = []
        op.idx = len(self.all_ops)
        self.all_ops.append(op)
        return op

    def op(self, stream, fn, reads=(), writes=()):
        return self._add(stream, fn, reads, writes)

    def dma(self, q, out, in_, reads=(), writes=(), **kw):
        i = self.dcnt[q]
        self.dcnt[q] += 1
        slot = i % self.NDMA
        prev = self.dring[q][slot]
        extra = (prev,) if prev is not None else ()
        op = self._add(q, lambda e: e.dma_start(out=out, in_=in_, **kw), reads, writes, is_dma=True, extra=extra)
        op.sem = self.dsem[q][slot]
        op.val = 16 * (i // self.NDMA + 1)
        op.needed = True
        op.dq_index = i
        self.dring[q][slot] = op
        return op

    def barrier(self):
        self.seg += 1

    def _schedule(self):
        import heapq
        streams = {k: [] for k in self.streams}
        ops = self.all_ops
        if not self.REORDER:
            self.est_ns = 0.0
            cur_seg = 0
            fence = []
            first = {k: False for k in self.streams}
            last_dma = {q: {} for q in self.dsem}
            for op in ops:
                if op.seg != cur_seg:
                    cur_seg = op.seg
                    fence = []
                    for k, lst in streams.items():
                        for o2 in reversed(lst):
                            if not o2.is_dma:
                                fence.append(o2)
                                break
                    for q in last_dma:
                        fence.extend(last_dma[q].values())
                    first = {k: True for k in self.streams}
                if first[op.stream]:
                    first[op.stream] = False
                    op.adeps = list(op.adeps) + [f for f in fence if f is not op]
                streams[op.stream].append(op)
                if op.is_dma:
                    last_dma[op.stream][op.dq_index % self.NDMA] = op
            return streams
        nseg = self.seg + 1
        by_seg = [[] for _ in range(nseg)]
        for op in ops:
            by_seg[op.seg].append(op)
        fin = {}
        free = {k: 0.0 for k in self.streams}
        last_dma = {q: {} for q in self.dsem}
        fence = []
        tnow = 0.0
        for s in range(nseg):
            seg_ops = by_seg[s]
            if not seg_ops:
                continue
            first_in_stream = {k: True for k in self.streams}
            nun = {}
            users = {}
            for op in seg_ops:
                c = 0
                for d in op.adeps:
                    if d.seg == s:
                        c += 1
                        users.setdefault(id(d), []).append(op)
                nun[id(op)] = c
            ready = {k: [] for k in self.streams}

            def est_ready(op):
                t = tnow
                for d in op.adeps:
                    if d.seg == s:
                        f = fin[id(d)]
                        if d.stream != op.stream or d.is_dma:
                            f += d.lat
                        elif op.stream in self.STRICT:
                            f += 120.0
                        t = max(t, f)
                return t
            for op in seg_ops:
                if nun[id(op)] == 0:
                    heapq.heappush(ready[op.stream], (est_ready(op), op.idx, op))
            nleft = len(seg_ops)
            while nleft:
                best = None
                for k in self.streams:
                    if not ready[k]:
                        continue
                    if self.REORDER:
                        cand = None
                        tmp = []
                        while ready[k] and ready[k][0][0] <= free[k]:
                            tmp.append(heapq.heappop(ready[k]))
                        if tmp:
                            cand = min(tmp, key=lambda x: x[1])
                            for x in tmp:
                                if x is not cand:
                                    heapq.heappush(ready[k], x)
                            st = free[k]
                        else:
                            cand = heapq.heappop(ready[k])
                            st = cand[0]
                    else:
                        cand = min(ready[k], key=lambda x: x[1])
                        ready[k].remove(cand)
                        heapq.heapify(ready[k])
                        st = max(cand[0], free[k])
                    if best is None or (st, cand[1]) < (best[0], best[1][1]):
                        if best is not None:
                            heapq.heappush(ready[best[2]], best[1])
                        best = (st, cand, k)
                    else:
                        heapq.heappush(ready[k], cand)
                st, cand, k = best
                op = cand[2]
                if first_in_stream[k]:
                    first_in_stream[k] = False
                    if fence:
                        op.adeps = list(op.adeps) + [f for f in fence if f is not op]
                free[k] = st + op.cost
                fin[id(op)] = st + op.cost
                streams[k].append(op)
                if op.is_dma:
                    last_dma[k][op.dq_index % self.NDMA] = op
                nleft -= 1
                for u in users.get(id(op), ()):
                    nun[id(u)] -= 1
                    if nun[id(u)] == 0:
                        heapq.heappush(ready[u.stream], (est_ready(u), u.idx, u))
            fence = []
            for k, lst in streams.items():
                for op in reversed(lst):
                    if not op.is_dma:
                        fence.append(op)
                        break
            for q in last_dma:
                fence.extend(last_dma[q].values())
            tnow = max(list(free.values()) + [fin[id(f)] + f.lat for f in fence])
            for k in free:
                free[k] = tnow
        self.est_ns = max(free.values())
        return streams

    def emit(self, final_ops=()):
        streams = self._schedule()
        self.streams = streams
        for k, lst in streams.items():
            for op in lst:
                dd = []
                for d in op.adeps:
                    if (not d.is_dma) and d.stream == k and k not in self.STRICT:
                        continue
                    dd.append(d)
                    d.needed = True
                op.deps = dd
        for k, lst in streams.items():
            cnt = 0
            for op in lst:
                if op.is_dma:
                    continue
                if op.needed:
                    cnt += 1
                    op.sem = self.sem[k]
                    op.val = cnt
        print('SCHED ops', {k: len(v) for k, v in streams.items()}, 'semmax',
              {k: max([o.val for o in v if not o.is_dma] + [0]) for k, v in streams.items()},
              'est_ms', round(self.est_ns / 1e6, 3), 'busy_ms', {k: round(sum(o.cost for o in v) / 1e6, 3) for k, v in streams.items()}, flush=True)
        with self.nc.Block() as block:
            def run(k):
                def body(e):
                    waited = {}

                    def wait(d):
                        key = id(d.sem)
                        if waited.get(key, 0) < d.val:
                            e.wait_ge(d.sem, d.val)
                            waited[key] = d.val
                    for op in streams[k]:
                        pend = []
                        for d in op.deps:
                            key = id(d.sem)
                            if waited.get(key, 0) < d.val:
                                pend = [p for p in pend if id(p.sem) != key or p.val > d.val]
                                if not any(id(p.sem) == key for p in pend):
                                    pend.append(d)
                        attach = None
                        if ATTACH_WAIT and pend and not op.is_dma and k in ('pe', 'act', 'dve'):
                            attach = pend.pop()
                        for d in pend:
                            wait(d)
                        ins = op.fn(e)
                        if attach is not None:
                            ins._wait_ge(attach.sem, attach.val)
                            waited[id(attach.sem)] = attach.val
                        if op.is_dma:
                            ins.then_inc(op.sem, 16)
                        elif op.needed:
                            ins.then_inc(op.sem, 1)
                    if k == "sp":
                        for d in final_ops:
                            wait(d)
                return body
            block.tensor(run("pe"))
            block.scalar(run("act"))
            block.vector(run("dve"))
            block.gpsimd(run("pool"))
            block.sync(run("sp"))


class Ctx:
    pass


_UID = [0]


def _sb(c, es, name, shape, dt):
    _UID[0] += 1
    return es.enter_context(c.nc.sbuf_tensor(f"{name}_{_UID[0]}", list(shape), dt))


class PsumPool:
    def __init__(self, c, n=8):
        self.t = [c.es.enter_context(c.nc.psum_tensor(f"ps{i}", [128, 512], F32)) for i in range(n)]
        self.b = [Buf(f"ps{i}") for i in range(n)]
        self.i = 0
        self.n = n

    def get(self):
        i = self.i
        self.i = (self.i + 1) % self.n
        return self.t[i], self.b[i]


def load_bcast_row(c, q, dst_tile, dst_buf, dram_ap_row):
    n = dst_tile.shape[-1]
    return c.S.dma(q, dst_tile[:, :], dram_ap_row.to_broadcast([128, n]), writes=[dst_buf])


def emit_xT(c, xT, xT_buf, tslot, x_tile, x_buf, xbf, xbf_buf):
    S = c.S
    S.op("act", lambda e: e.copy(out=xbf[:, :], in_=x_tile[:, :]), reads=[x_buf], writes=[xbf_buf])
    pt, pb = c.ps.get()
    ptb = pt[:, :].bitcast(BF16)
    for k in range(8):
        S.op("pe", lambda e, k=k: e.transpose(out=ptb[:, k * 128:(k + 1) * 128], in_=xbf[:, k * 128:(k + 1) * 128],
                                               identity=c.ident_bf[:, :]),
             reads=[xbf_buf, c.ident_buf], writes=[pb])
    S.op("dve", lambda e: e.tensor_copy(out=xT[:, :, tslot * 128:(tslot + 1) * 128],
                                        in_=ptb.rearrange("p (k t) -> p k t", k=8)),
         reads=[pb], writes=[xT_buf])


def emit_ln_epilogue(c, es_bufs, ps_halves, x_old, x_old_buf, g_t, b_t, gb_buf, out_tile, out_buf):
    S = c.S
    v, v_buf, st, mv, sm_buf = es_bufs
    for hf in range(2):
        pt, pb = ps_halves[hf]
        S.op("dve", lambda e, hf=hf, pt=pt: e.scalar_tensor_tensor(
            out=v[:, hf * 512:(hf + 1) * 512], in0=x_old[:, hf * 512:(hf + 1) * 512], scalar=float(ALPHA),
            in1=pt[:, :], op0=ALU.mult, op1=ALU.add), reads=[x_old_buf, pb], writes=[v_buf])
    ln_core(c, v, v_buf, st, mv, sm_buf, g_t, b_t, gb_buf, out_tile, out_buf)


def ln_core(c, v, v_buf, st, mv, sm_buf, g_t, b_t, gb_buf, out_tile, out_buf):
    S = c.S
    for hf in range(2):
        S.op("dve", lambda e, hf=hf: e.bn_stats(out=st[:, hf * 6:(hf + 1) * 6], in_=v[:, hf * 512:(hf + 1) * 512]),
             reads=[v_buf], writes=[sm_buf])
    S.op("dve", lambda e: e.bn_aggr(out=mv[:, 0:2], in_=st[:, 0:12]), reads=[sm_buf], writes=[sm_buf])
    S.op("dve", lambda e: e.tensor_scalar_add(out=mv[:, 2:3], in0=mv[:, 1:2], scalar1=float(EPS)),
         reads=[sm_buf], writes=[sm_buf])
    S.op("act", lambda e: e.activation(out=mv[:, 2:3], in_=mv[:, 2:3], func=AF.Sqrt), reads=[sm_buf], writes=[sm_buf])
    S.op("dve", lambda e: e.reciprocal(out=mv[:, 2:3], in_=mv[:, 2:3]), reads=[sm_buf], writes=[sm_buf])
    S.op("dve", lambda e: e.scalar_tensor_tensor(out=mv[:, 3:4], in0=mv[:, 0:1], scalar=-1.0, in1=mv[:, 2:3],
                                                 op0=ALU.mult, op1=ALU.mult), reads=[sm_buf], writes=[sm_buf])
    S.op("act", lambda e: e.activation(out=v[:, :], in_=v[:, :], func=AF.Identity, bias=mv[:, 3:4], scale=mv[:, 2:3]),
         reads=[v_buf, sm_buf], writes=[v_buf])
    S.op("dve", lambda e: e.tensor_tensor(out=v[:, :], in0=v[:, :], in1=g_t[:, :], op=ALU.mult),
         reads=[v_buf, gb_buf], writes=[v_buf])
    S.op("dve", lambda e: e.tensor_tensor(out=out_tile[:, :], in0=v[:, :], in1=b_t[:, :], op=ALU.add),
         reads=[v_buf, gb_buf], writes=[out_buf])


def emit_ffn(c, layer, src, dst, NT):
    S, nc = c.S, c.nc
    T = min(1024, NT)
    ntile = T // 128
    nblk = T // 512
    outs = []
    with ExitStack() as es:
        xT = _sb(c, es, "f_xT", [128, 8, T], BF16)
        xT_buf = Buf("f_xT")
        hT = _sb(c, es, "f_hT", [128, 22, T], BF16)
        hT_bufs = [Buf(f"f_hT{j}") for j in range(22)]
        NWB = 2
        wg = [_sb(c, es, f"f_wg{i}", [128, 8, 512], BF16) for i in range(NWB)]
        wu = [_sb(c, es, f"f_wu{i}", [128, 8, 512], BF16) for i in range(NWB)]
        wg_b = [Buf(f"f_wg{i}") for i in range(NWB)]
        wu_b = [Buf(f"f_wu{i}") for i in range(NWB)]
        wd = _sb(c, es, "f_wd", [128, 22, 1024], BF16)
        jgroups = [(j0, min(4, 22 - j0)) for j0 in range(0, 22, 4)]
        wd_b = [Buf(f"f_wd{g}") for g in range(len(jgroups))]
        NXB = 2
        xin = [_sb(c, es, f"f_xin{i}", [128, 1024], F32) for i in range(NXB)]
        xin_b = [Buf(f"f_xin{i}") for i in range(NXB)]
        xbf = [_sb(c, es, f"f_xbf{i}", [128, 1024], BF16) for i in range(NXB)]
        xbf_b = [Buf(f"f_xbf{i}") for i in range(NXB)]
        sg = [_sb(c, es, f"f_sg{i}", [128, 512], F32) for i in range(2)]
        sg_b = [Buf(f"f_sg{i}") for i in range(2)]
        v = [_sb(c, es, f"f_v{i}", [128, 1024], F32) for i in range(2)]
        v_b = [Buf(f"f_v{i}") for i in range(2)]
        st = [_sb(c, es, f"f_st{i}", [128, 12], F32) for i in range(2)]
        mv = [_sb(c, es, f"f_mv{i}", [128, 4], F32) for i in range(2)]
        sm_b = [Buf(f"f_sm{i}") for i in range(2)]
        xo = [_sb(c, es, f"f_xo{i}", [128, 1024], F32) for i in range(2)]
        xo_b = [Buf(f"f_xo{i}") for i in range(2)]
        g_t = _sb(c, es, "f_g", [128, 1024], F32)
        b_t = _sb(c, es, "f_b", [128, 1024], F32)
        gb_buf = Buf("f_gb")
        load_bcast_row(c, "sp", g_t, gb_buf, c.dram[f"ln_ffn_g{layer}"])
        load_bcast_row(c, "sp", b_t, gb_buf, c.dram[f"ln_ffn_b{layer}"])
        Wg = c.dram[f"ffn_w_gate{layer}"]
        Wu = c.dram[f"ffn_w_up{layer}"]
        Wd = c.dram[f"ffn_w_down{layer}"]
        xcnt = 0
        wcnt = 0
        ecnt = 0
        first = True
        for g0 in range(0, NT, T):
            for t in range(ntile):
                i = xcnt % NXB
                xcnt += 1
                r0 = g0 + t * 128
                S.dma("sp", xin[i][:, :], src[r0:r0 + 128, :], writes=[xin_b[i]])
                emit_xT(c, xT, xT_buf, t, xin[i], xin_b[i], xbf[i], xbf_b[i])
            for gi, (j0, nj) in enumerate(jgroups):
                wi = wcnt % NWB
                wcnt += 1
                c0 = j0 * 128
                ncol = nj * 128
                S.dma("pool", wg[wi][:, :, 0:ncol], Wg[:, c0:c0 + ncol].rearrange("(k p) n -> p k n", p=128),
                      writes=[wg_b[wi]])
                S.dma("pool", wu[wi][:, :, 0:ncol], Wu[:, c0:c0 + ncol].rearrange("(k p) n -> p k n", p=128),
                      writes=[wu_b[wi]])
                if first:
                    S.dma("pool", wd[:, j0:j0 + nj, :], Wd[c0:c0 + ncol, :].rearrange("(j p) n -> p j n", p=128),
                          writes=[wd_b[gi]])
                for jj in range(nj):
                    j = j0 + jj
                    for nb in range(nblk):
                        pg, pgb = c.ps.get()
                        pu, pub = c.ps.get()
                        for k in range(8):
                            S.op("pe", lambda e, k=k, pg=pg, jj=jj, nb=nb, wi=wi: e.matmul(
                                pg[:, :], lhsT=wg[wi][:, k, jj * 128:(jj + 1) * 128], rhs=xT[:, k, nb * 512:(nb + 1) * 512],
                                start=(k == 0), stop=(k == 7)), reads=[wg_b[wi], xT_buf], writes=[pgb])
                        for k in range(8):
                            S.op("pe", lambda e, k=k, pu=pu, jj=jj, nb=nb, wi=wi: e.matmul(
                                pu[:, :], lhsT=wu[wi][:, k, jj * 128:(jj + 1) * 128], rhs=xT[:, k, nb * 512:(nb + 1) * 512],
                                start=(k == 0), stop=(k == 7)), reads=[wu_b[wi], xT_buf], writes=[pub])
                        si = (j * nblk + nb) % 2
                        S.op("act", lambda e, si=si, pg=pg: e.activation(out=sg[si][:, :], in_=pg[:, :], func=AF.Silu),
                             reads=[pgb], writes=[sg_b[si]])
                        S.op("dve", lambda e, si=si, pu=pu, j=j, nb=nb: e.tensor_tensor(
                            out=hT[:, j, nb * 512:(nb + 1) * 512], in0=sg[si][:, :], in1=pu[:, :], op=ALU.mult),
                            reads=[sg_b[si], pub], writes=[hT_bufs[j]])
            first = False
            for t in range(ntile):
                halves = [c.ps.get(), c.ps.get()]
                for j in range(22):
                    for hf in range(2):
                        ph, phb = halves[hf]
                        S.op("pe", lambda e, ph=ph, j=j, hf=hf, t=t: e.matmul(
                            ph[:, :], lhsT=hT[:, j, t * 128:(t + 1) * 128], rhs=wd[:, j, hf * 512:(hf + 1) * 512],
                            start=(j == 0), stop=(j == 21)), reads=[hT_bufs[j], wd_b[j // 4]], writes=[phb])
                i = ecnt % 2
                ecnt += 1
                r0 = g0 + t * 128
                S.dma("sp", xin[i][:, :], src[r0:r0 + 128, :], writes=[xin_b[i]])
                emit_ln_epilogue(c, (v[i], v_b[i], st[i], mv[i], sm_b[i]), halves, xin[i], xin_b[i],
                                 g_t, b_t, gb_buf, xo[i], xo_b[i])
                outs.append(S.dma("sp", dst[r0:r0 + 128, :], xo[i][:, :], reads=[xo_b[i]]))
    S.barrier()
    return outs


A_MASK_NEG = -30000.0
import os as _os
DBG_STOP = int(_os.environ.get("DBG_STOP", "0"))


class _Stop(Exception):
    pass


def _chk(n):
    if DBG_STOP == n:
        raise _Stop()


def ps_bf(pt):
    return pt[:, :].bitcast(BF16)


def emit_even(c, e, layer, src, dst, halo, NT):
    S, nc = c.S, c.nc
    GT = 512
    ngroups = NT // GT
    outs = []
    dr = c.dram
    with ExitStack() as es:
      try:
          sb = lambda name, shape, dt: _sb(c, es, "e_" + name, shape, dt)
          Win = sb("win", [128, 8, 1792], BF16); Win_b = Buf("win")
          Wout = sb("wout", [128, 8, 1024], BF16); Wout_b = Buf("wout")
          dg = sb("dg", [128, 4, 31, 128], BF16); dg_b = Buf("dg")
          wk = sb("wk", [31, 512], F32); wk_b = Buf("wk")
          wcol = sb("wcol", [128, 4, 32], F32); wcol_b = Buf("wcol")
          cvec = sb("cvec", [128, 16], F32); cvec_b = Buf("cvec")
          sinks = sb("sinks", [128, 8], F32); sinks_b = Buf("sinks")
          amask = sb("amask", [128, 256], F32); amask0 = sb("amask0", [128, 256], F32); am_b = Buf("amask")
          rope = sb("rope", [128, NT // 128 + 1, 16], F32); rope_b = Buf("rope")
          g_t = sb("g", [128, 1024], F32); b_t = sb("b", [128, 1024], F32); gb_buf = Buf("gb")
          xT = sb("xT", [128, 8, GT], BF16); xT_b = Buf("xT")
          xTh = sb("xTh", [128, 8, 128], BF16); xTh_b = Buf("xTh")
          hbuf = [sb(f"hbuf{i}", [128, 4, 32 + GT], BF16) for i in range(2)]
          hb_body = [Buf(f"hb{i}") for i in range(2)]
          hb_pre = [Buf(f"hp{i}") for i in range(2)]
          cv = sb("cv", [128, 4, GT], F32); cv_b = [Buf(f"cv{i}") for i in range(4)]
          sq = sb("sq", [128, GT], F32); sq_b = Buf("sq")
          tg = [sb(f"tg{i}", [128, GT], F32) for i in range(2)]; tg_b = [Buf(f"tg{i}") for i in range(2)]
          mean = sb("mean", [128, GT], F32); msq = sb("msq", [128, GT], F32); rstd = sb("rstd", [128, GT], F32)
          stat_b = Buf("stat")
          ta = [sb(f"ta{i}", [128, GT], F32) for i in range(2)]; ta_b = [Buf(f"ta{i}") for i in range(2)]
          yT = sb("yT", [128, 8, GT], BF16); yT_b = [Buf(f"yT{i}") for i in range(8)]
          xin = [sb(f"xin{i}", [128, 1024], F32) for i in range(2)]; xin_b = [Buf(f"xin{i}") for i in range(2)]
          xbf = [sb(f"xbf{i}", [128, 1024], BF16) for i in range(2)]; xbf_b = [Buf(f"xbf{i}") for i in range(2)]
          qb = sb("qb", [128, 8, 64], BF16); qb_b = Buf("qb")
          kb = sb("kb", [128, 2, 64], BF16); kb_b = Buf("kb")
          rt = [sb(f"rt{i}", [128, 8, 8], F32) for i in range(2)]; rt_b = [Buf(f"rt{i}") for i in range(2)]
          NR = 3
          vb = [sb(f"vb{i}", [128, 128], BF16) for i in range(NR)]; vb_b = [Buf(f"vb{i}") for i in range(NR)]
          kT = [sb(f"kT{i}", [128, 128], BF16) for i in range(NR)]; kT_b = [Buf(f"kT{i}") for i in range(NR)]
          qT = sb("qT", [128, 4, 128], BF16); qT_b = Buf("qT")
          sm = sb("sm", [128, 8, 256], F32); sm_b = [Buf(f"sm{i}") for i in range(4)]
          pb = sb("pb", [128, 8, 256], BF16); pb_b = Buf("pb")
          pT = sb("pT", [128, 16, 128], BF16); pT_b = [Buf(f"pT{i}") for i in range(2)]
          att = sb("att", [128, 40], F32); att_b = Buf("att")
          ob = sb("ob", [128, 8, 64], BF16); ob_b = Buf("ob")
          v = [sb(f"v{i}", [128, 1024], F32) for i in range(2)]; v_b = [Buf(f"v{i}") for i in range(2)]
          st = [sb(f"st{i}", [128, 12], F32) for i in range(2)]
          mv = [sb(f"mv{i}", [128, 4], F32) for i in range(2)]; smm_b = [Buf(f"smm{i}") for i in range(2)]
          xo = [sb(f"xo{i}", [128, 1024], F32) for i in range(2)]; xo_b = [Buf(f"xo{i}") for i in range(2)]

          S.dma("pool", Win[:, :, :], dr[f"ev_w_in{e}"].rearrange("(k p) n -> p k n", p=128), writes=[Win_b])
          S.dma("pool", Wout[:, :, :], dr[f"ev_w_out{e}"].rearrange("(k p) n -> p k n", p=128), writes=[Wout_b])
          S.dma("sp", wk[:, :], dr[f"ev_dw_w{e}"], writes=[wk_b])
          S.dma("sp", cvec[:, 0:4], dr[f"ev_dw_b{e}"].rearrange("o (c p) -> p (o c)", p=128), writes=[cvec_b], allow_slow_non_contiguous=True)
          S.dma("sp", cvec[:, 4:8], dr[f"ev_cn_g{e}"].rearrange("o (c p) -> p (o c)", p=128), writes=[cvec_b], allow_slow_non_contiguous=True)
          S.dma("sp", cvec[:, 8:12], dr[f"ev_cn_b{e}"].rearrange("o (c p) -> p (o c)", p=128), writes=[cvec_b], allow_slow_non_contiguous=True)
          load_bcast_row(c, "sp", sinks, sinks_b, dr[f"ev_sinks{e}"])
          S.dma("sp", amask[:, :], dr["amask"], writes=[am_b])
          S.dma("sp", amask0[:, :], dr["amask0"], writes=[am_b])
          S.dma("sp", rope[:, :, :], dr["rope_a"], writes=[rope_b])
          load_bcast_row(c, "sp", g_t, gb_buf, dr[f"ln_mix_g{layer}"])
          load_bcast_row(c, "sp", b_t, gb_buf, dr[f"ln_mix_b{layer}"])
          S.op("dve", lambda en: en.tensor_scalar_mul(out=cvec[:, 4:12], in0=cvec[:, 4:12], scalar1=0.5),
               reads=[cvec_b], writes=[cvec_b])
          for cc in range(4):
              pt, ptb_ = c.ps.get()
              S.op("pe", lambda en, cc=cc, pt=pt: en.transpose(out=pt[:, 0:31], in_=wk[0:31, cc * 128:(cc + 1) * 128],
                                                              identity=c.ident_f[0:31, 0:31]),
                   reads=[wk_b, c.identf_buf], writes=[ptb_])
              S.op("dve", lambda en, cc=cc, pt=pt: en.tensor_copy(out=wcol[:, cc, 0:31], in_=pt[:, 0:31]),
                   reads=[ptb_], writes=[wcol_b])
          for cc in range(4):
              for k in range(31):
                  S.op("dve", lambda en, cc=cc, k=k: en.tensor_scalar(
                      out=dg[:, cc, k, :], in0=c.ident_f[:, :], scalar1=wcol[:, cc, k:k + 1], scalar2=0.5,
                      op0=ALU.mult, op1=ALU.mult), reads=[wcol_b, c.identf_buf], writes=[dg_b])

          _chk(1)
          xcnt = [0]

          def load_xT(row0, dstT, dstT_b, slot, src_ap):
              i = xcnt[0] % 2
              xcnt[0] += 1
              S.dma("sp", xin[i][:, :], src_ap[row0:row0 + 128, :], writes=[xin_b[i]])
              emit_xT(c, dstT, dstT_b, slot, xin[i], xin_b[i], xbf[i], xbf_b[i])

          def glu_chunk(xTsrc, xTsrc_b, ncols, hb, hb_buf, col0):
              for cc in range(4):
                  pa, pab = c.ps.get()
                  pg, pgb = c.ps.get()
                  for k in range(8):
                      S.op("pe", lambda en, k=k, cc=cc, pa=pa: en.matmul(
                          pa[:, 0:ncols], lhsT=Win[:, k, 768 + cc * 128:768 + (cc + 1) * 128], rhs=xTsrc[:, k, 0:ncols],
                          start=(k == 0), stop=(k == 7)), reads=[Win_b, xTsrc_b], writes=[pab])
                  for k in range(8):
                      S.op("pe", lambda en, k=k, cc=cc, pg=pg: en.matmul(
                          pg[:, 0:ncols], lhsT=Win[:, k, 1280 + cc * 128:1280 + (cc + 1) * 128], rhs=xTsrc[:, k, 0:ncols],
                          start=(k == 0), stop=(k == 7)), reads=[Win_b, xTsrc_b], writes=[pgb])
                  ti = cc % 2
                  S.op("act", lambda en, ti=ti, pg=pg: en.activation(out=tg[ti][:, 0:ncols], in_=pg[:, 0:ncols],
                                                                      func=AF.Tanh, scale=0.5),
                       reads=[pgb], writes=[tg_b[ti]])
                  S.op("dve", lambda en, ti=ti, pa=pa, cc=cc: en.scalar_tensor_tensor(
                      out=hb[:, cc, col0:col0 + ncols], in0=tg[ti][:, 0:ncols], scalar=1.0, in1=pa[:, 0:ncols],
                      op0=ALU.add, op1=ALU.mult), reads=[tg_b[ti], pab], writes=[hb_buf])

          def kv_tile(xTsrc, xTsrc_b, col0, ring_i, tile_idx, with_q):
              pkv, pkvb = c.ps.get()
              for k in range(8):
                  S.op("pe", lambda en, k=k: en.matmul(pkv[:, 0:256], lhsT=xTsrc[:, k, col0:col0 + 128],
                                                       rhs=Win[:, k, 512:768], start=(k == 0), stop=(k == 7)),
                       reads=[Win_b, xTsrc_b], writes=[pkvb])
              if with_q:
                  pq, pqb = c.ps.get()
                  for k in range(8):
                      S.op("pe", lambda en, k=k: en.matmul(pq[:, :], lhsT=xTsrc[:, k, col0:col0 + 128],
                                                           rhs=Win[:, k, 0:512], start=(k == 0), stop=(k == 7)),
                           reads=[Win_b, xTsrc_b], writes=[pqb])

              def rope_apply(s3, psrc_b, dstt, dst_b, hs):
                  nh = len(hs)
                  H = int(np.prod(hs))
                  full = [128] + list(hs)
                  cs = rope[:, tile_idx, 0:8]
                  sn = rope[:, tile_idx, 8:16]
                  for _ in range(nh):
                      cs = cs.unsqueeze(1)
                      sn = sn.unsqueeze(1)
                  cs = cs.to_broadcast(full + [8])
                  sn = sn.to_broadcast(full + [8])
                  if nh == 1:
                      r0, r1 = rt[0][:, 0:H, :], rt[1][:, 0:H, :]
                  else:
                      r0 = rt[0][:, 0:H, :].rearrange("p (a b) d -> p a b d", a=hs[0])
                      r1 = rt[1][:, 0:H, :].rearrange("p (a b) d -> p a b d", a=hs[0])
                  sl = (slice(None),) * (1 + nh)
                  t1, t2 = s3[sl + (slice(0, 8),)], s3[sl + (slice(8, 16),)]
                  S.op("dve", lambda en: en.tensor_tensor(out=r0, in0=t1, in1=cs, op=ALU.mult),
                       reads=[psrc_b, rope_b], writes=[rt_b[0]])
                  S.op("dve", lambda en: en.tensor_tensor(out=r1, in0=t2, in1=sn, op=ALU.mult),
                       reads=[psrc_b, rope_b], writes=[rt_b[1]])
                  S.op("dve", lambda en: en.tensor_tensor(out=dstt[sl + (slice(0, 8),)], in0=r0, in1=r1, op=ALU.subtract),
                       reads=[rt_b[0], rt_b[1]], writes=[dst_b])
                  S.op("dve", lambda en: en.tensor_tensor(out=r0, in0=t2, in1=cs, op=ALU.mult),
                       reads=[psrc_b, rope_b], writes=[rt_b[0]])
                  S.op("dve", lambda en: en.tensor_tensor(out=r1, in0=t1, in1=sn, op=ALU.mult),
                       reads=[psrc_b, rope_b], writes=[rt_b[1]])
                  S.op("dve", lambda en: en.tensor_tensor(out=dstt[sl + (slice(8, 16),)], in0=r0, in1=r1, op=ALU.add),
                       reads=[rt_b[0], rt_b[1]], writes=[dst_b])
                  S.op("act", lambda en: en.copy(out=dstt[sl + (slice(16, 64),)], in_=s3[sl + (slice(16, 64),)]),
                       reads=[psrc_b], writes=[dst_b])

              rope_apply(pkv[:, 0:128].rearrange("p (h d) -> p h d", h=2), pkvb, kb[:, :, :], kb_b, [2])
              S.op("act", lambda en: en.copy(out=vb[ring_i][:, :], in_=pkv[:, 128:256]), reads=[pkvb], writes=[vb_b[ring_i]])
              pt, ptb_ = c.ps.get()
              ptb = ps_bf(pt)
              S.op("pe", lambda en: en.transpose(out=ptb[:, 0:128], in_=kb[:, :, :].rearrange("p h d -> p (h d)"),
                                                 identity=c.ident_bf[:, :]), reads=[kb_b, c.ident_buf], writes=[ptb_])
              S.op("act", lambda en: en.copy(out=kT[ring_i][:, :], in_=ptb[:, 0:128]), reads=[ptb_], writes=[kT_b[ring_i]])
              if with_q:
                  rope_apply(pq[:, :].rearrange("p (g j d) -> p g j d", g=2, j=4), pqb,
                             qb[:, :, :].rearrange("p (j g) d -> p g j d", g=2), qb_b, [2, 4])
                  pt2, pt2b_ = c.ps.get()
                  pt2b = ps_bf(pt2)
                  qflat = qb[:, :, :].rearrange("p h d -> p (h d)")
                  for j in range(4):
                      S.op("pe", lambda en, j=j: en.transpose(out=pt2b[:, j * 128:(j + 1) * 128],
                                                              in_=qflat[:, j * 128:(j + 1) * 128], identity=c.ident_bf[:, :]),
                           reads=[qb_b, c.ident_buf], writes=[pt2b_])
                  S.op("dve", lambda en: en.tensor_copy(out=qT[:, :, :], in_=pt2b[:, 0:512].rearrange("p (j t) -> p j t", j=4)),
                       reads=[pt2b_], writes=[qT_b])

          _chk(2)
          load_xT(0, xTh, xTh_b, 0, halo)
          glu_chunk(xTh, xTh_b, 128, hbuf[1], hb_body[1], 32 + GT - 128)
          kv_tile(xTh, xTh_b, 0, (NR - 1), 0, False)
          _chk(3)
          blk_global = 0
          for g in range(ngroups):
              hb = hbuf[g % 2]
              hprev = hbuf[(g + 1) % 2]
              for t in range(4):
                  load_xT(g * GT + t * 128, xT, xT_b, t, src)
              S.op("act", lambda en, hb=hb, hprev=hprev: en.copy(out=hb[:, :, 2:32], in_=hprev[:, :, GT + 2:GT + 32]),
                   reads=[hb_body[(g + 1) % 2]], writes=[hb_pre[g % 2]])
              glu_chunk(xT, xT_b, GT, hb, hb_body[g % 2], 32)
              _chk(4)
              for cc in range(4):
                  pc, pcb = c.ps.get()
                  for k in range(31):
                      S.op("pe", lambda en, cc=cc, k=k, pc=pc, hb=hb: en.matmul(
                          pc[:, :], lhsT=dg[:, cc, k, :], rhs=hb[:, cc, 2 + k:2 + k + GT], start=(k == 0), stop=(k == 30)),
                          reads=[dg_b, hb_body[g % 2], hb_pre[g % 2]], writes=[pcb])
                  S.op("act", lambda en, cc=cc, pc=pc: en.activation(out=cv[:, cc, :], in_=pc[:, :], func=AF.Identity,
                                                                      bias=cvec[:, cc:cc + 1], scale=1.0),
                       reads=[pcb, cvec_b], writes=[cv_b[cc]])
              _chk(5)
              p1, p1b = c.ps.get()
              p2, p2b = c.ps.get()
              for cc in range(4):
                  S.op("pe", lambda en, cc=cc: en.matmul(p1[:, :], lhsT=c.ones_f[:, :], rhs=cv[:, cc, :],
                                                         start=(cc == 0), stop=(cc == 3)),
                       reads=[cv_b[cc], c.ones_buf], writes=[p1b])
              for cc in range(4):
                  S.op("act", lambda en, cc=cc: en.activation(out=sq[:, :], in_=cv[:, cc, :], func=AF.Square),
                       reads=[cv_b[cc]], writes=[sq_b])
                  S.op("pe", lambda en, cc=cc: en.matmul(p2[:, :], lhsT=c.ones_f[:, :], rhs=sq[:, :],
                                                         start=(cc == 0), stop=(cc == 3)),
                       reads=[sq_b, c.ones_buf], writes=[p2b])
              S.op("dve", lambda en: en.tensor_scalar_mul(out=mean[:, :], in0=p1[:, :], scalar1=1.0 / 512.0),
                   reads=[p1b], writes=[stat_b])
              S.op("dve", lambda en: en.tensor_tensor(out=msq[:, :], in0=mean[:, :], in1=mean[:, :], op=ALU.mult),
                   reads=[stat_b], writes=[stat_b])
              S.op("dve", lambda en: en.scalar_tensor_tensor(out=rstd[:, :], in0=p2[:, :], scalar=1.0 / 512.0, in1=msq[:, :],
                                                             op0=ALU.mult, op1=ALU.subtract), reads=[p2b, stat_b], writes=[stat_b])
              S.op("dve", lambda en: en.tensor_scalar_add(out=rstd[:, :], in0=rstd[:, :], scalar1=float(EPS)),
                   reads=[stat_b], writes=[stat_b])
              S.op("act", lambda en: en.activation(out=rstd[:, :], in_=rstd[:, :], func=AF.Sqrt), reads=[stat_b], writes=[stat_b])
              S.op("dve", lambda en: en.reciprocal(out=rstd[:, :], in_=rstd[:, :]), reads=[stat_b], writes=[stat_b])
              for cc in range(4):
                  i = cc % 2
                  S.op("dve", lambda en, cc=cc, i=i: en.tensor_tensor(out=ta[i][:, :], in0=cv[:, cc, :], in1=mean[:, :],
                                                                     op=ALU.subtract), reads=[cv_b[cc], stat_b], writes=[ta_b[i]])
                  S.op("dve", lambda en, i=i: en.tensor_tensor(out=ta[i][:, :], in0=ta[i][:, :], in1=rstd[:, :], op=ALU.mult),
                       reads=[ta_b[i], stat_b], writes=[ta_b[i]])
                  S.op("act", lambda en, cc=cc, i=i: en.activation(out=ta[i][:, :], in_=ta[i][:, :], func=AF.Identity,
                                                                    bias=cvec[:, 8 + cc:9 + cc], scale=cvec[:, 4 + cc:5 + cc]),
                       reads=[ta_b[i], cvec_b], writes=[ta_b[i]])
                  S.op("act", lambda en, i=i: en.activation(out=tg[i][:, :], in_=ta[i][:, :], func=AF.Tanh),
                       reads=[ta_b[i]], writes=[tg_b[i]])
                  S.op("dve", lambda en, cc=cc, i=i: en.scalar_tensor_tensor(
                      out=yT[:, 4 + cc, :], in0=tg[i][:, :], scalar=1.0, in1=ta[i][:, :], op0=ALU.add, op1=ALU.mult),
                      reads=[tg_b[i], ta_b[i]], writes=[yT_b[4 + cc]])
              _chk(6)
              for t in range(4):
                  bi = blk_global
                  blk_global += 1
                  cur = bi % NR
                  prv = (bi - 1) % NR
                  kv_tile(xT, xT_b, t * 128, cur, bi + 1, True)
                  msk = amask0 if bi == 0 else amask
                  for bank in range(4):
                      pscr, pscb = c.ps.get()
                      for hh in range(2):
                          h = bank * 2 + hh
                          gk = h // 4
                          lq = qT[gk * 64:gk * 64 + 64, h % 4, :]
                          S.op("pe", lambda en, pscr=pscr, hh=hh, lq=lq, gk=gk, prv=prv: en.matmul(
                              pscr[:, hh * 256:hh * 256 + 128], lhsT=lq, rhs=kT[prv][gk * 64:gk * 64 + 64, :],
                              start=True, stop=True), reads=[qT_b, kT_b[prv]], writes=[pscb])
                          S.op("pe", lambda en, pscr=pscr, hh=hh, lq=lq, gk=gk, cur=cur: en.matmul(
                              pscr[:, hh * 256 + 128:hh * 256 + 256], lhsT=lq, rhs=kT[cur][gk * 64:gk * 64 + 64, :],
                              start=True, stop=True), reads=[qT_b, kT_b[cur]], writes=[pscb])
                      S.op("dve", lambda en, bank=bank, pscr=pscr, msk=msk: en.tensor_tensor(
                          out=sm[:, bank * 2:bank * 2 + 2, :], in0=pscr[:, :].rearrange("p (h k) -> p h k", h=2),
                          in1=msk[:, :].unsqueeze(1).to_broadcast([128, 2, 256]), op=ALU.add),
                          reads=[pscb, am_b], writes=[sm_b[bank]])
                  S.op("dve", lambda en: en.tensor_reduce(out=att[:, 0:8], in_=sm[:, :, :], axis=mybir.AxisListType.X, op=ALU.max),
                       reads=sm_b, writes=[att_b])
                  S.op("dve", lambda en: en.scalar_tensor_tensor(out=att[:, 0:8], in0=att[:, 0:8], scalar=0.125, in1=sinks[:, :],
                                                                 op0=ALU.mult, op1=ALU.max), reads=[att_b, sinks_b], writes=[att_b])
                  S.op("dve", lambda en: en.tensor_scalar_mul(out=att[:, 8:16], in0=att[:, 0:8], scalar1=-1.0),
                       reads=[att_b], writes=[att_b])
                  for h in range(8):
                      S.op("act", lambda en, h=h: en.activation(out=pb[:, h, :], in_=sm[:, h, :], func=AF.Exp,
                                                                bias=att[:, 8 + h:9 + h], scale=0.125, accum_out=att[:, 16 + h:17 + h]),
                           reads=[sm_b[h // 2], att_b], writes=[pb_b, att_b])
                  S.op("dve", lambda en: en.tensor_tensor(out=att[:, 24:32], in0=sinks[:, :], in1=att[:, 0:8], op=ALU.subtract),
                       reads=[att_b, sinks_b], writes=[att_b])
                  S.op("act", lambda en: en.activation(out=att[:, 24:32], in_=att[:, 24:32], func=AF.Exp), reads=[att_b], writes=[att_b])
                  S.op("dve", lambda en: en.tensor_tensor(out=att[:, 32:40], in0=att[:, 16:24], in1=att[:, 24:32], op=ALU.add),
                       reads=[att_b], writes=[att_b])
                  S.op("dve", lambda en: en.reciprocal(out=att[:, 32:40], in_=att[:, 32:40]), reads=[att_b], writes=[att_b])
                  for half2 in range(2):
                      ptt, pttb_ = c.ps.get()
                      pttb = ps_bf(ptt)
                      for j in range(8):
                          idx = half2 * 8 + j
                          h, hf = idx // 2, idx % 2
                          S.op("pe", lambda en, j=j, h=h, hf=hf, pttb=pttb: en.transpose(
                              out=pttb[:, j * 128:(j + 1) * 128], in_=pb[:, h, hf * 128:(hf + 1) * 128], identity=c.ident_bf[:, :]),
                              reads=[pb_b, c.ident_buf], writes=[pttb_])
                      eng = "act" if half2 == 0 else "dve"
                      if eng == "act":
                          S.op("act", lambda en, half2=half2, pttb=pttb: en.copy(
                              out=pT[:, half2 * 8:half2 * 8 + 8, :], in_=pttb[:, :].rearrange("p (j t) -> p j t", j=8)),
                              reads=[pttb_], writes=[pT_b[half2]])
                      else:
                          S.op("dve", lambda en, half2=half2, pttb=pttb: en.tensor_copy(
                              out=pT[:, half2 * 8:half2 * 8 + 8, :], in_=pttb[:, :].rearrange("p (j t) -> p j t", j=8)),
                              reads=[pttb_], writes=[pT_b[half2]])
                  po, pob = c.ps.get()
                  for h in range(8):
                      gk = h // 4
                      S.op("pe", lambda en, h=h, gk=gk, prv=prv: en.matmul(po[:, h * 64:(h + 1) * 64], lhsT=pT[:, h * 2, :],
                                                                             rhs=vb[prv][:, gk * 64:(gk + 1) * 64], start=True, stop=False),
                           reads=[pT_b[h // 4], vb_b[prv]], writes=[pob])
                      S.op("pe", lambda en, h=h, gk=gk, cur=cur: en.matmul(po[:, h * 64:(h + 1) * 64], lhsT=pT[:, h * 2 + 1, :],
                                                                             rhs=vb[cur][:, gk * 64:(gk + 1) * 64], start=False, stop=True),
                           reads=[pT_b[h // 4], vb_b[cur]], writes=[pob])
                  S.op("dve", lambda en: en.tensor_tensor(out=ob[:, :, :], in0=po[:, :].rearrange("p (h d) -> p h d", h=8),
                                                          in1=att[:, 32:40].unsqueeze(2).to_broadcast([128, 8, 64]), op=ALU.mult),
                       reads=[pob, att_b], writes=[ob_b])
                  pt3, pt3b_ = c.ps.get()
                  pt3b = ps_bf(pt3)
                  oflat = ob[:, :, :].rearrange("p h d -> p (h d)")
                  for j in range(4):
                      S.op("pe", lambda en, j=j: en.transpose(out=pt3b[:, j * 128:(j + 1) * 128], in_=oflat[:, j * 128:(j + 1) * 128],
                                                              identity=c.ident_bf[:, :]), reads=[ob_b, c.ident_buf], writes=[pt3b_])
                  S.op("act", lambda en, t=t: en.copy(out=yT[:, 0:4, t * 128:(t + 1) * 128],
                                                      in_=pt3b[:, 0:512].rearrange("p (j t) -> p j t", j=4)),
                       reads=[pt3b_], writes=yT_b[0:4])
              _chk(7)
              for t in range(4):
                  halves = [c.ps.get(), c.ps.get()]
                  for kc in range(8):
                      for hf in range(2):
                          ph, phb = halves[hf]
                          S.op("pe", lambda en, ph=ph, kc=kc, hf=hf, t=t: en.matmul(
                              ph[:, :], lhsT=yT[:, kc, t * 128:(t + 1) * 128], rhs=Wout[:, kc, hf * 512:(hf + 1) * 512],
                              start=(kc == 0), stop=(kc == 7)), reads=[yT_b[kc], Wout_b], writes=[phb])
                  i = xcnt[0] % 2
                  xcnt[0] += 1
                  r0 = g * GT + t * 128
                  S.dma("sp", xin[i][:, :], src[r0:r0 + 128, :], writes=[xin_b[i]])
                  emit_ln_epilogue(c, (v[i], v_b[i], st[i], mv[i], smm_b[i]), halves, xin[i], xin_b[i],
                                   g_t, b_t, gb_buf, xo[i], xo_b[i])
                  outs.append(S.dma("sp", dst[r0:r0 + 128, :], xo[i][:, :], reads=[xo_b[i]]))
      except _Stop:
        pass
    S.barrier()
    return outs


OZ, OXBC, ODT, ORQ, ORK, ORV, ORG = 0, 1024, 2560, 2576, 3088, 3600, 4624
ST_W = 2064


def emit_odd(c, o, layer, src, dst, halo, NT):
    S, nc = c.S, c.nc
    GT = 512
    ngroups = NT // GT
    nchunks = NT // 128
    dr = c.dram
    outs = []
    Win_d = dr[f"od_w_in{o}"]
    with ExitStack() as es0:
        sb0 = lambda name, shape, dt: _sb(c, es0, "o_" + name, shape, dt)
        Sst = sb0("Sst", [128, 1024], F32); Sst_b = Buf("Sst")
        Rst = sb0("Rst", [128, 1024], F32); Rst_b = Buf("Rst")
        Atot = sb0("Atot", [128, 16], F32); Atot_b = Buf("Atot")
        otab = sb0("otab", [128, 32], F32); otab_b = Buf("otab")
        ptab = sb0("ptab", [128, 64], F32); ptab_b = Buf("ptab")
        wk5 = sb0("wk5", [5, 1536], F32); wk5_b = Buf("wk5")
        cw = sb0("cw", [128, 12, 5], F32); cw_b = Buf("cw")
        triu = sb0("triu", [128, 128], F32)
        trisl = sb0("trisl", [128, 128], F32)
        m01 = sb0("m01", [128, 128], F32)
        tri_b = Buf("tri")
        xin = [sb0(f"xin{i}", [128, 1024], F32) for i in range(2)]; xin_b = [Buf(f"oxin{i}") for i in range(2)]
        xbf = [sb0(f"xbf{i}", [128, 1024], BF16) for i in range(2)]; xbf_b = [Buf(f"oxbf{i}") for i in range(2)]
        xT_r = [(sb0(f"xT{i}", [128, 8, GT], BF16), Buf(f"oxT{i}")) for i in range(2)]
        xT, xT_b = xT_r[0]
        xTh = sb0("xTh", [128, 8, 128], BF16); xTh_b = Buf("oxTh")
        smg_r = [(sb0(f"smg{i}", [128, 8, 16], F32), Buf(f"smg{i}")) for i in range(2)]
        smc_r = [(sb0(f"smc{i}", [128, 4, 16], F32), Buf(f"smc{i}")) for i in range(3)]
        smg, smg_b = smg_r[0]
        smc, smc_b = smc_r[0]
        S.dma("sp", otab[:, :], dr["odd_tab"], writes=[otab_b])
        load_bcast_row(c, "sp", ptab[:, 0:16], ptab_b, dr[f"od_a_log{o}"])
        load_bcast_row(c, "sp", ptab[:, 16:32], ptab_b, dr[f"od_dt_bias{o}"])
        load_bcast_row(c, "sp", ptab[:, 32:48], ptab_b, dr[f"od_d_skip{o}"])
        S.dma("sp", wk5[0:4, :], dr[f"od_conv_w{o}"], writes=[wk5_b])
        S.dma("sp", wk5[4:5, :], dr[f"od_conv_b{o}"], writes=[wk5_b])
        S.dma("sp", triu[:, :], dr["triu"], writes=[tri_b])
        S.dma("sp", trisl[:, :], dr["trisl"], writes=[tri_b])
        S.dma("sp", m01[:, :], dr["triu"], writes=[tri_b])
        S.op("act", lambda en: en.activation(out=ptab[:, 0:16], in_=ptab[:, 0:16], func=AF.Exp), reads=[ptab_b], writes=[ptab_b])
        S.op("dve", lambda en: en.tensor_scalar_mul(out=ptab[:, 0:16], in0=ptab[:, 0:16], scalar1=-1.0), reads=[ptab_b], writes=[ptab_b])
        for cc in range(12):
            pt, ptb_ = c.ps.get()
            S.op("pe", lambda en, cc=cc, pt=pt: en.transpose(out=pt[:, 0:5], in_=wk5[0:5, cc * 128:(cc + 1) * 128],
                                                            identity=c.ident_f[0:5, 0:5]), reads=[wk5_b, c.identf_buf], writes=[ptb_])
            S.op("dve", lambda en, cc=cc, pt=pt: en.tensor_scalar_mul(out=cw[:, cc, :], in0=pt[:, 0:5], scalar1=0.5),
                 reads=[ptb_], writes=[cw_b])

        xcnt = [0]

        def load_xT(row0, dstT, dstT_b, slot, src_ap):
            i = xcnt[0] % 2
            xcnt[0] += 1
            S.dma("sp", xin[i][:, :], src_ap[row0:row0 + 128, :], writes=[xin_b[i]])
            emit_xT(c, dstT, dstT_b, slot, xin[i], xin_b[i], xbf[i], xbf_b[i])

        def wload(es, name, col0, ncol):
            t = _sb(c, es, "o_w" + name, [128, 8, ncol], BF16)
            b = Buf("w" + name)
            step = 1024
            for s0 in range(0, ncol, step):
                n = min(step, ncol - s0)
                S.dma("pool", t[:, :, s0:s0 + n], Win_d[:, col0 + s0:col0 + s0 + n].rearrange("(k p) n -> p k n", p=128),
                      writes=[b])
            return t, b

        def rot(d, names, i):
            for nm in names:
                r = d[nm + "_r"]
                d[nm], d[nm + "_b"] = r[i % len(r)]

        def ssd_setup(es, need_w=True):
            d = {}
            sb = lambda name, shape, dt: _sb(c, es, "s_" + name, shape, dt)
            if need_w:
                d["Wx"], d["Wx_b"] = wload(es, "xbc", OXBC, 1536 + 16)
            d["cin"] = [sb(f"cin{i}", [128, 3 + GT], F32) for i in range(2)]; d["cin_b"] = [Buf(f"cin{i}") for i in range(2)]
            d["acc"] = [sb(f"acc{i}", [128, GT], F32) for i in range(2)]; d["acc_b"] = [Buf(f"acc{i}") for i in range(2)]
            d["tg"] = [sb(f"tg{i}", [128, GT], F32) for i in range(2)]; d["tg_b"] = [Buf(f"stg{i}") for i in range(2)]
            d["hist"] = sb("hist", [128, 12, 3], F32); d["hist_b"] = [Buf(f"hist{i}") for i in range(12)]
            d["xbcT_r"] = [(sb(f"xbcT{j}", [128, 12, GT], BF16), [Buf(f"xbcT{j}_{i}") for i in range(12)]) for j in range(2)]
            rot(d, ("xbcT",), 0)
            d["Btok_r"] = [(sb(f"Btok{i}", [128, 256], BF16), Buf(f"Btok{i}")) for i in range(2)]
            d["xdte_r"] = [(sb(f"xdte{i}", [128, 1024], BF16), Buf(f"xdte{i}")) for i in range(2)]
            rot(d, ("Btok", "xdte"), 0)
            return d

        def ssd_features(d, xTsrc, xTsrc_b, ncols, halo_mode):
            Wx, Wx_b = d["Wx"], d["Wx_b"]
            for cc in range(12):
                pp, ppb = c.ps.get()
                for k in range(8):
                    S.op("pe", lambda en, k=k, cc=cc, pp=pp: en.matmul(pp[:, 0:ncols], lhsT=Wx[:, k, cc * 128:(cc + 1) * 128],
                                                                       rhs=xTsrc[:, k, 0:ncols], start=(k == 0), stop=(k == 7)),
                         reads=[Wx_b, xTsrc_b], writes=[ppb])
                if halo_mode:
                    S.op("act", lambda en, cc=cc, pp=pp: en.copy(out=d["hist"][:, cc, :], in_=pp[:, ncols - 3:ncols]),
                         reads=[ppb], writes=[d["hist_b"][cc]])
                    continue
                i = cc % 2
                cin, cin_b = d["cin"][i], d["cin_b"][i]
                acc, acc_b = d["acc"][i], d["acc_b"][i]
                tgx, tgx_b = d["tg"][i], d["tg_b"][i]
                S.op("act", lambda en, cin=cin, pp=pp: en.copy(out=cin[:, 3:3 + ncols], in_=pp[:, 0:ncols]), reads=[ppb], writes=[cin_b])
                S.op("act", lambda en, cin=cin, cc=cc: en.copy(out=cin[:, 0:3], in_=d["hist"][:, cc, :]),
                     reads=[d["hist_b"][cc]], writes=[cin_b])
                S.op("act", lambda en, cin=cin, cc=cc: en.copy(out=d["hist"][:, cc, :], in_=cin[:, ncols:ncols + 3]),
                     reads=[cin_b], writes=[d["hist_b"][cc]])
                S.op("dve", lambda en, cin=cin, acc=acc, cc=cc: en.tensor_scalar(
                    out=acc[:, 0:ncols], in0=cin[:, 0:ncols], scalar1=cw[:, cc, 0:1], scalar2=cw[:, cc, 4:5],
                    op0=ALU.mult, op1=ALU.add), reads=[cin_b, cw_b], writes=[acc_b])
                for k in range(1, 4):
                    S.op("dve", lambda en, cin=cin, acc=acc, cc=cc, k=k: en.scalar_tensor_tensor(
                        out=acc[:, 0:ncols], in0=cin[:, k:k + ncols], scalar=cw[:, cc, k:k + 1], in1=acc[:, 0:ncols],
                        op0=ALU.mult, op1=ALU.add), reads=[cin_b, cw_b, acc_b], writes=[acc_b])
                S.op("act", lambda en, acc=acc, tgx=tgx: en.activation(out=tgx[:, 0:ncols], in_=acc[:, 0:ncols], func=AF.Tanh),
                     reads=[acc_b], writes=[tgx_b])
                S.op("dve", lambda en, acc=acc, tgx=tgx, cc=cc: en.scalar_tensor_tensor(
                    out=d["xbcT"][:, cc, 0:ncols], in0=tgx[:, 0:ncols], scalar=1.0, in1=acc[:, 0:ncols],
                    op0=ALU.add, op1=ALU.mult), reads=[tgx_b, acc_b], writes=[d["xbcT_b"][cc]])

        def ssd_dt_group(d):
            Wx, Wx_b = d["Wx"], d["Wx_b"]
            pd, pdb = c.ps.get()
            for t in range(4):
                for k in range(8):
                    S.op("pe", lambda en, k=k, t=t: en.matmul(pd[:, t * 16:(t + 1) * 16], lhsT=xT[:, k, t * 128:(t + 1) * 128],
                                                              rhs=Wx[:, k, 1536:1552], start=(k == 0), stop=(k == 7)),
                         reads=[Wx_b, xT_b], writes=[pdb])
            xr = smg[:, 0:4, :]
            S.op("dve", lambda en: en.tensor_tensor(out=xr, in0=pd[:, 0:64].rearrange("p (t h) -> p t h", t=4),
                                                    in1=ptab[:, 16:32].unsqueeze(1).to_broadcast([128, 4, 16]), op=ALU.add),
                 reads=[pdb, ptab_b], writes=[smg_b])
            ab = smg[:, 4:8, :]
            S.op("act", lambda en: en.activation(out=ab, in_=xr, func=AF.Abs), reads=[smg_b], writes=[smg_b])
            S.op("act", lambda en: en.activation(out=ab, in_=ab, func=AF.Exp, scale=-1.0), reads=[smg_b], writes=[smg_b])
            S.op("act", lambda en: en.activation(out=ab, in_=ab, func=AF.Ln, bias=1.0, scale=1.0), reads=[smg_b], writes=[smg_b])
            S.op("dve", lambda en: en.tensor_scalar_max(out=xr, in0=xr, scalar1=0.0), reads=[smg_b], writes=[smg_b])
            S.op("dve", lambda en: en.tensor_tensor(out=xr, in0=xr, in1=ab, op=ALU.add), reads=[smg_b], writes=[smg_b])
            S.op("dve", lambda en: en.tensor_tensor(out=ab, in0=xr, in1=ptab[:, 0:16].unsqueeze(1).to_broadcast([128, 4, 16]),
                                                    op=ALU.mult), reads=[smg_b, ptab_b], writes=[smg_b])

        def ssd_chunk_scalars(t):
            da = smg[:, 4 + t, :]
            pa, pab = c.ps.get()
            S.op("pe", lambda en: en.matmul(pa[:, 0:16], lhsT=triu[:, :], rhs=da, start=True, stop=True),
                 reads=[tri_b, smg_b], writes=[pab])
            S.op("pe", lambda en: en.matmul(pa[:, 16:32], lhsT=c.ones_f[:, :], rhs=da, start=True, stop=True),
                 reads=[c.ones_buf, smg_b], writes=[pab])
            S.op("act", lambda en: en.copy(out=smc[:, 0, :], in_=pa[:, 0:16]), reads=[pab], writes=[smc_b])
            S.op("act", lambda en: en.activation(out=smc[:, 1, :], in_=pa[:, 0:16], func=AF.Exp), reads=[pab], writes=[smc_b])
            S.op("dve", lambda en: en.tensor_tensor(out=smc[:, 2, :], in0=pa[:, 16:32], in1=smc[:, 0, :], op=ALU.subtract),
                 reads=[pab, smc_b], writes=[smc_b])
            S.op("act", lambda en: en.activation(out=smc[:, 2, :], in_=smc[:, 2, :], func=AF.Exp), reads=[smc_b], writes=[smc_b])
            S.op("dve", lambda en: en.tensor_tensor(out=smc[:, 2, :], in0=smc[:, 2, :], in1=smg[:, t, :], op=ALU.mult),
                 reads=[smc_b, smg_b], writes=[smc_b])
            S.op("act", lambda en: en.activation(out=smc[:, 3, :], in_=pa[:, 16:32], func=AF.Exp), reads=[pab], writes=[smc_b])
            S.op("dve", lambda en: en.tensor_tensor(out=Atot[:, :], in0=Atot[:, :], in1=pa[:, 16:32], op=ALU.add),
                 reads=[pab, Atot_b], writes=[Atot_b])

        def ssd_tok_and_state(d, t, xs_extra=None):
            xbcT, xbcT_b = d["xbcT"], d["xbcT_b"]
            px, pxb = c.ps.get()
            pxv = ps_bf(px)
            for j in range(8):
                S.op("pe", lambda en, j=j: en.transpose(out=pxv[:, j * 128:(j + 1) * 128], in_=xbcT[:, j, t * 128:(t + 1) * 128],
                                                        identity=c.ident_bf[:, :]), reads=[xbcT_b[j], c.ident_buf], writes=[pxb])
            pB, pBb = c.ps.get()
            pBv = ps_bf(pB)
            for j in range(2):
                S.op("pe", lambda en, j=j: en.transpose(out=pBv[:, j * 128:(j + 1) * 128], in_=xbcT[:, 8 + j, t * 128:(t + 1) * 128],
                                                        identity=c.ident_bf[:, :]), reads=[xbcT_b[8 + j], c.ident_buf], writes=[pBb])
            S.op("act", lambda en: en.copy(out=d["Btok"][:, :], in_=pBv[:, 0:256]), reads=[pBb], writes=[d["Btok_b"]])
            xs3 = pxv[:, 0:1024].rearrange("p (h q) -> p h q", h=16)
            S.op("dve", lambda en: en.tensor_tensor(out=d["xdte"][:, :].rearrange("p (h q) -> p h q", h=16), in0=xs3,
                                                    in1=smc[:, 2, :].unsqueeze(2).to_broadcast([128, 16, 64]), op=ALU.mult),
                 reads=[pxb, smc_b], writes=[d["xdte_b"]])
            if xs_extra is not None:
                xs_extra(xs3, pxb)
            pst = [c.ps.get(), c.ps.get()]
            for g in range(2):
                S.op("pe", lambda en, g=g: en.matmul(pst[g][0][:, :], lhsT=d["Btok"][:, g * 128:(g + 1) * 128],
                                                     rhs=d["xdte"][:, g * 512:(g + 1) * 512], start=True, stop=True),
                     reads=[d["Btok_b"], d["xdte_b"]], writes=[pst[g][1]])
            return pst

        def ssd_state_update(d, pst):
            S.op("dve", lambda en: en.tensor_tensor(out=Sst[:, :].rearrange("p (h q) -> p h q", h=16),
                                                    in0=Sst[:, :].rearrange("p (h q) -> p h q", h=16),
                                                    in1=smc[:, 3, :].unsqueeze(2).to_broadcast([128, 16, 64]), op=ALU.mult),
                 reads=[Sst_b, smc_b], writes=[Sst_b])
            for g in range(2):
                S.op("dve", lambda en, g=g: en.tensor_tensor(out=Sst[:, g * 512:(g + 1) * 512], in0=Sst[:, g * 512:(g + 1) * 512],
                                                             in1=pst[g][0][:, :], op=ALU.add), reads=[Sst_b, pst[g][1]], writes=[Sst_b])

        def ret_setup(es, with_q):
            d = {}
            sb = lambda name, shape, dt: _sb(c, es, "r_" + name, shape, dt)
            if with_q:
                t_ = _sb(c, es, "o_wretqg", [128, 8, 1536], BF16)
                b_ = Buf("wretqg")
                S.dma("pool", t_[:, :, 0:512], Win_d[:, ORQ:ORQ + 512].rearrange("(k p) n -> p k n", p=128), writes=[b_])
                S.dma("pool", t_[:, :, 512:1536], Win_d[:, ORG:ORG + 1024].rearrange("(k p) n -> p k n", p=128), writes=[b_])
                d["Wr"], d["Wr_b"] = t_, b_
                d["off"] = {"q": 0, "g": 512}
            else:
                d["Wr"], d["Wr_b"] = wload(es, "ret", ORK, 1536)
                d["off"] = {"k": 0, "v": 512}
            d["rope"] = [sb(f"rope{i}", [128, 128], F32) for i in range(2)]; d["rope_b"] = [Buf(f"rrope{i}") for i in range(2)]
            d["rr_r"] = [([sb(f"rr{j}_{i}", [128, 4, 64], F32) for i in range(2)], [Buf(f"rr{j}_{i}") for i in range(2)]) for j in range(2)]
            d["kr_r"] = [(sb(f"kr{i}", [128, 4, 128], F32), Buf(f"kr{i}")) for i in range(2)]
            d["kp_r"] = [(sb(f"kp{i}", [128, 512], BF16), Buf(f"kp{i}")) for i in range(2)]
            d["vb_r"] = [(sb(f"vb{i}", [128, 1024], BF16), Buf(f"rvb{i}")) for i in range(2)]
            rot(d, ("rr", "kr", "kp", "vb"), 0)
            return d

        def ret_rope(d, psrc, psrc_b, ri, scale_cols, dstt, dst_b):
            s3 = psrc.rearrange("p (h e) -> p h e", h=4)
            rope_t, rope_tb = d["rope"][ri], d["rope_b"][ri]
            cs = rope_t[:, 0:64].unsqueeze(1).to_broadcast([128, 4, 64])
            sn = rope_t[:, 64:128].unsqueeze(1).to_broadcast([128, 4, 64])
            t1, t2 = s3[:, :, 0:64], s3[:, :, 64:128]
            r0, r1 = d["rr"][0], d["rr"][1]
            kr = d["kr"]
            S.op("dve", lambda en: en.tensor_tensor(out=r0[:, :, :], in0=t1, in1=cs, op=ALU.mult), reads=[psrc_b, rope_tb], writes=[d["rr_b"][0]])
            S.op("dve", lambda en: en.tensor_tensor(out=r1[:, :, :], in0=t2, in1=sn, op=ALU.mult), reads=[psrc_b, rope_tb], writes=[d["rr_b"][1]])
            S.op("dve", lambda en: en.tensor_tensor(out=kr[:, :, 0:64], in0=r0[:, :, :], in1=r1[:, :, :], op=ALU.subtract),
                 reads=d["rr_b"], writes=[d["kr_b"]])
            S.op("dve", lambda en: en.tensor_tensor(out=r0[:, :, :], in0=t2, in1=cs, op=ALU.mult), reads=[psrc_b, rope_tb], writes=[d["rr_b"][0]])
            S.op("dve", lambda en: en.tensor_tensor(out=r1[:, :, :], in0=t1, in1=sn, op=ALU.mult), reads=[psrc_b, rope_tb], writes=[d["rr_b"][1]])
            S.op("dve", lambda en: en.tensor_tensor(out=kr[:, :, 64:128], in0=r0[:, :, :], in1=r1[:, :, :], op=ALU.add),
                 reads=d["rr_b"], writes=[d["kr_b"]])
            S.op("dve", lambda en: en.tensor_tensor(out=dstt[:, :].rearrange("p (h e) -> p h e", h=4), in0=kr[:, :, :],
                                                    in1=otab[:, scale_cols[0]:scale_cols[1]].unsqueeze(2).to_broadcast([128, 4, 128]),
                                                    op=ALU.mult), reads=[d["kr_b"], otab_b], writes=[dst_b])

        def ret_kv(d, t, chunk_idx):
            Wr, Wr_b, off = d["Wr"], d["Wr_b"], d["off"]
            ri = chunk_idx % 2
            S.dma("sp", d["rope"][ri][:, :], dr["rope_d"][:, chunk_idx, :], writes=[d["rope_b"][ri]])
            pk, pkb = c.ps.get()
            for k in range(8):
                S.op("pe", lambda en, k=k: en.matmul(pk[:, :], lhsT=xT[:, k, t * 128:(t + 1) * 128],
                                                     rhs=Wr[:, k, off["k"]:off["k"] + 512], start=(k == 0), stop=(k == 7)),
                     reads=[Wr_b, xT_b], writes=[pkb])
            ret_rope(d, pk[:, :], pkb, ri, (4, 8), d["kp"], d["kp_b"])
            for hf in range(2):
                pv, pvb = c.ps.get()
                for k in range(8):
                    S.op("pe", lambda en, k=k, hf=hf, pv=pv: en.matmul(
                        pv[:, :], lhsT=xT[:, k, t * 128:(t + 1) * 128],
                        rhs=Wr[:, k, off["v"] + hf * 512:off["v"] + (hf + 1) * 512], start=(k == 0), stop=(k == 7)),
                        reads=[Wr_b, xT_b], writes=[pvb])
                S.op("act", lambda en, hf=hf, pv=pv: en.copy(out=d["vb"][:, hf * 512:(hf + 1) * 512], in_=pv[:, :]),
                     reads=[pvb], writes=[d["vb_b"]])

        def ret_state_mm(d):
            pkv = [c.ps.get(), c.ps.get()]
            for h in range(4):
                pt, ptb_ = pkv[h // 2]
                S.op("pe", lambda en, h=h, pt=pt: en.matmul(pt[:, (h % 2) * 256:(h % 2) * 256 + 256], lhsT=d["kp"][:, h * 128:(h + 1) * 128],
                                                            rhs=d["vb"][:, h * 256:(h + 1) * 256], start=True, stop=True),
                     reads=[d["kp_b"], d["vb_b"]], writes=[ptb_])
            return pkv

        def ret_state_update(pkv):
            for hf in range(2):
                S.op("dve", lambda en, hf=hf: en.tensor_tensor(out=Rst[:, hf * 512:(hf + 1) * 512], in0=Rst[:, hf * 512:(hf + 1) * 512],
                                                               in1=pkv[hf][0][:, :], op=ALU.add), reads=[Rst_b, pkv[hf][1]], writes=[Rst_b])
            S.op("dve", lambda en: en.tensor_tensor(out=Rst[:, :].rearrange("p (h v) -> p h v", h=4),
                                                    in0=Rst[:, :].rearrange("p (h v) -> p h v", h=4),
                                                    in1=otab[:, 8:12].unsqueeze(2).to_broadcast([128, 4, 256]), op=ALU.mult),
                 reads=[Rst_b, otab_b], writes=[Rst_b])

        def zero_states():
            S.op("dve", lambda en: en.memset(Sst[:, :], 0.0), writes=[Sst_b])
            S.op("dve", lambda en: en.memset(Rst[:, :], 0.0), writes=[Rst_b])
            S.op("dve", lambda en: en.memset(Atot[:, :], 0.0), writes=[Atot_b])

        xbcd_b = [Buf(f"xbcd{g}") for g in range(ngroups)]
        smgd_b = [Buf(f"smgd{g}") for g in range(ngroups)]
        kpd_b = [Buf(f"kpd{i}") for i in range(nchunks)]
        vbd_b = [Buf(f"vbd{i}") for i in range(nchunks)]
        zero_states()
        with ExitStack() as es:
            ds = ssd_setup(es)
            dq = ret_setup(es, False)
            load_xT(0, xTh, xTh_b, 0, halo)
            ssd_features(ds, xTh, xTh_b, 128, True)
            for g in range(ngroups):
                xT, xT_b = xT_r[g % 2]
                smg, smg_b = smg_r[g % 2]
                for t in range(4):
                    load_xT(g * GT + t * 128, xT, xT_b, t, src)
                rot(ds, ("xbcT",), g)
                ssd_features(ds, xT, xT_b, GT, False)
                ssd_dt_group(ds)
                S.dma("sp", c.xbc_d[:, :, g * GT:(g + 1) * GT].rearrange("c p t -> p c t"), ds["xbcT"][:, :, :],
                      reads=ds["xbcT_b"], writes=[xbcd_b[g]])
                S.dma("sp", c.smg_d[g, :, :], smg[:, :, :].rearrange("p a b -> p (a b)"), reads=[smg_b], writes=[smgd_b[g]])
                for t in range(4):
                    ci = g * 4 + t
                    smc, smc_b = smc_r[ci % 3]
                    rot(ds, ("Btok", "xdte"), t)
                    rot(dq, ("rr", "kr", "kp", "vb"), t)
                    ssd_chunk_scalars(t)
                    pst = ssd_tok_and_state(ds, t)
                    ssd_state_update(ds, pst)
                    ret_kv(dq, t, ci)
                    S.dma("sp", c.kp_d[ci * 128:(ci + 1) * 128, :], dq["kp"][:, :], reads=[dq["kp_b"]], writes=[kpd_b[ci]])
                    S.dma("sp", c.vb_d[ci * 128:(ci + 1) * 128, :], dq["vb"][:, :], reads=[dq["vb_b"]], writes=[vbd_b[ci]])
                    pkv = ret_state_mm(dq)
                    ret_state_update(pkv)
        S.barrier()
        (loc_s, loc_r), (all_s, all_r) = c.st_loc[o], c.st_all[o]
        locb = [Buf("loc_s"), Buf("loc_r")]
        allb = [Buf("all_s"), Buf("all_r")]
        S.dma("sp", loc_s[:, 0:1024], Sst[:, :], reads=[Sst_b], writes=[locb[0]])
        S.dma("sp", loc_s[:, 1024:1040], Atot[:, :], reads=[Atot_b], writes=[locb[0]])
        S.dma("sp", loc_r[:, :], Rst[:, :], reads=[Rst_b], writes=[locb[1]])
        for (lo, al, lb, ab_) in ((loc_s, all_s, locb[0], allb[0]), (loc_r, all_r, locb[1], allb[1])):
            if c.use_cc:
                S.op("pool", lambda en, lo=lo, al=al: en.collective_compute(
                    "AllGather", ALU.bypass, replica_groups=[[0, 1, 2, 3], [4, 5, 6, 7]], ins=[lo], outs=[al]),
                    reads=[lb], writes=[ab_])
            else:
                for i in range(4):
                    S.dma("sp", al[i * 128:(i + 1) * 128, :], lo, reads=[lb], writes=[ab_])
        with ExitStack() as es:
            sb = lambda name, shape, dt: _sb(c, es, "c_" + name, shape, dt)
            rec = [sb(f"rec{i}", [128, 1040], F32) for i in range(2)]; rec_b = [Buf(f"rec{i}") for i in range(2)]
            rer = [sb(f"rer{i}", [128, 1024], F32) for i in range(2)]; rer_b = [Buf(f"rer{i}") for i in range(2)]
            cf = sb("cf", [128, 16], F32); cf_b = Buf("cf")
            zero_states()
            for i in range(4):
                rb, rbb = rec[i % 2], rec_b[i % 2]
                rr_, rrb = rer[i % 2], rer_b[i % 2]
                S.dma("sp", rb[:, :], all_s[i * 128:(i + 1) * 128, :], reads=[allb[0]], writes=[rbb])
                S.dma("sp", rr_[:, :], all_r[i * 128:(i + 1) * 128, :], reads=[allb[1]], writes=[rrb])
                S.op("act", lambda en, rb=rb: en.activation(out=cf[:, :], in_=rb[:, 1024:1040], func=AF.Exp), reads=[rbb], writes=[cf_b])
                S.op("dve", lambda en, i=i: en.tensor_scalar(out=cf[:, :], in0=cf[:, :], scalar1=-1.0, scalar2=otab[:, 28 + i:29 + i],
                                                             op0=ALU.add, op1=ALU.mult), reads=[cf_b, otab_b], writes=[cf_b])
                S.op("dve", lambda en: en.tensor_scalar_add(out=cf[:, :], in0=cf[:, :], scalar1=1.0), reads=[cf_b], writes=[cf_b])
                S.op("dve", lambda en: en.tensor_tensor(out=Sst[:, :].rearrange("p (h q) -> p h q", h=16),
                                                        in0=Sst[:, :].rearrange("p (h q) -> p h q", h=16),
                                                        in1=cf[:, :].unsqueeze(2).to_broadcast([128, 16, 64]), op=ALU.mult),
                     reads=[Sst_b, cf_b], writes=[Sst_b])
                S.op("dve", lambda en, i=i, rb=rb: en.scalar_tensor_tensor(out=Sst[:, :], in0=rb[:, 0:1024], scalar=otab[:, 28 + i:29 + i],
                                                                           in1=Sst[:, :], op0=ALU.mult, op1=ALU.add),
                     reads=[rbb, otab_b, Sst_b], writes=[Sst_b])
                S.op("dve", lambda en, i=i, rr_=rr_: en.tensor_tensor(
                    out=rr_[:, :].rearrange("p (h v) -> p h v", h=4), in0=rr_[:, :].rearrange("p (h v) -> p h v", h=4),
                    in1=otab[:, 12 + 4 * i:16 + 4 * i].unsqueeze(2).to_broadcast([128, 4, 256]), op=ALU.mult),
                    reads=[rrb, otab_b], writes=[rrb])
                S.op("dve", lambda en, rr_=rr_: en.tensor_tensor(out=Rst[:, :], in0=Rst[:, :], in1=rr_[:, :], op=ALU.add),
                     reads=[rrb, Rst_b], writes=[Rst_b])
        S.barrier()

        ysT_d = c.ysT_d
        ysd_b = [Buf(f"ysd{i}") for i in range(nchunks)]
        with ExitStack() as es:
            ds = ssd_setup(es, need_w=False)
            sb = lambda name, shape, dt: _sb(c, es, "b_" + name, shape, dt)
            Wz, Wz_b = wload(es, "z", OZ, 1024)
            R2 = lambda nm, shape, dt: [(sb(f"{nm}{i}", shape, dt), Buf(f"{nm}{i}")) for i in range(2)]
            sz_r = R2("sz", [128, 1024], F32)
            thz_r = R2("thz", [128, 1024], F32)
            Sbf_r = R2("Sbf", [128, 1024], BF16)
            cbm_r = R2("cbm", [128, 2, 128], F32)
            Xs_r = R2("Xs", [128, 8, 128], F32)
            ET_r = R2("ET", [128, 8, 128], F32)
            PT_r = [(sb(f"PT{i}", [128, 16, 128], BF16), [Buf(f"PT{i}_{g}") for g in range(2)]) for i in range(2)]
            xdt_r = R2("xdt", [128, 1024], BF16)
            xsD_r = R2("xsD", [128, 1024], BF16)
            yv_r = R2("yv", [128, 1024], F32)
            ysb_r = R2("ysb", [128, 1024], BF16)
            rms_r = R2("rms", [128, 8], F32)
            ysT = [sb(f"ysT{i}", [128, 8, 128], BF16) for i in range(2)]; ysT_b = [Buf(f"ysT{i}") for i in range(2)]
            S.op("dve", lambda en: en.memset(Atot[:, :], 0.0), writes=[Atot_b])
            for g in range(ngroups):
                xT, xT_b = xT_r[g % 2]
                smg, smg_b = smg_r[g % 2]
                for t in range(4):
                    load_xT(g * GT + t * 128, xT, xT_b, t, src)
                rot(ds, ("xbcT",), g)
                S.dma("sp", ds["xbcT"][:, :, :], c.xbc_d[:, :, g * GT:(g + 1) * GT].rearrange("c p t -> p c t"),
                      reads=[xbcd_b[g]], writes=ds["xbcT_b"])
                S.dma("sp", smg[:, :, :].rearrange("p a b -> p (a b)"), c.smg_d[g, :, :], reads=[smgd_b[g]], writes=[smg_b])
                xbcT, xbcT_b = ds["xbcT"], ds["xbcT_b"]
                for t in range(4):
                    ci = g * 4 + t
                    tc0 = t * 128
                    smc, smc_b = smc_r[ci % 3]
                    rot(ds, ("Btok", "xdte"), ci)
                    sz, sz_b = sz_r[ci % 2]; thz, thz_b = thz_r[ci % 2]; Sbf, Sbf_b = Sbf_r[ci % 2]
                    cbm, cbm_b = cbm_r[ci % 2]; PT, PT_b = PT_r[ci % 2]; xdt, xdt_b = xdt_r[ci % 2]
                    xsD, xsD_b = xsD_r[ci % 2]; yv, yv_b = yv_r[ci % 2]; ysb, ysb_b = ysb_r[ci % 2]
                    rms, rms_b = rms_r[ci % 2]
                    ssd_chunk_scalars(t)
                    S.op("act", lambda en: en.copy(out=Sbf[:, :], in_=Sst[:, :]), reads=[Sst_b], writes=[Sbf_b])
                    for hf in range(2):
                        pz, pzb = c.ps.get()
                        for k in range(8):
                            S.op("pe", lambda en, k=k, hf=hf, pz=pz: en.matmul(
                                pz[:, :], lhsT=xT[:, k, tc0:tc0 + 128], rhs=Wz[:, k, hf * 512:(hf + 1) * 512],
                                start=(k == 0), stop=(k == 7)), reads=[Wz_b, xT_b], writes=[pzb])
                        S.op("act", lambda en, hf=hf, pz=pz: en.activation(out=thz[:, hf * 512:(hf + 1) * 512], in_=pz[:, :],
                                                                            func=AF.Tanh, scale=0.5), reads=[pzb], writes=[thz_b])
                        S.op("dve", lambda en, hf=hf, pz=pz: en.scalar_tensor_tensor(
                            out=sz[:, hf * 512:(hf + 1) * 512], in0=thz[:, hf * 512:(hf + 1) * 512], scalar=1.0, in1=pz[:, :],
                            op0=ALU.add, op1=ALU.mult), reads=[thz_b, pzb], writes=[sz_b])
                    pcb, pcbb = c.ps.get()
                    for gg in range(2):
                        S.op("pe", lambda en, gg=gg: en.matmul(pcb[:, gg * 128:(gg + 1) * 128], lhsT=xbcT[:, 8 + gg, tc0:tc0 + 128],
                                                               rhs=xbcT[:, 10 + gg, tc0:tc0 + 128], start=True, stop=True),
                             reads=[xbcT_b[8 + gg], xbcT_b[10 + gg]], writes=[pcbb])
                    S.op("dve", lambda en: en.tensor_tensor(out=cbm[:, :, :], in0=pcb[:, 0:256].rearrange("p (g l) -> p g l", g=2),
                                                            in1=m01[:, :].unsqueeze(1).to_broadcast([128, 2, 128]), op=ALU.mult),
                         reads=[pcbb, tri_b], writes=[cbm_b])

                    def xs_extra(xs3, pxb):
                        S.op("dve", lambda en: en.tensor_tensor(out=xdt[:, :].rearrange("p (h q) -> p h q", h=16), in0=xs3,
                                                                in1=smg[:, t, :].unsqueeze(2).to_broadcast([128, 16, 64]), op=ALU.mult),
                             reads=[pxb, smg_b], writes=[xdt_b])
                        S.op("dve", lambda en: en.tensor_tensor(out=xsD[:, :].rearrange("p (h q) -> p h q", h=16), in0=xs3,
                                                                in1=ptab[:, 32:48].unsqueeze(2).to_broadcast([128, 16, 64]), op=ALU.mult),
                             reads=[pxb, ptab_b], writes=[xsD_b])
                    pst = ssd_tok_and_state(ds, t, xs_extra)
                    for gg in range(2):
                        Xs, Xs_b = Xs_r[gg]
                        ET, ET_b = ET_r[gg]
                        S.op("dve", lambda en, gg=gg: en.tensor_tensor(
                            out=Xs[:, :, :], in0=smg[:, 4 + t, gg * 8:(gg + 1) * 8].unsqueeze(2).to_broadcast([128, 8, 128]),
                            in1=triu[:, :].unsqueeze(1).to_broadcast([128, 8, 128]), op=ALU.mult),
                            reads=[smg_b, tri_b], writes=[Xs_b])
                        pseg = [c.ps.get(), c.ps.get()]
                        for q in range(2):
                            S.op("pe", lambda en, q=q, pseg=pseg: en.matmul(
                                pseg[q][0][:, :], lhsT=trisl[:, :], rhs=Xs[:, q * 4:(q + 1) * 4, :].rearrange("p h l -> p (h l)"),
                                start=True, stop=True), reads=[tri_b, Xs_b], writes=[pseg[q][1]])
                            S.op("act", lambda en, q=q, pseg=pseg: en.activation(
                                out=ET[:, q * 4:(q + 1) * 4, :].rearrange("p h l -> p (h l)"), in_=pseg[q][0][:, :], func=AF.Exp),
                                reads=[pseg[q][1]], writes=[ET_b])
                        S.op("dve", lambda en, gg=gg: en.tensor_tensor(
                            out=PT[:, gg * 8:(gg + 1) * 8, :], in0=ET[:, :, :],
                            in1=cbm[:, gg, :].unsqueeze(1).to_broadcast([128, 8, 128]), op=ALU.mult),
                            reads=[ET_b, cbm_b], writes=[PT_b[gg]])
                    for gg in range(2):
                        po, pob = c.ps.get()
                        S.op("pe", lambda en, gg=gg, po=po: en.matmul(po[:, :], lhsT=xbcT[:, 10 + gg, tc0:tc0 + 128],
                                                                      rhs=Sbf[:, gg * 512:(gg + 1) * 512], start=True, stop=True),
                             reads=[xbcT_b[10 + gg], Sbf_b], writes=[pob])
                        S.op("dve", lambda en, gg=gg, po=po: en.tensor_tensor(
                            out=yv[:, gg * 512:(gg + 1) * 512].rearrange("p (h q) -> p h q", h=8),
                            in0=po[:, :].rearrange("p (h q) -> p h q", h=8),
                            in1=smc[:, 1, gg * 8:(gg + 1) * 8].unsqueeze(2).to_broadcast([128, 8, 64]), op=ALU.mult),
                            reads=[pob, smc_b], writes=[yv_b])
                    ssd_state_update(ds, pst)
                    for gg in range(2):
                        pd_, pdb_ = c.ps.get()
                        S.op("pe", lambda en, gg=gg, pd_=pd_: en.matmul(pd_[:, :], lhsT=c.ident_bf[:, :], rhs=xsD[:, gg * 512:(gg + 1) * 512],
                                                                        start=True, stop=False), reads=[c.ident_buf, xsD_b], writes=[pdb_])
                        for hh in range(8):
                            h = gg * 8 + hh
                            S.op("pe", lambda en, h=h, hh=hh, pd_=pd_: en.matmul(
                                pd_[:, hh * 64:(hh + 1) * 64], lhsT=PT[:, h, :], rhs=xdt[:, h * 64:(h + 1) * 64],
                                start=False, stop=(hh == 7)), reads=[PT_b[gg], xdt_b], writes=[pdb_])
                        S.op("dve", lambda en, gg=gg, pd_=pd_: en.tensor_tensor(out=yv[:, gg * 512:(gg + 1) * 512],
                                                                                in0=yv[:, gg * 512:(gg + 1) * 512], in1=pd_[:, :], op=ALU.add),
                             reads=[yv_b, pdb_], writes=[yv_b])
                    S.op("dve", lambda en: en.scalar_tensor_tensor(out=yv[:, :], in0=yv[:, :], scalar=0.5, in1=sz[:, :],
                                                                   op0=ALU.mult, op1=ALU.mult), reads=[yv_b, sz_b], writes=[yv_b])
                    for gg in range(2):
                        S.op("act", lambda en, gg=gg: en.activation(out=thz[:, gg * 512:(gg + 1) * 512], in_=yv[:, gg * 512:(gg + 1) * 512],
                                                                    func=AF.Square, accum_out=rms[:, gg:gg + 1]),
                             reads=[yv_b], writes=[thz_b, rms_b])
                    S.op("dve", lambda en: en.tensor_scalar(out=rms[:, 2:4], in0=rms[:, 0:2], scalar1=1.0 / 512.0, scalar2=float(EPS),
                                                            op0=ALU.mult, op1=ALU.add), reads=[rms_b], writes=[rms_b])
                    S.op("act", lambda en: en.activation(out=rms[:, 2:4], in_=rms[:, 2:4], func=AF.Sqrt), reads=[rms_b], writes=[rms_b])
                    S.op("dve", lambda en: en.reciprocal(out=rms[:, 2:4], in_=rms[:, 2:4]), reads=[rms_b], writes=[rms_b])
                    S.op("dve", lambda en: en.tensor_tensor(out=ysb[:, :].rearrange("p (g q) -> p g q", g=2),
                                                            in0=yv[:, :].rearrange("p (g q) -> p g q", g=2),
                                                            in1=rms[:, 2:4].unsqueeze(2).to_broadcast([128, 2, 512]), op=ALU.mult),
                         reads=[yv_b, rms_b], writes=[ysb_b])
                    pt, ptb_ = c.ps.get()
                    ptv = ps_bf(pt)
                    for j in range(8):
                        S.op("pe", lambda en, j=j, ptv=ptv: en.transpose(out=ptv[:, j * 128:(j + 1) * 128], in_=ysb[:, j * 128:(j + 1) * 128],
                                                                         identity=c.ident_bf[:, :]), reads=[ysb_b, c.ident_buf], writes=[ptb_])
                    yi = ci % 2
                    S.op("act", lambda en, yi=yi, ptv=ptv: en.copy(out=ysT[yi][:, :, :], in_=ptv[:, :].rearrange("p (j t) -> p j t", j=8)),
                         reads=[ptb_], writes=[ysT_b[yi]])
                    S.dma("sp", ysT_d[:, :, ci * 128:(ci + 1) * 128].rearrange("k p t -> p k t"), ysT[yi][:, :, :],
                          reads=[ysT_b[yi]], writes=[ysd_b[ci]])
        S.barrier()

        with ExitStack() as es:
            dq = ret_setup(es, True)
            sb = lambda name, shape, dt: _sb(c, es, "d_" + name, shape, dt)
            Wr, Wr_b, off = dq["Wr"], dq["Wr_b"], dq["off"]
            Wout = sb("wout", [128, 16, 1024], BF16); Wout_b = Buf("owout")
            ng = sb("ng", [128, 8], F32); ng_b = Buf("ng")
            gng = sb("gng", [128, 1024], F32); gnb = sb("gnb", [128, 1024], F32); gn_b = Buf("gn")
            g_t = sb("g", [128, 1024], F32); b_t = sb("b", [128, 1024], F32); gb_buf = Buf("ogb")
            R2 = lambda nm, shape, dt: [(sb(f"{nm}{i}", shape, dt), Buf(f"{nm}{i}")) for i in range(2)]
            qp_r = R2("qp", [128, 512], BF16)
            qT_r = R2("qT", [128, 4, 128], BF16)
            kT_r = R2("kT", [128, 4, 128], BF16)
            scT_r = R2("scT", [128, 4, 128], BF16)
            yrT_r = R2("yrT", [128, 8, 128], BF16)
            sg = sb("sg", [128, 1024], F32); sg_b = Buf("sg")
            thg = sb("thg", [128, 1024], F32); thg_b = Buf("thg")
            Rbf = sb("Rbf", [128, 1024], BF16); Rbf_b = Buf("Rbf")
            yr = sb("yr", [128, 1024], F32); yr_b = Buf("yr")
            yrb = sb("yrb", [128, 1024], BF16); yrb_b = Buf("yrb")
            ysl = [sb(f"ysl{i}", [128, 8, 128], BF16) for i in range(2)]; ysl_b = [Buf(f"ysl{i}") for i in range(2)]
            gst = sb("gst", [128, 4, 6], F32); gmv = sb("gmv", [128, 4, 4], F32); gs_b = Buf("gs")
            v_r = R2("v", [128, 1024], F32)
            st_r = R2("st", [128, 12], F32)
            mv_r = [sb(f"mv{i}", [128, 4], F32) for i in range(2)]
            xres = sb("xres", [128, 1024], F32); xres_b = Buf("xres")
            S.dma("pool", Wout[:, 0:8, :], dr[f"od_w_out{o}"][0:1024, :].rearrange("(k p) n -> p k n", p=128), writes=[Wout_b])
            S.dma("pool", Wout[:, 8:16, :], dr[f"od_w_out{o}"][1024:2048, :].rearrange("(k p) n -> p k n", p=128), writes=[Wout_b])
            S.dma("sp", ng[:, :], dr[f"od_ssm_norm_g{o}"].rearrange("o (c p) -> p (o c)", p=128), writes=[ng_b], allow_slow_non_contiguous=True)
            load_bcast_row(c, "sp", gng, gn_b, dr[f"od_ret_gn_g{o}"])
            load_bcast_row(c, "sp", gnb, gn_b, dr[f"od_ret_gn_b{o}"])
            load_bcast_row(c, "sp", g_t, gb_buf, dr[f"ln_mix_g{layer}"])
            load_bcast_row(c, "sp", b_t, gb_buf, dr[f"ln_mix_b{layer}"])
            for kc in range(8):
                S.op("dve", lambda en, kc=kc: en.tensor_scalar_mul(out=Wout[:, kc, :], in0=Wout[:, kc, :], scalar1=ng[:, kc:kc + 1]),
                     reads=[Wout_b, ng_b], writes=[Wout_b])
            for g in range(ngroups):
                xT, xT_b = xT_r[g % 2]
                for t in range(4):
                    load_xT(g * GT + t * 128, xT, xT_b, t, src)
                for t in range(4):
                    ci = g * 4 + t
                    tc0 = t * 128
                    ri = ci % 2
                    yi = ci % 2
                    rot(dq, ("rr", "kr", "kp", "vb"), ci)
                    qp, qp_b = qp_r[ci % 2]; qT, qT_b = qT_r[ci % 2]; kT, kT_b = kT_r[ci % 2]
                    scT, scT_b = scT_r[ci % 2]; yrT, yrT_b = yrT_r[ci % 2]
                    S.dma("sp", ysl[yi][:, :, :], ysT_d[:, :, ci * 128:(ci + 1) * 128].rearrange("k p t -> p k t"),
                          reads=[ysd_b[ci]], writes=[ysl_b[yi]])
                    S.dma("sp", dq["rope"][ri][:, :], dr["rope_d"][:, ci, :], writes=[dq["rope_b"][ri]])
                    S.dma("sp", dq["kp"][:, :], c.kp_d[ci * 128:(ci + 1) * 128, :], reads=[kpd_b[ci]], writes=[dq["kp_b"]])
                    S.dma("sp", dq["vb"][:, :], c.vb_d[ci * 128:(ci + 1) * 128, :], reads=[vbd_b[ci]], writes=[dq["vb_b"]])
                    pq, pqb = c.ps.get()
                    for k in range(8):
                        S.op("pe", lambda en, k=k: en.matmul(pq[:, :], lhsT=xT[:, k, tc0:tc0 + 128], rhs=Wr[:, k, 0:512],
                                                             start=(k == 0), stop=(k == 7)), reads=[Wr_b, xT_b], writes=[pqb])
                    ret_rope(dq, pq[:, :], pqb, ri, (0, 4), qp, qp_b)
                    for (srcp, srcp_b, dT, dT_b) in ((qp, qp_b, qT, qT_b), (dq["kp"], dq["kp_b"], kT, kT_b)):
                        pt, ptb_ = c.ps.get()
                        ptv = ps_bf(pt)
                        for j in range(4):
                            S.op("pe", lambda en, j=j, ptv=ptv, srcp=srcp: en.transpose(
                                out=ptv[:, j * 128:(j + 1) * 128], in_=srcp[:, j * 128:(j + 1) * 128], identity=c.ident_bf[:, :]),
                                reads=[srcp_b, c.ident_buf], writes=[ptb_])
                        S.op("act", lambda en, ptv=ptv, dT=dT: en.copy(out=dT[:, :, :], in_=ptv[:, 0:512].rearrange("p (j t) -> p j t", j=4)),
                             reads=[ptb_], writes=[dT_b])
                    for hf in range(2):
                        pg, pgb = c.ps.get()
                        for k in range(8):
                            S.op("pe", lambda en, k=k, hf=hf, pg=pg: en.matmul(
                                pg[:, :], lhsT=xT[:, k, tc0:tc0 + 128], rhs=Wr[:, k, 512 + hf * 512:512 + (hf + 1) * 512],
                                start=(k == 0), stop=(k == 7)), reads=[Wr_b, xT_b], writes=[pgb])
                        S.op("act", lambda en, hf=hf, pg=pg: en.activation(out=thg[:, hf * 512:(hf + 1) * 512], in_=pg[:, :],
                                                                            func=AF.Tanh, scale=0.5), reads=[pgb], writes=[thg_b])
                        S.op("dve", lambda en, hf=hf, pg=pg: en.scalar_tensor_tensor(
                            out=sg[:, hf * 512:(hf + 1) * 512], in0=thg[:, hf * 512:(hf + 1) * 512], scalar=1.0, in1=pg[:, :],
                            op0=ALU.add, op1=ALU.mult), reads=[thg_b, pgb], writes=[sg_b])
                    psc, pscb = c.ps.get()
                    for h in range(4):
                        S.op("pe", lambda en, h=h: en.matmul(psc[:, h * 128:(h + 1) * 128], lhsT=kT[:, h, :], rhs=qT[:, h, :],
                                                             start=True, stop=True), reads=[kT_b, qT_b], writes=[pscb])
                    S.op("dve", lambda en: en.tensor_tensor(out=scT[:, :, :], in0=psc[:, :].rearrange("p (h l) -> p h l", h=4),
                                                            in1=m01[:, :].unsqueeze(1).to_broadcast([128, 4, 128]), op=ALU.mult),
                         reads=[pscb, tri_b], writes=[scT_b])
                    S.op("act", lambda en: en.copy(out=Rbf[:, :], in_=Rst[:, :]), reads=[Rst_b], writes=[Rbf_b])
                    py = [c.ps.get(), c.ps.get()]
                    for h in range(4):
                        pt, ptb_ = py[h // 2]
                        cs_ = slice((h % 2) * 256, (h % 2) * 256 + 256)
                        S.op("pe", lambda en, h=h, pt=pt, cs_=cs_: en.matmul(pt[:, cs_], lhsT=scT[:, h, :], rhs=dq["vb"][:, h * 256:(h + 1) * 256],
                                                                             start=True, stop=False), reads=[scT_b, dq["vb_b"]], writes=[ptb_])
                        S.op("pe", lambda en, h=h, pt=pt, cs_=cs_: en.matmul(pt[:, cs_], lhsT=qT[:, h, :], rhs=Rbf[:, h * 256:(h + 1) * 256],
                                                                             start=False, stop=True), reads=[qT_b, Rbf_b], writes=[ptb_])
                    pkv = ret_state_mm(dq)
                    ret_state_update(pkv)
                    for h in range(4):
                        pt, ptb_ = py[h // 2]
                        cs_ = slice((h % 2) * 256, (h % 2) * 256 + 256)
                        S.op("dve", lambda en, h=h, pt=pt, cs_=cs_: en.bn_stats(out=gst[:, h, :], in_=pt[:, cs_]), reads=[ptb_], writes=[gs_b])
                        S.op("dve", lambda en, h=h: en.bn_aggr(out=gmv[:, h, 0:2], in_=gst[:, h, :]), reads=[gs_b], writes=[gs_b])
                    S.op("dve", lambda en: en.tensor_scalar_add(out=gmv[:, :, 2:3], in0=gmv[:, :, 1:2], scalar1=float(EPS)), reads=[gs_b], writes=[gs_b])
                    S.op("act", lambda en: en.activation(out=gmv[:, :, 2:3], in_=gmv[:, :, 2:3], func=AF.Sqrt), reads=[gs_b], writes=[gs_b])
                    S.op("dve", lambda en: en.reciprocal(out=gmv[:, :, 2:3], in_=gmv[:, :, 2:3]), reads=[gs_b], writes=[gs_b])
                    S.op("dve", lambda en: en.scalar_tensor_tensor(out=gmv[:, :, 3:4], in0=gmv[:, :, 0:1], scalar=-1.0, in1=gmv[:, :, 2:3],
                                                                   op0=ALU.mult, op1=ALU.mult), reads=[gs_b], writes=[gs_b])
                    for h in range(4):
                        pt, ptb_ = py[h // 2]
                        cs_ = slice((h % 2) * 256, (h % 2) * 256 + 256)
                        S.op("act", lambda en, h=h, pt=pt, cs_=cs_: en.activation(out=yr[:, h * 256:(h + 1) * 256], in_=pt[:, cs_], func=AF.Identity,
                                                                                  bias=gmv[:, h, 3:4], scale=gmv[:, h, 2:3]),
                             reads=[ptb_, gs_b], writes=[yr_b])
                    S.op("dve", lambda en: en.tensor_tensor(out=yr[:, :], in0=yr[:, :], in1=gng[:, :], op=ALU.mult), reads=[yr_b, gn_b], writes=[yr_b])
                    S.op("dve", lambda en: en.tensor_tensor(out=yr[:, :], in0=yr[:, :], in1=gnb[:, :], op=ALU.add), reads=[yr_b, gn_b], writes=[yr_b])
                    S.op("dve", lambda en: en.scalar_tensor_tensor(out=yrb[:, :], in0=yr[:, :], scalar=0.5, in1=sg[:, :],
                                                                   op0=ALU.mult, op1=ALU.mult), reads=[yr_b, sg_b], writes=[yrb_b])
                    pt, ptb_ = c.ps.get()
                    ptv = ps_bf(pt)
                    for j in range(8):
                        S.op("pe", lambda en, j=j, ptv=ptv: en.transpose(out=ptv[:, j * 128:(j + 1) * 128], in_=yrb[:, j * 128:(j + 1) * 128],
                                                                         identity=c.ident_bf[:, :]), reads=[yrb_b, c.ident_buf], writes=[ptb_])
                    S.op("act", lambda en, ptv=ptv: en.copy(out=yrT[:, :, :], in_=ptv[:, :].rearrange("p (j t) -> p j t", j=8)),
                         reads=[ptb_], writes=[yrT_b])
                    halves = [c.ps.get(), c.ps.get()]
                    for kc in range(16):
                        lt = ysl[yi][:, kc, :] if kc < 8 else yrT[:, kc - 8, :]
                        lb = ysl_b[yi] if kc < 8 else yrT_b
                        for hf in range(2):
                            ph, phb = halves[hf]
                            S.op("pe", lambda en, ph=ph, kc=kc, hf=hf, lt=lt: en.matmul(
                                ph[:, :], lhsT=lt, rhs=Wout[:, kc, hf * 512:(hf + 1) * 512], start=(kc == 0), stop=(kc == 15)),
                                reads=[lb, Wout_b], writes=[phb])
                    r0 = g * GT + t * 128
                    S.dma("sp", xres[:, :], src[r0:r0 + 128, :], writes=[xres_b])
                    xi = ci % 2
                    v, v_b = v_r[xi]
                    st, smm_b = st_r[xi]
                    mv = mv_r[xi]
                    emit_ln_epilogue(c, (v, v_b, st, mv, smm_b), halves, xres, xres_b, g_t, b_t, gb_buf, v, v_b)
                    outs.append(S.dma("sp", dst[r0:r0 + 128, :], v[:, :], reads=[v_b]))
    S.barrier()
    return outs


def emit_halo_exchange(c, xsrc, NT, hidx):
    S, nc = c.S, c.nc
    hl_loc = nc.dram_tensor(f"hl_loc{hidx}", [128, D], F32, kind="Internal").ap()
    hl_all = nc.dram_tensor(f"hl_all{hidx}", [4 * 128, D], F32, kind="Internal").ap()
    halo_d = nc.dram_tensor(f"halo_d{hidx}", [128, D], F32, kind="Internal").ap()
    lb, ab_, hb_ = Buf("hl_loc"), Buf("hl_all"), Buf("halo_d")
    with ExitStack() as es:
        sb = lambda name, shape, dt: _sb(c, es, "h_" + name, shape, dt)
        t0 = sb("t0", [128, D], F32); t0_b = Buf("ht0")
        rec = [sb(f"rec{i}", [128, D], F32) for i in range(2)]; rec_b = [Buf(f"hrec{i}") for i in range(2)]
        acc = sb("acc", [128, D], F32); acc_b = Buf("hacc")
        hsel = sb("hsel", [128, 4], F32); hsel_b = Buf("hsel")
        S.dma("sp", hsel[:, :], c.dram["hsel"], writes=[hsel_b])
        S.dma("sp", t0[:, :], xsrc[NT - 128:NT, :], writes=[t0_b])
        S.dma("sp", hl_loc, t0[:, :], reads=[t0_b], writes=[lb])
        if c.use_cc:
            S.op("pool", lambda en: en.collective_compute("AllGather", ALU.bypass, replica_groups=[[0, 1, 2, 3], [4, 5, 6, 7]],
                                                          ins=[hl_loc], outs=[hl_all]), reads=[lb], writes=[ab_])
        else:
            for i in range(4):
                S.dma("sp", hl_all[i * 128:(i + 1) * 128, :], hl_loc, reads=[lb], writes=[ab_])
        for i in range(4):
            rb, rbb = rec[i % 2], rec_b[i % 2]
            S.dma("sp", rb[:, :], hl_all[i * 128:(i + 1) * 128, :], reads=[ab_], writes=[rbb])
            if i == 0:
                S.op("dve", lambda en, rb=rb: en.tensor_scalar_mul(out=acc[:, :], in0=rb[:, :], scalar1=hsel[:, 0:1]),
                     reads=[rbb, hsel_b], writes=[acc_b])
            else:
                S.op("dve", lambda en, rb=rb, i=i: en.scalar_tensor_tensor(out=acc[:, :], in0=rb[:, :], scalar=hsel[:, i:i + 1],
                                                                           in1=acc[:, :], op0=ALU.mult, op1=ALU.add),
                     reads=[rbb, hsel_b, acc_b], writes=[acc_b])
        S.dma("sp", halo_d, acc[:, :], reads=[acc_b], writes=[hb_])
    S.barrier()
    return halo_d


EVEN_IN = 1792
ODD_IN = 5648

PER_LAYER_SHAPES = {
    "ln_mix_g": [1, D], "ln_mix_b": [1, D], "ln_ffn_g": [1, D], "ln_ffn_b": [1, D],
    "ffn_w_gate": [D, FH], "ffn_w_up": [D, FH], "ffn_w_down": [FH, D],
}
EVEN_SHAPES = {
    "ev_w_in": [D, EVEN_IN], "ev_sinks": [1, 8], "ev_dw_w": [31, 512], "ev_dw_b": [1, 512],
    "ev_cn_g": [1, 512], "ev_cn_b": [1, 512], "ev_w_out": [D, D],
}
ODD_SHAPES = {
    "od_w_in": [D, ODD_IN], "od_conv_w": [4, 1536], "od_conv_b": [1, 1536], "od_dt_bias": [1, 16],
    "od_a_log": [1, 16], "od_d_skip": [1, 16], "od_ssm_norm_g": [1, D], "od_ret_gn_g": [1, D],
    "od_ret_gn_b": [1, D], "od_w_out": [2 * D, D],
}


def stage_inputs(stages):
    need = {"x": None, "ident": [128, 128], "ones": [128, 128]}
    for kind, idx in stages:
        if kind == "ffn":
            for k in ("ln_ffn_g", "ln_ffn_b", "ffn_w_gate", "ffn_w_up", "ffn_w_down"):
                need[f"{k}{idx}"] = PER_LAYER_SHAPES[k]
        elif kind == "even":
            layer = 2 * idx
            for k in ("ln_mix_g", "ln_mix_b"):
                need[f"{k}{layer}"] = PER_LAYER_SHAPES[k]
            for k, s in EVEN_SHAPES.items():
                need[f"{k}{idx}"] = s
            need["amask"] = [128, 256]
            need["amask0"] = [128, 256]
            need["rope_a"] = None
            need["halo"] = [128, D]
        elif kind == "odd":
            layer = 2 * idx + 1
            for k in ("ln_mix_g", "ln_mix_b"):
                need[f"{k}{layer}"] = PER_LAYER_SHAPES[k]
            for k, s in ODD_SHAPES.items():
                need[f"{k}{idx}"] = s
            need["halo"] = [128, D]
            need["odd_tab"] = [128, 32]
            need["triu"] = [128, 128]
            need["trisl"] = [128, 128]
            need["rope_d"] = None
    if sum(1 for k, _ in stages if k in ("even", "odd")) > 1:
        need["hsel"] = [128, 4]
    return need


def build_program(NT, stages, use_cc=True):
    nc = bass.Bass("TRN2", target_bir_lowering=False)
    c = Ctx()
    c.nc = nc
    c.NT = NT
    c.use_cc = use_cc
    c.st_loc = {}
    c.st_all = {}
    for kind, idx in stages:
        if kind == "odd":
            c.st_loc[idx] = (nc.dram_tensor(f"loc_s{idx}", [128, 1040], F32, kind="Internal").ap(),
                             nc.dram_tensor(f"loc_r{idx}", [128, 1024], F32, kind="Internal").ap())
            c.st_all[idx] = (nc.dram_tensor(f"all_s{idx}", [4 * 128, 1040], F32, kind="Internal").ap(),
                             nc.dram_tensor(f"all_r{idx}", [4 * 128, 1024], F32, kind="Internal").ap())
            if not hasattr(c, "ysT_d"):
                c.ysT_d = nc.dram_tensor("ysT_d", [8, 128, NT], BF16, kind="Internal").ap()
                c.xbc_d = nc.dram_tensor("xbc_d", [12, 128, NT], BF16, kind="Internal").ap()
                c.smg_d = nc.dram_tensor("smg_d", [NT // 512, 128, 128], F32, kind="Internal").ap()
                c.kp_d = nc.dram_tensor("kp_d", [NT, 512], BF16, kind="Internal").ap()
                c.vb_d = nc.dram_tensor("vb_d", [NT, 1024], BF16, kind="Internal").ap()
    es = ExitStack()
    c.es = es
    c.dram = {}
    need = stage_inputs(stages)
    need["x"] = [NT, D]
    if "rope_a" in need:
        need["rope_a"] = [128, NT // 128 + 1, 16]
    if "rope_d" in need:
        need["rope_d"] = [128, NT // 128, 128]
    for name, shape in need.items():
        c.dram[name] = nc.dram_tensor(name, list(shape), F32, kind="ExternalInput").ap()
    y = nc.dram_tensor("y", [NT, D], F32, kind="ExternalOutput").ap()
    xa = nc.dram_tensor("xa", [NT, D], F32, kind="Internal").ap()
    xb = nc.dram_tensor("xb", [NT, D], F32, kind="Internal").ap()
    with es:
        c.S = Sched(nc, es)
        c.ps = PsumPool(c)
        c.ident_f = _sb(c, es, "ident_f", [128, 128], F32)
        c.ident_bf = _sb(c, es, "ident_bf", [128, 128], BF16)
        c.ones_f = _sb(c, es, "ones_f", [128, 128], F32)
        c.ident_buf = Buf("ident")
        c.identf_buf = Buf("identf")
        c.ones_buf = Buf("ones")
        c.S.dma("sp", c.ident_f[:, :], c.dram["ident"], writes=[c.identf_buf])
        c.S.dma("sp", c.ones_f[:, :], c.dram["ones"], writes=[c.ones_buf])
        c.S.op("dve", lambda e: e.tensor_copy(out=c.ident_bf[:, :], in_=c.ident_f[:, :]), reads=[c.identf_buf],
               writes=[c.ident_buf])
        cur = c.dram["x"]
        bufs = [xa, xb]
        outs = []
        nmix = 0
        for si, (kind, idx) in enumerate(stages):
            last = si == len(stages) - 1
            dst = y if last else bufs[si % 2]
            if kind in ("even", "odd"):
                halo = c.dram["halo"] if nmix == 0 else emit_halo_exchange(c, cur, NT, nmix)
                nmix += 1
            if kind == "ffn":
                outs = emit_ffn(c, idx, cur, dst, NT)
            elif kind == "even":
                outs = emit_even(c, idx, 2 * idx, cur, dst, halo, NT)
            elif kind == "odd":
                outs = emit_odd(c, idx, 2 * idx + 1, cur, dst, halo, NT)
            else:
                raise NotImplementedError(kind)
            cur = dst
        c.S.emit(final_ops=outs)
    return nc


ROPE_THETA = 500000.0


def rope_table_a(pos0, NT):
    nt = NT // 128 + 1
    pos = (pos0 - 128 + np.arange(nt * 128)).astype(np.float32)
    inv = np.power(np.float32(ROPE_THETA), -np.arange(8, dtype=np.float32) / np.float32(8)).astype(np.float32)
    ang = (pos[:, None] * inv[None, :]).astype(np.float32)
    tab = np.concatenate([np.cos(ang), np.sin(ang)], axis=1).astype(np.float32)
    return np.ascontiguousarray(tab.reshape(nt, 128, 16).transpose(1, 0, 2))


RET_THETA = 10000.0


def rope_table_d(pos0, NT):
    nt = NT // 128
    pos = (pos0 + np.arange(nt * 128)).astype(np.float32)
    inv = (1.0 / np.power(np.float32(RET_THETA), np.linspace(0.0, 1.0, 64, dtype=np.float32))).astype(np.float32)
    ang = (pos[:, None] * inv[None, :]).astype(np.float32)
    tab = np.concatenate([np.cos(ang), np.sin(ang)], axis=1).astype(np.float32)
    return np.ascontiguousarray(tab.reshape(nt, 128, 128).transpose(1, 0, 2))


def odd_table(r, NT):
    h = np.arange(4, dtype=np.float64)
    lg = np.log(1.0 - np.power(2.0, -5.0 - h))
    l = np.arange(128, dtype=np.float64)[:, None]
    t = np.zeros((128, 32), np.float64)
    t[:, 0:4] = np.exp(lg[None, :] * (l + 1.0))
    t[:, 4:8] = np.exp(-lg[None, :] * (l + 1.0)) * (128.0 ** -0.5)
    t[:, 8:12] = np.exp(lg * 128.0)[None, :]
    for i in range(4):
        if i < r:
            t[:, 12 + 4 * i:16 + 4 * i] = np.exp(lg * float(NT * (r - 1 - i)))[None, :]
            t[:, 28 + i] = 1.0
    return t.astype(np.float32)


def attn_masks(first):
    i = np.arange(128)[:, None]
    j = np.arange(256)[None, :]
    valid = (j > i) & (j <= i + 128)
    m = np.where(valid, 0.0, A_MASK_NEG).astype(np.float32)
    m0 = m.copy()
    if first:
        m0[:, :128] = A_MASK_NEG
    return m, m0


def make_in_map(inp, x_shard, halo, cidx, NT, stages):
    need = stage_inputs(stages)
    r = cidx % 4
    m = {}
    for name in need:
        if name == "x":
            m[name] = np.ascontiguousarray(x_shard, dtype=np.float32)
        elif name == "ident":
            m[name] = np.eye(128, dtype=np.float32)
        elif name == "ones":
            m[name] = np.ones((128, 128), dtype=np.float32)
        elif name == "halo":
            m[name] = np.ascontiguousarray(halo, dtype=np.float32)
        elif name == "amask":
            m[name] = attn_masks(False)[0]
        elif name == "amask0":
            m[name] = attn_masks(r == 0)[1]
        elif name == "rope_a":
            m[name] = rope_table_a(r * NT, NT)
        elif name == "rope_d":
            m[name] = rope_table_d(r * NT, NT)
        elif name == "odd_tab":
            m[name] = odd_table(r, NT)
        elif name == "hsel":
            hs = np.zeros((128, 4), np.float32)
            if r > 0:
                hs[:, r - 1] = 1.0
            m[name] = hs
        elif name == "triu":
            m[name] = np.triu(np.ones((128, 128), np.float32))
        elif name == "trisl":
            m[name] = np.tril(np.ones((128, 128), np.float32), -1).T.copy().T if False else (np.arange(128)[:, None] > np.arange(128)[None, :]).astype(np.float32)
        else:
            base = name.rstrip("0123456789")
            idx = int(name[len(base):])
            m[name] = np.ascontiguousarray(np.asarray(inp[base][idx], dtype=np.float32).reshape(need[name]))
    return m


SEQ = 16384
BATCH = 2
FUSED = True
ALL_STAGES = [("even", 0), ("ffn", 0), ("odd", 0), ("ffn", 1), ("even", 1), ("ffn", 2), ("odd", 1), ("ffn", 3)]


def run_stages(inp, x, stages):
    S_ = x.shape[1]
    NT = S_ // 4
    nc = build_program(NT, stages)
    in_maps = []
    for cidx in range(NCORES):
        b, r = cidx // 4, cidx % 4
        halo = x[b, r * NT - 128:r * NT] if r > 0 else np.zeros((128, D), np.float32)
        in_maps.append(make_in_map(inp, x[b, r * NT:(r + 1) * NT], halo, cidx, NT, stages))
    res = run_bass_kernel_spmd(nc, in_maps, core_ids=list(range(NCORES)))
    out = np.empty_like(x)
    for cidx in range(NCORES):
        b, r = cidx // 4, cidx % 4
        out[b, r * NT:(r + 1) * NT] = res.results[cidx]["y"]
    return out


def kernel(**inputs):
    inp = {k: np.asarray(v) for k, v in inputs.items()}
    x = np.ascontiguousarray(inp["x"], dtype=np.float32)
    if FUSED:
        return run_stages(inp, x, ALL_STAGES)
    for li in range(DEPTH):
        x = run_stages(inp, x, ALL_STAGES[2 * li:2 * li + 2])
    return x
```
